# Optimizing a Trainium2 kernel written in Bass

```python
import math
import jax, jax.numpy as jnp
from jax import lax
import numpy as np

D_MODEL = 1024
BATCH = 16
SEQ = 2048
DEPTH = 4

GRID_W = 64
CTX_LEN = 256
D_CHUNK = D_MODEL // 4
D_ATTN = D_MODEL // 2
D_LRU = D_MODEL // 4
D_MIX = D_CHUNK + D_ATTN + D_LRU
CHUNK = 128
A_GROUPS = 4
A_GDIM = D_CHUNK // A_GROUPS
HEAD_DIM = 64
N_HEADS = D_ATTN // HEAD_DIM
N_KV_HEADS = N_HEADS // 4
GQA_GROUP = N_HEADS // N_KV_HEADS
D_KV = N_KV_HEADS * HEAD_DIM
Q_BLOCK = 128
ROPE_THETA = 10000.0
LRU_BLOCKS = 4
LRU_BDIM = D_LRU // LRU_BLOCKS
CONV_W = 4
LRU_C = 8.0
N_DIR = 2
IN_SPLITS = (D_CHUNK, D_CHUNK, D_CHUNK,
             D_ATTN, D_KV, D_KV, D_ATTN,
             D_LRU, D_LRU)
IN_OFFSETS = tuple(int(s) for s in np.cumsum(IN_SPLITS)[:-1])
D_IN = int(sum(IN_SPLITS))
ALPHA = (2.0 * DEPTH) ** 0.25
BETA = (8.0 * DEPTH) ** -0.25
LN_EPS = 1e-6
RMS_EPS = 1e-6

kernel_name = "hybrid_parallel_groups_dit_block"


def layer_norm(x, g, b):
    xf = x.astype(jnp.float32)
    mu = jnp.mean(xf, -1, keepdims=True)
    var = jnp.mean(jnp.square(xf - mu), -1, keepdims=True)
    return ((xf - mu) * lax.rsqrt(var + LN_EPS) * g + b).astype(x.dtype)


def rms_norm(x, g):
    xf = x.astype(jnp.float32)
    ms = jnp.mean(jnp.square(xf), -1, keepdims=True)
    return (xf * lax.rsqrt(ms + RMS_EPS) * g).astype(x.dtype)


def axial_rope(x, rows, cols):
    half = HEAD_DIM // 2
    nf = half // 2
    inv = ROPE_THETA ** (-jnp.arange(nf, dtype=jnp.float32) / nf)

    def rot(xp, p):
        ang = p.astype(jnp.float32)[:, None] * inv
        cos = jnp.cos(ang)[None, :, None, :]
        sin = jnp.sin(ang)[None, :, None, :]
        x1 = xp[..., :nf].astype(jnp.float32)
        x2 = xp[..., nf:].astype(jnp.float32)
        return jnp.concatenate([x1 * cos - x2 * sin, x1 * sin + x2 * cos], -1)

    out = jnp.concatenate([rot(x[..., :half], rows), rot(x[..., half:], cols)], -1)
    return out.astype(x.dtype)


def chunk_gmlp(u, v, g, b, w_s, b_s):
    Bn, L, _ = u.shape
    u = jax.nn.gelu(u)
    v = layer_norm(jax.nn.gelu(v), g, b)
    v = v.reshape(Bn, L // CHUNK, CHUNK, A_GROUPS, A_GDIM)
    s = jnp.einsum('gpq,bnqgc->bnpgc', w_s, v) + b_s.T[:, :, None]
    return u * s.reshape(Bn, L, D_CHUNK)


def blocked_attention(q, k, v):
    Bn, L = q.shape[:2]
    nblk = L // Q_BLOCK
    qb = q.reshape(Bn, nblk, Q_BLOCK, N_KV_HEADS, GQA_GROUP, HEAD_DIM).transpose(1, 0, 2, 3, 4, 5)
    scale = HEAD_DIM ** -0.5

    def one_block(qblk):
        s = jnp.einsum('bqgrd,bkgd->bgrqk', qblk, k).astype(jnp.float32) * scale
        p = jax.nn.softmax(s, axis=-1).astype(v.dtype)
        return jnp.einsum('bgrqk,bkgd->bqgrd', p, v)

    o = lax.map(one_block, qb)
    return o.transpose(1, 0, 2, 3, 4, 5).reshape(Bn, L, D_ATTN)


def centred_dwconv(x, w, b):
    L = x.shape[1]
    left = CONV_W // 2
    xp = jnp.pad(x, ((0, 0), (left, CONV_W - 1 - left), (0, 0)))
    y = b
    for t in range(CONV_W):
        y = y + xp[:, t:t + L] * w[t]
    return y


def rglru_coeffs(x, w_r, b_r, w_i, b_i, lam):
    xb = x.reshape(*x.shape[:-1], LRU_BLOCKS, LRU_BDIM)
    r = jax.nn.sigmoid(jnp.einsum('blhi,hij->blhj', xb, w_r).reshape(x.shape) + b_r)
    i = jax.nn.sigmoid(jnp.einsum('blhi,hij->blhj', xb, w_i).reshape(x.shape) + b_i)
    log_a = -LRU_C * r.astype(jnp.float32) * jax.nn.softplus(-lam.astype(jnp.float32))
    a = jnp.exp(log_a)
    mult = jnp.sqrt(-jnp.expm1(2.0 * log_a))
    return a, mult * (i * x).astype(jnp.float32)


def linear_scan(a, bx, h0):
    bx = bx.at[:, 0].add(a[:, 0] * h0)

    def comb(l, r):
        return (l[0] * r[0], r[0] * l[1] + r[1])

    _, h = lax.associative_scan(comb, (a, bx), axis=1)
    return h


def rglru_direction(xl, xc, w_r, b_r, w_i, b_i, lam, reverse):
    a_c, b_c = rglru_coeffs(xc, w_r, b_r, w_i, b_i, lam)
    a_l, b_l = rglru_coeffs(xl, w_r, b_r, w_i, b_i, lam)
    if reverse:
        a_c, b_c, a_l, b_l = (jnp.flip(t, 1) for t in (a_c, b_c, a_l, b_l))
    h_c = linear_scan(a_c, b_c, jnp.zeros(a_c.shape[::2], jnp.float32))
    h_l = linear_scan(a_l, b_l, h_c[:, -1])
    if reverse:
        h_c, h_l = jnp.flip(h_c, 1), jnp.flip(h_l, 1)
    return h_l, h_c


def hybrid_layer(x, xc, c, c_ctx, rows, cols, w_ada, b_ada, w_in, a_norm_g, a_norm_b, a_ws, a_bs,
                 q_norm_g, k_norm_g, conv_w, conv_b, lru_wr, lru_br, lru_wi, lru_bi, lru_lam,
                 w_o, ln_g, ln_b, with_ctx_out):
    Bn, L, _ = x.shape
    Lc = xc.shape[1]
    shift, scale, gate = jnp.split(jax.nn.silu(c) @ w_ada + b_ada, 3, -1)
    shift_c, scale_c, gate_c = jnp.split(jax.nn.silu(c_ctx) @ w_ada + b_ada, 3, -1)
    z = (x * (1 + scale[:, None]) + shift[:, None]) @ w_in
    zc = (xc * (1 + scale_c) + shift_c) @ w_in
    a_u, a_v, a_g, q, k, v, b_g, r_x, r_g = jnp.split(z, IN_OFFSETS, -1)
    ac_u, ac_v, ac_g, qc, kc, vc, bc_g, rc_x, rc_g = jnp.split(zc, IN_OFFSETS, -1)

    q = axial_rope(rms_norm(q.reshape(Bn, L, N_HEADS, HEAD_DIM), q_norm_g), rows, cols)
    k = axial_rope(rms_norm(k.reshape(Bn, L, N_KV_HEADS, HEAD_DIM), k_norm_g), rows, cols)
    v = v.reshape(Bn, L, N_KV_HEADS, HEAD_DIM)
    kc = rms_norm(kc.reshape(Bn, Lc, N_KV_HEADS, HEAD_DIM), k_norm_g)
    vc = vc.reshape(Bn, Lc, N_KV_HEADS, HEAD_DIM)
    attn = blocked_attention(q, jnp.concatenate([k, kc], 1), jnp.concatenate([v, vc], 1))

    mix_a = chunk_gmlp(a_u, a_v, a_norm_g, a_norm_b, a_ws, a_bs)

    xr = centred_dwconv(r_x, conv_w, conv_b)
    xrc = centred_dwconv(rc_x, conv_w, conv_b)
    h_f, hc_f = rglru_direction(xr, xrc, lru_wr[0], lru_br[0], lru_wi[0], lru_bi[0], lru_lam[0], False)
    h_b, hc_b = rglru_direction(xr, xrc, lru_wr[1], lru_br[1], lru_wi[1], lru_bi[1], lru_lam[1], True)
    lru = (h_f + h_b).astype(x.dtype)

    y = jnp.concatenate([mix_a * jax.nn.silu(a_g), attn * jax.nn.silu(b_g),
                         lru * jax.nn.silu(r_g)], -1) @ w_o
    x_new = layer_norm(ALPHA * x + gate[:, None] * y, ln_g, ln_b)

    if not with_ctx_out:
        return x_new, xc
    qc = rms_norm(qc.reshape(Bn, Lc, N_HEADS, HEAD_DIM), q_norm_g)
    attn_c = blocked_attention(qc, kc, vc)
    mix_ac = chunk_gmlp(ac_u, ac_v, a_norm_g, a_norm_b, a_ws, a_bs)
    lru_c = (hc_f + hc_b).astype(xc.dtype)
    yc = jnp.concatenate([mix_ac * jax.nn.silu(ac_g), attn_c * jax.nn.silu(bc_g),
                          lru_c * jax.nn.silu(rc_g)], -1) @ w_o
    xc_new = layer_norm(ALPHA * xc + gate_c * yc, ln_g, ln_b)
    return x_new, xc_new


def setup_inputs(seed: int = 0) -> dict:
    key = jax.random.key(seed)
    ks = jax.random.split(key, 24)
    f32 = jnp.float32
    nrm = lambda k, shape, s: jax.random.normal(k, shape, f32) * s
    a0 = jax.random.uniform(ks[18], (DEPTH, N_DIR, D_LRU), f32, minval=0.9, maxval=0.999)
    return {
        "x": nrm(ks[0], (BATCH, SEQ, D_MODEL), 1.0),
        "c": nrm(ks[1], (BATCH, D_MODEL), 1.0),
        "ctx": nrm(ks[2], (BATCH, CTX_LEN, D_MODEL), 1.0),
        "c_ctx": nrm(ks[3], (D_MODEL,), 1.0),
        "w_ada": nrm(ks[4], (DEPTH, D_MODEL, 3 * D_MODEL), 0.5 * D_MODEL ** -0.5),
        "b_ada": nrm(ks[5], (DEPTH, 3 * D_MODEL), 0.01),
        "w_in": nrm(ks[6], (DEPTH, D_MODEL, D_IN), D_MODEL ** -0.5),
        "a_norm_g": 1.0 + nrm(ks[7], (DEPTH, D_CHUNK), 0.01),
        "a_norm_b": nrm(ks[8], (DEPTH, D_CHUNK), 0.01),
        "a_ws": nrm(ks[9], (DEPTH, A_GROUPS, CHUNK, CHUNK), CHUNK ** -0.5),
        "a_bs": 1.0 + nrm(ks[10], (DEPTH, A_GROUPS, CHUNK), 0.01),
        "q_norm_g": 1.0 + nrm(ks[11], (DEPTH, HEAD_DIM), 0.01),
        "k_norm_g": 1.0 + nrm(ks[12], (DEPTH, HEAD_DIM), 0.01),
        "conv_w": nrm(ks[13], (DEPTH, CONV_W, D_LRU), CONV_W ** -0.5),
        "conv_b": nrm(ks[14], (DEPTH, D_LRU), 0.01),
        "lru_wr": nrm(ks[15], (DEPTH, N_DIR, LRU_BLOCKS, LRU_BDIM, LRU_BDIM), LRU_BDIM ** -0.5),
        "lru_br": nrm(ks[16], (DEPTH, N_DIR, D_LRU), 0.01),
        "lru_wi": nrm(ks[17], (DEPTH, N_DIR, LRU_BLOCKS, LRU_BDIM, LRU_BDIM), LRU_BDIM ** -0.5),
        "lru_bi": nrm(ks[19], (DEPTH, N_DIR, D_LRU), 0.01),
        "lru_lam": jnp.log(a0) - jnp.log1p(-a0),
        "w_o": nrm(ks[20], (DEPTH, D_MIX, D_MODEL), BETA * D_MIX ** -0.5),
        "ln_g": 1.0 + nrm(ks[21], (DEPTH, D_MODEL), 0.01),
        "ln_b": nrm(ks[22], (DEPTH, D_MODEL), 0.01),
    }


def reference(x, c, ctx, c_ctx, w_ada, b_ada, w_in, a_norm_g, a_norm_b, a_ws, a_bs, q_norm_g,
              k_norm_g, conv_w, conv_b, lru_wr, lru_br, lru_wi, lru_bi, lru_lam, w_o, ln_g, ln_b):
    n_lat = x.shape[1]
    ROWS = n_lat // GRID_W
    rows = jnp.repeat(jnp.arange(ROWS, dtype=jnp.int32), GRID_W)
    cols = jnp.tile(jnp.arange(GRID_W, dtype=jnp.int32), ROWS)
    xl, xc = x, ctx
    for l in range(DEPTH):
        xl, xc = hybrid_layer(
            xl, xc, c, c_ctx, rows, cols, w_ada[l], b_ada[l], w_in[l], a_norm_g[l], a_norm_b[l],
            a_ws[l], a_bs[l], q_norm_g[l], k_norm_g[l], conv_w[l], conv_b[l], lru_wr[l], lru_br[l],
            lru_wi[l], lru_bi[l], lru_lam[l], w_o[l], ln_g[l], ln_b[l], with_ctx_out=(l < DEPTH - 1))
    return xl
```

```python
import math
import numpy as np
from contextlib import ExitStack
import concourse.bass as bass
import concourse.mybir as mybir
from concourse.bass_utils import run_bass_kernel_spmd

F32 = mybir.dt.float32
BF16 = mybir.dt.bfloat16
AF = mybir.ActivationFunctionType
ALU = mybir.AluOpType

D = 1024
L_LAT = 2048
L_CTX = 256
TOK = L_LAT + L_CTX
DEPTH = 4
ALPHA = (2.0 * DEPTH) ** 0.25
LN_EPS = 1e-6
RMS_EPS = 1e-6
NC1 = 896
NC2 = 1792
C_K0, C_K1, C_V, C_RX, C_RG = 0, 128, 256, 384, 640
C_U, C_GA, C_VA, C_Q, C_BG = 0, 256, 512, 768, 1280
NV = 40
V_LNG, V_LNB, V_QG, V_KG, V_CW, V_CB, V_BR, V_BI, V_LAM = 0, 8, 16, 17, 18, 26, 28, 32, 36

ENGS = ("pe", "act", "dve", "pool", "sp")


class Prog:
    def __init__(self, nc, es):
        self.nc = nc
        self.es = es
        self.q = {e: [] for e in ENGS}
        self.cnt = {}
        self.sem = {}
        for e in ENGS:
            self.sem[e] = es.enter_context(nc.semaphore("c_" + e))
            self.cnt[e] = 0
        self.known = {e: {} for e in ENGS}
        self.lastw = {}
        self.readers = {}
        self.pending = {e: False for e in ENGS}
        self.ninst = 0

    def newsem(self, name):
        self.sem[name] = self.es.enter_context(self.nc.semaphore(name))
        self.cnt[name] = 0
        return name

    def _deps(self, eng, reads, writes):
        deps = {}

        def add(sk, v):
            if v > deps.get(sk, 0):
                deps[sk] = v

        for k in reads:
            w = self.lastw.get(k)
            if w is not None:
                add(*w)
        for k in writes:
            w = self.lastw.get(k)
            if w is not None and w[0] != eng:
                add(*w)
            for sk, v in self.readers.get(k, {}).items():
                if sk != eng:
                    add(sk, v)
        if eng == "pe":
            deps.pop("pe", None)
        return deps

    def _emit_waits(self, eng, deps):
        kn = self.known[eng]
        for sk, v in deps.items():
            if sk not in ENGS:
                v = max(v, self.cnt[sk])
            if sk == eng and v > self.cnt[eng]:
                raise RuntimeError("self-dependency on pending op: " + eng)
            if kn.get(sk, 0) >= v:
                continue
            kn[sk] = v
            sem = self.sem[sk]
            self.q[eng].append(lambda E, sem=sem, v=v: E.wait_ge(sem, v))

    def op(self, eng, fn, reads=(), writes=(), inc=True):
        deps = self._deps(eng, reads, writes)
        self._emit_waits(eng, deps)
        n = self.cnt[eng] + 1
        for k in reads:
            self.readers.setdefault(k, {})[eng] = n
        for k in writes:
            self.lastw[k] = (eng, n)
            self.readers[k] = {}
        self.ninst += 1
        if inc:
            self.cnt[eng] = n
            sem = self.sem[eng]
            self.q[eng].append(lambda E, sem=sem: fn(E).then_inc(sem, 1))
            self.pending[eng] = False
        else:
            self.q[eng].append(lambda E: fn(E))
            self.pending[eng] = True

    def dma(self, queue, semname, out, in_, reads=(), writes=()):
        deps = self._deps(queue, reads, writes)
        self._emit_waits(queue, deps)
        n = self.cnt[semname] + 16
        self.cnt[semname] = n
        for k in reads:
            self.readers.setdefault(k, {})[semname] = n
        for k in writes:
            self.lastw[k] = (semname, n)
            self.readers[k] = {}
        sem = self.sem[semname]
        self.q[queue].append(lambda E, sem=sem: E.dma_start(out=out, in_=in_).then_inc(sem, 16))
        self.ninst += 1

    def fence(self):
        for e in ENGS:
            kn = self.known[e]
            for s, v in self.cnt.items():
                if s == e or v == 0 or kn.get(s, 0) >= v:
                    continue
                kn[s] = v
                sem = self.sem[s]
                self.q[e].append(lambda E, sem=sem, v=v: E.wait_ge(sem, v))

    def wait_all(self, eng, semnames):
        for s in semnames:
            v = self.cnt[s]
            if v > 0 and self.known[eng].get(s, 0) < v:
                self.known[eng][s] = v
                sem = self.sem[s]
                self.q[eng].append(lambda E, sem=sem, v=v: E.wait_ge(sem, v))

    def emit(self, block):
        for e in ENGS:
            assert not self.pending[e], "engine %s ends with un-inc'd op" % e
        q = self.q

        @block.tensor
        def _(E):
            for f in q["pe"]:
                f(E)

        @block.scalar
        def _(E):
            for f in q["act"]:
                f(E)

        @block.vector
        def _(E):
            for f in q["dve"]:
                f(E)

        @block.gpsimd
        def _(E):
            for f in q["pool"]:
                f(E)

        @block.sync
        def _(E):
            for f in q["sp"]:
                f(E)


class Carver:
    def __init__(self, scr, nbytes):
        self.scr = scr
        self.cap = nbytes
        self.off = 0
        self.hi = 0

    def alloc(self, shape, dt):
        esz = 4 if dt == F32 else 2
        n = 1
        for s in shape:
            n *= s
        nb = (n * esz + 31) // 32 * 32
        assert self.off + nb <= self.cap, ("SBUF scratch overflow", self.off, nb, self.cap)
        w0 = self.off // 4
        ap = self.scr[:, w0:w0 + nb // 4]
        if dt != F32:
            ap = ap.bitcast(dt)
        ap = ap[:, 0:n]
        if len(shape) == 2:
            ap = ap.rearrange("p (a b) -> p a b", a=shape[0])
        elif len(shape) == 3:
            ap = ap.rearrange("p (a b c) -> p a b c", a=shape[0], b=shape[1])
        self.off += nb
        self.hi = max(self.hi, self.off)
        return ap


def rev_ap(ap_):
    n = ap_.shape[1]
    pp = ap_.ap[0]
    return bass.AP(ap_.tensor, ap_.offset + (n - 1), [[pp[0], pp[1]], [-1, n]])


class _Stop(Exception):
    pass


def build_program(layers, nseq, first_in_program=True, dbg=None, stop=99):
    nc = bass.Bass("TRN2", target_bir_lowering=False)
    NL = DEPTH
    dram = lambda name, shape, dt=F32, kind="ExternalInput": nc.dram_tensor(name, shape, dt, kind=kind).ap()
    xin = dram("xin", [nseq, 128, 8, TOK])
    cT_d = dram("cT", [128, 8, 3])
    wada_d = dram("w_ada", [NL, 128, 8, 3072])
    bada_d = dram("b_adaT", [128, NL, 24])
    win_d = dram("w_in", [NL, 128, 8, NC1 + NC2])
    wo_d = dram("w_o", [NL, 128, 8, 1024])
    vecs_d = dram("vecs", [128, NL, NV])
    anorm_d = dram("anorm", [NL, 128, 2, 256])
    absb_d = dram("absb", [NL, 128, 2, 128])
    aws_d = dram("awsT", [NL, 128, 4, 128])
    lruw_d = dram("lruW", [NL, 128, 8, 128])
    c32_d = dram("cst32", [128, 256])
    c16_d = dram("cst16", [128, 256 + 2 * L_LAT])
    xout = dram("xout", [nseq, 128, 8, TOK], kind="ExternalOutput")
    dbg_out = {}
    if dbg:
        for name, shape in dbg.items():
            dbg_out[name] = dram("dbg_" + name, shape, kind="ExternalOutput")

    with ExitStack() as es:
        CAP = 212000
        scr = es.enter_context(nc.sbuf_tensor("scr", [128, CAP // 4], F32))
        PSB = [es.enter_context(nc.psum_tensor("psb%d" % i, [128, 512], F32))[:, :] for i in range(8)]
        P = Prog(nc, es)
        for s in ("ld_x", "ld_c", "ld_w1", "ld_w2", "ld_wo", "ld_ada0", "ld_ada1", "ld_lw", "st_o", "st_d"):
            P.newsem(s)
        C = Carver(scr, CAP)

        xT = C.alloc([8, TOK], F32)
        tabs = C.alloc([2, L_LAT], BF16)
        KT = C.alloc([2, TOK], BF16)
        VA = C.alloc([18, 2, 128], BF16)
        ylru = C.alloc([2, TOK], BF16)
        c32 = C.alloc([256], F32)[:, :]
        c16 = C.alloc([256], BF16)
        mods = C.alloc([NL, 24, 3], F32)
        sc1 = C.alloc([NL, 8, 3], F32)
        gA = C.alloc([NL, 8, 3], F32)
        vecs = C.alloc([NL, NV], F32)
        cvec = C.alloc([NL, 4], F32)
        badaT = C.alloc([NL, 24], F32)
        anorm = C.alloc([2, 256], F32)
        absb = C.alloc([2, 128], F32)
        awsT = C.alloc([4, 128], BF16)
        lruW = C.alloc([8, 128], BF16)
        sil = C.alloc([8, 3], F32)
        epsr = C.alloc([4], F32)
        onesb = C.alloc([128], BF16)
        PBASE = C.off
        ident = c32[:, 0:128]
        onesD = c32[:, 128:256]
        Rmat = c16[:, 0:128]
        Bones = c16[:, 128:256]
        cosT = tabs[:, 0, :]
        sinT = tabs[:, 1, :]

        psi = [0, 0, 0]

        def nextps():
            i = psi[0] % 4
            psi[0] += 1
            return PSB[i], "ps%d" % i

        def nextps_s():
            i = 4 + psi[1] % 2
            psi[1] += 1
            return PSB[i], "ps%d" % i

        def nextps_o():
            i = 6 + psi[2] % 2
            psi[2] += 1
            return PSB[i], "ps%d" % i

        def act(out, in_, func, r, w, **kw):
            P.op("act", lambda E: E.activation(out=out, in_=in_, func=func, **kw), r, w)

        def tt(eng, out, a, b, op, r, w):
            P.op(eng, lambda E: E.tensor_tensor(out=out, in0=a, in1=b, op=op), r, w)

        def ts(eng, out, a, s1, s2, op0, op1, r, w):
            if s2 is None:
                P.op(eng, lambda E: E.tensor_single_scalar(out=out, in_=a, scalar=s1, op=op0), r, w)
            else:
                P.op(eng, lambda E: E.tensor_scalar(out=out, in0=a, scalar1=s1, scalar2=s2, op0=op0, op1=op1), r, w)

        def scan(out, d0, d1, init, r, w):
            P.op("dve", lambda E: E.tensor_tensor_scan(out=out, data0=d0, data1=d1, initial=init, op0=ALU.mult, op1=ALU.add), r, w)

        def stt(eng, out, a, s, b, op0, op1, r, w):
            P.op(eng, lambda E: E.scalar_tensor_tensor(out=out, in0=a, scalar=s, in1=b, op0=op0, op1=op1), r, w)

        def mm(out, lhsT, rhs, start, stop, r, w, inc=True):
            P.op("pe", lambda E: E.matmul(out, lhsT, rhs, start=start, stop=stop), r, w, inc=inc)

        def recip(out, in_, r, w):
            P.op("dve", lambda E: E.reciprocal(out=out, in_=in_), r, w)

        def memset(eng, ap, val, w):
            P.op(eng, lambda E: E.memset(ap, val), (), w)

        def copy(eng, out, in_, r, w):
            P.op(eng, lambda E: E.tensor_copy(out=out, in_=in_), r, w)

        P.dma("sp", "ld_c", c32, c32_d[:, :], writes=["c32"])
        P.dma("pool", "ld_lw", c16, c16_d[:, 0:256], writes=["c16"])
        for a_ in range(2):
            P.dma("pool", "ld_lw", tabs[:, a_, :], c16_d[:, 256 + a_ * L_LAT:256 + (a_ + 1) * L_LAT], writes=["tabs"])
        P.dma("sp", "ld_c", vecs, vecs_d[:, :, :], writes=["vecs"])
        P.dma("sp", "ld_c", badaT, bada_d[:, :, :], writes=["badaT"])
        P.dma("sp", "ld_c", sil, cT_d[:, :, :], writes=["sil"])
        memset("dve", epsr[:, 0:1], RMS_EPS, ["epsr"])
        memset("dve", epsr[:, 1:2], LN_EPS / (ALPHA * ALPHA), ["epsr"])
        memset("dve", epsr[:, 2:3], 1.0, ["epsr"])
        memset("dve", epsr[:, 3:4], 0.0, ["epsr"])
        memset("dve", onesb, 1.0 / 1024.0, ["onesb"])
        memset("pool", VA[:, :, :, 64:128], 1.0, ["VA"])

        act(sil, sil, AF.Silu, ["sil"], ["sil"])
        for l in layers:
            act(cvec[:, l, :], vecs[:, l, V_LAM:V_LAM + 4], AF.Exp, ["vecs"], ["cvec"], scale=-1.0)
            act(cvec[:, l, :], cvec[:, l, :], AF.Ln, ["cvec"], ["cvec"], bias=epsr[:, 2:3])
            ts("dve", cvec[:, l, :], cvec[:, l, :], -8.0, None, ALU.mult, ALU.bypass, ["cvec"], ["cvec"])
        C.off = PBASE
        stg = [C.alloc([8, 512], F32) for _ in range(2)]
        modrow = C.alloc([3072], F32)[:, :]
        bi = 0
        for l in layers:
            for nb in range(6):
                sb_ = stg[bi % 2]
                key = "stg%d" % (bi % 2)
                P.dma("sp", "ld_ada%d" % (bi % 2), sb_, wada_d[l, :, :, nb * 512:(nb + 1) * 512], writes=[key])
                ps, pk = nextps()
                for kc in range(8):
                    mm(ps[0:3, :], sil[:, kc, :], sb_[:, kc, :], kc == 0, kc == 7, [key, "sil"], [pk], inc=(kc == 7))
                copy("dve", modrow[0:3, nb * 512:(nb + 1) * 512], ps[0:3, :], [pk], ["modrow"])
                bi += 1
            ps, pk = nextps()
            for j in range(24):
                mm(ps[:, j * 3:(j + 1) * 3], modrow[0:3, j * 128:(j + 1) * 128], ident[0:3, 0:3], True, True,
                   ["modrow", "c32"], [pk], inc=(j == 23))
            tt("dve", mods[:, l, :, :], ps[:, 0:72].rearrange("p (a b) -> p a b", a=24),
               badaT[:, l, :].unsqueeze(2).to_broadcast([128, 24, 3]), ALU.add, [pk, "badaT"], ["mods"])
            ts("dve", sc1[:, l, :, :], mods[:, l, 8:16, :], 1.0, None, ALU.add, ALU.bypass, ["mods"], ["sc1"])
            ts("dve", gA[:, l, :, :], mods[:, l, 16:24, :], 1.0 / ALPHA, None, ALU.mult, ALU.bypass, ["mods"], ["gA"])
        P.fence()

        def rms_rope(zps, zk, n, gvec, dst, dstk, t0, latent, S):
            act(S["sq"][:, 0:n], zps[:, 0:n], AF.Square, [zk], ["r_sq"])
            ps2, pk2 = nextps()
            mm(ps2[:, 0:n], Bones, S["sq"][:, 0:n], True, True, ["r_sq", "c16"], [pk2])
            act(S["sd"][:, 0:n], ps2[:, 0:n], AF.Sqrt, [pk2, "epsr"], ["r_sd"], bias=epsr[:, 0:1])
            recip(S["rs"][:, 0:n], S["sd"][:, 0:n], ["r_sd"], ["r_rs"])
            if not latent:
                stt("dve", dst, zps[:, 0:n], gvec, S["rs"][:, 0:n], ALU.mult, ALU.mult, [zk, "r_rs", "vecs"], [dstk])
                return
            stt("dve", S["qn"][:, 0:n], zps[:, 0:n], gvec, S["rs"][:, 0:n], ALU.mult, ALU.mult,
                [zk, "r_rs", "vecs"], ["r_qn"])
            ps3, pk3 = nextps()
            mm(ps3[:, 0:n], Rmat, S["qn"][:, 0:n], True, True, ["r_qn", "c16"], [pk3])
            tt("pool", S["t1"][:, 0:n], S["qn"][:, 0:n], cosT[:, t0:t0 + n], ALU.mult, ["r_qn", "tabs"], ["r_t1"])
            tt("dve", S["t2"][:, 0:n], ps3[:, 0:n], sinT[:, t0:t0 + n], ALU.mult, [pk3, "tabs"], ["r_t2"])
            tt("pool", dst, S["t1"][:, 0:n], S["t2"][:, 0:n], ALU.add, ["r_t1", "r_t2"], [dstk])

        def modulate(dst, dk, l, r, t0, n):
            for c in range(8):
                eng = "pool" if c % 2 == 0 else "dve"
                ts(eng, dst[:, c, 0:n], xT[:, c, t0:t0 + n], sc1[:, l, c, r:r + 1], mods[:, l, c, r:r + 1],
                   ALU.mult, ALU.add, ["xT", "sc1", "mods"], [dk])

        def inproj_fm(ps, pk, W, wk, col0, xm, xk, n):
            for kc in range(8):
                mm(ps[:, 0:n], W[:, kc, col0:col0 + 128], xm[:, kc, 0:n], kc == 0, kc == 7, [wk, xk], [pk], inc=(kc == 7))

        def stage(n):
            if n > stop:
                raise _Stop()

        for s in range(nseq):
            try:
                for c in range(8):
                    P.dma("sp", "ld_x", xT[:, c, :], xin[s, :, c, :], writes=["xT"])
                for l in layers:
                    last = (l == DEPTH - 1)
                    P.dma("sp", "ld_c", anorm, anorm_d[l, :, :, :], writes=["anorm"])
                    P.dma("sp", "ld_c", absb, absb_d[l, :, :, :], writes=["absb"])
                    P.dma("pool", "ld_lw", awsT, aws_d[l, :, :, :], writes=["awsT"])
                    P.dma("pool", "ld_lw", lruW, lruw_d[l, :, :, :], writes=["lruW"])
                    stage(1)
                    P.fence()
                    C.off = PBASE
                    win1 = C.alloc([8, NC1], BF16)
                    xm1 = C.alloc([8, 512], BF16)
                    S1 = {"sq": C.alloc([512], BF16)[:, :], "sd": C.alloc([512], F32)[:, :], "rs": C.alloc([512], F32)[:, :],
                          "qn": C.alloc([512], BF16)[:, :], "t1": C.alloc([512], F32)[:, :], "t2": C.alloc([512], F32)[:, :]}
                    sg = C.alloc([2, TOK], BF16)
                    xrb = C.alloc([2, TOK], BF16)
                    GBASE = C.off
                    rx = C.alloc([2, 2312], F32)
                    acc = C.alloc([TOK], F32)[:, :]
                    P.dma("pool", "ld_w1", win1, win_d[l, :, :, 0:NC1], writes=["win1"])
                    memset("pool", rx[:, :, 0:2], 0.0, ["rx"])
                    memset("pool", rx[:, :, 2050:2052], 0.0, ["rx"])
                    memset("pool", rx[:, :, 2308:2312], 0.0, ["rx"])
                    for (t0, n) in ((0, 512), (512, 512), (1024, 512), (1536, 512), (2048, 256)):
                        latent = t0 < L_LAT
                        r = s if latent else 2
                        modulate(xm1, "xm1", l, r, t0, n)
                        for g in range(2):
                            ps, pk = nextps()
                            inproj_fm(ps, pk, win1, "win1", C_K0 + g * 128, xm1, "xm1", n)
                            rms_rope(ps, pk, n, vecs[:, l, V_KG:V_KG + 1], KT[:, g, t0:t0 + n], "KT", t0, latent, S1)
                        for tl in range(n // 128):
                            ps, pk = nextps()
                            for kc in range(8):
                                mm(ps[:, 0:128], xm1[:, kc, tl * 128:(tl + 1) * 128], win1[:, kc, C_V:C_V + 128],
                                   kc == 0, kc == 7, ["win1", "xm1"], [pk], inc=(kc == 7))
                            tix = t0 // 128 + tl
                            copy("dve", VA[:, tix, :, 0:64], ps[:, 0:128].rearrange("p (g d) -> p g d", g=2), [pk], ["VA"])
                        for ct in range(2):
                            ps, pk = nextps()
                            inproj_fm(ps, pk, win1, "win1", C_RX + ct * 128, xm1, "xm1", n)
                            off = 2 + t0 if latent else 2052 + (t0 - L_LAT)
                            act(rx[:, ct, off:off + n], ps[:, 0:n], AF.Copy, [pk], ["rx"])
                            ps, pk = nextps()
                            inproj_fm(ps, pk, win1, "win1", C_RG + ct * 128, xm1, "xm1", n)
                            act(sg[:, ct, t0:t0 + n], ps[:, 0:n], AF.Silu, [pk], ["sg"])
                    stage(2)
                    for ct in range(2):
                        for (d0, o0, n) in ((2, 0, L_LAT), (2052, L_LAT, L_CTX)):
                            cw = lambda j: vecs[:, l, V_CW + ct * 4 + j:V_CW + ct * 4 + j + 1]
                            eng = "dve"
                            ts(eng, acc[:, 0:n], rx[:, ct, d0 - 2:d0 - 2 + n], cw(0), vecs[:, l, V_CB + ct:V_CB + ct + 1],
                               ALU.mult, ALU.add, ["rx", "vecs"], ["acc"])
                            stt(eng, acc[:, 0:n], rx[:, ct, d0 - 1:d0 - 1 + n], cw(1), acc[:, 0:n], ALU.mult, ALU.add,
                                ["rx", "vecs", "acc"], ["acc"])
                            stt(eng, acc[:, 0:n], rx[:, ct, d0:d0 + n], cw(2), acc[:, 0:n], ALU.mult, ALU.add,
                                ["rx", "vecs", "acc"], ["acc"])
                            stt(eng, xrb[:, ct, o0:o0 + n], rx[:, ct, d0 + 1:d0 + 1 + n], cw(3), acc[:, 0:n], ALU.mult, ALU.add,
                                ["rx", "vecs", "acc"], ["xrb"])
                    stage(3)
                    P.fence()
                    C.off = GBASE
                    Ab = C.alloc([TOK], F32)[:, :]
                    Tb = C.alloc([TOK], F32)[:, :]
                    Bb = C.alloc([TOK], F32)[:, :]
                    Hf = C.alloc([TOK], F32)[:, :]
                    for ct in range(2):
                        for dd in range(2):
                            vi = dd * 2 + ct
                            for (t0, n) in ((0, 512), (512, 512), (1024, 512), (1536, 512), (2048, 256)):
                                ps, pk = nextps()
                                mm(ps[:, 0:n], lruW[:, (dd * 2 + 0) * 2 + ct, :], xrb[:, ct, t0:t0 + n], True, True,
                                   ["lruW", "xrb"], [pk])
                                act(Ab[:, t0:t0 + n], ps[:, 0:n], AF.Sigmoid, [pk, "vecs"], ["Ab"],
                                    bias=vecs[:, l, V_BR + vi:V_BR + vi + 1])
                                ps, pk = nextps()
                                mm(ps[:, 0:n], lruW[:, (dd * 2 + 1) * 2 + ct, :], xrb[:, ct, t0:t0 + n], True, True,
                                   ["lruW", "xrb"], [pk])
                                act(Bb[:, t0:t0 + n], ps[:, 0:n], AF.Sigmoid, [pk, "vecs"], ["Bb"],
                                    bias=vecs[:, l, V_BI + vi:V_BI + vi + 1])
                            act(Ab, Ab, AF.Exp, ["Ab", "cvec"], ["Ab"], scale=cvec[:, l, vi:vi + 1])
                            act(Tb, Ab, AF.Square, ["Ab"], ["Tb"])
                            act(Tb, Tb, AF.Sqrt, ["Tb", "epsr"], ["Tb"], scale=-1.0, bias=epsr[:, 2:3])
                            tt("dve", Bb, Bb, xrb[:, ct, :], ALU.mult, ["Bb", "xrb"], ["Bb"])
                            tt("dve", Bb, Bb, Tb, ALU.mult, ["Bb", "Tb"], ["Bb"])
                            if dd == 0:
                                scan(Hf[:, L_LAT:TOK], Ab[:, L_LAT:TOK], Bb[:, L_LAT:TOK], 0.0, ["Ab", "Bb"], ["Hf"])
                                scan(Hf[:, 0:L_LAT], Ab[:, 0:L_LAT], Bb[:, 0:L_LAT], Hf[:, TOK - 1:TOK], ["Ab", "Bb", "Hf"], ["Hf"])
                            else:
                                scan(rev_ap(Tb), rev_ap(Ab), rev_ap(Bb), 0.0, ["Ab", "Bb"], ["Tb"])
                        tt("pool", Hf, Hf, Tb, ALU.add, ["Hf", "Tb"], ["Hf"])
                        tt("pool", ylru[:, ct, :], Hf, sg[:, ct, :], ALU.mult, ["Hf", "sg"], ["ylru"])
                    stage(4)
                    P.fence()
                    C.off = PBASE
                    win2 = C.alloc([8, NC2], BF16)
                    wo = C.alloc([8, 1024], BF16)
                    P.dma("pool", "ld_w2", win2, win_d[l, :, :, NC1:NC1 + NC2], writes=["win2"])
                    for kc in range(8):
                        P.dma("pool", "ld_wo", wo[:, kc, :], wo_d[l, :, kc, :], writes=["wo"])
                    xm = C.alloc([8, 256], BF16)
                    yT = C.alloc([6, 256], BF16)
                    ug = C.alloc([2, 256], F32)
                    sga = C.alloc([2, 256], F32)
                    vg = C.alloc([2, 256], F32)
                    junk = C.alloc([256], F32)[:, :]
                    vln = C.alloc([2, 256], BF16)
                    tmpg = C.alloc([512], F32)[:, :]
                    st = C.alloc([2, 8], F32)
                    qT = C.alloc([4, 256], BF16)
                    sbg = C.alloc([4, 256], BF16)
                    S2 = {"sq": C.alloc([256], BF16)[:, :], "sd": C.alloc([256], F32)[:, :], "rs": C.alloc([256], F32)[:, :],
                          "qn": C.alloc([256], BF16)[:, :], "t1": C.alloc([256], F32)[:, :], "t2": C.alloc([256], F32)[:, :]}
                    pT = [C.alloc([512], BF16)[:, :] for _ in range(3)]
                    rden = [C.alloc([256], F32)[:, :] for _ in range(2)]
                    otmp = [C.alloc([256], F32)[:, :] for _ in range(2)]
                    sqb = C.alloc([8, 256], BF16)
                    tln = C.alloc([4, 256], F32)
                    mean_sb = C.alloc([256], F32)[:, :]
                    m2 = C.alloc([256], F32)[:, :]
                    sdl = C.alloc([256], F32)[:, :]
                    rsl = C.alloc([256], F32)[:, :]
                    nblk = 8 if last else 9
                    pti = 0
                    hi = 0
                    for qb in range(nblk):
                        t0 = qb * 256
                        latent = qb < 8
                        r = s if latent else 2
                        modulate(xm, "xm", l, r, t0, 256)
                        for ct in range(2):
                            ps, pk = nextps()
                            inproj_fm(ps, pk, win2, "win2", C_U + ct * 128, xm, "xm", 256)
                            act(ug[:, ct, :], ps[:, 0:256], AF.Gelu_apprx_tanh, [pk], ["ug"])
                            ps, pk = nextps()
                            inproj_fm(ps, pk, win2, "win2", C_GA + ct * 128, xm, "xm", 256)
                            act(sga[:, ct, :], ps[:, 0:256], AF.Silu, [pk], ["sga"])
                        tt("pool", ug, ug, sga, ALU.mult, ["ug", "sga"], ["ug"])
                        for tl in range(2):
                            ps, pk = nextps()
                            for kc in range(8):
                                mm(ps[:, 0:256], xm[:, kc, tl * 128:(tl + 1) * 128], win2[:, kc, C_VA:C_VA + 256],
                                   kc == 0, kc == 7, ["win2", "xm"], [pk], inc=(kc == 7))
                            act(vg[:, tl, :], ps[:, 0:256], AF.Gelu_apprx_tanh, [pk], ["vg", "st"], accum_out=st[:, tl, 0:1])
                            act(junk, vg[:, tl, :], AF.Square, ["vg"], ["junk", "st"], accum_out=st[:, tl, 1:2])
                            ts("dve", st[:, tl, 2:3], st[:, tl, 0:1], 1.0 / 256.0, None, ALU.mult, ALU.bypass, ["st"], ["st"])
                            tt("dve", st[:, tl, 3:4], st[:, tl, 2:3], st[:, tl, 2:3], ALU.mult, ["st"], ["st"])
                            stt("dve", st[:, tl, 4:5], st[:, tl, 1:2], 1.0 / 256.0, st[:, tl, 3:4], ALU.mult, ALU.subtract,
                                ["st"], ["st"])
                            act(st[:, tl, 5:6], st[:, tl, 4:5], AF.Sqrt, ["st", "epsr"], ["st"], bias=epsr[:, 0:1], scale=1.0)
                            recip(st[:, tl, 6:7], st[:, tl, 5:6], ["st"], ["st"])
                            ts("dve", vg[:, tl, :], vg[:, tl, :], st[:, tl, 2:3], st[:, tl, 6:7], ALU.subtract, ALU.mult,
                               ["vg", "st"], ["vg"])
                            tt("pool", vg[:, tl, :], vg[:, tl, :], anorm[:, 0, :], ALU.mult, ["vg", "anorm"], ["vg"])
                            tt("pool", vln[:, tl, :], vg[:, tl, :], anorm[:, 1, :], ALU.add, ["vg", "anorm"], ["vln"])
                        ps, pk = nextps()
                        for tl in range(2):
                            for ct in range(2):
                                for hh in range(2):
                                    g = 2 * ct + hh
                                    co = (tl * 2 + ct) * 128
                                    mm(ps[hh * 64:(hh + 1) * 64, co:co + 128], vln[:, tl, g * 64:(g + 1) * 64], awsT[:, g, :],
                                       True, True, ["vln", "awsT"], [pk], inc=(tl == 1 and ct == 1 and hh == 1))
                        tt("dve", tmpg.rearrange("p (t c q) -> p t c q", t=2, c=2), ps.rearrange("p (t c q) -> p t c q", t=2, c=2),
                           absb.unsqueeze(1).to_broadcast([128, 2, 2, 128]), ALU.add, [pk, "absb"], ["tmpg"])
                        tt("pool", yT[:, 0:2, :].rearrange("p c (t q) -> p t c q", t=2),
                           tmpg.rearrange("p (t c q) -> p t c q", t=2, c=2),
                           ug.rearrange("p c (t q) -> p t c q", t=2), ALU.mult, ["tmpg", "ug"], ["yT"])
                        stage(5)
                        for ct in range(4):
                            ps, pk = nextps()
                            inproj_fm(ps, pk, win2, "win2", C_Q + ct * 128, xm, "xm", 256)
                            rms_rope(ps, pk, 256, vecs[:, l, V_QG:V_QG + 1], qT[:, ct, :], "qT", t0, latent, S2)
                            ps, pk = nextps()
                            inproj_fm(ps, pk, win2, "win2", C_BG + ct * 128, xm, "xm", 256)
                            act(sbg[:, ct, :], ps[:, 0:256], AF.Silu, [pk], ["sbg"])
                        stage(6)
                        ktiles = list(range(18)) if latent else [16, 17]
                        pairs = [(ktiles[i], ktiles[i + 1]) for i in range(0, len(ktiles), 2)]
                        for h in range(8):
                            ct, hp, g = h // 2, h % 2, h // 4
                            lo, hi_ = hp * 64, hp * 64 + 64
                            pO, kO = nextps_o()
                            qs = qT[lo:hi_, ct, :]

                            def qk(pair):
                                ps_, pk_ = nextps_s()
                                for i, j in enumerate(pair):
                                    mm(ps_[:, i * 256:(i + 1) * 256], KT[lo:hi_, g, j * 128:(j + 1) * 128], qs, True, True,
                                       ["KT", "qT"], [pk_], inc=(i == 1))
                                return ps_, pk_

                            pend = qk(pairs[0])
                            for pi, pair in enumerate(pairs):
                                ps_, pk_ = pend
                                if pi + 1 < len(pairs):
                                    pend = qk(pairs[pi + 1])
                                pb = pT[pti % 3]
                                pbk = "pT%d" % (pti % 3)
                                pti += 1
                                act(pb, ps_, AF.Exp, [pk_], [pbk], scale=0.125)
                                for i, j in enumerate(pair):
                                    fin = (pi == len(pairs) - 1 and i == 1)
                                    mm(pO[:, 0:256], VA[:, j, g, :], pb[:, i * 256:(i + 1) * 256], pi == 0 and i == 0, fin,
                                       ["VA", pbk], [kO], inc=(i == 1))
                            rd = rden[hi % 2]
                            ot = otmp[hi % 2]
                            rk, ok = "rden%d" % (hi % 2), "otmp%d" % (hi % 2)
                            hi += 1
                            recip(rd[lo:hi_, :], pO[64:128, 0:256], [kO], [rk])
                            tt("dve", ot[lo:hi_, :], pO[0:64, 0:256], rd[lo:hi_, :], ALU.mult, [kO, rk], [ok])
                            tt("pool", yT[lo:hi_, 2 + ct, :], ot[lo:hi_, :], sbg[lo:hi_, ct, :], ALU.mult, [ok, "sbg"], ["yT"])
                        stage(7)
                        for dt_ in range(8):
                            ps, pk = nextps()
                            for kc in range(8):
                                rhs = yT[:, kc, :] if kc < 6 else ylru[:, kc - 6, t0:t0 + 256]
                                mm(ps[:, 0:256], wo[:, kc, dt_ * 128:(dt_ + 1) * 128], rhs, kc == 0, kc == 7,
                                   ["wo", "yT", "ylru"], [pk], inc=(kc == 7))
                            stt("dve", xT[:, dt_, t0:t0 + 256], ps[:, 0:256], gA[:, l, dt_, r:r + 1], xT[:, dt_, t0:t0 + 256],
                                ALU.mult, ALU.add, [pk, "gA", "xT"], ["xT"])
                        stage(8)
                        xblk = xT[:, :, t0:t0 + 256]
                        act(sqb, xblk, AF.Square, ["xT"], ["sqb"])
                        pm, km = nextps()
                        for dt_ in range(8):
                            mm(pm[:, 0:256], onesD, xT[:, dt_, t0:t0 + 256], dt_ == 0, dt_ == 7, ["xT", "c32"], [km], inc=(dt_ == 7))
                        pq, kq = nextps()
                        for dt_ in range(8):
                            mm(pq[:, 0:256], onesb, sqb[:, dt_, :], dt_ == 0, dt_ == 7, ["sqb", "onesb"], [kq], inc=(dt_ == 7))
                        act(mean_sb, pm[:, 0:256], AF.Copy, [km], ["mean_sb"])
                        tt("pool", m2, mean_sb, mean_sb, ALU.mult, ["mean_sb"], ["m2"])
                        tt("dve", m2, pq[:, 0:256], m2, ALU.subtract, [kq, "m2"], ["m2"])
                        act(sdl, m2, AF.Sqrt, ["m2", "epsr"], ["sdl"], bias=epsr[:, 1:2], scale=1.0)
                        recip(rsl, sdl, ["sdl"], ["rsl"])
                        for half in range(2):
                            xs_ = xT[:, half * 4:(half + 1) * 4, t0:t0 + 256]
                            tt("pool", tln, xs_, mean_sb.unsqueeze(1).to_broadcast([128, 4, 256]), ALU.subtract,
                               ["xT", "mean_sb"], ["tln"])
                            tt("dve", tln, tln, rsl.unsqueeze(1).to_broadcast([128, 4, 256]), ALU.mult, ["tln", "rsl"], ["tln"])
                            for dd_ in range(4):
                                dt_ = half * 4 + dd_
                                act(xT[:, dt_, t0:t0 + 256], tln[:, dd_, :], AF.Identity, ["tln", "vecs"], ["xT"],
                                    scale=vecs[:, l, V_LNG + dt_:V_LNG + dt_ + 1], bias=vecs[:, l, V_LNB + dt_:V_LNB + dt_ + 1])

            except _Stop:
                pass
            for c in range(8):
                P.dma("sp", "st_o", xout[s, :, c, :], xT[:, c, :], reads=["xT"])
        P.wait_all("sp", ["st_o", "st_d"])
        block = es.enter_context(nc.Block())
        P.emit(block)
        build_program.last_stats = (P.ninst, dict(P.cnt), C.hi)
    return nc


def _rope_tables():
    half, nf = 32, 16
    inv = (10000.0 ** (-np.arange(nf, dtype=np.float32) / nf)).astype(np.float32)
    t = np.arange(L_LAT)
    rows = (t // 64).astype(np.float32)
    cols = (t % 64).astype(np.float32)
    cos = np.zeros((64, L_LAT), np.float32)
    sin = np.zeros((64, L_LAT), np.float32)
    for d in range(64):
        pos = rows if d < 32 else cols
        f = d % 16
        ang = (pos * inv[f]).astype(np.float32)
        cos[d] = np.cos(ang)
        sgn = -1.0 if (d % 32) < 16 else 1.0
        sin[d] = sgn * np.sin(ang)
    return np.tile(cos, (2, 1)), np.tile(sin, (2, 1))


def _consts():
    ident = np.eye(128, dtype=np.float32)
    onesD = np.full((128, 128), 1.0 / 1024.0, np.float32)
    c32 = np.concatenate([ident, onesD], 1)
    R = np.zeros((128, 128), np.float32)
    for m in range(128):
        d = m % 64
        partner = d + 16 if (d % 32) < 16 else d - 16
        R[(m // 64) * 64 + partner, m] = 1.0
    B = np.zeros((128, 128), np.float32)
    B[0:64, 0:64] = 1.0 / 64.0
    B[64:128, 64:128] = 1.0 / 64.0
    cos, sin = _rope_tables()
    c16 = np.concatenate([R, B, cos, sin], 1)
    return np.ascontiguousarray(c32), np.ascontiguousarray(c16)


def _kc_layout(w):
    Ln, K, N = w.shape
    return np.ascontiguousarray(w.reshape(Ln, 8, 128, N).transpose(0, 2, 1, 3))


def prep_shared(inp):
    f = lambda k: np.asarray(inp[k], dtype=np.float32)
    w_in = f("w_in")
    a_u, a_v, a_g = w_in[:, :, 0:256], w_in[:, :, 256:512], w_in[:, :, 512:768]
    q, k, v = w_in[:, :, 768:1280], w_in[:, :, 1280:1408], w_in[:, :, 1408:1536]
    b_g, r_x, r_g = w_in[:, :, 1536:2048], w_in[:, :, 2048:2304], w_in[:, :, 2304:2560]
    k0, k1 = k[:, :, 0:64], k[:, :, 64:128]
    wperm = np.concatenate([k0, k0, k1, k1, v, r_x, r_g, a_u, a_g, a_v, q, b_g], axis=2)
    assert wperm.shape[2] == NC1 + NC2
    sh = {}
    sh["w_in"] = _kc_layout(wperm)
    sh["w_o"] = _kc_layout(f("w_o"))
    sh["w_ada"] = _kc_layout(f("w_ada"))
    sh["b_adaT"] = np.ascontiguousarray(f("b_ada").reshape(DEPTH, 24, 128).transpose(2, 0, 1))
    fm8 = lambda a: a.reshape(DEPTH, -1, 128).transpose(2, 0, 1)
    vec = np.zeros((128, DEPTH, NV), np.float32)
    vec[:, :, V_LNG:V_LNG + 8] = fm8(f("ln_g"))
    vec[:, :, V_LNB:V_LNB + 8] = fm8(f("ln_b"))
    vec[:, :, V_QG] = np.tile(f("q_norm_g"), (1, 2)).T
    vec[:, :, V_KG] = np.tile(f("k_norm_g"), (1, 2)).T
    cw = f("conv_w")
    for ct in range(2):
        for j in range(4):
            vec[:, :, V_CW + ct * 4 + j] = cw[:, j, ct * 128:(ct + 1) * 128].T
    vec[:, :, V_CB:V_CB + 2] = fm8(f("conv_b"))
    for nm, base in (("lru_br", V_BR), ("lru_bi", V_BI), ("lru_lam", V_LAM)):
        a = f(nm)
        for dd in range(2):
            for ct in range(2):
                vec[:, :, base + dd * 2 + ct] = a[:, dd, ct * 128:(ct + 1) * 128].T
    sh["vecs"] = vec
    an = np.stack([f("a_norm_g"), f("a_norm_b")], 1)
    sh["anorm"] = np.ascontiguousarray(np.broadcast_to(an[:, None], (DEPTH, 128, 2, 256)))
    bs = f("a_bs")
    ab = np.zeros((DEPTH, 128, 2, 128), np.float32)
    for ct in range(2):
        ab[:, 0:64, ct, :] = bs[:, 2 * ct, None, :]
        ab[:, 64:128, ct, :] = bs[:, 2 * ct + 1, None, :]
    sh["absb"] = ab
    sh["awsT"] = np.ascontiguousarray(f("a_ws").transpose(0, 3, 1, 2))
    lw = np.zeros((DEPTH, 128, 8, 128), np.float32)
    for dd in range(2):
        for gi, nm in enumerate(("lru_wr", "lru_wi")):
            w = f(nm)
            for ct in range(2):
                idx = (dd * 2 + gi) * 2 + ct
                for hb in range(2):
                    lw[:, hb * 64:(hb + 1) * 64, idx, hb * 64:(hb + 1) * 64] = w[:, dd, 2 * ct + hb]
    sh["lruW"] = lw
    sh["cst32"], sh["cst16"] = _consts()
    return sh


def prep_core(inp, b0, nseq):
    x = np.asarray(inp["x"], np.float32)
    ctx = np.asarray(inp["ctx"], np.float32)
    c = np.asarray(inp["c"], np.float32)
    c_ctx = np.asarray(inp["c_ctx"], np.float32)
    xin = np.empty((nseq, 128, 8, TOK), np.float32)
    for s in range(nseq):
        full = np.concatenate([x[b0 + s], ctx[b0 + s]], 0)
        xin[s] = full.T.reshape(8, 128, TOK).transpose(1, 0, 2)
    rows = [c[b0 + s] if s < nseq else c_ctx for s in range(2)] + [c_ctx]
    crow = np.stack(rows, 0)
    cT = np.ascontiguousarray(crow.T.reshape(8, 128, 3).transpose(1, 0, 2))
    return {"xin": xin, "cT": cT}


_PROG_CACHE = {}


def kernel(**inputs):
    n = 8
    nseq = 2
    key = ("full", nseq)
    if key not in _PROG_CACHE:
        _PROG_CACHE[key] = build_program(list(range(DEPTH)), nseq)
    nc = _PROG_CACHE[key]
    sh = prep_shared(inputs)
    in_maps = []
    for i in range(n):
        m = dict(sh)
        m.update(prep_core(inputs, i * nseq, nseq))
        in_maps.append(m)
    res = run_bass_kernel_spmd(nc, in_maps, core_ids=list(range(n)))
    out = np.empty((16, L_LAT, D), np.float32)
    for i in range(n):
        xo = np.asarray(res.results[i]["xout"])
        for s in range(nseq):
            out[i * nseq + s] = xo[s][:, :, 0:L_LAT].transpose(2, 1, 0).reshape(L_LAT, D)
    return out
```

```python
import math
import numpy as np
from contextlib import ExitStack
import concourse.bass as bass
import concourse.mybir as mybir
from concourse.bass_utils import run_bass_kernel_spmd

F32 = mybir.dt.float32
BF16 = mybir.dt.bfloat16
AF = mybir.ActivationFunctionType
ALU = mybir.AluOpType

D = 1024
L_LAT = 2048
L_CTX = 256
TOK = L_LAT + L_CTX
DEPTH = 4
ALPHA = (2.0 * DEPTH) ** 0.25
LN_EPS = 1e-6
RMS_EPS = 1e-6
NC1 = 896
NC2 = 1792
C_K0, C_K1, C_V, C_RX, C_RG = 0, 128, 256, 384, 640
C_U, C_GA, C_VA, C_Q, C_BG = 0, 256, 512, 768, 1280
NV = 40
V_LNG, V_LNB, V_QG, V_KG, V_CW, V_CB, V_BR, V_BI, V_LAM = 0, 8, 16, 17, 18, 26, 28, 32, 36

ENGS = ("pe", "act", "dve", "pool", "sp")


class Prog:
    def __init__(self, nc, es):
        self.nc = nc
        self.es = es
        self.q = {e: [] for e in ENGS}
        self.cnt = {}
        self.sem = {}
        for e in ENGS:
            self.sem[e] = es.enter_context(nc.semaphore("c_" + e))
            self.cnt[e] = 0
        self.known = {e: {} for e in ENGS}
        self.lastw = {}
        self.readers = {}
        self.pending = {e: False for e in ENGS}
        self.ninst = 0

    def newsem(self, name):
        self.sem[name] = self.es.enter_context(self.nc.semaphore(name))
        self.cnt[name] = 0
        return name

    def _deps(self, eng, reads, writes):
        deps = {}

        def add(sk, v):
            if v > deps.get(sk, 0):
                deps[sk] = v

        for k in reads:
            w = self.lastw.get(k)
            if w is not None:
                add(*w)
        for k in writes:
            w = self.lastw.get(k)
            if w is not None and w[0] != eng:
                add(*w)
            for sk, v in self.readers.get(k, {}).items():
                if sk != eng:
                    add(sk, v)
        if eng == "pe":
            deps.pop("pe", None)
        return deps

    def _emit_waits(self, eng, deps):
        kn = self.known[eng]
        for sk, v in deps.items():
            if sk not in ENGS:
                v = max(v, self.cnt[sk])
            if sk == eng and v > self.cnt[eng]:
                raise RuntimeError("self-dependency on pending op: " + eng)
            if kn.get(sk, 0) >= v:
                continue
            kn[sk] = v
            sem = self.sem[sk]
            self.q[eng].append(lambda E, sem=sem, v=v: E.wait_ge(sem, v))

    def op(self, eng, fn, reads=(), writes=(), inc=True):
        deps = self._deps(eng, reads, writes)
        self._emit_waits(eng, deps)
        n = self.cnt[eng] + 1
        for k in reads:
            self.readers.setdefault(k, {})[eng] = n
        for k in writes:
            self.lastw[k] = (eng, n)
            self.readers[k] = {}
        self.ninst += 1
        if inc:
            self.cnt[eng] = n
            sem = self.sem[eng]
            self.q[eng].append(lambda E, sem=sem: fn(E).then_inc(sem, 1))
            self.pending[eng] = False
        else:
            self.q[eng].append(lambda E: fn(E))
            self.pending[eng] = True

    def dma(self, queue, semname, out, in_, reads=(), writes=()):
        deps = self._deps(queue, reads, writes)
        self._emit_waits(queue, deps)
        n = self.cnt[semname] + 16
        self.cnt[semname] = n
        for k in reads:
            self.readers.setdefault(k, {})[semname] = n
        for k in writes:
            self.lastw[k] = (semname, n)
            self.readers[k] = {}
        sem = self.sem[semname]
        self.q[queue].append(lambda E, sem=sem: E.dma_start(out=out, in_=in_).then_inc(sem, 16))
        self.ninst += 1

    def fence(self):
        for e in ENGS:
            kn = self.known[e]
            for s, v in self.cnt.items():
                if s == e or v == 0 or kn.get(s, 0) >= v:
                    continue
                kn[s] = v
                sem = self.sem[s]
                self.q[e].append(lambda E, sem=sem, v=v: E.wait_ge(sem, v))

    def wait_all(self, eng, semnames):
        for s in semnames:
            v = self.cnt[s]
            if v > 0 and self.known[eng].get(s, 0) < v:
                self.known[eng][s] = v
                sem = self.sem[s]
                self.q[eng].append(lambda E, sem=sem, v=v: E.wait_ge(sem, v))

    def emit(self, block):
        for e in ENGS:
            assert not self.pending[e], "engine %s ends with un-inc'd op" % e
        q = self.q

        @block.tensor
        def _(E):
            for f in q["pe"]:
                f(E)

        @block.scalar
        def _(E):
            for f in q["act"]:
                f(E)

        @block.vector
        def _(E):
            for f in q["dve"]:
                f(E)

        @block.gpsimd
        def _(E):
            for f in q["pool"]:
                f(E)

        @block.sync
        def _(E):
            for f in q["sp"]:
                f(E)


class Carver:
    def __init__(self, scr, nbytes):
        self.scr = scr
        self.cap = nbytes
        self.off = 0
        self.hi = 0

    def alloc(self, shape, dt):
        esz = 4 if dt == F32 else 2
        n = 1
        for s in shape:
            n *= s
        nb = (n * esz + 31) // 32 * 32
        assert self.off + nb <= self.cap, ("SBUF scratch overflow", self.off, nb, self.cap)
        w0 = self.off // 4
        ap = self.scr[:, w0:w0 + nb // 4]
        if dt != F32:
            ap = ap.bitcast(dt)
        ap = ap[:, 0:n]
        if len(shape) == 2:
            ap = ap.rearrange("p (a b) -> p a b", a=shape[0])
        elif len(shape) == 3:
            ap = ap.rearrange("p (a b c) -> p a b c", a=shape[0], b=shape[1])
        self.off += nb
        self.hi = max(self.hi, self.off)
        return ap


def rev_ap(ap_):
    n = ap_.shape[1]
    pp = ap_.ap[0]
    return bass.AP(ap_.tensor, ap_.offset + (n - 1), [[pp[0], pp[1]], [-1, n]])


class _Stop(Exception):
    pass


def build_program(layers, nseq, first_in_program=True, dbg=None, stop=99):
    nc = bass.Bass("TRN2", target_bir_lowering=False)
    NL = DEPTH
    dram = lambda name, shape, dt=F32, kind="ExternalInput": nc.dram_tensor(name, shape, dt, kind=kind).ap()
    xin = dram("xin", [nseq, 128, 8, TOK])
    cT_d = dram("cT", [128, 8, 3])
    wada_d = dram("w_ada", [NL, 128, 8, 3072])
    bada_d = dram("b_adaT", [128, NL, 24])
    win_d = dram("w_in", [NL, 128, 8, NC1 + NC2])
    wo_d = dram("w_o", [NL, 128, 8, 1024])
    vecs_d = dram("vecs", [128, NL, NV])
    anorm_d = dram("anorm", [NL, 128, 2, 256])
    absb_d = dram("absb", [NL, 128, 2, 128])
    aws_d = dram("awsT", [NL, 128, 4, 128])
    lruw_d = dram("lruW", [NL, 128, 8, 128])
    c32_d = dram("cst32", [128, 256])
    c16_d = dram("cst16", [128, 256 + 2 * L_LAT])
    xout = dram("xout", [nseq, 128, 8, TOK], kind="ExternalOutput")
    dbg_out = {}
    if dbg:
        for name, shape in dbg.items():
            dbg_out[name] = dram("dbg_" + name, shape, kind="ExternalOutput")

    with ExitStack() as es:
        CAP = 212000
        scr = es.enter_context(nc.sbuf_tensor("scr", [128, CAP // 4], F32))
        PSB = [es.enter_context(nc.psum_tensor("psb%d" % i, [128, 512], F32))[:, :] for i in range(8)]
        P = Prog(nc, es)
        for s in ("ld_x", "ld_c", "ld_w1", "ld_w2", "ld_wo", "ld_ada0", "ld_ada1", "ld_lw", "st_o", "st_d"):
            P.newsem(s)
        C = Carver(scr, CAP)

        xT = C.alloc([8, TOK], F32)
        tabs = C.alloc([2, L_LAT], BF16)
        KT = C.alloc([2, TOK], BF16)
        VA = C.alloc([18, 2, 128], BF16)
        ylru = C.alloc([2, TOK], BF16)
        c32 = C.alloc([256], F32)[:, :]
        c16 = C.alloc([256], BF16)
        mods = C.alloc([NL, 24, 3], F32)
        sc1 = C.alloc([NL, 8, 3], F32)
        gA = C.alloc([NL, 8, 3], F32)
        vecs = C.alloc([NL, NV], F32)
        cvec = C.alloc([NL, 4], F32)
        badaT = C.alloc([NL, 24], F32)
        anorm = C.alloc([2, 256], F32)
        absb = C.alloc([2, 128], F32)
        awsT = C.alloc([4, 128], BF16)
        lruW = C.alloc([8, 128], BF16)
        sil = C.alloc([8, 3], F32)
        epsr = C.alloc([4], F32)
        onesb = C.alloc([128], BF16)
        PBASE = C.off
        ident = c32[:, 0:128]
        onesD = c32[:, 128:256]
        Rmat = c16[:, 0:128]
        Bones = c16[:, 128:256]
        cosT = tabs[:, 0, :]
        sinT = tabs[:, 1, :]

        psi = [0, 0, 0]

        def nextps():
            i = psi[0] % 4
            psi[0] += 1
            return PSB[i], "ps%d" % i

        def nextps_s():
            i = 4 + psi[1] % 2
            psi[1] += 1
            return PSB[i], "ps%d" % i

        def nextps_o():
            i = 6 + psi[2] % 2
            psi[2] += 1
            return PSB[i], "ps%d" % i

        def act(out, in_, func, r, w, **kw):
            P.op("act", lambda E: E.activation(out=out, in_=in_, func=func, **kw), r, w)

        def tt(eng, out, a, b, op, r, w):
            P.op(eng, lambda E: E.tensor_tensor(out=out, in0=a, in1=b, op=op), r, w)

        def ts(eng, out, a, s1, s2, op0, op1, r, w):
            if s2 is None:
                P.op(eng, lambda E: E.tensor_single_scalar(out=out, in_=a, scalar=s1, op=op0), r, w)
            else:
                P.op(eng, lambda E: E.tensor_scalar(out=out, in0=a, scalar1=s1, scalar2=s2, op0=op0, op1=op1), r, w)

        def scan(out, d0, d1, init, r, w):
            P.op("dve", lambda E: E.tensor_tensor_scan(out=out, data0=d0, data1=d1, initial=init, op0=ALU.mult, op1=ALU.add), r, w)

        def stt(eng, out, a, s, b, op0, op1, r, w):
            P.op(eng, lambda E: E.scalar_tensor_tensor(out=out, in0=a, scalar=s, in1=b, op0=op0, op1=op1), r, w)

        def mm(out, lhsT, rhs, start, stop, r, w, inc=True):
            P.op("pe", lambda E: E.matmul(out, lhsT, rhs, start=start, stop=stop), r, w, inc=inc)

        def recip(out, in_, r, w):
            P.op("dve", lambda E: E.reciprocal(out=out, in_=in_), r, w)

        def memset(eng, ap, val, w):
            P.op(eng, lambda E: E.memset(ap, val), (), w)

        def copy(eng, out, in_, r, w):
            P.op(eng, lambda E: E.tensor_copy(out=out, in_=in_), r, w)

        P.dma("sp", "ld_c", c32, c32_d[:, :], writes=["c32"])
        P.dma("pool", "ld_lw", c16, c16_d[:, 0:256], writes=["c16"])
        for a_ in range(2):
            P.dma("pool", "ld_lw", tabs[:, a_, :], c16_d[:, 256 + a_ * L_LAT:256 + (a_ + 1) * L_LAT], writes=["tabs"])
        P.dma("sp", "ld_c", vecs, vecs_d[:, :, :], writes=["vecs"])
        P.dma("sp", "ld_c", badaT, bada_d[:, :, :], writes=["badaT"])
        P.dma("sp", "ld_c", sil, cT_d[:, :, :], writes=["sil"])
        memset("dve", epsr[:, 0:1], RMS_EPS, ["epsr"])
        memset("dve", epsr[:, 1:2], LN_EPS / (ALPHA * ALPHA), ["epsr"])
        memset("dve", epsr[:, 2:3], 1.0, ["epsr"])
        memset("dve", epsr[:, 3:4], 0.0, ["epsr"])
        memset("dve", onesb, 1.0 / 1024.0, ["onesb"])
        memset("pool", VA[:, :, :, 64:128], 1.0, ["VA"])

        act(sil, sil, AF.Silu, ["sil"], ["sil"])
        for l in layers:
            act(cvec[:, l, :], vecs[:, l, V_LAM:V_LAM + 4], AF.Exp, ["vecs"], ["cvec"], scale=-1.0)
            act(cvec[:, l, :], cvec[:, l, :], AF.Ln, ["cvec"], ["cvec"], bias=epsr[:, 2:3])
            ts("dve", cvec[:, l, :], cvec[:, l, :], -8.0, None, ALU.mult, ALU.bypass, ["cvec"], ["cvec"])
        C.off = PBASE
        stg = [C.alloc([8, 512], F32) for _ in range(2)]
        modrow = C.alloc([3072], F32)[:, :]
        bi = 0
        for l in layers:
            for nb in range(6):
                sb_ = stg[bi % 2]
                key = "stg%d" % (bi % 2)
                P.dma("sp", "ld_ada%d" % (bi % 2), sb_, wada_d[l, :, :, nb * 512:(nb + 1) * 512], writes=[key])
                ps, pk = nextps()
                for kc in range(8):
                    mm(ps[0:3, :], sil[:, kc, :], sb_[:, kc, :], kc == 0, kc == 7, [key, "sil"], [pk], inc=(kc == 7))
                copy("dve", modrow[0:3, nb * 512:(nb + 1) * 512], ps[0:3, :], [pk], ["modrow"])
                bi += 1
            ps, pk = nextps()
            for j in range(24):
                mm(ps[:, j * 3:(j + 1) * 3], modrow[0:3, j * 128:(j + 1) * 128], ident[0:3, 0:3], True, True,
                   ["modrow", "c32"], [pk], inc=(j == 23))
            tt("dve", mods[:, l, :, :], ps[:, 0:72].rearrange("p (a b) -> p a b", a=24),
               badaT[:, l, :].unsqueeze(2).to_broadcast([128, 24, 3]), ALU.add, [pk, "badaT"], ["mods"])
            ts("dve", sc1[:, l, :, :], mods[:, l, 8:16, :], 1.0, None, ALU.add, ALU.bypass, ["mods"], ["sc1"])
            ts("dve", gA[:, l, :, :], mods[:, l, 16:24, :], 1.0 / ALPHA, None, ALU.mult, ALU.bypass, ["mods"], ["gA"])
        P.fence()

        def rms_rope(zps, zk, n, gvec, dsts, dstk, t0, latent, S):
            act(S["sq"][:, 0:n], zps[:, 0:n], AF.Square, [zk], ["r_sq"])
            ps2, pk2 = nextps()
            mm(ps2[:, 0:n], Bones, S["sq"][:, 0:n], True, True, ["r_sq", "c16"], [pk2])
            act(S["sd"][:, 0:n], ps2[:, 0:n], AF.Ln, [pk2, "epsr"], ["r_sd"], bias=epsr[:, 0:1])
            act(S["sd"][:, 0:n], S["sd"][:, 0:n], AF.Exp, ["r_sd"], ["r_sd"], scale=-0.5)
            if not latent:
                for (lo, hi_, dst) in dsts:
                    stt("dve", dst, zps[lo:hi_, 0:n], gvec[lo:hi_, :], S["sd"][lo:hi_, 0:n], ALU.mult, ALU.mult,
                        [zk, "r_sd", "vecs"], [dstk])
                return
            stt("dve", S["qn"][:, 0:n], zps[:, 0:n], gvec, S["sd"][:, 0:n], ALU.mult, ALU.mult,
                [zk, "r_sd", "vecs"], ["r_qn"])
            ps3, pk3 = nextps()
            mm(ps3[:, 0:n], Rmat, S["qn"][:, 0:n], True, True, ["r_qn", "c16"], [pk3])
            tt("pool", S["t1"][:, 0:n], S["qn"][:, 0:n], cosT[:, t0:t0 + n], ALU.mult, ["r_qn", "tabs"], ["r_t1"])
            tt("dve", S["t2"][:, 0:n], ps3[:, 0:n], sinT[:, t0:t0 + n], ALU.mult, [pk3, "tabs"], ["r_t2"])
            for (lo, hi_, dst) in dsts:
                tt("pool", dst, S["t1"][lo:hi_, 0:n], S["t2"][lo:hi_, 0:n], ALU.add, ["r_t1", "r_t2"], [dstk])

        def xkeys(t0, n):
            return ["xT%d" % b for b in range(t0 // 256, (t0 + n + 255) // 256)]

        ALLX = ["xT%d" % b for b in range(9)]

        def modulate(dst, dk, l, r, t0, n):
            for c in range(8):
                eng = "pool" if c % 2 == 0 else "dve"
                ts(eng, dst[:, c, 0:n], xT[:, c, t0:t0 + n], sc1[:, l, c, r:r + 1], mods[:, l, c, r:r + 1],
                   ALU.mult, ALU.add, xkeys(t0, n) + ["sc1", "mods"], [dk])

        def inproj_fm(ps, pk, W, wk, col0, xm, xk, n):
            for kc in range(8):
                mm(ps[:, 0:n], W[:, kc, col0:col0 + 128], xm[:, kc, 0:n], kc == 0, kc == 7, [wk, xk], [pk], inc=(kc == 7))

        def stage(n):
            if n > stop:
                raise _Stop()

        W1P = (("k", 0, 256), ("v", 256, 128), ("rx", 384, 256), ("rg", 640, 256))
        W2P = (("u", C_U, 256), ("va", C_VA, 256), ("ga", C_GA, 256), ("bg", C_BG, 512), ("q", C_Q, 512))
        for nm, _, _ in W1P:
            P.newsem("ld_w1" + nm)
        for nm, _, _ in W2P:
            P.newsem("ld_w2" + nm)

        for s in range(nseq):
            try:
                for c in range(8):
                    P.dma("sp", "ld_x", xT[:, c, :], xin[s, :, c, :], writes=ALLX)
                for l in layers:
                    last = (l == DEPTH - 1)
                    P.dma("sp", "ld_c", anorm, anorm_d[l, :, :, :], writes=["anorm"])
                    P.dma("sp", "ld_c", absb, absb_d[l, :, :, :], writes=["absb"])
                    P.dma("pool", "ld_lw", awsT, aws_d[l, :, :, :], writes=["awsT"])
                    P.dma("pool", "ld_lw", lruW, lruw_d[l, :, :, :], writes=["lruW"])
                    stage(1)
                    P.fence()
                    C.off = PBASE
                    win1 = C.alloc([8, NC1], BF16)
                    xm1 = C.alloc([8, 512], BF16)
                    S1 = {"sq": C.alloc([512], BF16)[:, :], "sd": C.alloc([512], F32)[:, :],
                          "qn": C.alloc([512], BF16)[:, :], "t1": C.alloc([512], F32)[:, :], "t2": C.alloc([512], F32)[:, :]}
                    sg = C.alloc([2, TOK], BF16)
                    xrb = C.alloc([2, TOK], BF16)
                    GBASE = C.off
                    rx = C.alloc([2, 2312], F32)
                    acc = C.alloc([TOK], F32)[:, :]
                    for nm, c0, ncl in W1P:
                        P.dma("pool", "ld_w1" + nm, win1[:, :, c0:c0 + ncl], win_d[l, :, :, c0:c0 + ncl], writes=["win1" + nm])
                    memset("pool", rx[:, :, 0:2], 0.0, ["rx"])
                    memset("pool", rx[:, :, 2050:2052], 0.0, ["rx"])
                    memset("pool", rx[:, :, 2308:2312], 0.0, ["rx"])
                    for (t0, n) in ((0, 512), (512, 512), (1024, 512), (1536, 512), (2048, 256)):
                        latent = t0 < L_LAT
                        r = s if latent else 2
                        modulate(xm1, "xm1", l, r, t0, n)
                        for g in range(2):
                            ps, pk = nextps()
                            inproj_fm(ps, pk, win1, "win1k", C_K0 + g * 128, xm1, "xm1", n)
                            rms_rope(ps, pk, n, vecs[:, l, V_KG:V_KG + 1], [(0, 128, KT[:, g, t0:t0 + n])], "KT", t0, latent, S1)
                        for tl in range(n // 128):
                            ps, pk = nextps()
                            for kc in range(8):
                                mm(ps[:, 0:128], xm1[:, kc, tl * 128:(tl + 1) * 128], win1[:, kc, C_V:C_V + 128],
                                   kc == 0, kc == 7, ["win1v", "xm1"], [pk], inc=(kc == 7))
                            tix = t0 // 128 + tl
                            copy("dve", VA[:, tix, :, 0:64], ps[:, 0:128].rearrange("p (g d) -> p g d", g=2), [pk], ["VA"])
                        for ct in range(2):
                            ps, pk = nextps()
                            inproj_fm(ps, pk, win1, "win1rx", C_RX + ct * 128, xm1, "xm1", n)
                            off = 2 + t0 if latent else 2052 + (t0 - L_LAT)
                            act(rx[:, ct, off:off + n], ps[:, 0:n], AF.Copy, [pk], ["rx"])
                        for ct in range(2):
                            ps, pk = nextps()
                            inproj_fm(ps, pk, win1, "win1rg", C_RG + ct * 128, xm1, "xm1", n)
                            act(sg[:, ct, t0:t0 + n], ps[:, 0:n], AF.Silu, [pk], ["sg"])
                    stage(2)
                    for ct in range(2):
                        for (d0, o0, n) in ((2, 0, L_LAT), (2052, L_LAT, L_CTX)):
                            cw = lambda j: vecs[:, l, V_CW + ct * 4 + j:V_CW + ct * 4 + j + 1]
                            eng = "dve"
                            ts(eng, acc[:, 0:n], rx[:, ct, d0 - 2:d0 - 2 + n], cw(0), vecs[:, l, V_CB + ct:V_CB + ct + 1],
                               ALU.mult, ALU.add, ["rx", "vecs"], ["acc"])
                            stt(eng, acc[:, 0:n], rx[:, ct, d0 - 1:d0 - 1 + n], cw(1), acc[:, 0:n], ALU.mult, ALU.add,
                                ["rx", "vecs", "acc"], ["acc"])
                            stt(eng, acc[:, 0:n], rx[:, ct, d0:d0 + n], cw(2), acc[:, 0:n], ALU.mult, ALU.add,
                                ["rx", "vecs", "acc"], ["acc"])
                            stt(eng, xrb[:, ct, o0:o0 + n], rx[:, ct, d0 + 1:d0 + 1 + n], cw(3), acc[:, 0:n], ALU.mult, ALU.add,
                                ["rx", "vecs", "acc"], ["xrb"])
                    stage(3)
                    P.fence()
                    C.off = GBASE
                    Ab = C.alloc([TOK], F32)[:, :]
                    Tb = C.alloc([TOK], F32)[:, :]
                    Bb = C.alloc([TOK], F32)[:, :]
                    Hf = C.alloc([TOK], F32)[:, :]
                    for ct in range(2):
                        for dd in range(2):
                            vi = dd * 2 + ct
                            for (t0, n) in ((0, 512), (512, 512), (1024, 512), (1536, 512), (2048, 256)):
                                ps, pk = nextps()
                                mm(ps[:, 0:n], lruW[:, (dd * 2 + 0) * 2 + ct, :], xrb[:, ct, t0:t0 + n], True, True,
                                   ["lruW", "xrb"], [pk])
                                act(Ab[:, t0:t0 + n], ps[:, 0:n], AF.Sigmoid, [pk, "vecs"], ["Ab"],
                                    bias=vecs[:, l, V_BR + vi:V_BR + vi + 1])
                                ps, pk = nextps()
                                mm(ps[:, 0:n], lruW[:, (dd * 2 + 1) * 2 + ct, :], xrb[:, ct, t0:t0 + n], True, True,
                                   ["lruW", "xrb"], [pk])
                                act(Bb[:, t0:t0 + n], ps[:, 0:n], AF.Sigmoid, [pk, "vecs"], ["Bb"],
                                    bias=vecs[:, l, V_BI + vi:V_BI + vi + 1])
                            act(Ab, Ab, AF.Exp, ["Ab", "cvec"], ["Ab"], scale=cvec[:, l, vi:vi + 1])
                            act(Tb, Ab, AF.Square, ["Ab"], ["Tb"])
                            act(Tb, Tb, AF.Ln, ["Tb", "epsr"], ["Tb"], scale=-1.0, bias=epsr[:, 2:3])
                            act(Tb, Tb, AF.Exp, ["Tb"], ["Tb"], scale=0.5)
                            tt("dve", Bb, Bb, xrb[:, ct, :], ALU.mult, ["Bb", "xrb"], ["Bb"])
                            tt("dve", Bb, Bb, Tb, ALU.mult, ["Bb", "Tb"], ["Bb"])
                            if dd == 0:
                                scan(Hf[:, L_LAT:TOK], Ab[:, L_LAT:TOK], Bb[:, L_LAT:TOK], 0.0, ["Ab", "Bb"], ["Hf"])
                                scan(Hf[:, 0:L_LAT], Ab[:, 0:L_LAT], Bb[:, 0:L_LAT], Hf[:, TOK - 1:TOK], ["Ab", "Bb", "Hf"], ["Hf"])
                            else:
                                scan(rev_ap(Tb), rev_ap(Ab), rev_ap(Bb), 0.0, ["Ab", "Bb"], ["Tb"])
                        tt("pool", Hf, Hf, Tb, ALU.add, ["Hf", "Tb"], ["Hf"])
                        tt("pool", ylru[:, ct, :], Hf, sg[:, ct, :], ALU.mult, ["Hf", "sg"], ["ylru"])
                    stage(4)
                    P.fence()
                    C.off = PBASE
                    win2 = C.alloc([8, NC2], BF16)
                    wo = C.alloc([8, 1024], BF16)
                    for nm, c0, ncl in W2P:
                        P.dma("pool", "ld_w2" + nm, win2[:, :, c0:c0 + ncl], win_d[l, :, :, NC1 + c0:NC1 + c0 + ncl],
                              writes=["win2" + nm])
                    for kc in range(8):
                        P.dma("pool", "ld_wo", wo[:, kc, :], wo_d[l, :, kc, :], writes=["wo"])
                    xm = C.alloc([8, 256], BF16)
                    yT = C.alloc([6, 256], BF16)
                    ug = C.alloc([2, 256], F32)
                    sga = C.alloc([2, 256], F32)
                    vg = C.alloc([2, 256], F32)
                    vln = C.alloc([2, 256], BF16)
                    tmpg = C.alloc([512], F32)[:, :]
                    st = C.alloc([2, 8], F32)
                    qbd = C.alloc([4, 512], BF16)
                    sbg = C.alloc([4, 256], BF16)
                    S2 = {"sq": C.alloc([256], BF16)[:, :], "sd": C.alloc([256], F32)[:, :],
                          "qn": C.alloc([256], BF16)[:, :], "t1": C.alloc([256], F32)[:, :], "t2": C.alloc([256], F32)[:, :]}
                    pT = [C.alloc([512], BF16)[:, :] for _ in range(3)]
                    rden = [C.alloc([256], F32)[:, :] for _ in range(2)]
                    otmp = [C.alloc([256], F32)[:, :] for _ in range(2)]
                    sqb = C.alloc([8, 256], BF16)
                    tln = C.alloc([4, 256], F32)
                    mean_sb = C.alloc([256], F32)[:, :]
                    m2 = C.alloc([256], F32)[:, :]
                    rsl = C.alloc([256], F32)[:, :]
                    memset("pool", qbd[64:128, :, 0:256], 0.0, ["qbd"])
                    memset("pool", qbd[0:64, :, 256:512], 0.0, ["qbd"])
                    nblk = 8 if last else 9
                    pti = 0
                    hi = 0
                    for qb in range(nblk):
                        t0 = qb * 256
                        latent = qb < 8
                        r = s if latent else 2
                        xk = "xT%d" % qb
                        modulate(xm, "xm", l, r, t0, 256)
                        for ct in range(2):
                            ps, pk = nextps()
                            inproj_fm(ps, pk, win2, "win2u", C_U + ct * 128, xm, "xm", 256)
                            act(ug[:, ct, :], ps[:, 0:256], AF.Gelu_apprx_tanh, [pk], ["ug"])
                        for tl in range(2):
                            ps, pk = nextps()
                            for kc in range(8):
                                mm(ps[:, 0:256], xm[:, kc, tl * 128:(tl + 1) * 128], win2[:, kc, C_VA:C_VA + 256],
                                   kc == 0, kc == 7, ["win2va", "xm"], [pk], inc=(kc == 7))
                            act(vg[:, tl, :], ps[:, 0:256], AF.Gelu_apprx_tanh, [pk], ["vg", "st"], accum_out=st[:, tl, 0:1])
                        for ct in range(2):
                            ps, pk = nextps()
                            inproj_fm(ps, pk, win2, "win2ga", C_GA + ct * 128, xm, "xm", 256)
                            act(sga[:, ct, :], ps[:, 0:256], AF.Silu, [pk], ["sga"])
                        for ct in range(4):
                            ps, pk = nextps()
                            inproj_fm(ps, pk, win2, "win2bg", C_BG + ct * 128, xm, "xm", 256)
                            act(sbg[:, ct, :], ps[:, 0:256], AF.Silu, [pk], ["sbg"])
                        tt("pool", ug, ug, sga, ALU.mult, ["ug", "sga"], ["ug"])
                        stage(5)
                        for tl in range(2):
                            act(tmpg[:, 0:256], vg[:, tl, :], AF.Square, ["vg"], ["tmpg", "st"], accum_out=st[:, tl, 1:2])
                            ts("dve", st[:, tl, 2:3], st[:, tl, 0:1], 1.0 / 256.0, None, ALU.mult, ALU.bypass, ["st"], ["st"])
                            tt("dve", st[:, tl, 3:4], st[:, tl, 2:3], st[:, tl, 2:3], ALU.mult, ["st"], ["st"])
                            stt("dve", st[:, tl, 4:5], st[:, tl, 1:2], 1.0 / 256.0, st[:, tl, 3:4], ALU.mult, ALU.subtract,
                                ["st"], ["st"])
                            act(st[:, tl, 5:6], st[:, tl, 4:5], AF.Ln, ["st", "epsr"], ["st"], bias=epsr[:, 0:1], scale=1.0)
                            act(st[:, tl, 6:7], st[:, tl, 5:6], AF.Exp, ["st"], ["st"], scale=-0.5)
                            ts("dve", vg[:, tl, :], vg[:, tl, :], st[:, tl, 2:3], st[:, tl, 6:7], ALU.subtract, ALU.mult,
                               ["vg", "st"], ["vg"])
                            tt("pool", vg[:, tl, :], vg[:, tl, :], anorm[:, 0, :], ALU.mult, ["vg", "anorm"], ["vg"])
                            tt("pool", vln[:, tl, :], vg[:, tl, :], anorm[:, 1, :], ALU.add, ["vg", "anorm"], ["vln"])
                        ps, pk = nextps()
                        for tl in range(2):
                            for ct in range(2):
                                for hh in range(2):
                                    g = 2 * ct + hh
                                    co = (tl * 2 + ct) * 128
                                    mm(ps[hh * 64:(hh + 1) * 64, co:co + 128], vln[:, tl, g * 64:(g + 1) * 64], awsT[:, g, :],
                                       True, True, ["vln", "awsT"], [pk], inc=(tl == 1 and ct == 1 and hh == 1))
                        tt("dve", tmpg.rearrange("p (t c q) -> p t c q", t=2, c=2), ps.rearrange("p (t c q) -> p t c q", t=2, c=2),
                           absb.unsqueeze(1).to_broadcast([128, 2, 2, 128]), ALU.add, [pk, "absb"], ["tmpg"])
                        tt("pool", yT[:, 0:2, :].rearrange("p c (t q) -> p t c q", t=2),
                           tmpg.rearrange("p (t c q) -> p t c q", t=2, c=2),
                           ug.rearrange("p c (t q) -> p t c q", t=2), ALU.mult, ["tmpg", "ug"], ["yT"])
                        for ct in range(4):
                            ps, pk = nextps()
                            inproj_fm(ps, pk, win2, "win2q", C_Q + ct * 128, xm, "xm", 256)
                            rms_rope(ps, pk, 256, vecs[:, l, V_QG:V_QG + 1],
                                     [(0, 64, qbd[0:64, ct, 0:256]), (64, 128, qbd[64:128, ct, 256:512])],
                                     "qbd", t0, latent, S2)
                        stage(6)
                        ktiles = list(range(18)) if latent else [16, 17]
                        for ct in range(4):
                            g = ct // 2
                            pO, kO = nextps_o()

                            def qk(j):
                                ps_, pk_ = nextps_s()
                                mm(ps_[:, 0:512], KT[:, g, j * 128:(j + 1) * 128], qbd[:, ct, :], True, True,
                                   ["KT", "qbd"], [pk_])
                                return ps_, pk_

                            pend = qk(ktiles[0])
                            for ji, j in enumerate(ktiles):
                                ps_, pk_ = pend
                                if ji + 1 < len(ktiles):
                                    pend = qk(ktiles[ji + 1])
                                pb = pT[pti % 3]
                                pbk = "pT%d" % (pti % 3)
                                pti += 1
                                act(pb, ps_, AF.Exp, [pk_], [pbk], scale=0.125)
                                mm(pO[:, 0:512], VA[:, j, g, :], pb, ji == 0, ji == len(ktiles) - 1, ["VA", pbk], [kO])
                            rd = rden[hi % 2]
                            ot = otmp[hi % 2]
                            rk, ok = "rden%d" % (hi % 2), "otmp%d" % (hi % 2)
                            hi += 1
                            recip(rd[0:64, :], pO[64:128, 0:256], [kO], [rk])
                            recip(rd[64:128, :], pO[64:128, 256:512], [kO], [rk])
                            tt("dve", ot[0:64, :], pO[0:64, 0:256], rd[0:64, :], ALU.mult, [kO, rk], [ok])
                            tt("dve", ot[64:128, :], pO[0:64, 256:512], rd[64:128, :], ALU.mult, [kO, rk], [ok])
                            tt("pool", yT[:, 2 + ct, :], ot, sbg[:, ct, :], ALU.mult, [ok, "sbg"], ["yT"])
                        stage(7)
                        for dt_ in range(8):
                            ps, pk = nextps()
                            for kc in range(8):
                                rhs = yT[:, kc, :] if kc < 6 else ylru[:, kc - 6, t0:t0 + 256]
                                mm(ps[:, 0:256], wo[:, kc, dt_ * 128:(dt_ + 1) * 128], rhs, kc == 0, kc == 7,
                                   ["wo", "yT", "ylru"], [pk], inc=(kc == 7))
                            stt("dve", xT[:, dt_, t0:t0 + 256], ps[:, 0:256], gA[:, l, dt_, r:r + 1], xT[:, dt_, t0:t0 + 256],
                                ALU.mult, ALU.add, [pk, "gA", xk], [xk])
                        stage(8)
                        xblk = xT[:, :, t0:t0 + 256]
                        act(sqb, xblk, AF.Square, [xk], ["sqb"])
                        pm, km = nextps()
                        for dt_ in range(8):
                            mm(pm[:, 0:256], onesD, xT[:, dt_, t0:t0 + 256], dt_ == 0, dt_ == 7, [xk, "c32"], [km], inc=(dt_ == 7))
                        pq, kq = nextps()
                        for dt_ in range(8):
                            mm(pq[:, 0:256], onesb, sqb[:, dt_, :], dt_ == 0, dt_ == 7, ["sqb", "onesb"], [kq], inc=(dt_ == 7))
                        act(mean_sb, pm[:, 0:256], AF.Copy, [km], ["mean_sb"])
                        tt("pool", m2, mean_sb, mean_sb, ALU.mult, ["mean_sb"], ["m2"])
                        tt("dve", m2, pq[:, 0:256], m2, ALU.subtract, [kq, "m2"], ["m2"])
                        act(rsl, m2, AF.Ln, ["m2", "epsr"], ["rsl"], bias=epsr[:, 1:2], scale=1.0)
                        act(rsl, rsl, AF.Exp, ["rsl"], ["rsl"], scale=-0.5)
                        for half in range(2):
                            xs_ = xT[:, half * 4:(half + 1) * 4, t0:t0 + 256]
                            tt("pool", tln, xs_, mean_sb.unsqueeze(1).to_broadcast([128, 4, 256]), ALU.subtract,
                               [xk, "mean_sb"], ["tln"])
                            tt("dve", tln, tln, rsl.unsqueeze(1).to_broadcast([128, 4, 256]), ALU.mult, ["tln", "rsl"], ["tln"])
                            for dd_ in range(4):
                                dt_ = half * 4 + dd_
                                act(xT[:, dt_, t0:t0 + 256], tln[:, dd_, :], AF.Identity, ["tln", "vecs"], [xk],
                                    scale=vecs[:, l, V_LNG + dt_:V_LNG + dt_ + 1], bias=vecs[:, l, V_LNB + dt_:V_LNB + dt_ + 1])
            except _Stop:
                pass
            for c in range(8):
                P.dma("sp", "st_o", xout[s, :, c, :], xT[:, c, :], reads=ALLX)
        P.wait_all("sp", ["st_o", "st_d"])
        block = es.enter_context(nc.Block())
        P.emit(block)
        build_program.last_stats = (P.ninst, dict(P.cnt), C.hi)
    return nc


def _rope_tables():
    half, nf = 32, 16
    inv = (10000.0 ** (-np.arange(nf, dtype=np.float32) / nf)).astype(np.float32)
    t = np.arange(L_LAT)
    rows = (t // 64).astype(np.float32)
    cols = (t % 64).astype(np.float32)
    cos = np.zeros((64, L_LAT), np.float32)
    sin = np.zeros((64, L_LAT), np.float32)
    for d in range(64):
        pos = rows if d < 32 else cols
        f = d % 16
        ang = (pos * inv[f]).astype(np.float32)
        cos[d] = np.cos(ang)
        sgn = -1.0 if (d % 32) < 16 else 1.0
        sin[d] = sgn * np.sin(ang)
    return np.tile(cos, (2, 1)), np.tile(sin, (2, 1))


def _consts():
    ident = np.eye(128, dtype=np.float32)
    onesD = np.full((128, 128), 1.0 / 1024.0, np.float32)
    c32 = np.concatenate([ident, onesD], 1)
    R = np.zeros((128, 128), np.float32)
    for m in range(128):
        d = m % 64
        partner = d + 16 if (d % 32) < 16 else d - 16
        R[(m // 64) * 64 + partner, m] = 1.0
    B = np.zeros((128, 128), np.float32)
    B[0:64, 0:64] = 1.0 / 64.0
    B[64:128, 64:128] = 1.0 / 64.0
    cos, sin = _rope_tables()
    c16 = np.concatenate([R, B, cos, sin], 1)
    return np.ascontiguousarray(c32), np.ascontiguousarray(c16)


def _kc_layout(w):
    Ln, K, N = w.shape
    return np.ascontiguousarray(w.reshape(Ln, 8, 128, N).transpose(0, 2, 1, 3))


def prep_shared(inp):
    f = lambda k: np.asarray(inp[k], dtype=np.float32)
    w_in = f("w_in")
    a_u, a_v, a_g = w_in[:, :, 0:256], w_in[:, :, 256:512], w_in[:, :, 512:768]
    q, k, v = w_in[:, :, 768:1280], w_in[:, :, 1280:1408], w_in[:, :, 1408:1536]
    b_g, r_x, r_g = w_in[:, :, 1536:2048], w_in[:, :, 2048:2304], w_in[:, :, 2304:2560]
    k0, k1 = k[:, :, 0:64], k[:, :, 64:128]
    wperm = np.concatenate([k0, k0, k1, k1, v, r_x, r_g, a_u, a_g, a_v, q, b_g], axis=2)
    assert wperm.shape[2] == NC1 + NC2
    sh = {}
    sh["w_in"] = _kc_layout(wperm)
    sh["w_o"] = _kc_layout(f("w_o"))
    sh["w_ada"] = _kc_layout(f("w_ada"))
    sh["b_adaT"] = np.ascontiguousarray(f("b_ada").reshape(DEPTH, 24, 128).transpose(2, 0, 1))
    fm8 = lambda a: a.reshape(DEPTH, -1, 128).transpose(2, 0, 1)
    vec = np.zeros((128, DEPTH, NV), np.float32)
    vec[:, :, V_LNG:V_LNG + 8] = fm8(f("ln_g"))
    vec[:, :, V_LNB:V_LNB + 8] = fm8(f("ln_b"))
    vec[:, :, V_QG] = np.tile(f("q_norm_g"), (1, 2)).T
    vec[:, :, V_KG] = np.tile(f("k_norm_g"), (1, 2)).T
    cw = f("conv_w")
    for ct in range(2):
        for j in range(4):
            vec[:, :, V_CW + ct * 4 + j] = cw[:, j, ct * 128:(ct + 1) * 128].T
    vec[:, :, V_CB:V_CB + 2] = fm8(f("conv_b"))
    for nm, base in (("lru_br", V_BR), ("lru_bi", V_BI), ("lru_lam", V_LAM)):
        a = f(nm)
        for dd in range(2):
            for ct in range(2):
                vec[:, :, base + dd * 2 + ct] = a[:, dd, ct * 128:(ct + 1) * 128].T
    sh["vecs"] = vec
    an = np.stack([f("a_norm_g"), f("a_norm_b")], 1)
    sh["anorm"] = np.ascontiguousarray(np.broadcast_to(an[:, None], (DEPTH, 128, 2, 256)))
    bs = f("a_bs")
    ab = np.zeros((DEPTH, 128, 2, 128), np.float32)
    for ct in range(2):
        ab[:, 0:64, ct, :] = bs[:, 2 * ct, None, :]
        ab[:, 64:128, ct, :] = bs[:, 2 * ct + 1, None, :]
    sh["absb"] = ab
    sh["awsT"] = np.ascontiguousarray(f("a_ws").transpose(0, 3, 1, 2))
    lw = np.zeros((DEPTH, 128, 8, 128), np.float32)
    for dd in range(2):
        for gi, nm in enumerate(("lru_wr", "lru_wi")):
            w = f(nm)
            for ct in range(2):
                idx = (dd * 2 + gi) * 2 + ct
                for hb in range(2):
                    lw[:, hb * 64:(hb + 1) * 64, idx, hb * 64:(hb + 1) * 64] = w[:, dd, 2 * ct + hb]
    sh["lruW"] = lw
    sh["cst32"], sh["cst16"] = _consts()
    return sh


def prep_core(inp, b0, nseq):
    x = np.asarray(inp["x"], np.float32)
    ctx = np.asarray(inp["ctx"], np.float32)
    c = np.asarray(inp["c"], np.float32)
    c_ctx = np.asarray(inp["c_ctx"], np.float32)
    xin = np.empty((nseq, 128, 8, TOK), np.float32)
    for s in range(nseq):
        full = np.concatenate([x[b0 + s], ctx[b0 + s]], 0)
        xin[s] = full.T.reshape(8, 128, TOK).transpose(1, 0, 2)
    rows = [c[b0 + s] if s < nseq else c_ctx for s in range(2)] + [c_ctx]
    crow = np.stack(rows, 0)
    cT = np.ascontiguousarray(crow.T.reshape(8, 128, 3).transpose(1, 0, 2))
    return {"xin": xin, "cT": cT}


_PROG_CACHE = {}


def kernel(**inputs):
    n = 8
    nseq = 2
    key = ("full", nseq)
    if key not in _PROG_CACHE:
        _PROG_CACHE[key] = build_program(list(range(DEPTH)), nseq)
    nc = _PROG_CACHE[key]
    sh = prep_shared(inputs)
    in_maps = []
    for i in range(n):
        m = dict(sh)
        m.update(prep_core(inputs, i * nseq, nseq))
        in_maps.append(m)
    res = run_bass_kernel_spmd(nc, in_maps, core_ids=list(range(n)))
    out = np.empty((16, L_LAT, D), np.float32)
    for i in range(n):
        xo = np.asarray(res.results[i]["xout"])
        for s in range(nseq):
            out[i * nseq + s] = xo[s][:, :, 0:L_LAT].transpose(2, 1, 0).reshape(L_LAT, D)
    return out
```

```python
import math
import numpy as np
from contextlib import ExitStack
import concourse.bass as bass
import concourse.mybir as mybir
from concourse.bass_utils import run_bass_kernel_spmd

F32 = mybir.dt.float32
BF16 = mybir.dt.bfloat16
AF = mybir.ActivationFunctionType
ALU = mybir.AluOpType

D = 1024
L_LAT = 2048
L_CTX = 256
TOK = L_LAT + L_CTX
DEPTH = 4
ALPHA = (2.0 * DEPTH) ** 0.25
LN_EPS = 1e-6
RMS_EPS = 1e-6
NC1 = 896
NC2 = 1792
C_K0, C_K1, C_V, C_RX, C_RG = 0, 128, 256, 384, 640
C_U, C_GA, C_VA, C_Q, C_BG = 0, 256, 512, 768, 1280
NV = 40
V_LNG, V_LNB, V_QG, V_KG, V_CW, V_CB, V_BR, V_BI, V_LAM = 0, 8, 16, 17, 18, 26, 28, 32, 36

ENGS = ("pe", "act", "dve", "pool", "sp")


class Prog:
    def __init__(self, nc, es):
        self.nc = nc
        self.es = es
        self.q = {e: [] for e in ENGS}
        self.cnt = {}
        self.sem = {}
        for e in ENGS:
            self.sem[e] = es.enter_context(nc.semaphore("c_" + e))
            self.cnt[e] = 0
        self.known = {e: {} for e in ENGS}
        self.lastw = {}
        self.readers = {}
        self.pending = {e: False for e in ENGS}
        self.ninst = 0

    def newsem(self, name):
        self.sem[name] = self.es.enter_context(self.nc.semaphore(name))
        self.cnt[name] = 0
        return name

    def _deps(self, eng, reads, writes):
        deps = {}

        def add(sk, v):
            if v > deps.get(sk, 0):
                deps[sk] = v

        for k in reads:
            w = self.lastw.get(k)
            if w is not None:
                add(*w)
        for k in writes:
            w = self.lastw.get(k)
            if w is not None and w[0] != eng:
                add(*w)
            for sk, v in self.readers.get(k, {}).items():
                if sk != eng:
                    add(sk, v)
        if eng == "pe":
            deps.pop("pe", None)
        return deps

    def _emit_waits(self, eng, deps):
        kn = self.known[eng]
        for sk, v in deps.items():
            if sk not in ENGS:
                v = max(v, self.cnt[sk])
            if sk == eng and v > self.cnt[eng]:
                raise RuntimeError("self-dependency on pending op: " + eng)
            if kn.get(sk, 0) >= v:
                continue
            kn[sk] = v
            sem = self.sem[sk]
            self.q[eng].append(lambda E, sem=sem, v=v: E.wait_ge(sem, v))

    def op(self, eng, fn, reads=(), writes=(), inc=True):
        deps = self._deps(eng, reads, writes)
        self._emit_waits(eng, deps)
        n = self.cnt[eng] + 1
        for k in reads:
            self.readers.setdefault(k, {})[eng] = n
        for k in writes:
            self.lastw[k] = (eng, n)
            self.readers[k] = {}
        self.ninst += 1
        if inc:
            self.cnt[eng] = n
            sem = self.sem[eng]
            self.q[eng].append(lambda E, sem=sem: fn(E).then_inc(sem, 1))
            self.pending[eng] = False
        else:
            self.q[eng].append(lambda E: fn(E))
            self.pending[eng] = True

    def dma(self, queue, semname, out, in_, reads=(), writes=()):
        deps = self._deps(queue, reads, writes)
        self._emit_waits(queue, deps)
        n = self.cnt[semname] + 16
        self.cnt[semname] = n
        for k in reads:
            self.readers.setdefault(k, {})[semname] = n
        for k in writes:
            self.lastw[k] = (semname, n)
            self.readers[k] = {}
        sem = self.sem[semname]
        self.q[queue].append(lambda E, sem=sem: E.dma_start(out=out, in_=in_).then_inc(sem, 16))
        self.ninst += 1

    def fence(self):
        for e in ENGS:
            kn = self.known[e]
            for s, v in self.cnt.items():
                if s == e or v == 0 or kn.get(s, 0) >= v:
                    continue
                kn[s] = v
                sem = self.sem[s]
                self.q[e].append(lambda E, sem=sem, v=v: E.wait_ge(sem, v))

    def wait_all(self, eng, semnames):
        for s in semnames:
            v = self.cnt[s]
            if v > 0 and self.known[eng].get(s, 0) < v:
                self.known[eng][s] = v
                sem = self.sem[s]
                self.q[eng].append(lambda E, sem=sem, v=v: E.wait_ge(sem, v))

    def emit(self, block):
        for e in ENGS:
            assert not self.pending[e], "engine %s ends with un-inc'd op" % e
        q = self.q

        @block.tensor
        def _(E):
            for f in q["pe"]:
                f(E)

        @block.scalar
        def _(E):
            for f in q["act"]:
                f(E)

        @block.vector
        def _(E):
            for f in q["dve"]:
                f(E)

        @block.gpsimd
        def _(E):
            for f in q["pool"]:
                f(E)

        @block.sync
        def _(E):
            for f in q["sp"]:
                f(E)


class Carver:
    def __init__(self, scr, nbytes):
        self.scr = scr
        self.cap = nbytes
        self.off = 0
        self.hi = 0

    def alloc(self, shape, dt):
        esz = 4 if dt == F32 else 2
        n = 1
        for s in shape:
            n *= s
        nb = (n * esz + 31) // 32 * 32
        assert self.off + nb <= self.cap, ("SBUF scratch overflow", self.off, nb, self.cap)
        w0 = self.off // 4
        ap = self.scr[:, w0:w0 + nb // 4]
        if dt != F32:
            ap = ap.bitcast(dt)
        ap = ap[:, 0:n]
        if len(shape) == 2:
            ap = ap.rearrange("p (a b) -> p a b", a=shape[0])
        elif len(shape) == 3:
            ap = ap.rearrange("p (a b c) -> p a b c", a=shape[0], b=shape[1])
        self.off += nb
        self.hi = max(self.hi, self.off)
        return ap


def rev_ap(ap_):
    n = ap_.shape[1]
    pp = ap_.ap[0]
    return bass.AP(ap_.tensor, ap_.offset + (n - 1), [[pp[0], pp[1]], [-1, n]])


class _Stop(Exception):
    pass


def build_program(layers, nseq, first_in_program=True, dbg=None, stop=99):
    nc = bass.Bass("TRN2", target_bir_lowering=False)
    NL = DEPTH
    dram = lambda name, shape, dt=F32, kind="ExternalInput": nc.dram_tensor(name, shape, dt, kind=kind).ap()
    xin = dram("xin", [nseq, 128, 8, TOK])
    cT_d = dram("cT", [128, 8, 3])
    wada_d = dram("w_ada", [NL, 128, 8, 3072])
    bada_d = dram("b_adaT", [128, NL, 24])
    win_d = dram("w_in", [NL, 128, 8, NC1 + NC2])
    wo_d = dram("w_o", [NL, 128, 8, 1024])
    vecs_d = dram("vecs", [128, NL, NV])
    anorm_d = dram("anorm", [NL, 128, 2, 256])
    absb_d = dram("absb", [NL, 128, 2, 128])
    aws_d = dram("awsT", [NL, 128, 4, 128])
    lruw_d = dram("lruW", [NL, 128, 8, 128])
    c32_d = dram("cst32", [128, 256])
    c16_d = dram("cst16", [128, 256 + 2 * L_LAT])
    xout = dram("xout", [nseq, 128, 8, TOK], kind="ExternalOutput")
    dbg_out = {}
    if dbg:
        for name, shape in dbg.items():
            dbg_out[name] = dram("dbg_" + name, shape, kind="ExternalOutput")

    with ExitStack() as es:
        CAP = 212800
        scr = es.enter_context(nc.sbuf_tensor("scr", [128, CAP // 4], F32))
        psall = es.enter_context(nc.psum_tensor("psall", [128, 4096], F32))[:, :]
        PSB = [psall[:, i * 512:(i + 1) * 512] for i in range(8)]
        P = Prog(nc, es)
        for s in ("ld_x", "ld_c", "ld_w1", "ld_w2", "ld_wo", "ld_ada0", "ld_ada1", "ld_lw", "st_o", "st_d"):
            P.newsem(s)
        C = Carver(scr, CAP)

        xT = C.alloc([8, TOK], F32)
        tabs = C.alloc([2, L_LAT], BF16)
        KT = C.alloc([2, TOK], BF16)
        VA = C.alloc([18, 2, 128], BF16)
        ylru = C.alloc([2, TOK], BF16)
        c32 = C.alloc([256], F32)[:, :]
        c16 = C.alloc([256], BF16)
        mods = C.alloc([NL, 24, 3], F32)
        sc1 = C.alloc([NL, 8, 3], F32)
        gA = C.alloc([NL, 8, 3], F32)
        vecs = C.alloc([NL, NV], F32)
        cvec = C.alloc([NL, 4], F32)
        badaT = C.alloc([NL, 24], F32)
        anorm = C.alloc([2, 256], F32)
        absb = C.alloc([2, 128], F32)
        awsT = C.alloc([4, 128], BF16)
        lruW = C.alloc([8, 128], BF16)
        sil = C.alloc([8, 3], F32)
        epsr = C.alloc([4], F32)
        onesb = C.alloc([128], BF16)
        PBASE = C.off
        ident = c32[:, 0:128]
        onesD = c32[:, 128:256]
        Rmat = c16[:, 0:128]
        Bones = c16[:, 128:256]
        cosT = tabs[:, 0, :]
        sinT = tabs[:, 1, :]

        psi = [0, 0, 0]

        def nextps():
            i = psi[0] % 4
            psi[0] += 1
            return PSB[i], "ps%d" % i

        def nextps2():
            i = ((psi[0] + 1) // 2 * 2) % 4
            psi[0] = i + 2
            return psall[:, i * 512:(i + 2) * 512], ["ps%d" % i, "ps%d" % (i + 1)]

        def nextps_s():
            i = 4 + psi[1] % 2
            psi[1] += 1
            return PSB[i], "ps%d" % i

        def nextps_o():
            i = 6 + psi[2] % 2
            psi[2] += 1
            return PSB[i], "ps%d" % i

        def act(out, in_, func, r, w, **kw):
            P.op("act", lambda E: E.activation(out=out, in_=in_, func=func, **kw), r, w)

        def tt(eng, out, a, b, op, r, w):
            P.op(eng, lambda E: E.tensor_tensor(out=out, in0=a, in1=b, op=op), r, w)

        def ts(eng, out, a, s1, s2, op0, op1, r, w):
            if s2 is None:
                P.op(eng, lambda E: E.tensor_single_scalar(out=out, in_=a, scalar=s1, op=op0), r, w)
            else:
                P.op(eng, lambda E: E.tensor_scalar(out=out, in0=a, scalar1=s1, scalar2=s2, op0=op0, op1=op1), r, w)

        def scan(out, d0, d1, init, r, w):
            P.op("dve", lambda E: E.tensor_tensor_scan(out=out, data0=d0, data1=d1, initial=init, op0=ALU.mult, op1=ALU.add), r, w)

        def stt(eng, out, a, s, b, op0, op1, r, w):
            P.op(eng, lambda E: E.scalar_tensor_tensor(out=out, in0=a, scalar=s, in1=b, op0=op0, op1=op1), r, w)

        def mm(out, lhsT, rhs, start, stop, r, w, inc=True):
            P.op("pe", lambda E: E.matmul(out, lhsT, rhs, start=start, stop=stop), r, w, inc=inc)

        def recip(out, in_, r, w):
            P.op("dve", lambda E: E.reciprocal(out=out, in_=in_), r, w)

        def memset(eng, ap, val, w):
            P.op(eng, lambda E: E.memset(ap, val), (), w)

        def copy(eng, out, in_, r, w):
            P.op(eng, lambda E: E.tensor_copy(out=out, in_=in_), r, w)

        P.dma("sp", "ld_c", c32, c32_d[:, :], writes=["c32"])
        P.dma("pool", "ld_lw", c16, c16_d[:, 0:256], writes=["c16"])
        for a_ in range(2):
            P.dma("pool", "ld_lw", tabs[:, a_, :], c16_d[:, 256 + a_ * L_LAT:256 + (a_ + 1) * L_LAT], writes=["tabs"])
        P.dma("sp", "ld_c", vecs, vecs_d[:, :, :], writes=["vecs"])
        P.dma("sp", "ld_c", badaT, bada_d[:, :, :], writes=["badaT"])
        P.dma("sp", "ld_c", sil, cT_d[:, :, :], writes=["sil"])
        memset("dve", epsr[:, 0:1], RMS_EPS, ["epsr"])
        memset("dve", epsr[:, 1:2], LN_EPS / (ALPHA * ALPHA), ["epsr"])
        memset("dve", epsr[:, 2:3], 1.0, ["epsr"])
        memset("dve", epsr[:, 3:4], 0.0, ["epsr"])
        memset("dve", onesb, 1.0 / 1024.0, ["onesb"])
        memset("pool", VA[:, :, :, 64:128], 1.0, ["VA"])

        act(sil, sil, AF.Silu, ["sil"], ["sil"])
        for l in layers:
            act(cvec[:, l, :], vecs[:, l, V_LAM:V_LAM + 4], AF.Exp, ["vecs"], ["cvec"], scale=-1.0)
            act(cvec[:, l, :], cvec[:, l, :], AF.Ln, ["cvec"], ["cvec"], bias=epsr[:, 2:3])
            ts("dve", cvec[:, l, :], cvec[:, l, :], -8.0, None, ALU.mult, ALU.bypass, ["cvec"], ["cvec"])
        C.off = PBASE
        stg = [C.alloc([8, 512], F32) for _ in range(2)]
        modrow = C.alloc([3072], F32)[:, :]
        bi = 0
        for l in layers:
            for nb in range(6):
                sb_ = stg[bi % 2]
                key = "stg%d" % (bi % 2)
                P.dma("sp", "ld_ada%d" % (bi % 2), sb_, wada_d[l, :, :, nb * 512:(nb + 1) * 512], writes=[key])
                ps, pk = nextps()
                for kc in range(8):
                    mm(ps[0:3, :], sil[:, kc, :], sb_[:, kc, :], kc == 0, kc == 7, [key, "sil"], [pk], inc=(kc == 7))
                copy("dve", modrow[0:3, nb * 512:(nb + 1) * 512], ps[0:3, :], [pk], ["modrow"])
                bi += 1
            ps, pk = nextps()
            for j in range(24):
                mm(ps[:, j * 3:(j + 1) * 3], modrow[0:3, j * 128:(j + 1) * 128], ident[0:3, 0:3], True, True,
                   ["modrow", "c32"], [pk], inc=(j == 23))
            tt("dve", mods[:, l, :, :], ps[:, 0:72].rearrange("p (a b) -> p a b", a=24),
               badaT[:, l, :].unsqueeze(2).to_broadcast([128, 24, 3]), ALU.add, [pk, "badaT"], ["mods"])
            ts("dve", sc1[:, l, :, :], mods[:, l, 8:16, :], 1.0, None, ALU.add, ALU.bypass, ["mods"], ["sc1"])
            ts("dve", gA[:, l, :, :], mods[:, l, 16:24, :], 1.0 / ALPHA, None, ALU.mult, ALU.bypass, ["mods"], ["gA"])
        P.fence()

        def rms_rope(zps, zk, n, gvec, dsts, dstk, t0, latent, S):
            act(S["sq"][:, 0:n], zps[:, 0:n], AF.Square, [zk], ["r_sq"])
            ps2, pk2 = nextps()
            mm(ps2[:, 0:n], Bones, S["sq"][:, 0:n], True, True, ["r_sq", "c16"], [pk2])
            act(S["sd"][:, 0:n], ps2[:, 0:n], AF.Ln, [pk2, "epsr"], ["r_sd"], bias=epsr[:, 0:1])
            act(S["sd"][:, 0:n], S["sd"][:, 0:n], AF.Exp, ["r_sd"], ["r_sd"], scale=-0.5)
            if not latent:
                for (lo, hi_, dst) in dsts:
                    stt("dve", dst, zps[lo:hi_, 0:n], gvec[lo:hi_, :], S["sd"][lo:hi_, 0:n], ALU.mult, ALU.mult,
                        [zk, "r_sd", "vecs"], [dstk])
                return
            stt("dve", S["qn"][:, 0:n], zps[:, 0:n], gvec, S["sd"][:, 0:n], ALU.mult, ALU.mult,
                [zk, "r_sd", "vecs"], ["r_qn"])
            ps3, pk3 = nextps()
            mm(ps3[:, 0:n], Rmat, S["qn"][:, 0:n], True, True, ["r_qn", "c16"], [pk3])
            tt("pool", S["t1"][:, 0:n], S["qn"][:, 0:n], cosT[:, t0:t0 + n], ALU.mult, ["r_qn", "tabs"], ["r_t1"])
            tt("dve", S["t2"][:, 0:n], ps3[:, 0:n], sinT[:, t0:t0 + n], ALU.mult, [pk3, "tabs"], ["r_t2"])
            for (lo, hi_, dst) in dsts:
                tt("pool", dst, S["t1"][lo:hi_, 0:n], S["t2"][lo:hi_, 0:n], ALU.add, ["r_t1", "r_t2"], [dstk])

        def xkeys(t0, n):
            return ["xT%d" % b for b in range(t0 // 256, (t0 + n + 255) // 256)]

        ALLX = ["xT%d" % b for b in range(9)]

        def modulate(dst, dk, l, r, t0, n, first=False):
            for c in range(8):
                eng = "dve" if (first or c % 2 == 1) else "pool"
                ts(eng, dst[:, c, 0:n], xT[:, c, t0:t0 + n], sc1[:, l, c, r:r + 1], mods[:, l, c, r:r + 1],
                   ALU.mult, ALU.add, xkeys(t0, n) + ["sc1", "mods"], [dk])

        def inproj_fm(ps, pk, W, wk, col0, xm, xk, n):
            for kc in range(8):
                mm(ps[:, 0:n], W[:, kc, col0:col0 + 128], xm[:, kc, 0:n], kc == 0, kc == 7, [wk, xk], [pk], inc=(kc == 7))

        def stage(n):
            if n > stop:
                raise _Stop()

        W1P = (("k", 0, 256), ("v", 256, 128), ("rx", 384, 256), ("rg", 640, 256))
        W2P = (("u", C_U, 256), ("va", C_VA, 256), ("ga", C_GA, 256), ("bg", C_BG, 512), ("q", C_Q, 512))
        for nm, _, _ in W1P:
            P.newsem("ld_w1" + nm)
        for nm, _, _ in W2P:
            P.newsem("ld_w2" + nm)

        for s in range(nseq):
            try:
                for c in range(8):
                    P.dma("sp", "ld_x", xT[:, c, :], xin[s, :, c, :], writes=ALLX)
                for l in layers:
                    last = (l == DEPTH - 1)
                    P.dma("sp", "ld_c", anorm, anorm_d[l, :, :, :], writes=["anorm"])
                    P.dma("sp", "ld_c", absb, absb_d[l, :, :, :], writes=["absb"])
                    P.dma("pool", "ld_lw", awsT, aws_d[l, :, :, :], writes=["awsT"])
                    P.dma("pool", "ld_lw", lruW, lruw_d[l, :, :, :], writes=["lruW"])
                    stage(1)
                    P.fence()
                    C.off = PBASE
                    win1 = C.alloc([8, NC1], BF16)
                    xm1 = C.alloc([8, 512], BF16)
                    S1 = {"sq": C.alloc([512], BF16)[:, :], "sd": C.alloc([512], F32)[:, :],
                          "qn": C.alloc([512], BF16)[:, :], "t1": C.alloc([512], F32)[:, :]}
                    assert C.off - PBASE == NC2 * 8 * 2, (C.off - PBASE)
                    S1["t2"] = C.alloc([512], F32)[:, :]
                    sg = C.alloc([2, TOK], BF16)
                    xrb = C.alloc([2, TOK], BF16)
                    GBASE = C.off
                    rx = C.alloc([2, 2312], F32)
                    acc = C.alloc([TOK], F32)[:, :]
                    for nm, c0, ncl in W1P:
                        P.dma("pool", "ld_w1" + nm, win1[:, :, c0:c0 + ncl], win_d[l, :, :, c0:c0 + ncl], writes=["win1" + nm])
                    memset("pool", rx[:, :, 0:2], 0.0, ["rx"])
                    memset("pool", rx[:, :, 2050:2052], 0.0, ["rx"])
                    memset("pool", rx[:, :, 2308:2312], 0.0, ["rx"])
                    for (t0, n) in ((0, 512), (512, 512), (1024, 512), (1536, 512), (2048, 256)):
                        latent = t0 < L_LAT
                        r = s if latent else 2
                        modulate(xm1, "xm1", l, r, t0, n, first=(t0 == 0))
                        for g in range(2):
                            ps, pk = nextps()
                            inproj_fm(ps, pk, win1, "win1k", C_K0 + g * 128, xm1, "xm1", n)
                            rms_rope(ps, pk, n, vecs[:, l, V_KG:V_KG + 1], [(0, 128, KT[:, g, t0:t0 + n])], "KT", t0, latent, S1)
                        for tl in range(n // 128):
                            ps, pk = nextps()
                            for kc in range(8):
                                mm(ps[:, 0:128], xm1[:, kc, tl * 128:(tl + 1) * 128], win1[:, kc, C_V:C_V + 128],
                                   kc == 0, kc == 7, ["win1v", "xm1"], [pk], inc=(kc == 7))
                            tix = t0 // 128 + tl
                            copy("dve", VA[:, tix, :, 0:64], ps[:, 0:128].rearrange("p (g d) -> p g d", g=2), [pk], ["VA"])
                        for ct in range(2):
                            ps, pk = nextps()
                            inproj_fm(ps, pk, win1, "win1rx", C_RX + ct * 128, xm1, "xm1", n)
                            off = 2 + t0 if latent else 2052 + (t0 - L_LAT)
                            act(rx[:, ct, off:off + n], ps[:, 0:n], AF.Copy, [pk], ["rx"])
                        for ct in range(2):
                            ps, pk = nextps()
                            inproj_fm(ps, pk, win1, "win1rg", C_RG + ct * 128, xm1, "xm1", n)
                            act(sg[:, ct, t0:t0 + n], ps[:, 0:n], AF.Silu, [pk], ["sg"])
                    stage(2)
                    for ct in range(2):
                        for (d0, o0, n) in ((2, 0, L_LAT), (2052, L_LAT, L_CTX)):
                            cw = lambda j: vecs[:, l, V_CW + ct * 4 + j:V_CW + ct * 4 + j + 1]
                            eng = "dve"
                            ts(eng, acc[:, 0:n], rx[:, ct, d0 - 2:d0 - 2 + n], cw(0), vecs[:, l, V_CB + ct:V_CB + ct + 1],
                               ALU.mult, ALU.add, ["rx", "vecs"], ["acc"])
                            stt(eng, acc[:, 0:n], rx[:, ct, d0 - 1:d0 - 1 + n], cw(1), acc[:, 0:n], ALU.mult, ALU.add,
                                ["rx", "vecs", "acc"], ["acc"])
                            stt(eng, acc[:, 0:n], rx[:, ct, d0:d0 + n], cw(2), acc[:, 0:n], ALU.mult, ALU.add,
                                ["rx", "vecs", "acc"], ["acc"])
                            stt(eng, xrb[:, ct, o0:o0 + n], rx[:, ct, d0 + 1:d0 + 1 + n], cw(3), acc[:, 0:n], ALU.mult, ALU.add,
                                ["rx", "vecs", "acc"], ["xrb"])
                    stage(3)
                    P.fence()
                    C.off = PBASE
                    win2 = C.alloc([8, NC2], BF16)
                    for nm, c0, ncl in W2P:
                        P.dma("pool", "ld_w2" + nm, win2[:, :, c0:c0 + ncl], win_d[l, :, :, NC1 + c0:NC1 + c0 + ncl],
                              writes=["win2" + nm])
                    C.off = GBASE
                    Ab = C.alloc([TOK], F32)[:, :]
                    Tb = C.alloc([TOK], F32)[:, :]
                    Bb = C.alloc([TOK], F32)[:, :]
                    Hf = C.alloc([TOK], F32)[:, :]
                    for ct in range(2):
                        for dd in range(2):
                            vi = dd * 2 + ct
                            for (t0, n) in ((0, 512), (512, 512), (1024, 512), (1536, 512), (2048, 256)):
                                ps, pk = nextps()
                                mm(ps[:, 0:n], lruW[:, (dd * 2 + 0) * 2 + ct, :], xrb[:, ct, t0:t0 + n], True, True,
                                   ["lruW", "xrb"], [pk])
                                act(Ab[:, t0:t0 + n], ps[:, 0:n], AF.Sigmoid, [pk, "vecs"], ["Ab"],
                                    bias=vecs[:, l, V_BR + vi:V_BR + vi + 1])
                                ps, pk = nextps()
                                mm(ps[:, 0:n], lruW[:, (dd * 2 + 1) * 2 + ct, :], xrb[:, ct, t0:t0 + n], True, True,
                                   ["lruW", "xrb"], [pk])
                                act(Bb[:, t0:t0 + n], ps[:, 0:n], AF.Sigmoid, [pk, "vecs"], ["Bb"],
                                    bias=vecs[:, l, V_BI + vi:V_BI + vi + 1])
                            act(Ab, Ab, AF.Exp, ["Ab", "cvec"], ["Ab"], scale=cvec[:, l, vi:vi + 1])
                            act(Tb, Ab, AF.Square, ["Ab"], ["Tb"])
                            act(Tb, Tb, AF.Ln, ["Tb", "epsr"], ["Tb"], scale=-1.0, bias=epsr[:, 2:3])
                            act(Tb, Tb, AF.Exp, ["Tb"], ["Tb"], scale=0.5)
                            tt("dve", Bb, Bb, xrb[:, ct, :], ALU.mult, ["Bb", "xrb"], ["Bb"])
                            tt("dve", Bb, Bb, Tb, ALU.mult, ["Bb", "Tb"], ["Bb"])
                            if dd == 0:
                                scan(Hf[:, L_LAT:TOK], Ab[:, L_LAT:TOK], Bb[:, L_LAT:TOK], 0.0, ["Ab", "Bb"], ["Hf"])
                                scan(Hf[:, 0:L_LAT], Ab[:, 0:L_LAT], Bb[:, 0:L_LAT], Hf[:, TOK - 1:TOK], ["Ab", "Bb", "Hf"], ["Hf"])
                            else:
                                scan(rev_ap(Tb), rev_ap(Ab), rev_ap(Bb), 0.0, ["Ab", "Bb"], ["Tb"])
                        tt("pool", Hf, Hf, Tb, ALU.add, ["Hf", "Tb"], ["Hf"])
                        tt("pool", ylru[:, ct, :], Hf, sg[:, ct, :], ALU.mult, ["Hf", "sg"], ["ylru"])
                    stage(4)
                    P.fence()
                    C.off = PBASE
                    win2 = C.alloc([8, NC2], BF16)
                    wo = C.alloc([8, 1024], BF16)
                    for kc in range(8):
                        P.dma("pool", "ld_wo", wo[:, kc, :], wo_d[l, :, kc, :], writes=["wo"])
                    xm = C.alloc([8, 256], BF16)
                    yT = C.alloc([6, 256], BF16)
                    ug = C.alloc([2, 256], F32)
                    sga = C.alloc([2, 256], F32)
                    vg = C.alloc([2, 256], F32)
                    vln = C.alloc([2, 256], BF16)
                    tmpg = C.alloc([512], F32)[:, :]
                    st = C.alloc([2, 8], F32)
                    qbd = C.alloc([4, 512], BF16)
                    sbg = C.alloc([4, 256], BF16)
                    bufA = C.alloc([1024], BF16)[:, :]
                    bufBC = C.alloc([2048], F32)[:, :]
                    bufB = bufBC[:, 0:1024]
                    bufC = bufBC[:, 1024:2048]
                    pT = [C.alloc([512], BF16)[:, :] for _ in range(3)]
                    rden = [C.alloc([256], F32)[:, :] for _ in range(2)]
                    otmp = [C.alloc([256], F32)[:, :]] * 2
                    sqb = C.alloc([8, 256], BF16)
                    mean_sb = C.alloc([256], F32)[:, :]
                    m2 = C.alloc([256], F32)[:, :]
                    rsl = C.alloc([256], F32)[:, :]
                    memset("pool", qbd[64:128, :, 0:256], 0.0, ["qbd"])
                    memset("pool", qbd[0:64, :, 256:512], 0.0, ["qbd"])
                    nblk = 8 if last else 9
                    pti = 0
                    hi = 0
                    for qb in range(nblk):
                        t0 = qb * 256
                        latent = qb < 8
                        r = s if latent else 2
                        xk = "xT%d" % qb
                        modulate(xm, "xm", l, r, t0, 256, first=(qb == 0))
                        ps, pk = nextps()
                        for ct in range(2):
                            inproj_fm(ps[:, ct * 256:(ct + 1) * 256], pk, win2, "win2u", C_U + ct * 128, xm, "xm", 256)
                        act(ug.rearrange("p c t -> p (c t)"), ps, AF.Gelu_apprx_tanh, [pk], ["ug"])
                        ps, pk = nextps()
                        for tl in range(2):
                            for kc in range(8):
                                mm(ps[:, tl * 256:(tl + 1) * 256], xm[:, kc, tl * 128:(tl + 1) * 128], win2[:, kc, C_VA:C_VA + 256],
                                   kc == 0, kc == 7, ["win2va", "xm"], [pk], inc=(kc == 7))
                        for tl in range(2):
                            act(vg[:, tl, :], ps[:, tl * 256:(tl + 1) * 256], AF.Gelu_apprx_tanh, [pk], ["vg", "st"],
                                accum_out=st[:, tl, 0:1])
                        ps, pk = nextps()
                        for ct in range(2):
                            inproj_fm(ps[:, ct * 256:(ct + 1) * 256], pk, win2, "win2ga", C_GA + ct * 128, xm, "xm", 256)
                        act(sga.rearrange("p c t -> p (c t)"), ps, AF.Silu, [pk], ["sga"])
                        ps2b, pk2b = nextps2()
                        for ct in range(4):
                            inproj_fm(ps2b[:, ct * 256:(ct + 1) * 256], pk2b[ct // 2], win2, "win2bg", C_BG + ct * 128, xm, "xm", 256)
                        act(sbg.rearrange("p c t -> p (c t)"), ps2b, AF.Silu, pk2b, ["sbg"])
                        tt("pool", ug, ug, sga, ALU.mult, ["ug", "sga"], ["ug"])
                        stage(5)
                        psQ, pkQ = nextps2()
                        for ct in range(4):
                            inproj_fm(psQ[:, ct * 256:(ct + 1) * 256], pkQ[ct // 2], win2, "win2q", C_Q + ct * 128, xm, "xm", 256)
                        act(bufA, psQ, AF.Square, pkQ, ["bufA"])
                        for tl in range(2):
                            act(tmpg[:, 0:256], vg[:, tl, :], AF.Square, ["vg"], ["tmpg", "st"], accum_out=st[:, tl, 1:2])
                            ts("dve", st[:, tl, 2:3], st[:, tl, 0:1], 1.0 / 256.0, None, ALU.mult, ALU.bypass, ["st"], ["st"])
                            tt("dve", st[:, tl, 3:4], st[:, tl, 2:3], st[:, tl, 2:3], ALU.mult, ["st"], ["st"])
                            stt("dve", st[:, tl, 4:5], st[:, tl, 1:2], 1.0 / 256.0, st[:, tl, 3:4], ALU.mult, ALU.subtract,
                                ["st"], ["st"])
                        psM, pkM = nextps2()
                        for hf in range(2):
                            mm(psM[:, hf * 512:(hf + 1) * 512], Bones, bufA[:, hf * 512:(hf + 1) * 512], True, True,
                               ["bufA", "c16"], [pkM[hf]])
                        for tl in range(2):
                            act(st[:, tl, 5:6], st[:, tl, 4:5], AF.Ln, ["st", "epsr"], ["st"], bias=epsr[:, 0:1], scale=1.0)
                            act(st[:, tl, 6:7], st[:, tl, 5:6], AF.Exp, ["st"], ["st"], scale=-0.5)
                        act(bufB, psM, AF.Ln, pkM + ["epsr"], ["bufB"], bias=epsr[:, 0:1])
                        act(bufB, bufB, AF.Exp, ["bufB"], ["bufB"], scale=-0.5)
                        gq = vecs[:, l, V_QG:V_QG + 1]
                        v4 = lambda ap_, lo, hi_: ap_[lo:hi_, :].rearrange("p (c t) -> p c t", c=4)
                        if latent:
                            stt("dve", bufA, psQ, gq, bufB, ALU.mult, ALU.mult, pkQ + ["bufB", "vecs"], ["bufA"])
                            psR, pkR = nextps2()
                            for hf in range(2):
                                mm(psR[:, hf * 512:(hf + 1) * 512], Rmat, bufA[:, hf * 512:(hf + 1) * 512], True, True,
                                   ["bufA", "c16"], [pkR[hf]])
                            cosb = cosT[:, t0:t0 + 256].unsqueeze(1).to_broadcast([128, 4, 256])
                            sinb = sinT[:, t0:t0 + 256].unsqueeze(1).to_broadcast([128, 4, 256])
                            tt("pool", v4(bufB, 0, 128), v4(bufA, 0, 128), cosb, ALU.mult, ["bufA", "tabs"], ["bufB"])
                            tt("dve", v4(bufC, 0, 128), v4(psR, 0, 128), sinb, ALU.mult, pkR + ["tabs"], ["bufC"])
                            tt("pool", qbd[0:64, :, 0:256], v4(bufB, 0, 64), v4(bufC, 0, 64), ALU.add, ["bufB", "bufC"], ["qbd"])
                            tt("pool", qbd[64:128, :, 256:512], v4(bufB, 64, 128), v4(bufC, 64, 128), ALU.add,
                               ["bufB", "bufC"], ["qbd"])
                        else:
                            stt("dve", qbd[0:64, :, 0:256], v4(psQ, 0, 64), gq[0:64, :], v4(bufB, 0, 64), ALU.mult, ALU.mult,
                                pkQ + ["bufB", "vecs"], ["qbd"])
                            stt("dve", qbd[64:128, :, 256:512], v4(psQ, 64, 128), gq[64:128, :], v4(bufB, 64, 128), ALU.mult, ALU.mult,
                                pkQ + ["bufB", "vecs"], ["qbd"])
                        for tl in range(2):
                            ts("dve", vg[:, tl, :], vg[:, tl, :], st[:, tl, 2:3], st[:, tl, 6:7], ALU.subtract, ALU.mult,
                               ["vg", "st"], ["vg"])
                            tt("pool", vg[:, tl, :], vg[:, tl, :], anorm[:, 0, :], ALU.mult, ["vg", "anorm"], ["vg"])
                            tt("pool", vln[:, tl, :], vg[:, tl, :], anorm[:, 1, :], ALU.add, ["vg", "anorm"], ["vln"])
                        ps, pk = nextps()
                        for tl in range(2):
                            for ct in range(2):
                                for hh in range(2):
                                    g = 2 * ct + hh
                                    co = (tl * 2 + ct) * 128
                                    mm(ps[hh * 64:(hh + 1) * 64, co:co + 128], vln[:, tl, g * 64:(g + 1) * 64], awsT[:, g, :],
                                       True, True, ["vln", "awsT"], [pk], inc=(tl == 1 and ct == 1 and hh == 1))
                        tt("dve", tmpg.rearrange("p (t c q) -> p t c q", t=2, c=2), ps.rearrange("p (t c q) -> p t c q", t=2, c=2),
                           absb.unsqueeze(1).to_broadcast([128, 2, 2, 128]), ALU.add, [pk, "absb"], ["tmpg"])
                        tt("pool", yT[:, 0:2, :].rearrange("p c (t q) -> p t c q", t=2),
                           tmpg.rearrange("p (t c q) -> p t c q", t=2, c=2),
                           ug.rearrange("p c (t q) -> p t c q", t=2), ALU.mult, ["tmpg", "ug"], ["yT"])
                        stage(6)
                        ktiles = list(range(18)) if latent else [16, 17]
                        for ct in range(4):
                            g = ct // 2
                            pO, kO = nextps_o()

                            def qk(j):
                                ps_, pk_ = nextps_s()
                                mm(ps_[:, 0:512], KT[:, g, j * 128:(j + 1) * 128], qbd[:, ct, :], True, True,
                                   ["KT", "qbd"], [pk_])
                                return ps_, pk_

                            pend = qk(ktiles[0])
                            for ji, j in enumerate(ktiles):
                                ps_, pk_ = pend
                                if ji + 1 < len(ktiles):
                                    pend = qk(ktiles[ji + 1])
                                pb = pT[pti % 3]
                                pbk = "pT%d" % (pti % 3)
                                pti += 1
                                act(pb, ps_, AF.Exp, [pk_], [pbk], scale=0.125)
                                mm(pO[:, 0:512], VA[:, j, g, :], pb, ji == 0, ji == len(ktiles) - 1, ["VA", pbk], [kO])
                            rd = rden[hi % 2]
                            ot = otmp[hi % 2]
                            rk, ok = "rden%d" % (hi % 2), "otmp"
                            hi += 1
                            recip(rd[0:64, :], pO[64:128, 0:256], [kO], [rk])
                            recip(rd[64:128, :], pO[64:128, 256:512], [kO], [rk])
                            tt("dve", ot[0:64, :], pO[0:64, 0:256], rd[0:64, :], ALU.mult, [kO, rk], [ok])
                            tt("dve", ot[64:128, :], pO[0:64, 256:512], rd[64:128, :], ALU.mult, [kO, rk], [ok])
                            tt("pool", yT[:, 2 + ct, :], ot, sbg[:, ct, :], ALU.mult, [ok, "sbg"], ["yT"])
                        stage(7)
                        for dt_ in range(8):
                            ps, pk = nextps()
                            for kc in range(8):
                                rhs = yT[:, kc, :] if kc < 6 else ylru[:, kc - 6, t0:t0 + 256]
                                mm(ps[:, 0:256], wo[:, kc, dt_ * 128:(dt_ + 1) * 128], rhs, kc == 0, kc == 7,
                                   ["wo", "yT", "ylru"], [pk], inc=(kc == 7))
                            stt("dve", xT[:, dt_, t0:t0 + 256], ps[:, 0:256], gA[:, l, dt_, r:r + 1], xT[:, dt_, t0:t0 + 256],
                                ALU.mult, ALU.add, [pk, "gA", xk], [xk])
                        stage(8)
                        xblk = xT[:, :, t0:t0 + 256]
                        act(sqb, xblk, AF.Square, [xk], ["sqb"])
                        pm, km = nextps()
                        for dt_ in range(8):
                            mm(pm[:, 0:256], onesD, xT[:, dt_, t0:t0 + 256], dt_ == 0, dt_ == 7, [xk, "c32"], [km], inc=(dt_ == 7))
                        pq, kq = nextps()
                        for dt_ in range(8):
                            mm(pq[:, 0:256], onesb, sqb[:, dt_, :], dt_ == 0, dt_ == 7, ["sqb", "onesb"], [kq], inc=(dt_ == 7))
                        act(mean_sb, pm[:, 0:256], AF.Copy, [km], ["mean_sb"])
                        tt("pool", m2, mean_sb, mean_sb, ALU.mult, ["mean_sb"], ["m2"])
                        tt("dve", m2, pq[:, 0:256], m2, ALU.subtract, [kq, "m2"], ["m2"])
                        act(rsl, m2, AF.Ln, ["m2", "epsr"], ["rsl"], bias=epsr[:, 1:2], scale=1.0)
                        act(rsl, rsl, AF.Exp, ["rsl"], ["rsl"], scale=-0.5)
                        tln8 = bufBC.rearrange("p (c t) -> p c t", c=8)
                        tt("pool", tln8, xblk, mean_sb.unsqueeze(1).to_broadcast([128, 8, 256]), ALU.subtract,
                           [xk, "mean_sb"], ["bufB", "bufC"])
                        tt("dve", tln8, tln8, rsl.unsqueeze(1).to_broadcast([128, 8, 256]), ALU.mult,
                           ["bufB", "bufC", "rsl"], ["bufB", "bufC"])
                        for dt_ in range(8):
                            sc_, bi_ = vecs[:, l, V_LNG + dt_:V_LNG + dt_ + 1], vecs[:, l, V_LNB + dt_:V_LNB + dt_ + 1]
                            if dt_ % 2 == 0:
                                act(xT[:, dt_, t0:t0 + 256], tln8[:, dt_, :], AF.Identity, ["bufB", "bufC", "vecs"], [xk],
                                    scale=sc_, bias=bi_)
                            else:
                                ts("dve", xT[:, dt_, t0:t0 + 256], tln8[:, dt_, :], sc_, bi_, ALU.mult, ALU.add,
                                   ["bufB", "bufC", "vecs"], [xk])
            except _Stop:
                pass
            for c in range(8):
                P.dma("sp", "st_o", xout[s, :, c, :], xT[:, c, :], reads=ALLX)
        P.wait_all("sp", ["st_o", "st_d"])
        block = es.enter_context(nc.Block())
        P.emit(block)
        build_program.last_stats = (P.ninst, dict(P.cnt), C.hi)
    return nc


def _rope_tables():
    half, nf = 32, 16
    inv = (10000.0 ** (-np.arange(nf, dtype=np.float32) / nf)).astype(np.float32)
    t = np.arange(L_LAT)
    rows = (t // 64).astype(np.float32)
    cols = (t % 64).astype(np.float32)
    cos = np.zeros((64, L_LAT), np.float32)
    sin = np.zeros((64, L_LAT), np.float32)
    for d in range(64):
        pos = rows if d < 32 else cols
        f = d % 16
        ang = (pos * inv[f]).astype(np.float32)
        cos[d] = np.cos(ang)
        sgn = -1.0 if (d % 32) < 16 else 1.0
        sin[d] = sgn * np.sin(ang)
    return np.tile(cos, (2, 1)), np.tile(sin, (2, 1))


def _consts():
    ident = np.eye(128, dtype=np.float32)
    onesD = np.full((128, 128), 1.0 / 1024.0, np.float32)
    c32 = np.concatenate([ident, onesD], 1)
    R = np.zeros((128, 128), np.float32)
    for m in range(128):
        d = m % 64
        partner = d + 16 if (d % 32) < 16 else d - 16
        R[(m // 64) * 64 + partner, m] = 1.0
    B = np.zeros((128, 128), np.float32)
    B[0:64, 0:64] = 1.0 / 64.0
    B[64:128, 64:128] = 1.0 / 64.0
    cos, sin = _rope_tables()
    c16 = np.concatenate([R, B, cos, sin], 1)
    return np.ascontiguousarray(c32), np.ascontiguousarray(c16)


def _kc_layout(w):
    Ln, K, N = w.shape
    return np.ascontiguousarray(w.reshape(Ln, 8, 128, N).transpose(0, 2, 1, 3))


def prep_shared(inp):
    f = lambda k: np.asarray(inp[k], dtype=np.float32)
    w_in = f("w_in")
    a_u, a_v, a_g = w_in[:, :, 0:256], w_in[:, :, 256:512], w_in[:, :, 512:768]
    q, k, v = w_in[:, :, 768:1280], w_in[:, :, 1280:1408], w_in[:, :, 1408:1536]
    b_g, r_x, r_g = w_in[:, :, 1536:2048], w_in[:, :, 2048:2304], w_in[:, :, 2304:2560]
    k0, k1 = k[:, :, 0:64], k[:, :, 64:128]
    wperm = np.concatenate([k0, k0, k1, k1, v, r_x, r_g, a_u, a_g, a_v, q, b_g], axis=2)
    assert wperm.shape[2] == NC1 + NC2
    sh = {}
    sh["w_in"] = _kc_layout(wperm)
    sh["w_o"] = _kc_layout(f("w_o"))
    sh["w_ada"] = _kc_layout(f("w_ada"))
    sh["b_adaT"] = np.ascontiguousarray(f("b_ada").reshape(DEPTH, 24, 128).transpose(2, 0, 1))
    fm8 = lambda a: a.reshape(DEPTH, -1, 128).transpose(2, 0, 1)
    vec = np.zeros((128, DEPTH, NV), np.float32)
    vec[:, :, V_LNG:V_LNG + 8] = fm8(f("ln_g"))
    vec[:, :, V_LNB:V_LNB + 8] = fm8(f("ln_b"))
    vec[:, :, V_QG] = np.tile(f("q_norm_g"), (1, 2)).T
    vec[:, :, V_KG] = np.tile(f("k_norm_g"), (1, 2)).T
    cw = f("conv_w")
    for ct in range(2):
        for j in range(4):
            vec[:, :, V_CW + ct * 4 + j] = cw[:, j, ct * 128:(ct + 1) * 128].T
    vec[:, :, V_CB:V_CB + 2] = fm8(f("conv_b"))
    for nm, base in (("lru_br", V_BR), ("lru_bi", V_BI), ("lru_lam", V_LAM)):
        a = f(nm)
        for dd in range(2):
            for ct in range(2):
                vec[:, :, base + dd * 2 + ct] = a[:, dd, ct * 128:(ct + 1) * 128].T
    sh["vecs"] = vec
    an = np.stack([f("a_norm_g"), f("a_norm_b")], 1)
    sh["anorm"] = np.ascontiguousarray(np.broadcast_to(an[:, None], (DEPTH, 128, 2, 256)))
    bs = f("a_bs")
    ab = np.zeros((DEPTH, 128, 2, 128), np.float32)
    for ct in range(2):
        ab[:, 0:64, ct, :] = bs[:, 2 * ct, None, :]
        ab[:, 64:128, ct, :] = bs[:, 2 * ct + 1, None, :]
    sh["absb"] = ab
    sh["awsT"] = np.ascontiguousarray(f("a_ws").transpose(0, 3, 1, 2))
    lw = np.zeros((DEPTH, 128, 8, 128), np.float32)
    for dd in range(2):
        for gi, nm in enumerate(("lru_wr", "lru_wi")):
            w = f(nm)
            for ct in range(2):
                idx = (dd * 2 + gi) * 2 + ct
                for hb in range(2):
                    lw[:, hb * 64:(hb + 1) * 64, idx, hb * 64:(hb + 1) * 64] = w[:, dd, 2 * ct + hb]
    sh["lruW"] = lw
    sh["cst32"], sh["cst16"] = _consts()
    return sh


def prep_core(inp, b0, nseq):
    x = np.asarray(inp["x"], np.float32)
    ctx = np.asarray(inp["ctx"], np.float32)
    c = np.asarray(inp["c"], np.float32)
    c_ctx = np.asarray(inp["c_ctx"], np.float32)
    xin = np.empty((nseq, 128, 8, TOK), np.float32)
    for s in range(nseq):
        full = np.concatenate([x[b0 + s], ctx[b0 + s]], 0)
        xin[s] = full.T.reshape(8, 128, TOK).transpose(1, 0, 2)
    rows = [c[b0 + s] if s < nseq else c_ctx for s in range(2)] + [c_ctx]
    crow = np.stack(rows, 0)
    cT = np.ascontiguousarray(crow.T.reshape(8, 128, 3).transpose(1, 0, 2))
    return {"xin": xin, "cT": cT}


_PROG_CACHE = {}


def kernel(**inputs):
    n = 8
    nseq = 2
    key = ("full", nseq)
    if key not in _PROG_CACHE:
        _PROG_CACHE[key] = build_program(list(range(DEPTH)), nseq)
    nc = _PROG_CACHE[key]
    sh = prep_shared(inputs)
    in_maps = []
    for i in range(n):
        m = dict(sh)
        m.update(prep_core(inputs, i * nseq, nseq))
        in_maps.append(m)
    res = run_bass_kernel_spmd(nc, in_maps, core_ids=list(range(n)))
    out = np.empty((16, L_LAT, D), np.float32)
    for i in range(n):
        xo = np.asarray(res.results[i]["xout"])
        for s in range(nseq):
            out[i * nseq + s] = xo[s][:, :, 0:L_LAT].transpose(2, 1, 0).reshape(L_LAT, D)
    return out
```

```python
import math
import numpy as np
from contextlib import ExitStack
import concourse.bass as bass
import concourse.mybir as mybir
from concourse.bass_utils import run_bass_kernel_spmd

F32 = mybir.dt.float32
BF16 = mybir.dt.bfloat16
AF = mybir.ActivationFunctionType
ALU = mybir.AluOpType

D = 1024
L_LAT = 2048
L_CTX = 256
TOK = L_LAT + L_CTX
DEPTH = 4
ALPHA = (2.0 * DEPTH) ** 0.25
LN_EPS = 1e-6
RMS_EPS = 1e-6
NC1 = 896
NC2 = 1792
C_K0, C_K1, C_V, C_RX, C_RG = 0, 128, 256, 384, 640
C_U, C_GA, C_VA, C_Q, C_BG = 0, 256, 512, 768, 1280
NV = 40
V_LNG, V_LNB, V_QG, V_KG, V_CW, V_CB, V_BR, V_BI, V_LAM = 0, 8, 16, 17, 18, 26, 28, 32, 36

ENGS = ("pe", "act", "dve", "pool", "sp")


class Prog:
    def __init__(self, nc, es):
        self.nc = nc
        self.es = es
        self.q = {e: [] for e in ENGS}
        self.cnt = {}
        self.sem = {}
        for e in ENGS:
            self.sem[e] = es.enter_context(nc.semaphore("c_" + e))
            self.cnt[e] = 0
        self.known = {e: {} for e in ENGS}
        self.lastw = {}
        self.readers = {}
        self.pending = {e: False for e in ENGS}
        self.ninst = 0

    def newsem(self, name):
        self.sem[name] = self.es.enter_context(self.nc.semaphore(name))
        self.cnt[name] = 0
        return name

    def _deps(self, eng, reads, writes):
        deps = {}

        def add(sk, v):
            if v > deps.get(sk, 0):
                deps[sk] = v

        for k in reads:
            w = self.lastw.get(k)
            if w is not None:
                add(*w)
        for k in writes:
            w = self.lastw.get(k)
            if w is not None and w[0] != eng:
                add(*w)
            for sk, v in self.readers.get(k, {}).items():
                if sk != eng:
                    add(sk, v)
        if eng == "pe":
            deps.pop("pe", None)
        return deps

    def _emit_waits(self, eng, deps):
        kn = self.known[eng]
        for sk, v in deps.items():
            if sk not in ENGS:
                v = max(v, self.cnt[sk])
            if sk == eng and v > self.cnt[eng]:
                raise RuntimeError("self-dependency on pending op: " + eng)
            if kn.get(sk, 0) >= v:
                continue
            kn[sk] = v
            sem = self.sem[sk]
            self.q[eng].append(lambda E, sem=sem, v=v: E.wait_ge(sem, v))

    def op(self, eng, fn, reads=(), writes=(), inc=True):
        deps = self._deps(eng, reads, writes)
        self._emit_waits(eng, deps)
        n = self.cnt[eng] + 1
        for k in reads:
            self.readers.setdefault(k, {})[eng] = n
        for k in writes:
            self.lastw[k] = (eng, n)
            self.readers[k] = {}
        self.ninst += 1
        if inc:
            self.cnt[eng] = n
            sem = self.sem[eng]
            self.q[eng].append(lambda E, sem=sem: fn(E).then_inc(sem, 1))
            self.pending[eng] = False
        else:
            self.q[eng].append(lambda E: fn(E))
            self.pending[eng] = True

    def dma(self, queue, semname, out, in_, reads=(), writes=()):
        deps = self._deps(queue, reads, writes)
        self._emit_waits(queue, deps)
        n = self.cnt[semname] + 16
        self.cnt[semname] = n
        for k in reads:
            self.readers.setdefault(k, {})[semname] = n
        for k in writes:
            self.lastw[k] = (semname, n)
            self.readers[k] = {}
        sem = self.sem[semname]
        self.q[queue].append(lambda E, sem=sem: E.dma_start(out=out, in_=in_).then_inc(sem, 16))
        self.ninst += 1

    def fence(self):
        for e in ENGS:
            kn = self.known[e]
            for s, v in self.cnt.items():
                if s == e or v == 0 or kn.get(s, 0) >= v:
                    continue
                kn[s] = v
                sem = self.sem[s]
                self.q[e].append(lambda E, sem=sem, v=v: E.wait_ge(sem, v))

    def wait_all(self, eng, semnames):
        for s in semnames:
            v = self.cnt[s]
            if v > 0 and self.known[eng].get(s, 0) < v:
                self.known[eng][s] = v
                sem = self.sem[s]
                self.q[eng].append(lambda E, sem=sem, v=v: E.wait_ge(sem, v))

    def emit(self, block):
        for e in ENGS:
            assert not self.pending[e], "engine %s ends with un-inc'd op" % e
        q = self.q

        @block.tensor
        def _(E):
            for f in q["pe"]:
                f(E)

        @block.scalar
        def _(E):
            for f in q["act"]:
                f(E)

        @block.vector
        def _(E):
            for f in q["dve"]:
                f(E)

        @block.gpsimd
        def _(E):
            for f in q["pool"]:
                f(E)

        @block.sync
        def _(E):
            for f in q["sp"]:
                f(E)


class Carver:
    def __init__(self, scr, nbytes):
        self.scr = scr
        self.cap = nbytes
        self.off = 0
        self.hi = 0

    def alloc(self, shape, dt):
        esz = 4 if dt == F32 else 2
        n = 1
        for s in shape:
            n *= s
        nb = (n * esz + 31) // 32 * 32
        assert self.off + nb <= self.cap, ("SBUF scratch overflow", self.off, nb, self.cap)
        w0 = self.off // 4
        ap = self.scr[:, w0:w0 + nb // 4]
        if dt != F32:
            ap = ap.bitcast(dt)
        ap = ap[:, 0:n]
        if len(shape) == 2:
            ap = ap.rearrange("p (a b) -> p a b", a=shape[0])
        elif len(shape) == 3:
            ap = ap.rearrange("p (a b c) -> p a b c", a=shape[0], b=shape[1])
        self.off += nb
        self.hi = max(self.hi, self.off)
        return ap


def rev_ap(ap_):
    n = ap_.shape[1]
    pp = ap_.ap[0]
    return bass.AP(ap_.tensor, ap_.offset + (n - 1), [[pp[0], pp[1]], [-1, n]])


class _Stop(Exception):
    pass


def build_program(layers, nseq, first_in_program=True, dbg=None, stop=99):
    nc = bass.Bass("TRN2", target_bir_lowering=False)
    NL = DEPTH
    dram = lambda name, shape, dt=F32, kind="ExternalInput": nc.dram_tensor(name, shape, dt, kind=kind).ap()
    xin = dram("xin", [nseq, 128, 8, TOK])
    cT_d = dram("cT", [128, 8, 3])
    wada_d = dram("w_ada", [NL, 128, 8, 3072])
    bada_d = dram("b_adaT", [128, NL, 24])
    win_d = dram("w_in", [NL, 128, 8, NC1 + NC2])
    wo_d = dram("w_o", [NL, 128, 8, 1024])
    vecs_d = dram("vecs", [128, NL, NV])
    anorm_d = dram("anorm", [NL, 128, 2, 256])
    absb_d = dram("absb", [NL, 128, 2, 128])
    aws_d = dram("awsT", [NL, 128, 4, 128])
    lruw_d = dram("lruW", [NL, 128, 8, 128])
    c32_d = dram("cst32", [128, 256])
    c16_d = dram("cst16", [128, 256 + 2 * L_LAT])
    xout = dram("xout", [nseq, 128, 8, TOK], kind="ExternalOutput")
    dbg_out = {}
    if dbg:
        for name, shape in dbg.items():
            dbg_out[name] = dram("dbg_" + name, shape, kind="ExternalOutput")

    with ExitStack() as es:
        CAP = 212800
        scr = es.enter_context(nc.sbuf_tensor("scr", [128, CAP // 4], F32))
        psall = es.enter_context(nc.psum_tensor("psall", [128, 4096], F32))[:, :]
        PSB = [psall[:, i * 512:(i + 1) * 512] for i in range(8)]
        P = Prog(nc, es)
        for s in ("ld_x", "ld_c", "ld_w1", "ld_w2", "ld_wo", "ld_ada0", "ld_ada1", "ld_lw", "st_o", "st_d"):
            P.newsem(s)
        C = Carver(scr, CAP)

        xT = C.alloc([8, TOK], F32)
        tabs = C.alloc([2, L_LAT], BF16)
        KT = C.alloc([2, TOK], BF16)
        VA = C.alloc([18, 2, 128], BF16)
        ylru = C.alloc([2, TOK], BF16)
        c32 = C.alloc([256], F32)[:, :]
        c16 = C.alloc([256], BF16)
        mods = C.alloc([NL, 24, 3], F32)
        sc1 = C.alloc([NL, 8, 3], F32)
        gA = C.alloc([NL, 8, 3], F32)
        vecs = C.alloc([NL, NV], F32)
        cvec = C.alloc([NL, 4], F32)
        badaT = C.alloc([NL, 24], F32)
        anorm = C.alloc([2, 256], F32)
        absb = C.alloc([2, 128], F32)
        awsT = C.alloc([4, 128], BF16)
        lruW = C.alloc([8, 128], BF16)
        sil = C.alloc([8, 3], F32)
        epsr = C.alloc([4], F32)
        onesb = C.alloc([128], BF16)
        PBASE = C.off
        ident = c32[:, 0:128]
        onesD = c32[:, 128:256]
        Rmat = c16[:, 0:128]
        Bones = c16[:, 128:256]
        cosT = tabs[:, 0, :]
        sinT = tabs[:, 1, :]

        psi = [0, 0, 0]

        def nextps():
            i = psi[0] % 4
            psi[0] += 1
            return PSB[i], "ps%d" % i

        def nextps2():
            i = ((psi[0] + 1) // 2 * 2) % 4
            psi[0] = i + 2
            return psall[:, i * 512:(i + 2) * 512], ["ps%d" % i, "ps%d" % (i + 1)]

        def nextps_s():
            i = 3 + psi[1] % 3
            psi[1] += 1
            return PSB[i], "ps%d" % i

        def nextps_o():
            i = 6 + psi[2] % 2
            psi[2] += 1
            return PSB[i], "ps%d" % i

        def act(out, in_, func, r, w, **kw):
            P.op("act", lambda E: E.activation(out=out, in_=in_, func=func, **kw), r, w)

        def tt(eng, out, a, b, op, r, w):
            P.op(eng, lambda E: E.tensor_tensor(out=out, in0=a, in1=b, op=op), r, w)

        def ts(eng, out, a, s1, s2, op0, op1, r, w):
            if s2 is None:
                P.op(eng, lambda E: E.tensor_single_scalar(out=out, in_=a, scalar=s1, op=op0), r, w)
            else:
                P.op(eng, lambda E: E.tensor_scalar(out=out, in0=a, scalar1=s1, scalar2=s2, op0=op0, op1=op1), r, w)

        def scan(out, d0, d1, init, r, w):
            P.op("dve", lambda E: E.tensor_tensor_scan(out=out, data0=d0, data1=d1, initial=init, op0=ALU.mult, op1=ALU.add), r, w)

        def stt(eng, out, a, s, b, op0, op1, r, w):
            P.op(eng, lambda E: E.scalar_tensor_tensor(out=out, in0=a, scalar=s, in1=b, op0=op0, op1=op1), r, w)

        def mm(out, lhsT, rhs, start, stop, r, w, inc=True):
            P.op("pe", lambda E: E.matmul(out, lhsT, rhs, start=start, stop=stop), r, w, inc=inc)

        def recip(out, in_, r, w):
            P.op("dve", lambda E: E.reciprocal(out=out, in_=in_), r, w)

        def memset(eng, ap, val, w):
            P.op(eng, lambda E: E.memset(ap, val), (), w)

        def copy(eng, out, in_, r, w):
            P.op(eng, lambda E: E.tensor_copy(out=out, in_=in_), r, w)

        P.dma("sp", "ld_c", c32, c32_d[:, :], writes=["c32"])
        P.dma("pool", "ld_lw", c16, c16_d[:, 0:256], writes=["c16"])
        for a_ in range(2):
            P.dma("pool", "ld_lw", tabs[:, a_, :], c16_d[:, 256 + a_ * L_LAT:256 + (a_ + 1) * L_LAT], writes=["tabs"])
        P.dma("sp", "ld_c", vecs, vecs_d[:, :, :], writes=["vecs"])
        P.dma("sp", "ld_c", badaT, bada_d[:, :, :], writes=["badaT"])
        P.dma("sp", "ld_c", sil, cT_d[:, :, :], writes=["sil"])
        memset("dve", epsr[:, 0:1], RMS_EPS, ["epsr"])
        memset("dve", epsr[:, 1:2], LN_EPS / (ALPHA * ALPHA), ["epsr"])
        memset("dve", epsr[:, 2:3], 1.0, ["epsr"])
        memset("dve", epsr[:, 3:4], 0.0, ["epsr"])
        memset("dve", onesb, 1.0 / 1024.0, ["onesb"])
        memset("pool", VA[:, :, :, 64:128], 1.0, ["VA"])

        act(sil, sil, AF.Silu, ["sil"], ["sil"])
        for l in layers:
            act(cvec[:, l, :], vecs[:, l, V_LAM:V_LAM + 4], AF.Exp, ["vecs"], ["cvec"], scale=-1.0)
            act(cvec[:, l, :], cvec[:, l, :], AF.Ln, ["cvec"], ["cvec"], bias=epsr[:, 2:3])
            ts("dve", cvec[:, l, :], cvec[:, l, :], -8.0, None, ALU.mult, ALU.bypass, ["cvec"], ["cvec"])
        C.off = PBASE
        stg = [C.alloc([8, 512], F32) for _ in range(2)]
        modrow = C.alloc([3072], F32)[:, :]
        bi = 0
        for l in layers:
            for nb in range(6):
                sb_ = stg[bi % 2]
                key = "stg%d" % (bi % 2)
                P.dma("sp", "ld_ada%d" % (bi % 2), sb_, wada_d[l, :, :, nb * 512:(nb + 1) * 512], writes=[key])
                ps, pk = nextps()
                for kc in range(8):
                    mm(ps[0:3, :], sil[:, kc, :], sb_[:, kc, :], kc == 0, kc == 7, [key, "sil"], [pk], inc=(kc == 7))
                copy("dve", modrow[0:3, nb * 512:(nb + 1) * 512], ps[0:3, :], [pk], ["modrow"])
                bi += 1
            ps, pk = nextps()
            for j in range(24):
                mm(ps[:, j * 3:(j + 1) * 3], modrow[0:3, j * 128:(j + 1) * 128], ident[0:3, 0:3], True, True,
                   ["modrow", "c32"], [pk], inc=(j == 23))
            tt("dve", mods[:, l, :, :], ps[:, 0:72].rearrange("p (a b) -> p a b", a=24),
               badaT[:, l, :].unsqueeze(2).to_broadcast([128, 24, 3]), ALU.add, [pk, "badaT"], ["mods"])
            ts("dve", sc1[:, l, :, :], mods[:, l, 8:16, :], 1.0, None, ALU.add, ALU.bypass, ["mods"], ["sc1"])
            ts("dve", gA[:, l, :, :], mods[:, l, 16:24, :], 1.0 / ALPHA, None, ALU.mult, ALU.bypass, ["mods"], ["gA"])
        P.fence()

        def rms_rope(zps, zk, n, gvec, dsts, dstk, t0, latent, S):
            act(S["sq"][:, 0:n], zps[:, 0:n], AF.Square, [zk], ["r_sq"])
            ps2, pk2 = nextps()
            mm(ps2[:, 0:n], Bones, S["sq"][:, 0:n], True, True, ["r_sq", "c16"], [pk2])
            act(S["sd"][:, 0:n], ps2[:, 0:n], AF.Ln, [pk2, "epsr"], ["r_sd"], bias=epsr[:, 0:1])
            act(S["sd"][:, 0:n], S["sd"][:, 0:n], AF.Exp, ["r_sd"], ["r_sd"], scale=-0.5)
            if not latent:
                for (lo, hi_, dst) in dsts:
                    stt("dve", dst, zps[lo:hi_, 0:n], gvec[lo:hi_, :], S["sd"][lo:hi_, 0:n], ALU.mult, ALU.mult,
                        [zk, "r_sd", "vecs"], [dstk])
                return
            stt("dve", S["qn"][:, 0:n], zps[:, 0:n], gvec, S["sd"][:, 0:n], ALU.mult, ALU.mult,
                [zk, "r_sd", "vecs"], ["r_qn"])
            ps3, pk3 = nextps()
            mm(ps3[:, 0:n], Rmat, S["qn"][:, 0:n], True, True, ["r_qn", "c16"], [pk3])
            tt("pool", S["t1"][:, 0:n], S["qn"][:, 0:n], cosT[:, t0:t0 + n], ALU.mult, ["r_qn", "tabs"], ["r_t1"])
            tt("dve", S["t2"][:, 0:n], ps3[:, 0:n], sinT[:, t0:t0 + n], ALU.mult, [pk3, "tabs"], ["r_t2"])
            for (lo, hi_, dst) in dsts:
                tt("pool", dst, S["t1"][lo:hi_, 0:n], S["t2"][lo:hi_, 0:n], ALU.add, ["r_t1", "r_t2"], [dstk])

        def xkeys(t0, n):
            return ["xT%d" % b for b in range(t0 // 256, (t0 + n + 255) // 256)]

        ALLX = ["xT%d" % b for b in range(9)]

        def modulate(dst, dk, l, r, t0, n, first=False):
            for c in range(8):
                eng = "dve" if (first or c % 2 == 1) else "pool"
                ts(eng, dst[:, c, 0:n], xT[:, c, t0:t0 + n], sc1[:, l, c, r:r + 1], mods[:, l, c, r:r + 1],
                   ALU.mult, ALU.add, xkeys(t0, n) + ["sc1", "mods"], [dk])

        def inproj_fm(ps, pk, W, wk, col0, xm, xk, n):
            for kc in range(8):
                mm(ps[:, 0:n], W[:, kc, col0:col0 + 128], xm[:, kc, 0:n], kc == 0, kc == 7, [wk, xk], [pk], inc=(kc == 7))

        def stage(n):
            if n > stop:
                raise _Stop()

        W1P = (("k", 0, 256), ("v", 256, 128), ("rx", 384, 256), ("rg", 640, 256))
        W2P = (("u", C_U, 256), ("va", C_VA, 256), ("ga", C_GA, 256), ("bg", C_BG, 512), ("q", C_Q, 512))
        for nm, _, _ in W1P:
            P.newsem("ld_w1" + nm)
        for nm, _, _ in W2P:
            P.newsem("ld_w2" + nm)

        for s in range(nseq):
            try:
                for c in range(8):
                    P.dma("sp", "ld_x", xT[:, c, :], xin[s, :, c, :], writes=ALLX)
                for l in layers:
                    last = (l == DEPTH - 1)
                    P.dma("sp", "ld_c", anorm, anorm_d[l, :, :, :], writes=["anorm"])
                    P.dma("sp", "ld_c", absb, absb_d[l, :, :, :], writes=["absb"])
                    P.dma("pool", "ld_lw", awsT, aws_d[l, :, :, :], writes=["awsT"])
                    P.dma("pool", "ld_lw", lruW, lruw_d[l, :, :, :], writes=["lruW"])
                    stage(1)
                    P.fence()
                    C.off = PBASE
                    win1 = C.alloc([8, NC1], BF16)
                    xm1 = C.alloc([8, 512], BF16)
                    S1 = {"sq": C.alloc([512], BF16)[:, :], "sd": C.alloc([512], F32)[:, :],
                          "qn": C.alloc([512], BF16)[:, :], "t1": C.alloc([512], F32)[:, :]}
                    assert C.off - PBASE == NC2 * 8 * 2, (C.off - PBASE)
                    S1["t2"] = C.alloc([512], F32)[:, :]
                    sg = C.alloc([2, TOK], BF16)
                    xrb = C.alloc([2, TOK], BF16)
                    GBASE = C.off
                    rx = C.alloc([2, 2312], F32)
                    acc = C.alloc([TOK], F32)[:, :]
                    for nm, c0, ncl in W1P:
                        P.dma("pool", "ld_w1" + nm, win1[:, :, c0:c0 + ncl], win_d[l, :, :, c0:c0 + ncl], writes=["win1" + nm])
                    memset("pool", rx[:, :, 0:2], 0.0, ["rx"])
                    memset("pool", rx[:, :, 2050:2052], 0.0, ["rx"])
                    memset("pool", rx[:, :, 2308:2312], 0.0, ["rx"])
                    for (t0, n) in ((0, 512), (512, 512), (1024, 512), (1536, 512), (2048, 256)):
                        latent = t0 < L_LAT
                        r = s if latent else 2
                        modulate(xm1, "xm1", l, r, t0, n, first=(t0 == 0))
                        for g in range(2):
                            ps, pk = nextps()
                            inproj_fm(ps, pk, win1, "win1k", C_K0 + g * 128, xm1, "xm1", n)
                            rms_rope(ps, pk, n, vecs[:, l, V_KG:V_KG + 1], [(0, 128, KT[:, g, t0:t0 + n])], "KT", t0, latent, S1)
                        for tl in range(n // 128):
                            ps, pk = nextps()
                            for kc in range(8):
                                mm(ps[:, 0:128], xm1[:, kc, tl * 128:(tl + 1) * 128], win1[:, kc, C_V:C_V + 128],
                                   kc == 0, kc == 7, ["win1v", "xm1"], [pk], inc=(kc == 7))
                            tix = t0 // 128 + tl
                            copy("dve", VA[:, tix, :, 0:64], ps[:, 0:128].rearrange("p (g d) -> p g d", g=2), [pk], ["VA"])
                        for ct in range(2):
                            ps, pk = nextps()
                            inproj_fm(ps, pk, win1, "win1rx", C_RX + ct * 128, xm1, "xm1", n)
                            off = 2 + t0 if latent else 2052 + (t0 - L_LAT)
                            act(rx[:, ct, off:off + n], ps[:, 0:n], AF.Copy, [pk], ["rx"])
                        for ct in range(2):
                            ps, pk = nextps()
                            inproj_fm(ps, pk, win1, "win1rg", C_RG + ct * 128, xm1, "xm1", n)
                            act(sg[:, ct, t0:t0 + n], ps[:, 0:n], AF.Silu, [pk], ["sg"])
                    stage(2)
                    for ct in range(2):
                        for (d0, o0, n) in ((2, 0, L_LAT), (2052, L_LAT, L_CTX)):
                            cw = lambda j: vecs[:, l, V_CW + ct * 4 + j:V_CW + ct * 4 + j + 1]
                            eng = "dve"
                            ts(eng, acc[:, 0:n], rx[:, ct, d0 - 2:d0 - 2 + n], cw(0), vecs[:, l, V_CB + ct:V_CB + ct + 1],
                               ALU.mult, ALU.add, ["rx", "vecs"], ["acc"])
                            stt(eng, acc[:, 0:n], rx[:, ct, d0 - 1:d0 - 1 + n], cw(1), acc[:, 0:n], ALU.mult, ALU.add,
                                ["rx", "vecs", "acc"], ["acc"])
                            stt(eng, acc[:, 0:n], rx[:, ct, d0:d0 + n], cw(2), acc[:, 0:n], ALU.mult, ALU.add,
                                ["rx", "vecs", "acc"], ["acc"])
                            stt(eng, xrb[:, ct, o0:o0 + n], rx[:, ct, d0 + 1:d0 + 1 + n], cw(3), acc[:, 0:n], ALU.mult, ALU.add,
                                ["rx", "vecs", "acc"], ["xrb"])
                    stage(3)
                    P.fence()
                    C.off = PBASE
                    win2 = C.alloc([8, NC2], BF16)
                    for nm, c0, ncl in W2P:
                        P.dma("pool", "ld_w2" + nm, win2[:, :, c0:c0 + ncl], win_d[l, :, :, NC1 + c0:NC1 + c0 + ncl],
                              writes=["win2" + nm])
                    C.off = GBASE
                    Ab = C.alloc([TOK], F32)[:, :]
                    Tb = C.alloc([TOK], F32)[:, :]
                    Bb = C.alloc([TOK], F32)[:, :]
                    Hf = C.alloc([TOK], F32)[:, :]
                    for ct in range(2):
                        for dd in range(2):
                            vi = dd * 2 + ct
                            for (t0, n) in ((0, 512), (512, 512), (1024, 512), (1536, 512), (2048, 256)):
                                ps, pk = nextps()
                                mm(ps[:, 0:n], lruW[:, (dd * 2 + 0) * 2 + ct, :], xrb[:, ct, t0:t0 + n], True, True,
                                   ["lruW", "xrb"], [pk])
                                act(Ab[:, t0:t0 + n], ps[:, 0:n], AF.Sigmoid, [pk, "vecs"], ["Ab"],
                                    bias=vecs[:, l, V_BR + vi:V_BR + vi + 1])
                                ps, pk = nextps()
                                mm(ps[:, 0:n], lruW[:, (dd * 2 + 1) * 2 + ct, :], xrb[:, ct, t0:t0 + n], True, True,
                                   ["lruW", "xrb"], [pk])
                                act(Bb[:, t0:t0 + n], ps[:, 0:n], AF.Sigmoid, [pk, "vecs"], ["Bb"],
                                    bias=vecs[:, l, V_BI + vi:V_BI + vi + 1])
                            act(Ab, Ab, AF.Exp, ["Ab", "cvec"], ["Ab"], scale=cvec[:, l, vi:vi + 1])
                            act(Tb, Ab, AF.Square, ["Ab"], ["Tb"])
                            act(Tb, Tb, AF.Ln, ["Tb", "epsr"], ["Tb"], scale=-1.0, bias=epsr[:, 2:3])
                            act(Tb, Tb, AF.Exp, ["Tb"], ["Tb"], scale=0.5)
                            tt("dve", Bb, Bb, xrb[:, ct, :], ALU.mult, ["Bb", "xrb"], ["Bb"])
                            tt("dve", Bb, Bb, Tb, ALU.mult, ["Bb", "Tb"], ["Bb"])
                            if dd == 0:
                                scan(Hf[:, L_LAT:TOK], Ab[:, L_LAT:TOK], Bb[:, L_LAT:TOK], 0.0, ["Ab", "Bb"], ["Hf"])
                                scan(Hf[:, 0:L_LAT], Ab[:, 0:L_LAT], Bb[:, 0:L_LAT], Hf[:, TOK - 1:TOK], ["Ab", "Bb", "Hf"], ["Hf"])
                            else:
                                scan(rev_ap(Tb), rev_ap(Ab), rev_ap(Bb), 0.0, ["Ab", "Bb"], ["Tb"])
                        tt("pool", Hf, Hf, Tb, ALU.add, ["Hf", "Tb"], ["Hf"])
                        tt("pool", ylru[:, ct, :], Hf, sg[:, ct, :], ALU.mult, ["Hf", "sg"], ["ylru"])
                    stage(4)
                    P.fence()
                    C.off = PBASE
                    win2 = C.alloc([8, NC2], BF16)
                    wo = C.alloc([8, 1024], BF16)
                    for kc in range(8):
                        P.dma("pool", "ld_wo", wo[:, kc, :], wo_d[l, :, kc, :], writes=["wo"])
                    xm = C.alloc([8, 256], BF16)
                    yT = C.alloc([6, 256], BF16)
                    ug = C.alloc([2, 256], F32)
                    sga = C.alloc([2, 256], F32)
                    vg = C.alloc([2, 256], F32)
                    vln = C.alloc([2, 256], BF16)
                    tmpg = C.alloc([512], F32)[:, :]
                    st = C.alloc([2, 8], F32)
                    qbd = C.alloc([4, 512], BF16)
                    sbg = C.alloc([4, 256], BF16)
                    bufA = C.alloc([1024], BF16)[:, :]
                    bufBC = C.alloc([2048], F32)[:, :]
                    bufB = bufBC[:, 0:1024]
                    bufC = bufBC[:, 1024:2048]
                    pT = [C.alloc([512], BF16)[:, :] for _ in range(3)]
                    rden = [C.alloc([256], F32)[:, :] for _ in range(2)]
                    otmp = [C.alloc([256], F32)[:, :]] * 2
                    sqb = C.alloc([8, 256], BF16)
                    mean_sb = C.alloc([256], F32)[:, :]
                    m2 = C.alloc([256], F32)[:, :]
                    rsl = C.alloc([256], F32)[:, :]
                    memset("pool", qbd[64:128, :, 0:256], 0.0, ["qbd"])
                    memset("pool", qbd[0:64, :, 256:512], 0.0, ["qbd"])
                    nblk = 8 if last else 9
                    cn = [0, 0]
                    def mod(qb):
                        t0 = qb * 256
                        latent = qb < 8
                        r = s if latent else 2
                        xk = "xT%d" % qb
                        modulate(xm, "xm", l, r, t0, 256, first=(qb == 0))

                    def front_a(qb):
                        t0 = qb * 256
                        latent = qb < 8
                        r = s if latent else 2
                        xk = "xT%d" % qb
                        bank = lambda i: (psall[:, i * 512:(i + 1) * 512], "ps%d" % i)
                        bank2 = lambda i: (psall[:, i * 512:(i + 2) * 512], ["ps%d" % i, "ps%d" % (i + 1)])
                        gq = vecs[:, l, V_QG:V_QG + 1]
                        v4 = lambda ap_, lo, hi_: ap_[lo:hi_, :].rearrange("p (c t) -> p c t", c=4)
                        psQ, pkQ = bank2(0)
                        for ct in range(4):
                            inproj_fm(psQ[:, ct * 256:(ct + 1) * 256], pkQ[ct // 2], win2, "win2q", C_Q + ct * 128, xm, "xm", 256)
                        act(bufA, psQ, AF.Square, pkQ, ["bufA"])
                        psU, pkU = bank(2)
                        for ct in range(2):
                            inproj_fm(psU[:, ct * 256:(ct + 1) * 256], pkU, win2, "win2u", C_U + ct * 128, xm, "xm", 256)
                        psV, pkV = bank(3)
                        for tl in range(2):
                            for kc in range(8):
                                mm(psV[:, tl * 256:(tl + 1) * 256], xm[:, kc, tl * 128:(tl + 1) * 128], win2[:, kc, C_VA:C_VA + 256],
                                   kc == 0, kc == 7, ["win2va", "xm"], [pkV], inc=(kc == 7))
                        psM, pkM = bank2(4)
                        for hf in range(2):
                            mm(psM[:, hf * 512:(hf + 1) * 512], Bones, bufA[:, hf * 512:(hf + 1) * 512], True, True,
                               ["bufA", "c16"], [pkM[hf]])
                        act(bufB, psM, AF.Ln, pkM + ["epsr"], ["bufB"], bias=epsr[:, 0:1])
                        act(bufB, bufB, AF.Exp, ["bufB"], ["bufB"], scale=-0.5)
                        if latent:
                            stt("dve", bufA, psQ, gq, bufB, ALU.mult, ALU.mult, pkQ + ["bufB", "vecs"], ["bufA"])
                        else:
                            stt("dve", qbd[0:64, :, 0:256], v4(psQ, 0, 64), gq[0:64, :], v4(bufB, 0, 64), ALU.mult, ALU.mult,
                                pkQ + ["bufB", "vecs"], ["qbd"])
                            stt("dve", qbd[64:128, :, 256:512], v4(psQ, 64, 128), gq[64:128, :], v4(bufB, 64, 128), ALU.mult, ALU.mult,
                                pkQ + ["bufB", "vecs"], ["qbd"])
                        act(ug.rearrange("p c t -> p (c t)"), psU, AF.Gelu_apprx_tanh, [pkU], ["ug"])
                        for tl in range(2):
                            act(vg[:, tl, :], psV[:, tl * 256:(tl + 1) * 256], AF.Gelu_apprx_tanh, [pkV], ["vg", "st"],
                                accum_out=st[:, tl, 0:1])
                        ps2b, pk2b = bank2(6)
                        for ct in range(4):
                            inproj_fm(ps2b[:, ct * 256:(ct + 1) * 256], pk2b[ct // 2], win2, "win2bg", C_BG + ct * 128, xm, "xm", 256)
                        psG, pkG = bank(2)
                        for ct in range(2):
                            inproj_fm(psG[:, ct * 256:(ct + 1) * 256], pkG, win2, "win2ga", C_GA + ct * 128, xm, "xm", 256)
                        if latent:
                            psR, pkR = bank2(4)
                            for hf in range(2):
                                mm(psR[:, hf * 512:(hf + 1) * 512], Rmat, bufA[:, hf * 512:(hf + 1) * 512], True, True,
                                   ["bufA", "c16"], [pkR[hf]])
                        act(sga.rearrange("p c t -> p (c t)"), psG, AF.Silu, [pkG], ["sga"])
                        act(sbg.rearrange("p c t -> p (c t)"), ps2b, AF.Silu, pk2b, ["sbg"])
                        tt("pool", ug, ug, sga, ALU.mult, ["ug", "sga"], ["ug"])
                        if latent:
                            cosb = cosT[:, t0:t0 + 256].unsqueeze(1).to_broadcast([128, 4, 256])
                            sinb = sinT[:, t0:t0 + 256].unsqueeze(1).to_broadcast([128, 4, 256])
                            tt("pool", v4(bufB, 0, 128), v4(bufA, 0, 128), cosb, ALU.mult, ["bufA", "tabs"], ["bufB"])
                            tt("dve", v4(bufC, 0, 128), v4(psR, 0, 128), sinb, ALU.mult, pkR + ["tabs"], ["bufC"])
                            tt("pool", qbd[0:64, :, 0:256], v4(bufB, 0, 64), v4(bufC, 0, 64), ALU.add, ["bufB", "bufC"], ["qbd"])
                            tt("pool", qbd[64:128, :, 256:512], v4(bufB, 64, 128), v4(bufC, 64, 128), ALU.add,
                               ["bufB", "bufC"], ["qbd"])

                    def front_b(qb):
                        t0 = qb * 256
                        latent = qb < 8
                        r = s if latent else 2
                        xk = "xT%d" % qb
                        for tl in range(2):
                            act(tmpg[:, 0:256], vg[:, tl, :], AF.Square, ["vg"], ["tmpg", "st"], accum_out=st[:, tl, 1:2])
                            ts("dve", st[:, tl, 2:3], st[:, tl, 0:1], 1.0 / 256.0, None, ALU.mult, ALU.bypass, ["st"], ["st"])
                            tt("dve", st[:, tl, 3:4], st[:, tl, 2:3], st[:, tl, 2:3], ALU.mult, ["st"], ["st"])
                            stt("dve", st[:, tl, 4:5], st[:, tl, 1:2], 1.0 / 256.0, st[:, tl, 3:4], ALU.mult, ALU.subtract,
                                ["st"], ["st"])
                        for tl in range(2):
                            act(st[:, tl, 5:6], st[:, tl, 4:5], AF.Ln, ["st", "epsr"], ["st"], bias=epsr[:, 0:1], scale=1.0)
                            act(st[:, tl, 6:7], st[:, tl, 5:6], AF.Exp, ["st"], ["st"], scale=-0.5)
                        for tl in range(2):
                            ts("dve", vg[:, tl, :], vg[:, tl, :], st[:, tl, 2:3], st[:, tl, 6:7], ALU.subtract, ALU.mult,
                               ["vg", "st"], ["vg"])
                            tt("pool", vg[:, tl, :], vg[:, tl, :], anorm[:, 0, :], ALU.mult, ["vg", "anorm"], ["vg"])
                            tt("pool", vln[:, tl, :], vg[:, tl, :], anorm[:, 1, :], ALU.add, ["vg", "anorm"], ["vln"])

                    def front_c(qb):
                        t0 = qb * 256
                        latent = qb < 8
                        r = s if latent else 2
                        xk = "xT%d" % qb
                        ps, pk = nextps()
                        for tl in range(2):
                            for ct in range(2):
                                for hh in range(2):
                                    g = 2 * ct + hh
                                    co = (tl * 2 + ct) * 128
                                    mm(ps[hh * 64:(hh + 1) * 64, co:co + 128], vln[:, tl, g * 64:(g + 1) * 64], awsT[:, g, :],
                                       True, True, ["vln", "awsT"], [pk], inc=(tl == 1 and ct == 1 and hh == 1))
                        tt("dve", tmpg.rearrange("p (t c q) -> p t c q", t=2, c=2), ps.rearrange("p (t c q) -> p t c q", t=2, c=2),
                           absb.unsqueeze(1).to_broadcast([128, 2, 2, 128]), ALU.add, [pk, "absb"], ["tmpg"])
                        tt("pool", yT[:, 0:2, :].rearrange("p c (t q) -> p t c q", t=2),
                           tmpg.rearrange("p (t c q) -> p t c q", t=2, c=2),
                           ug.rearrange("p c (t q) -> p t c q", t=2), ALU.mult, ["tmpg", "ug"], ["yT"])

                    def attn(qb):
                        t0 = qb * 256
                        latent = qb < 8
                        r = s if latent else 2
                        xk = "xT%d" % qb
                        ktiles = list(range(18)) if latent else [16, 17]
                        for ct in range(4):
                            g = ct // 2
                            pO, kO = nextps_o()

                            def qk(j):
                                ps_, pk_ = nextps_s()
                                mm(ps_[:, 0:512], KT[:, g, j * 128:(j + 1) * 128], qbd[:, ct, :], True, True,
                                   ["KT", "qbd"], [pk_])
                                return ps_, pk_

                            pend = [qk(j) for j in ktiles[0:2]]
                            for ji, j in enumerate(ktiles):
                                ps_, pk_ = pend.pop(0)
                                if ji + 2 < len(ktiles):
                                    pend.append(qk(ktiles[ji + 2]))
                                pb = pT[cn[0] % 3]
                                pbk = "pT%d" % (cn[0] % 3)
                                cn[0] += 1
                                act(pb, ps_, AF.Exp, [pk_], [pbk], scale=0.125)
                                mm(pO[:, 0:512], VA[:, j, g, :], pb, ji == 0, ji == len(ktiles) - 1, ["VA", pbk], [kO])
                            rd = rden[cn[1] % 2]
                            ot = otmp[cn[1] % 2]
                            rk, ok = "rden%d" % (cn[1] % 2), "otmp"
                            cn[1] += 1
                            recip(rd[0:64, :], pO[64:128, 0:256], [kO], [rk])
                            recip(rd[64:128, :], pO[64:128, 256:512], [kO], [rk])
                            tt("dve", ot[0:64, :], pO[0:64, 0:256], rd[0:64, :], ALU.mult, [kO, rk], [ok])
                            tt("dve", ot[64:128, :], pO[0:64, 256:512], rd[64:128, :], ALU.mult, [kO, rk], [ok])
                            tt("pool", yT[:, 2 + ct, :], ot, sbg[:, ct, :], ALU.mult, [ok, "sbg"], ["yT"])

                    def back(qb):
                        t0 = qb * 256
                        latent = qb < 8
                        r = s if latent else 2
                        xk = "xT%d" % qb
                        pm, km = nextps_o()
                        pq, kq = nextps_o()
                        xblk = xT[:, :, t0:t0 + 256]

                        def ln_stats(dt_):
                            mm(pm[:, 0:256], onesD, xT[:, dt_, t0:t0 + 256], dt_ == 0, dt_ == 7, [xk, "c32"], [km], inc=True)
                            mm(pq[:, 0:256], onesb, sqb[:, dt_, :], dt_ == 0, dt_ == 7, ["sqb", "onesb"], [kq], inc=True)

                        for dt_ in range(8):
                            ps, pk = nextps()
                            for kc in range(8):
                                rhs = yT[:, kc, :] if kc < 6 else ylru[:, kc - 6, t0:t0 + 256]
                                mm(ps[:, 0:256], wo[:, kc, dt_ * 128:(dt_ + 1) * 128], rhs, kc == 0, kc == 7,
                                   ["wo", "yT", "ylru"], [pk], inc=(kc == 7))
                            stt("dve", xT[:, dt_, t0:t0 + 256], ps[:, 0:256], gA[:, l, dt_, r:r + 1], xT[:, dt_, t0:t0 + 256],
                                ALU.mult, ALU.add, [pk, "gA", xk], [xk])
                            act(sqb[:, dt_, :], xT[:, dt_, t0:t0 + 256], AF.Square, [xk], ["sqb"])
                            if dt_ >= 1:
                                ln_stats(dt_ - 1)
                        ln_stats(7)
                        act(mean_sb, pm[:, 0:256], AF.Copy, [km], ["mean_sb"])
                        tt("pool", m2, mean_sb, mean_sb, ALU.mult, ["mean_sb"], ["m2"])
                        tt("dve", m2, pq[:, 0:256], m2, ALU.subtract, [kq, "m2"], ["m2"])
                        act(rsl, m2, AF.Ln, ["m2", "epsr"], ["rsl"], bias=epsr[:, 1:2], scale=1.0)
                        act(rsl, rsl, AF.Exp, ["rsl"], ["rsl"], scale=-0.5)
                        tln8 = bufBC.rearrange("p (c t) -> p c t", c=8)
                        tt("pool", tln8, xblk, mean_sb.unsqueeze(1).to_broadcast([128, 8, 256]), ALU.subtract,
                           [xk, "mean_sb"], ["bufB", "bufC"])
                        tt("dve", tln8, tln8, rsl.unsqueeze(1).to_broadcast([128, 8, 256]), ALU.mult,
                           ["bufB", "bufC", "rsl"], ["bufB", "bufC"])
                        for dt_ in range(8):
                            sc_, bi_ = vecs[:, l, V_LNG + dt_:V_LNG + dt_ + 1], vecs[:, l, V_LNB + dt_:V_LNB + dt_ + 1]
                            if dt_ % 2 == 0:
                                act(xT[:, dt_, t0:t0 + 256], tln8[:, dt_, :], AF.Identity, ["bufB", "bufC", "vecs"], [xk],
                                    scale=sc_, bias=bi_)
                            else:
                                ts("dve", xT[:, dt_, t0:t0 + 256], tln8[:, dt_, :], sc_, bi_, ALU.mult, ALU.add,
                                   ["bufB", "bufC", "vecs"], [xk])

                    mod(0)
                    front_a(0)
                    front_b(0)
                    for qb in range(nblk):
                        if qb + 1 < nblk:
                            mod(qb + 1)
                        attn(qb)
                        front_c(qb)
                        if qb + 1 < nblk:
                            front_a(qb + 1)
                        back(qb)
                        if qb + 1 < nblk:
                            front_b(qb + 1)
            except _Stop:
                pass
            for c in range(8):
                P.dma("sp", "st_o", xout[s, :, c, :], xT[:, c, :], reads=ALLX)
        P.wait_all("sp", ["st_o", "st_d"])
        block = es.enter_context(nc.Block())
        P.emit(block)
        build_program.last_stats = (P.ninst, dict(P.cnt), C.hi)
    return nc


def _rope_tables():
    half, nf = 32, 16
    inv = (10000.0 ** (-np.arange(nf, dtype=np.float32) / nf)).astype(np.float32)
    t = np.arange(L_LAT)
    rows = (t // 64).astype(np.float32)
    cols = (t % 64).astype(np.float32)
    cos = np.zeros((64, L_LAT), np.float32)
    sin = np.zeros((64, L_LAT), np.float32)
    for d in range(64):
        pos = rows if d < 32 else cols
        f = d % 16
        ang = (pos * inv[f]).astype(np.float32)
        cos[d] = np.cos(ang)
        sgn = -1.0 if (d % 32) < 16 else 1.0
        sin[d] = sgn * np.sin(ang)
    return np.tile(cos, (2, 1)), np.tile(sin, (2, 1))


def _consts():
    ident = np.eye(128, dtype=np.float32)
    onesD = np.full((128, 128), 1.0 / 1024.0, np.float32)
    c32 = np.concatenate([ident, onesD], 1)
    R = np.zeros((128, 128), np.float32)
    for m in range(128):
        d = m % 64
        partner = d + 16 if (d % 32) < 16 else d - 16
        R[(m // 64) * 64 + partner, m] = 1.0
    B = np.zeros((128, 128), np.float32)
    B[0:64, 0:64] = 1.0 / 64.0
    B[64:128, 64:128] = 1.0 / 64.0
    cos, sin = _rope_tables()
    c16 = np.concatenate([R, B, cos, sin], 1)
    return np.ascontiguousarray(c32), np.ascontiguousarray(c16)


def _kc_layout(w):
    Ln, K, N = w.shape
    return np.ascontiguousarray(w.reshape(Ln, 8, 128, N).transpose(0, 2, 1, 3))


def prep_shared(inp):
    f = lambda k: np.asarray(inp[k], dtype=np.float32)
    w_in = f("w_in")
    a_u, a_v, a_g = w_in[:, :, 0:256], w_in[:, :, 256:512], w_in[:, :, 512:768]
    q, k, v = w_in[:, :, 768:1280], w_in[:, :, 1280:1408], w_in[:, :, 1408:1536]
    b_g, r_x, r_g = w_in[:, :, 1536:2048], w_in[:, :, 2048:2304], w_in[:, :, 2304:2560]
    k0, k1 = k[:, :, 0:64], k[:, :, 64:128]
    wperm = np.concatenate([k0, k0, k1, k1, v, r_x, r_g, a_u, a_g, a_v, q, b_g], axis=2)
    assert wperm.shape[2] == NC1 + NC2
    sh = {}
    sh["w_in"] = _kc_layout(wperm)
    sh["w_o"] = _kc_layout(f("w_o"))
    sh["w_ada"] = _kc_layout(f("w_ada"))
    sh["b_adaT"] = np.ascontiguousarray(f("b_ada").reshape(DEPTH, 24, 128).transpose(2, 0, 1))
    fm8 = lambda a: a.reshape(DEPTH, -1, 128).transpose(2, 0, 1)
    vec = np.zeros((128, DEPTH, NV), np.float32)
    vec[:, :, V_LNG:V_LNG + 8] = fm8(f("ln_g"))
    vec[:, :, V_LNB:V_LNB + 8] = fm8(f("ln_b"))
    vec[:, :, V_QG] = np.tile(f("q_norm_g"), (1, 2)).T
    vec[:, :, V_KG] = np.tile(f("k_norm_g"), (1, 2)).T
    cw = f("conv_w")
    for ct in range(2):
        for j in range(4):
            vec[:, :, V_CW + ct * 4 + j] = cw[:, j, ct * 128:(ct + 1) * 128].T
    vec[:, :, V_CB:V_CB + 2] = fm8(f("conv_b"))
    for nm, base in (("lru_br", V_BR), ("lru_bi", V_BI), ("lru_lam", V_LAM)):
        a = f(nm)
        for dd in range(2):
            for ct in range(2):
                vec[:, :, base + dd * 2 + ct] = a[:, dd, ct * 128:(ct + 1) * 128].T
    sh["vecs"] = vec
    an = np.stack([f("a_norm_g"), f("a_norm_b")], 1)
    sh["anorm"] = np.ascontiguousarray(np.broadcast_to(an[:, None], (DEPTH, 128, 2, 256)))
    bs = f("a_bs")
    ab = np.zeros((DEPTH, 128, 2, 128), np.float32)
    for ct in range(2):
        ab[:, 0:64, ct, :] = bs[:, 2 * ct, None, :]
        ab[:, 64:128, ct, :] = bs[:, 2 * ct + 1, None, :]
    sh["absb"] = ab
    sh["awsT"] = np.ascontiguousarray(f("a_ws").transpose(0, 3, 1, 2))
    lw = np.zeros((DEPTH, 128, 8, 128), np.float32)
    for dd in range(2):
        for gi, nm in enumerate(("lru_wr", "lru_wi")):
            w = f(nm)
            for ct in range(2):
                idx = (dd * 2 + gi) * 2 + ct
                for hb in range(2):
                    lw[:, hb * 64:(hb + 1) * 64, idx, hb * 64:(hb + 1) * 64] = w[:, dd, 2 * ct + hb]
    sh["lruW"] = lw
    sh["cst32"], sh["cst16"] = _consts()
    return sh


def prep_core(inp, b0, nseq):
    x = np.asarray(inp["x"], np.float32)
    ctx = np.asarray(inp["ctx"], np.float32)
    c = np.asarray(inp["c"], np.float32)
    c_ctx = np.asarray(inp["c_ctx"], np.float32)
    xin = np.empty((nseq, 128, 8, TOK), np.float32)
    for s in range(nseq):
        full = np.concatenate([x[b0 + s], ctx[b0 + s]], 0)
        xin[s] = full.T.reshape(8, 128, TOK).transpose(1, 0, 2)
    rows = [c[b0 + s] if s < nseq else c_ctx for s in range(2)] + [c_ctx]
    crow = np.stack(rows, 0)
    cT = np.ascontiguousarray(crow.T.reshape(8, 128, 3).transpose(1, 0, 2))
    return {"xin": xin, "cT": cT}


_PROG_CACHE = {}


def kernel(**inputs):
    n = 8
    nseq = 2
    key = ("full", nseq)
    if key not in _PROG_CACHE:
        _PROG_CACHE[key] = build_program(list(range(DEPTH)), nseq)
    nc = _PROG_CACHE[key]
    sh = prep_shared(inputs)
    in_maps = []
    for i in range(n):
        m = dict(sh)
        m.update(prep_core(inputs, i * nseq, nseq))
        in_maps.append(m)
    res = run_bass_kernel_spmd(nc, in_maps, core_ids=list(range(n)))
    out = np.empty((16, L_LAT, D), np.float32)
    for i in range(n):
        xo = np.asarray(res.results[i]["xout"])
        for s in range(nseq):
            out[i * nseq + s] = xo[s][:, :, 0:L_LAT].transpose(2, 1, 0).reshape(L_LAT, D)
    return out
```

```python
import math
import numpy as np
from contextlib import ExitStack
import concourse.bass as bass
import concourse.mybir as mybir
from concourse.bass_utils import run_bass_kernel_spmd

F32 = mybir.dt.float32
BF16 = mybir.dt.bfloat16
AF = mybir.ActivationFunctionType
ALU = mybir.AluOpType

D = 1024
L_LAT = 2048
L_CTX = 256
TOK = L_LAT + L_CTX
DEPTH = 4
ALPHA = (2.0 * DEPTH) ** 0.25
LN_EPS = 1e-6
RMS_EPS = 1e-6
NC1 = 896
NC2 = 1792
C_K0, C_K1, C_V, C_RX, C_RG = 0, 128, 256, 384, 640
C_U, C_GA, C_VA, C_Q, C_BG = 0, 256, 512, 768, 1280
NV = 40
V_LNG, V_LNB, V_QG, V_KG, V_CW, V_CB, V_BR, V_BI, V_LAM = 0, 8, 16, 17, 18, 26, 28, 32, 36

ENGS = ("pe", "act", "dve", "pool", "sp")


class Prog:
    def __init__(self, nc, es):
        self.nc = nc
        self.es = es
        self.q = {e: [] for e in ENGS}
        self.cnt = {}
        self.sem = {}
        for e in ENGS:
            self.sem[e] = es.enter_context(nc.semaphore("c_" + e))
            self.cnt[e] = 0
        self.known = {e: {} for e in ENGS}
        self.lastw = {}
        self.readers = {}
        self.pending = {e: False for e in ENGS}
        self.ninst = 0

    def newsem(self, name):
        self.sem[name] = self.es.enter_context(self.nc.semaphore(name))
        self.cnt[name] = 0
        return name

    def _deps(self, eng, reads, writes):
        deps = {}

        def add(sk, v):
            if v > deps.get(sk, 0):
                deps[sk] = v

        for k in reads:
            w = self.lastw.get(k)
            if w is not None:
                add(*w)
        for k in writes:
            w = self.lastw.get(k)
            if w is not None and w[0] != eng:
                add(*w)
            for sk, v in self.readers.get(k, {}).items():
                if sk != eng:
                    add(sk, v)
        if eng == "pe":
            deps.pop("pe", None)
        return deps

    def _emit_waits(self, eng, deps):
        kn = self.known[eng]
        for sk, v in deps.items():
            if sk not in ENGS:
                v = max(v, self.cnt[sk])
            if sk == eng and v > self.cnt[eng]:
                raise RuntimeError("self-dependency on pending op: " + eng)
            if kn.get(sk, 0) >= v:
                continue
            kn[sk] = v
            sem = self.sem[sk]
            self.q[eng].append(lambda E, sem=sem, v=v: E.wait_ge(sem, v))

    def op(self, eng, fn, reads=(), writes=(), inc=True):
        deps = self._deps(eng, reads, writes)
        self._emit_waits(eng, deps)
        n = self.cnt[eng] + 1
        for k in reads:
            self.readers.setdefault(k, {})[eng] = n
        for k in writes:
            self.lastw[k] = (eng, n)
            self.readers[k] = {}
        self.ninst += 1
        if inc:
            self.cnt[eng] = n
            sem = self.sem[eng]
            self.q[eng].append(lambda E, sem=sem: fn(E).then_inc(sem, 1))
            self.pending[eng] = False
        else:
            self.q[eng].append(lambda E: fn(E))
            self.pending[eng] = True

    def dma(self, queue, semname, out, in_, reads=(), writes=()):
        deps = self._deps(queue, reads, writes)
        self._emit_waits(queue, deps)
        n = self.cnt[semname] + 16
        self.cnt[semname] = n
        for k in reads:
            self.readers.setdefault(k, {})[semname] = n
        for k in writes:
            self.lastw[k] = (semname, n)
            self.readers[k] = {}
        sem = self.sem[semname]
        self.q[queue].append(lambda E, sem=sem: E.dma_start(out=out, in_=in_).then_inc(sem, 16))
        self.ninst += 1

    def fence(self):
        for e in ENGS:
            kn = self.known[e]
            for s, v in self.cnt.items():
                if s == e or v == 0 or kn.get(s, 0) >= v:
                    continue
                kn[s] = v
                sem = self.sem[s]
                self.q[e].append(lambda E, sem=sem, v=v: E.wait_ge(sem, v))

    def wait_all(self, eng, semnames):
        for s in semnames:
            v = self.cnt[s]
            if v > 0 and self.known[eng].get(s, 0) < v:
                self.known[eng][s] = v
                sem = self.sem[s]
                self.q[eng].append(lambda E, sem=sem, v=v: E.wait_ge(sem, v))

    def emit(self, block):
        for e in ENGS:
            assert not self.pending[e], "engine %s ends with un-inc'd op" % e
        q = self.q

        @block.tensor
        def _(E):
            for f in q["pe"]:
                f(E)

        @block.scalar
        def _(E):
            for f in q["act"]:
                f(E)

        @block.vector
        def _(E):
            for f in q["dve"]:
                f(E)

        @block.gpsimd
        def _(E):
            for f in q["pool"]:
                f(E)

        @block.sync
        def _(E):
            for f in q["sp"]:
                f(E)


class Carver:
    def __init__(self, scr, nbytes):
        self.scr = scr
        self.cap = nbytes
        self.off = 0
        self.hi = 0

    def alloc(self, shape, dt):
        esz = 4 if dt == F32 else 2
        n = 1
        for s in shape:
            n *= s
        nb = (n * esz + 31) // 32 * 32
        assert self.off + nb <= self.cap, ("SBUF scratch overflow", self.off, nb, self.cap)
        w0 = self.off // 4
        ap = self.scr[:, w0:w0 + nb // 4]
        if dt != F32:
            ap = ap.bitcast(dt)
        ap = ap[:, 0:n]
        if len(shape) == 2:
            ap = ap.rearrange("p (a b) -> p a b", a=shape[0])
        elif len(shape) == 3:
            ap = ap.rearrange("p (a b c) -> p a b c", a=shape[0], b=shape[1])
        self.off += nb
        self.hi = max(self.hi, self.off)
        return ap


def rev_ap(ap_):
    n = ap_.shape[1]
    pp = ap_.ap[0]
    return bass.AP(ap_.tensor, ap_.offset + (n - 1), [[pp[0], pp[1]], [-1, n]])


class _Stop(Exception):
    pass


def build_program(layers, nseq, first_in_program=True, dbg=None, stop=99):
    nc = bass.Bass("TRN2", target_bir_lowering=False)
    NL = DEPTH
    dram = lambda name, shape, dt=F32, kind="ExternalInput": nc.dram_tensor(name, shape, dt, kind=kind).ap()
    xin = dram("xin", [nseq, 128, 8, TOK])
    cT_d = dram("cT", [128, 8, 3])
    wada_d = dram("w_ada", [NL, 128, 8, 3072])
    bada_d = dram("b_adaT", [128, NL, 24])
    win_d = dram("w_in", [NL, 128, 8, NC1 + NC2])
    wo_d = dram("w_o", [NL, 128, 8, 1024])
    vecs_d = dram("vecs", [128, NL, NV])
    anorm_d = dram("anorm", [NL, 128, 2, 256])
    absb_d = dram("absb", [NL, 128, 2, 128])
    aws_d = dram("awsT", [NL, 128, 4, 128])
    lruw_d = dram("lruW", [NL, 128, 8, 128])
    c32_d = dram("cst32", [128, 256])
    c16_d = dram("cst16", [128, 256 + 2 * L_LAT])
    xout = dram("xout", [nseq, 128, 8, TOK], kind="ExternalOutput")
    dbg_out = {}
    if dbg:
        for name, shape in dbg.items():
            dbg_out[name] = dram("dbg_" + name, shape, kind="ExternalOutput")

    with ExitStack() as es:
        CAP = 212800
        scr = es.enter_context(nc.sbuf_tensor("scr", [128, CAP // 4], F32))
        psall = es.enter_context(nc.psum_tensor("psall", [128, 4096], F32))[:, :]
        PSB = [psall[:, i * 512:(i + 1) * 512] for i in range(8)]
        P = Prog(nc, es)
        for s in ("ld_x", "ld_c", "ld_w1", "ld_w2", "ld_wo", "ld_ada0", "ld_ada1", "ld_lw", "st_o", "st_d"):
            P.newsem(s)
        C = Carver(scr, CAP)

        xT = C.alloc([8, TOK], F32)
        tabs = C.alloc([2, L_LAT], BF16)
        KT = C.alloc([2, TOK], BF16)
        VA = C.alloc([18, 2, 128], BF16)
        ylru = C.alloc([2, TOK], BF16)
        c32 = C.alloc([256], F32)[:, :]
        c16 = C.alloc([256], BF16)
        mods = C.alloc([NL, 24, 3], F32)
        sc1 = C.alloc([NL, 8, 3], F32)
        gA = C.alloc([NL, 8, 3], F32)
        vecs = C.alloc([NL, NV], F32)
        cvec = C.alloc([NL, 4], F32)
        badaT = C.alloc([NL, 24], F32)
        anorm = C.alloc([2, 256], F32)
        absb = C.alloc([2, 128], F32)
        awsT = C.alloc([4, 128], BF16)
        lruW = C.alloc([8, 128], BF16)
        sil = C.alloc([8, 3], F32)
        epsr = C.alloc([4], F32)
        onesb = C.alloc([128], BF16)
        PBASE = C.off
        ident = c32[:, 0:128]
        onesD = c32[:, 128:256]
        Rmat = c16[:, 0:128]
        Bones = c16[:, 128:256]
        cosT = tabs[:, 0, :]
        sinT = tabs[:, 1, :]

        psi = [0, 0, 0]

        def nextps():
            i = psi[0] % 4
            psi[0] += 1
            return PSB[i], "ps%d" % i

        def nextps2():
            i = ((psi[0] + 1) // 2 * 2) % 4
            psi[0] = i + 2
            return psall[:, i * 512:(i + 2) * 512], ["ps%d" % i, "ps%d" % (i + 1)]

        def nextps_s():
            i = 3 + psi[1] % 3
            psi[1] += 1
            return PSB[i], "ps%d" % i

        def nextps_o():
            i = 6 + psi[2] % 2
            psi[2] += 1
            return PSB[i], "ps%d" % i

        def act(out, in_, func, r, w, **kw):
            P.op("act", lambda E: E.activation(out=out, in_=in_, func=func, **kw), r, w)

        def tt(eng, out, a, b, op, r, w):
            P.op(eng, lambda E: E.tensor_tensor(out=out, in0=a, in1=b, op=op), r, w)

        def ts(eng, out, a, s1, s2, op0, op1, r, w):
            if s2 is None:
                P.op(eng, lambda E: E.tensor_single_scalar(out=out, in_=a, scalar=s1, op=op0), r, w)
            else:
                P.op(eng, lambda E: E.tensor_scalar(out=out, in0=a, scalar1=s1, scalar2=s2, op0=op0, op1=op1), r, w)

        def scan(out, d0, d1, init, r, w):
            P.op("dve", lambda E: E.tensor_tensor_scan(out=out, data0=d0, data1=d1, initial=init, op0=ALU.mult, op1=ALU.add), r, w)

        def stt(eng, out, a, s, b, op0, op1, r, w):
            P.op(eng, lambda E: E.scalar_tensor_tensor(out=out, in0=a, scalar=s, in1=b, op0=op0, op1=op1), r, w)

        def mm(out, lhsT, rhs, start, stop, r, w, inc=True):
            P.op("pe", lambda E: E.matmul(out, lhsT, rhs, start=start, stop=stop), r, w, inc=inc)

        def recip(out, in_, r, w):
            P.op("dve", lambda E: E.reciprocal(out=out, in_=in_), r, w)

        def memset(eng, ap, val, w):
            P.op(eng, lambda E: E.memset(ap, val), (), w)

        def copy(eng, out, in_, r, w):
            P.op(eng, lambda E: E.tensor_copy(out=out, in_=in_), r, w)

        P.dma("sp", "ld_c", c32, c32_d[:, :], writes=["c32"])
        P.dma("pool", "ld_lw", c16, c16_d[:, 0:256], writes=["c16"])
        for a_ in range(2):
            P.dma("pool", "ld_lw", tabs[:, a_, :], c16_d[:, 256 + a_ * L_LAT:256 + (a_ + 1) * L_LAT], writes=["tabs"])
        P.dma("sp", "ld_c", vecs, vecs_d[:, :, :], writes=["vecs"])
        P.dma("sp", "ld_c", badaT, bada_d[:, :, :], writes=["badaT"])
        P.dma("sp", "ld_c", sil, cT_d[:, :, :], writes=["sil"])
        memset("dve", epsr[:, 0:1], RMS_EPS, ["epsr"])
        memset("dve", epsr[:, 1:2], LN_EPS / (ALPHA * ALPHA), ["epsr"])
        memset("dve", epsr[:, 2:3], 1.0, ["epsr"])
        memset("dve", epsr[:, 3:4], 0.0, ["epsr"])
        memset("dve", onesb, 1.0 / 1024.0, ["onesb"])
        memset("pool", VA[:, :, :, 64:128], 1.0, ["VA"])

        act(sil, sil, AF.Silu, ["sil"], ["sil"])
        for l in layers:
            act(cvec[:, l, :], vecs[:, l, V_LAM:V_LAM + 4], AF.Exp, ["vecs"], ["cvec"], scale=-1.0)
            act(cvec[:, l, :], cvec[:, l, :], AF.Ln, ["cvec", "epsr"], ["cvec"], bias=epsr[:, 2:3])
            ts("dve", cvec[:, l, :], cvec[:, l, :], -8.0, None, ALU.mult, ALU.bypass, ["cvec"], ["cvec"])
        C.off = PBASE
        stg = [C.alloc([8, 512], F32) for _ in range(2)]
        modrow = C.alloc([3072], F32)[:, :]
        bi = 0
        for l in layers:
            for nb in range(6):
                sb_ = stg[bi % 2]
                key = "stg%d" % (bi % 2)
                P.dma("sp", "ld_ada%d" % (bi % 2), sb_, wada_d[l, :, :, nb * 512:(nb + 1) * 512], writes=[key])
                ps, pk = nextps()
                for kc in range(8):
                    mm(ps[0:3, :], sil[:, kc, :], sb_[:, kc, :], kc == 0, kc == 7, [key, "sil"], [pk], inc=(kc == 7))
                copy("dve", modrow[0:3, nb * 512:(nb + 1) * 512], ps[0:3, :], [pk], ["modrow"])
                bi += 1
            ps, pk = nextps()
            for j in range(24):
                mm(ps[:, j * 3:(j + 1) * 3], modrow[0:3, j * 128:(j + 1) * 128], ident[0:3, 0:3], True, True,
                   ["modrow", "c32"], [pk], inc=(j == 23))
            tt("dve", mods[:, l, :, :], ps[:, 0:72].rearrange("p (a b) -> p a b", a=24),
               badaT[:, l, :].unsqueeze(2).to_broadcast([128, 24, 3]), ALU.add, [pk, "badaT"], ["mods"])
            ts("dve", sc1[:, l, :, :], mods[:, l, 8:16, :], 1.0, None, ALU.add, ALU.bypass, ["mods"], ["sc1"])
            ts("dve", gA[:, l, :, :], mods[:, l, 16:24, :], 1.0 / ALPHA, None, ALU.mult, ALU.bypass, ["mods"], ["gA"])
        P.fence()

        def rms_rope(zps, zk, n, gvec, dsts, dstk, t0, latent, S):
            act(S["sq"][:, 0:n], zps[:, 0:n], AF.Square, [zk], ["r_sq"])
            ps2, pk2 = nextps()
            mm(ps2[:, 0:n], Bones, S["sq"][:, 0:n], True, True, ["r_sq", "c16"], [pk2])
            act(S["sd"][:, 0:n], ps2[:, 0:n], AF.Ln, [pk2, "epsr"], ["r_sd"], bias=epsr[:, 0:1])
            act(S["sd"][:, 0:n], S["sd"][:, 0:n], AF.Exp, ["r_sd"], ["r_sd"], scale=-0.5)
            if not latent:
                for (lo, hi_, dst) in dsts:
                    stt("dve", dst, zps[lo:hi_, 0:n], gvec[lo:hi_, :], S["sd"][lo:hi_, 0:n], ALU.mult, ALU.mult,
                        [zk, "r_sd", "vecs"], [dstk])
                return
            stt("dve", S["qn"][:, 0:n], zps[:, 0:n], gvec, S["sd"][:, 0:n], ALU.mult, ALU.mult,
                [zk, "r_sd", "vecs"], ["r_qn"])
            ps3, pk3 = nextps()
            mm(ps3[:, 0:n], Rmat, S["qn"][:, 0:n], True, True, ["r_qn", "c16"], [pk3])
            tt("pool", S["t1"][:, 0:n], S["qn"][:, 0:n], cosT[:, t0:t0 + n], ALU.mult, ["r_qn", "tabs"], ["r_t1"])
            tt("dve", S["t2"][:, 0:n], ps3[:, 0:n], sinT[:, t0:t0 + n], ALU.mult, [pk3, "tabs"], ["r_t2"])
            for (lo, hi_, dst) in dsts:
                tt("pool", dst, S["t1"][lo:hi_, 0:n], S["t2"][lo:hi_, 0:n], ALU.add, ["r_t1", "r_t2"], [dstk])

        def xkeys(t0, n):
            return ["xT%d" % b for b in range(t0 // 256, (t0 + n + 255) // 256)]

        ALLX = ["xT%d" % b for b in range(9)]

        def modulate(dst, dk, l, r, t0, n, first=False):
            for c in range(8):
                eng = "dve" if (first or c % 2 == 1) else "pool"
                ts(eng, dst[:, c, 0:n], xT[:, c, t0:t0 + n], sc1[:, l, c, r:r + 1], mods[:, l, c, r:r + 1],
                   ALU.mult, ALU.add, xkeys(t0, n) + ["sc1", "mods"], [dk])

        def inproj_fm(ps, pk, W, wk, col0, xm, xk, n):
            for kc in range(8):
                mm(ps[:, 0:n], W[:, kc, col0:col0 + 128], xm[:, kc, 0:n], kc == 0, kc == 7, [wk, xk], [pk], inc=(kc == 7))

        def stage(n):
            if n > stop:
                raise _Stop()

        W1P = (("k", 0, 256), ("v", 256, 128), ("rx", 384, 256), ("rg", 640, 256))
        W2P = (("u", C_U, 256), ("va", C_VA, 256), ("ga", C_GA, 256), ("bg", C_BG, 512), ("q", C_Q, 512))
        for nm, _, _ in W1P:
            P.newsem("ld_w1" + nm)
        for nm, _, _ in W2P:
            P.newsem("ld_w2" + nm)

        for s in range(nseq):
            try:
                for c in range(8):
                    P.dma("sp", "ld_x", xT[:, c, :], xin[s, :, c, :], writes=ALLX)
                for l in layers:
                    last = (l == DEPTH - 1)
                    P.dma("sp", "ld_c", anorm, anorm_d[l, :, :, :], writes=["anorm"])
                    P.dma("sp", "ld_c", absb, absb_d[l, :, :, :], writes=["absb"])
                    P.dma("pool", "ld_lw", awsT, aws_d[l, :, :, :], writes=["awsT"])
                    P.dma("pool", "ld_lw", lruW, lruw_d[l, :, :, :], writes=["lruW"])
                    stage(1)
                    P.fence()
                    C.off = PBASE
                    win1 = C.alloc([8, NC1], BF16)
                    xm1 = C.alloc([8, 512], BF16)
                    S1 = {"sq": C.alloc([512], BF16)[:, :], "sd": C.alloc([512], F32)[:, :],
                          "qn": C.alloc([512], BF16)[:, :], "t1": C.alloc([512], F32)[:, :]}
                    assert C.off - PBASE == NC2 * 8 * 2, (C.off - PBASE)
                    S1["t2"] = C.alloc([512], F32)[:, :]
                    sg = C.alloc([2, TOK], BF16)
                    xrb = C.alloc([2, TOK], BF16)
                    GBASE = C.off
                    rx = C.alloc([2, 2312], F32)
                    acc = C.alloc([TOK], F32)[:, :]
                    for nm, c0, ncl in W1P:
                        P.dma("pool", "ld_w1" + nm, win1[:, :, c0:c0 + ncl], win_d[l, :, :, c0:c0 + ncl], writes=["win1" + nm])
                    memset("pool", rx[:, :, 0:2], 0.0, ["rx"])
                    memset("pool", rx[:, :, 2050:2052], 0.0, ["rx"])
                    memset("pool", rx[:, :, 2308:2312], 0.0, ["rx"])
                    for (t0, n) in ((0, 512), (512, 512), (1024, 512), (1536, 512), (2048, 256)):
                        latent = t0 < L_LAT
                        r = s if latent else 2
                        modulate(xm1, "xm1", l, r, t0, n, first=(t0 == 0))
                        for g in range(2):
                            ps, pk = nextps()
                            inproj_fm(ps, pk, win1, "win1k", C_K0 + g * 128, xm1, "xm1", n)
                            rms_rope(ps, pk, n, vecs[:, l, V_KG:V_KG + 1], [(0, 128, KT[:, g, t0:t0 + n])], "KT", t0, latent, S1)
                        for tl in range(n // 128):
                            ps, pk = nextps()
                            for kc in range(8):
                                mm(ps[:, 0:128], xm1[:, kc, tl * 128:(tl + 1) * 128], win1[:, kc, C_V:C_V + 128],
                                   kc == 0, kc == 7, ["win1v", "xm1"], [pk], inc=(kc == 7))
                            tix = t0 // 128 + tl
                            copy("dve", VA[:, tix, :, 0:64], ps[:, 0:128].rearrange("p (g d) -> p g d", g=2), [pk], ["VA"])
                        for ct in range(2):
                            ps, pk = nextps()
                            inproj_fm(ps, pk, win1, "win1rx", C_RX + ct * 128, xm1, "xm1", n)
                            off = 2 + t0 if latent else 2052 + (t0 - L_LAT)
                            act(rx[:, ct, off:off + n], ps[:, 0:n], AF.Copy, [pk], ["rx"])
                        for ct in range(2):
                            ps, pk = nextps()
                            inproj_fm(ps, pk, win1, "win1rg", C_RG + ct * 128, xm1, "xm1", n)
                            act(sg[:, ct, t0:t0 + n], ps[:, 0:n], AF.Silu, [pk], ["sg"])
                    stage(2)
                    for ct in range(2):
                        for (d0, o0, n) in ((2, 0, L_LAT), (2052, L_LAT, L_CTX)):
                            cw = lambda j: vecs[:, l, V_CW + ct * 4 + j:V_CW + ct * 4 + j + 1]
                            eng = "dve"
                            ts(eng, acc[:, 0:n], rx[:, ct, d0 - 2:d0 - 2 + n], cw(0), vecs[:, l, V_CB + ct:V_CB + ct + 1],
                               ALU.mult, ALU.add, ["rx", "vecs"], ["acc"])
                            stt(eng, acc[:, 0:n], rx[:, ct, d0 - 1:d0 - 1 + n], cw(1), acc[:, 0:n], ALU.mult, ALU.add,
                                ["rx", "vecs", "acc"], ["acc"])
                            stt(eng, acc[:, 0:n], rx[:, ct, d0:d0 + n], cw(2), acc[:, 0:n], ALU.mult, ALU.add,
                                ["rx", "vecs", "acc"], ["acc"])
                            stt(eng, xrb[:, ct, o0:o0 + n], rx[:, ct, d0 + 1:d0 + 1 + n], cw(3), acc[:, 0:n], ALU.mult, ALU.add,
                                ["rx", "vecs", "acc"], ["xrb"])
                    stage(3)
                    P.fence()
                    C.off = PBASE
                    win2 = C.alloc([8, NC2], BF16)
                    for nm, c0, ncl in W2P:
                        P.dma("pool", "ld_w2" + nm, win2[:, :, c0:c0 + ncl], win_d[l, :, :, NC1 + c0:NC1 + c0 + ncl],
                              writes=["win2" + nm])
                    C.off = GBASE
                    Ab = C.alloc([TOK], F32)[:, :]
                    Tb = C.alloc([TOK], F32)[:, :]
                    Bb = C.alloc([TOK], F32)[:, :]
                    Hf = C.alloc([TOK], F32)[:, :]
                    for ct in range(2):
                        for dd in range(2):
                            vi = dd * 2 + ct
                            for (t0, n) in ((0, 512), (512, 512), (1024, 512), (1536, 512), (2048, 256)):
                                ps, pk = nextps()
                                mm(ps[:, 0:n], lruW[:, (dd * 2 + 0) * 2 + ct, :], xrb[:, ct, t0:t0 + n], True, True,
                                   ["lruW", "xrb"], [pk])
                                act(Ab[:, t0:t0 + n], ps[:, 0:n], AF.Sigmoid, [pk, "vecs"], ["Ab"],
                                    bias=vecs[:, l, V_BR + vi:V_BR + vi + 1])
                                ps, pk = nextps()
                                mm(ps[:, 0:n], lruW[:, (dd * 2 + 1) * 2 + ct, :], xrb[:, ct, t0:t0 + n], True, True,
                                   ["lruW", "xrb"], [pk])
                                act(Bb[:, t0:t0 + n], ps[:, 0:n], AF.Sigmoid, [pk, "vecs"], ["Bb"],
                                    bias=vecs[:, l, V_BI + vi:V_BI + vi + 1])
                            act(Ab, Ab, AF.Exp, ["Ab", "cvec"], ["Ab"], scale=cvec[:, l, vi:vi + 1])
                            act(Tb, Ab, AF.Square, ["Ab"], ["Tb"])
                            act(Tb, Tb, AF.Ln, ["Tb", "epsr"], ["Tb"], scale=-1.0, bias=epsr[:, 2:3])
                            act(Tb, Tb, AF.Exp, ["Tb"], ["Tb"], scale=0.5)
                            tt("dve", Bb, Bb, xrb[:, ct, :], ALU.mult, ["Bb", "xrb"], ["Bb"])
                            tt("dve", Bb, Bb, Tb, ALU.mult, ["Bb", "Tb"], ["Bb"])
                            if dd == 0:
                                scan(Hf[:, L_LAT:TOK], Ab[:, L_LAT:TOK], Bb[:, L_LAT:TOK], 0.0, ["Ab", "Bb"], ["Hf"])
                                scan(Hf[:, 0:L_LAT], Ab[:, 0:L_LAT], Bb[:, 0:L_LAT], Hf[:, TOK - 1:TOK], ["Ab", "Bb", "Hf"], ["Hf"])
                            else:
                                scan(rev_ap(Tb), rev_ap(Ab), rev_ap(Bb), 0.0, ["Ab", "Bb"], ["Tb"])
                        tt("pool", Hf, Hf, Tb, ALU.add, ["Hf", "Tb"], ["Hf"])
                        tt("pool", ylru[:, ct, :], Hf, sg[:, ct, :], ALU.mult, ["Hf", "sg"], ["ylru"])
                    stage(4)
                    P.fence()
                    C.off = PBASE
                    win2 = C.alloc([8, NC2], BF16)
                    wo = C.alloc([8, 1024], BF16)
                    for kc in range(8):
                        P.dma("pool", "ld_wo", wo[:, kc, :], wo_d[l, :, kc, :], writes=["wo"])
                    xm = C.alloc([8, 256], BF16)
                    yT = C.alloc([6, 256], BF16)
                    ug = C.alloc([2, 256], F32)
                    sga = C.alloc([2, 256], F32)
                    vg = C.alloc([2, 256], F32)
                    vln = C.alloc([2, 256], BF16)
                    tmpg = C.alloc([512], F32)[:, :]
                    st = C.alloc([2, 8], F32)
                    qbd = C.alloc([4, 512], BF16)
                    sbg = C.alloc([4, 256], BF16)
                    bufA = C.alloc([1024], BF16)[:, :]
                    bufBC = C.alloc([2048], F32)[:, :]
                    bufB = bufBC[:, 0:1024]
                    bufC = bufBC[:, 1024:2048]
                    pT = [C.alloc([512], BF16)[:, :] for _ in range(3)]
                    rden = [C.alloc([256], F32)[:, :] for _ in range(2)]
                    otmp = [C.alloc([256], F32)[:, :]] * 2
                    sqb = C.alloc([8, 256], BF16)
                    mean_sb = C.alloc([256], F32)[:, :]
                    m2 = C.alloc([256], F32)[:, :]
                    rsl = C.alloc([256], F32)[:, :]
                    memset("pool", qbd[64:128, :, 0:256], 0.0, ["qbd"])
                    memset("pool", qbd[0:64, :, 256:512], 0.0, ["qbd"])
                    nblk = 8 if last else 9
                    cn = [0, 0]
                    def mod(qb):
                        t0 = qb * 256
                        latent = qb < 8
                        r = s if latent else 2
                        xk = "xT%d" % qb
                        modulate(xm, "xm", l, r, t0, 256, first=(qb == 0))

                    def front_a(qb):
                        t0 = qb * 256
                        latent = qb < 8
                        r = s if latent else 2
                        xk = "xT%d" % qb
                        bank = lambda i: (psall[:, i * 512:(i + 1) * 512], "ps%d" % i)
                        bank2 = lambda i: (psall[:, i * 512:(i + 2) * 512], ["ps%d" % i, "ps%d" % (i + 1)])
                        gq = vecs[:, l, V_QG:V_QG + 1]
                        v4 = lambda ap_, lo, hi_: ap_[lo:hi_, :].rearrange("p (c t) -> p c t", c=4)
                        psQ, pkQ = bank2(0)
                        for ct in range(4):
                            inproj_fm(psQ[:, ct * 256:(ct + 1) * 256], pkQ[ct // 2], win2, "win2q", C_Q + ct * 128, xm, "xm", 256)
                        act(bufA, psQ, AF.Square, pkQ, ["bufA"])
                        psU, pkU = bank(2)
                        for ct in range(2):
                            inproj_fm(psU[:, ct * 256:(ct + 1) * 256], pkU, win2, "win2u", C_U + ct * 128, xm, "xm", 256)
                        psV, pkV = bank(3)
                        for tl in range(2):
                            for kc in range(8):
                                mm(psV[:, tl * 256:(tl + 1) * 256], xm[:, kc, tl * 128:(tl + 1) * 128], win2[:, kc, C_VA:C_VA + 256],
                                   kc == 0, kc == 7, ["win2va", "xm"], [pkV], inc=(kc == 7))
                        psM, pkM = bank2(4)
                        for hf in range(2):
                            mm(psM[:, hf * 512:(hf + 1) * 512], Bones, bufA[:, hf * 512:(hf + 1) * 512], True, True,
                               ["bufA", "c16"], [pkM[hf]])
                        act(bufB, psM, AF.Ln, pkM + ["epsr"], ["bufB"], bias=epsr[:, 0:1])
                        act(bufB, bufB, AF.Exp, ["bufB"], ["bufB"], scale=-0.5)
                        if latent:
                            stt("dve", bufA, psQ, gq, bufB, ALU.mult, ALU.mult, pkQ + ["bufB", "vecs"], ["bufA"])
                        else:
                            stt("dve", qbd[0:64, :, 0:256], v4(psQ, 0, 64), gq[0:64, :], v4(bufB, 0, 64), ALU.mult, ALU.mult,
                                pkQ + ["bufB", "vecs"], ["qbd"])
                            stt("dve", qbd[64:128, :, 256:512], v4(psQ, 64, 128), gq[64:128, :], v4(bufB, 64, 128), ALU.mult, ALU.mult,
                                pkQ + ["bufB", "vecs"], ["qbd"])
                        act(ug.rearrange("p c t -> p (c t)"), psU, AF.Gelu_apprx_tanh, [pkU], ["ug"])
                        for tl in range(2):
                            act(vg[:, tl, :], psV[:, tl * 256:(tl + 1) * 256], AF.Gelu_apprx_tanh, [pkV], ["vg", "st"],
                                accum_out=st[:, tl, 0:1])
                        ps2b, pk2b = bank2(6)
                        for ct in range(4):
                            inproj_fm(ps2b[:, ct * 256:(ct + 1) * 256], pk2b[ct // 2], win2, "win2bg", C_BG + ct * 128, xm, "xm", 256)
                        psG, pkG = bank(2)
                        for ct in range(2):
                            inproj_fm(psG[:, ct * 256:(ct + 1) * 256], pkG, win2, "win2ga", C_GA + ct * 128, xm, "xm", 256)
                        if latent:
                            psR, pkR = bank2(4)
                            for hf in range(2):
                                mm(psR[:, hf * 512:(hf + 1) * 512], Rmat, bufA[:, hf * 512:(hf + 1) * 512], True, True,
                                   ["bufA", "c16"], [pkR[hf]])
                        act(sga.rearrange("p c t -> p (c t)"), psG, AF.Silu, [pkG], ["sga"])
                        act(sbg.rearrange("p c t -> p (c t)"), ps2b, AF.Silu, pk2b, ["sbg"])
                        tt("pool", ug, ug, sga, ALU.mult, ["ug", "sga"], ["ug"])
                        if latent:
                            cosb = cosT[:, t0:t0 + 256].unsqueeze(1).to_broadcast([128, 4, 256])
                            sinb = sinT[:, t0:t0 + 256].unsqueeze(1).to_broadcast([128, 4, 256])
                            tt("pool", v4(bufB, 0, 128), v4(bufA, 0, 128), cosb, ALU.mult, ["bufA", "tabs"], ["bufB"])
                            tt("dve", v4(bufC, 0, 128), v4(psR, 0, 128), sinb, ALU.mult, pkR + ["tabs"], ["bufC"])
                            tt("pool", qbd[0:64, :, 0:256], v4(bufB, 0, 64), v4(bufC, 0, 64), ALU.add, ["bufB", "bufC"], ["qbd"])
                            tt("pool", qbd[64:128, :, 256:512], v4(bufB, 64, 128), v4(bufC, 64, 128), ALU.add,
                               ["bufB", "bufC"], ["qbd"])

                    def front_b(qb):
                        t0 = qb * 256
                        latent = qb < 8
                        r = s if latent else 2
                        xk = "xT%d" % qb
                        for tl in range(2):
                            act(tmpg[:, 0:256], vg[:, tl, :], AF.Square, ["vg"], ["tmpg", "st"], accum_out=st[:, tl, 1:2])
                            ts("dve", st[:, tl, 2:3], st[:, tl, 0:1], 1.0 / 256.0, None, ALU.mult, ALU.bypass, ["st"], ["st"])
                            tt("dve", st[:, tl, 3:4], st[:, tl, 2:3], st[:, tl, 2:3], ALU.mult, ["st"], ["st"])
                            stt("dve", st[:, tl, 4:5], st[:, tl, 1:2], 1.0 / 256.0, st[:, tl, 3:4], ALU.mult, ALU.subtract,
                                ["st"], ["st"])
                        for tl in range(2):
                            act(st[:, tl, 5:6], st[:, tl, 4:5], AF.Ln, ["st", "epsr"], ["st"], bias=epsr[:, 0:1], scale=1.0)
                            act(st[:, tl, 6:7], st[:, tl, 5:6], AF.Exp, ["st"], ["st"], scale=-0.5)
                        for tl in range(2):
                            ts("dve", vg[:, tl, :], vg[:, tl, :], st[:, tl, 2:3], st[:, tl, 6:7], ALU.subtract, ALU.mult,
                               ["vg", "st"], ["vg"])
                            tt("pool", vg[:, tl, :], vg[:, tl, :], anorm[:, 0, :], ALU.mult, ["vg", "anorm"], ["vg"])
                            tt("pool", vln[:, tl, :], vg[:, tl, :], anorm[:, 1, :], ALU.add, ["vg", "anorm"], ["vln"])

                    def front_c(qb):
                        t0 = qb * 256
                        latent = qb < 8
                        r = s if latent else 2
                        xk = "xT%d" % qb
                        ps, pk = PSB[3], "ps3"
                        for tl in range(2):
                            for ct in range(2):
                                for hh in range(2):
                                    g = 2 * ct + hh
                                    co = (tl * 2 + ct) * 128
                                    mm(ps[hh * 64:(hh + 1) * 64, co:co + 128], vln[:, tl, g * 64:(g + 1) * 64], awsT[:, g, :],
                                       True, True, ["vln", "awsT"], [pk], inc=(tl == 1 and ct == 1 and hh == 1))
                        tt("dve", tmpg.rearrange("p (t c q) -> p t c q", t=2, c=2), ps.rearrange("p (t c q) -> p t c q", t=2, c=2),
                           absb.unsqueeze(1).to_broadcast([128, 2, 2, 128]), ALU.add, [pk, "absb"], ["tmpg"])
                        tt("pool", yT[:, 0:2, :].rearrange("p c (t q) -> p t c q", t=2),
                           tmpg.rearrange("p (t c q) -> p t c q", t=2, c=2),
                           ug.rearrange("p c (t q) -> p t c q", t=2), ALU.mult, ["tmpg", "ug"], ["yT"])

                    def attn(qb):
                        t0 = qb * 256
                        latent = qb < 8
                        r = s if latent else 2
                        xk = "xT%d" % qb
                        ktiles = list(range(18)) if latent else [16, 17]
                        for ct in range(4):
                            g = ct // 2
                            pO, kO = nextps_o()

                            def qk(j):
                                ps_, pk_ = nextps_s()
                                mm(ps_[:, 0:512], KT[:, g, j * 128:(j + 1) * 128], qbd[:, ct, :], True, True,
                                   ["KT", "qbd"], [pk_])
                                return ps_, pk_

                            pend = [qk(j) for j in ktiles[0:2]]
                            for ji, j in enumerate(ktiles):
                                ps_, pk_ = pend.pop(0)
                                if ji + 2 < len(ktiles):
                                    pend.append(qk(ktiles[ji + 2]))
                                pb = pT[cn[0] % 3]
                                pbk = "pT%d" % (cn[0] % 3)
                                cn[0] += 1
                                act(pb, ps_, AF.Exp, [pk_], [pbk], scale=0.125)
                                mm(pO[:, 0:512], VA[:, j, g, :], pb, ji == 0, ji == len(ktiles) - 1, ["VA", pbk], [kO])
                            rd = rden[cn[1] % 2]
                            ot = otmp[cn[1] % 2]
                            rk, ok = "rden%d" % (cn[1] % 2), "otmp"
                            cn[1] += 1
                            recip(rd[0:64, :], pO[64:128, 0:256], [kO], [rk])
                            recip(rd[64:128, :], pO[64:128, 256:512], [kO], [rk])
                            tt("dve", ot[0:64, :], pO[0:64, 0:256], rd[0:64, :], ALU.mult, [kO, rk], [ok])
                            tt("dve", ot[64:128, :], pO[0:64, 256:512], rd[64:128, :], ALU.mult, [kO, rk], [ok])
                            tt("pool", yT[:, 2 + ct, :], ot, sbg[:, ct, :], ALU.mult, [ok, "sbg"], ["yT"])

                    def back(qb):
                        t0 = qb * 256
                        latent = qb < 8
                        r = s if latent else 2
                        xk = "xT%d" % qb
                        pm, km = nextps_o()
                        pq, kq = nextps_o()
                        xblk = xT[:, :, t0:t0 + 256]

                        def ln_stats(dt_):
                            mm(pm[:, 0:256], onesb, xm[:, dt_, :], dt_ == 0, dt_ == 7, ["xm", "onesb"], [km], inc=True)
                            mm(pq[:, 0:256], onesb, sqb[:, dt_, :], dt_ == 0, dt_ == 7, ["sqb", "onesb"], [kq], inc=True)

                        for dt_ in range(8):
                            ps, pk = nextps()
                            for kc in range(8):
                                rhs = yT[:, kc, :] if kc < 6 else ylru[:, kc - 6, t0:t0 + 256]
                                mm(ps[:, 0:256], wo[:, kc, dt_ * 128:(dt_ + 1) * 128], rhs, kc == 0, kc == 7,
                                   ["wo", "yT", "ylru"], [pk], inc=(kc == 7))
                            stt("dve", xT[:, dt_, t0:t0 + 256], ps[:, 0:256], gA[:, l, dt_, r:r + 1], xT[:, dt_, t0:t0 + 256],
                                ALU.mult, ALU.add, [pk, "gA", xk], [xk])
                            act(xm[:, dt_, :], xT[:, dt_, t0:t0 + 256], AF.Copy, [xk], ["xm"])
                            act(sqb[:, dt_, :], xT[:, dt_, t0:t0 + 256], AF.Square, [xk], ["sqb"])
                            if dt_ >= 2:
                                ln_stats(dt_ - 2)
                        ln_stats(6)
                        ln_stats(7)
                        copy("dve", mean_sb, pm[:, 0:256], [km], ["mean_sb"])
                        tt("pool", m2, mean_sb, mean_sb, ALU.mult, ["mean_sb"], ["m2"])
                        tt("dve", m2, pq[:, 0:256], m2, ALU.subtract, [kq, "m2"], ["m2"])

                    def back_b(qb):
                        t0 = qb * 256
                        xk = "xT%d" % qb
                        xblk = xT[:, :, t0:t0 + 256]
                        act(rsl, m2, AF.Ln, ["m2", "epsr"], ["rsl"], bias=epsr[:, 1:2], scale=1.0)
                        act(rsl, rsl, AF.Exp, ["rsl"], ["rsl"], scale=-0.5)
                        tln8 = bufBC.rearrange("p (c t) -> p c t", c=8)
                        tt("pool", tln8, xblk, mean_sb.unsqueeze(1).to_broadcast([128, 8, 256]), ALU.subtract,
                           [xk, "mean_sb"], ["bufB", "bufC"])
                        tt("dve", tln8, tln8, rsl.unsqueeze(1).to_broadcast([128, 8, 256]), ALU.mult,
                           ["bufB", "bufC", "rsl"], ["bufB", "bufC"])
                        for dt_ in range(8):
                            sc_, bi_ = vecs[:, l, V_LNG + dt_:V_LNG + dt_ + 1], vecs[:, l, V_LNB + dt_:V_LNB + dt_ + 1]
                            ts("dve" if dt_ % 2 == 0 else "pool", xT[:, dt_, t0:t0 + 256], tln8[:, dt_, :], sc_, bi_,
                               ALU.mult, ALU.add, ["bufB", "bufC", "vecs"], [xk])

                    mod(0)
                    front_a(0)
                    front_b(0)
                    for qb in range(nblk):
                        if qb + 1 < nblk:
                            mod(qb + 1)
                        if qb >= 1:
                            back_b(qb - 1)
                        attn(qb)
                        front_c(qb)
                        if qb + 1 < nblk:
                            front_a(qb + 1)
                        back(qb)
                        if qb + 1 < nblk:
                            front_b(qb + 1)
                    back_b(nblk - 1)
            except _Stop:
                pass
            for c in range(8):
                P.dma("sp", "st_o", xout[s, :, c, :], xT[:, c, :], reads=ALLX)
        P.wait_all("sp", ["st_o", "st_d"])
        block = es.enter_context(nc.Block())
        P.emit(block)
        build_program.last_stats = (P.ninst, dict(P.cnt), C.hi)
    return nc


def _rope_tables():
    half, nf = 32, 16
    inv = (10000.0 ** (-np.arange(nf, dtype=np.float32) / nf)).astype(np.float32)
    t = np.arange(L_LAT)
    rows = (t // 64).astype(np.float32)
    cols = (t % 64).astype(np.float32)
    cos = np.zeros((64, L_LAT), np.float32)
    sin = np.zeros((64, L_LAT), np.float32)
    for d in range(64):
        pos = rows if d < 32 else cols
        f = d % 16
        ang = (pos * inv[f]).astype(np.float32)
        cos[d] = np.cos(ang)
        sgn = -1.0 if (d % 32) < 16 else 1.0
        sin[d] = sgn * np.sin(ang)
    return np.tile(cos, (2, 1)), np.tile(sin, (2, 1))


def _consts():
    ident = np.eye(128, dtype=np.float32)
    onesD = np.full((128, 128), 1.0 / 1024.0, np.float32)
    c32 = np.concatenate([ident, onesD], 1)
    R = np.zeros((128, 128), np.float32)
    for m in range(128):
        d = m % 64
        partner = d + 16 if (d % 32) < 16 else d - 16
        R[(m // 64) * 64 + partner, m] = 1.0
    B = np.zeros((128, 128), np.float32)
    B[0:64, 0:64] = 1.0 / 64.0
    B[64:128, 64:128] = 1.0 / 64.0
    cos, sin = _rope_tables()
    c16 = np.concatenate([R, B, cos, sin], 1)
    return np.ascontiguousarray(c32), np.ascontiguousarray(c16)


def _kc_layout(w):
    Ln, K, N = w.shape
    return np.ascontiguousarray(w.reshape(Ln, 8, 128, N).transpose(0, 2, 1, 3))


def prep_shared(inp):
    f = lambda k: np.asarray(inp[k], dtype=np.float32)
    w_in = f("w_in")
    a_u, a_v, a_g = w_in[:, :, 0:256], w_in[:, :, 256:512], w_in[:, :, 512:768]
    q, k, v = w_in[:, :, 768:1280], w_in[:, :, 1280:1408], w_in[:, :, 1408:1536]
    b_g, r_x, r_g = w_in[:, :, 1536:2048], w_in[:, :, 2048:2304], w_in[:, :, 2304:2560]
    k0, k1 = k[:, :, 0:64], k[:, :, 64:128]
    wperm = np.concatenate([k0, k0, k1, k1, v, r_x, r_g, a_u, a_g, a_v, q, b_g], axis=2)
    assert wperm.shape[2] == NC1 + NC2
    sh = {}
    sh["w_in"] = _kc_layout(wperm)
    sh["w_o"] = _kc_layout(f("w_o"))
    sh["w_ada"] = _kc_layout(f("w_ada"))
    sh["b_adaT"] = np.ascontiguousarray(f("b_ada").reshape(DEPTH, 24, 128).transpose(2, 0, 1))
    fm8 = lambda a: a.reshape(DEPTH, -1, 128).transpose(2, 0, 1)
    vec = np.zeros((128, DEPTH, NV), np.float32)
    vec[:, :, V_LNG:V_LNG + 8] = fm8(f("ln_g"))
    vec[:, :, V_LNB:V_LNB + 8] = fm8(f("ln_b"))
    vec[:, :, V_QG] = np.tile(f("q_norm_g"), (1, 2)).T
    vec[:, :, V_KG] = np.tile(f("k_norm_g"), (1, 2)).T
    cw = f("conv_w")
    for ct in range(2):
        for j in range(4):
            vec[:, :, V_CW + ct * 4 + j] = cw[:, j, ct * 128:(ct + 1) * 128].T
    vec[:, :, V_CB:V_CB + 2] = fm8(f("conv_b"))
    for nm, base in (("lru_br", V_BR), ("lru_bi", V_BI), ("lru_lam", V_LAM)):
        a = f(nm)
        for dd in range(2):
            for ct in range(2):
                vec[:, :, base + dd * 2 + ct] = a[:, dd, ct * 128:(ct + 1) * 128].T
    sh["vecs"] = vec
    an = np.stack([f("a_norm_g"), f("a_norm_b")], 1)
    sh["anorm"] = np.ascontiguousarray(np.broadcast_to(an[:, None], (DEPTH, 128, 2, 256)))
    bs = f("a_bs")
    ab = np.zeros((DEPTH, 128, 2, 128), np.float32)
    for ct in range(2):
        ab[:, 0:64, ct, :] = bs[:, 2 * ct, None, :]
        ab[:, 64:128, ct, :] = bs[:, 2 * ct + 1, None, :]
    sh["absb"] = ab
    sh["awsT"] = np.ascontiguousarray(f("a_ws").transpose(0, 3, 1, 2))
    lw = np.zeros((DEPTH, 128, 8, 128), np.float32)
    for dd in range(2):
        for gi, nm in enumerate(("lru_wr", "lru_wi")):
            w = f(nm)
            for ct in range(2):
                idx = (dd * 2 + gi) * 2 + ct
                for hb in range(2):
                    lw[:, hb * 64:(hb + 1) * 64, idx, hb * 64:(hb + 1) * 64] = w[:, dd, 2 * ct + hb]
    sh["lruW"] = lw
    sh["cst32"], sh["cst16"] = _consts()
    return sh


def prep_core(inp, b0, nseq):
    x = np.asarray(inp["x"], np.float32)
    ctx = np.asarray(inp["ctx"], np.float32)
    c = np.asarray(inp["c"], np.float32)
    c_ctx = np.asarray(inp["c_ctx"], np.float32)
    xin = np.empty((nseq, 128, 8, TOK), np.float32)
    for s in range(nseq):
        full = np.concatenate([x[b0 + s], ctx[b0 + s]], 0)
        xin[s] = full.T.reshape(8, 128, TOK).transpose(1, 0, 2)
    rows = [c[b0 + s] if s < nseq else c_ctx for s in range(2)] + [c_ctx]
    crow = np.stack(rows, 0)
    cT = np.ascontiguousarray(crow.T.reshape(8, 128, 3).transpose(1, 0, 2))
    return {"xin": xin, "cT": cT}


_PROG_CACHE = {}


def kernel(**inputs):
    n = 8
    nseq = 2
    key = ("full", nseq)
    if key not in _PROG_CACHE:
        _PROG_CACHE[key] = build_program(list(range(DEPTH)), nseq)
    nc = _PROG_CACHE[key]
    sh = prep_shared(inputs)
    in_maps = []
    for i in range(n):
        m = dict(sh)
        m.update(prep_core(inputs, i * nseq, nseq))
        in_maps.append(m)
    res = run_bass_kernel_spmd(nc, in_maps, core_ids=list(range(n)))
    out = np.empty((16, L_LAT, D), np.float32)
    for i in range(n):
        xo = np.asarray(res.results[i]["xout"])
        for s in range(nseq):
            out[i * nseq + s] = xo[s][:, :, 0:L_LAT].transpose(2, 1, 0).reshape(L_LAT, D)
    return out
```

```python
import math
import numpy as np
from contextlib import ExitStack
import concourse.bass as bass
import concourse.mybir as mybir
from concourse.bass_utils import run_bass_kernel_spmd

F32 = mybir.dt.float32
BF16 = mybir.dt.bfloat16
AF = mybir.ActivationFunctionType
ALU = mybir.AluOpType

D = 1024
L_LAT = 2048
L_CTX = 256
TOK = L_LAT + L_CTX
DEPTH = 4
ALPHA = (2.0 * DEPTH) ** 0.25
LN_EPS = 1e-6
RMS_EPS = 1e-6
NC1 = 896
NC2 = 1792
C_K0, C_K1, C_V, C_RX, C_RG = 0, 128, 256, 384, 640
C_U, C_GA, C_VA, C_Q, C_BG = 0, 256, 512, 768, 1280
NV = 40
V_LNG, V_LNB, V_QG, V_KG, V_CW, V_CB, V_BR, V_BI, V_LAM = 0, 8, 16, 17, 18, 26, 28, 32, 36

ENGS = ("pe", "act", "dve", "pool", "sp")


class Prog:
    def __init__(self, nc, es):
        self.nc = nc
        self.es = es
        self.q = {e: [] for e in ENGS}
        self.cnt = {}
        self.sem = {}
        for e in ENGS:
            self.sem[e] = es.enter_context(nc.semaphore("c_" + e))
            self.cnt[e] = 0
        self.known = {e: {} for e in ENGS}
        self.lastw = {}
        self.readers = {}
        self.pending = {e: False for e in ENGS}
        self.ninst = 0

    def newsem(self, name):
        self.sem[name] = self.es.enter_context(self.nc.semaphore(name))
        self.cnt[name] = 0
        return name

    def _deps(self, eng, reads, writes):
        deps = {}

        def add(sk, v):
            if v > deps.get(sk, 0):
                deps[sk] = v

        for k in reads:
            w = self.lastw.get(k)
            if w is not None:
                add(*w)
        for k in writes:
            w = self.lastw.get(k)
            if w is not None:
                add(*w)
            for sk, v in self.readers.get(k, {}).items():
                add(sk, v)
        if eng == "pe":
            deps.pop("pe", None)
        return deps

    def _emit_waits(self, eng, deps):
        kn = self.known[eng]
        for sk, v in deps.items():
            if sk not in ENGS:
                v = max(v, self.cnt[sk])
            if sk == eng and v > self.cnt[eng]:
                raise RuntimeError("self-dependency on pending op: " + eng)
            if kn.get(sk, 0) >= v:
                continue
            kn[sk] = v
            sem = self.sem[sk]
            self.q[eng].append(lambda E, sem=sem, v=v: E.wait_ge(sem, v))

    def op(self, eng, fn, reads=(), writes=(), inc=True):
        deps = self._deps(eng, reads, writes)
        self._emit_waits(eng, deps)
        n = self.cnt[eng] + 1
        for k in reads:
            self.readers.setdefault(k, {})[eng] = n
        for k in writes:
            self.lastw[k] = (eng, n)
            self.readers[k] = {}
        self.ninst += 1
        if inc:
            self.cnt[eng] = n
            sem = self.sem[eng]
            self.q[eng].append(lambda E, sem=sem: fn(E).then_inc(sem, 1))
            self.pending[eng] = False
        else:
            self.q[eng].append(lambda E: fn(E))
            self.pending[eng] = True

    def dma(self, queue, semname, out, in_, reads=(), writes=()):
        deps = self._deps(queue, reads, writes)
        self._emit_waits(queue, deps)
        n = self.cnt[semname] + 16
        self.cnt[semname] = n
        for k in reads:
            self.readers.setdefault(k, {})[semname] = n
        for k in writes:
            self.lastw[k] = (semname, n)
            self.readers[k] = {}
        sem = self.sem[semname]
        self.q[queue].append(lambda E, sem=sem: E.dma_start(out=out, in_=in_).then_inc(sem, 16))
        self.ninst += 1

    def fence(self):
        for e in ENGS:
            kn = self.known[e]
            for s, v in self.cnt.items():
                if s == e or v == 0 or kn.get(s, 0) >= v:
                    continue
                kn[s] = v
                sem = self.sem[s]
                self.q[e].append(lambda E, sem=sem, v=v: E.wait_ge(sem, v))

    def wait_all(self, eng, semnames):
        for s in semnames:
            v = self.cnt[s]
            if v > 0 and self.known[eng].get(s, 0) < v:
                self.known[eng][s] = v
                sem = self.sem[s]
                self.q[eng].append(lambda E, sem=sem, v=v: E.wait_ge(sem, v))

    def emit(self, block):
        for e in ENGS:
            assert not self.pending[e], "engine %s ends with un-inc'd op" % e
        q = self.q

        @block.tensor
        def _(E):
            for f in q["pe"]:
                f(E)

        @block.scalar
        def _(E):
            for f in q["act"]:
                f(E)

        @block.vector
        def _(E):
            for f in q["dve"]:
                f(E)

        @block.gpsimd
        def _(E):
            for f in q["pool"]:
                f(E)

        @block.sync
        def _(E):
            for f in q["sp"]:
                f(E)


class Carver:
    def __init__(self, scr, nbytes):
        self.scr = scr
        self.cap = nbytes
        self.off = 0
        self.hi = 0

    def alloc(self, shape, dt):
        esz = 4 if dt == F32 else 2
        n = 1
        for s in shape:
            n *= s
        nb = (n * esz + 31) // 32 * 32
        assert self.off + nb <= self.cap, ("SBUF scratch overflow", self.off, nb, self.cap)
        w0 = self.off // 4
        ap = self.scr[:, w0:w0 + nb // 4]
        if dt != F32:
            ap = ap.bitcast(dt)
        ap = ap[:, 0:n]
        if len(shape) == 2:
            ap = ap.rearrange("p (a b) -> p a b", a=shape[0])
        elif len(shape) == 3:
            ap = ap.rearrange("p (a b c) -> p a b c", a=shape[0], b=shape[1])
        self.off += nb
        self.hi = max(self.hi, self.off)
        return ap


def rev_ap(ap_):
    n = ap_.shape[1]
    pp = ap_.ap[0]
    return bass.AP(ap_.tensor, ap_.offset + (n - 1), [[pp[0], pp[1]], [-1, n]])


class _Stop(Exception):
    pass


def build_program(layers, nseq, first_in_program=True, dbg=None, stop=99):
    nc = bass.Bass("TRN2", target_bir_lowering=False)
    NL = DEPTH
    dram = lambda name, shape, dt=F32, kind="ExternalInput": nc.dram_tensor(name, shape, dt, kind=kind).ap()
    xin = dram("xin", [nseq, 128, 8, TOK])
    cT_d = dram("cT", [128, 8, 3])
    wada_d = dram("w_ada", [NL, 128, 8, 3072])
    bada_d = dram("b_adaT", [128, NL, 24])
    win_d = dram("w_in", [NL, 128, 8, NC1 + NC2])
    wo_d = dram("w_o", [NL, 128, 8, 1024])
    vecs_d = dram("vecs", [128, NL, NV])
    anorm_d = dram("anorm", [NL, 128, 2, 256])
    absb_d = dram("absb", [NL, 128, 2, 128])
    aws_d = dram("awsT", [NL, 128, 4, 128])
    lruw_d = dram("lruW", [NL, 128, 8, 128])
    c32_d = dram("cst32", [128, 256])
    c16_d = dram("cst16", [128, 256 + 2 * L_LAT])
    xout = dram("xout", [nseq, 128, 8, TOK], kind="ExternalOutput")
    dbg_out = {}
    if dbg:
        for name, shape in dbg.items():
            dbg_out[name] = dram("dbg_" + name, shape, kind="ExternalOutput")

    with ExitStack() as es:
        CAP = 212800
        scr = es.enter_context(nc.sbuf_tensor("scr", [128, CAP // 4], F32))
        psall = es.enter_context(nc.psum_tensor("psall", [128, 4096], F32))[:, :]
        PSB = [psall[:, i * 512:(i + 1) * 512] for i in range(8)]
        P = Prog(nc, es)
        for s in ("ld_x", "ld_c", "ld_w1", "ld_w2", "ld_wo", "ld_ada0", "ld_ada1", "ld_lw", "st_o", "st_d"):
            P.newsem(s)
        C = Carver(scr, CAP)

        xT = C.alloc([8, TOK], F32)
        tabs = C.alloc([2, L_LAT], BF16)
        KT = C.alloc([2, TOK], BF16)
        VA = C.alloc([18, 2, 128], BF16)
        ylru = C.alloc([2, TOK], BF16)
        c32 = C.alloc([256], F32)[:, :]
        c16 = C.alloc([256], BF16)
        mods = C.alloc([NL, 24, 3], F32)
        sc1 = C.alloc([NL, 8, 3], F32)
        gA = C.alloc([NL, 8, 3], F32)
        vecs = C.alloc([NL, NV], F32)
        cvec = C.alloc([NL, 4], F32)
        badaT = C.alloc([NL, 24], F32)
        anorm = C.alloc([2, 256], F32)
        absb = C.alloc([2, 128], F32)
        awsT = C.alloc([4, 128], BF16)
        lruW = C.alloc([8, 128], BF16)
        sil = C.alloc([8, 3], F32)
        epsr = C.alloc([4], F32)
        onesb = C.alloc([128], BF16)
        PBASE = C.off
        ident = c32[:, 0:128]
        onesD = c32[:, 128:256]
        Rmat = c16[:, 0:128]
        Bones = c16[:, 128:256]
        cosT = tabs[:, 0, :]
        sinT = tabs[:, 1, :]

        psi = [0, 0, 0]

        def nextps():
            i = psi[0] % 4
            psi[0] += 1
            return PSB[i], "ps%d" % i

        def nextps2():
            i = ((psi[0] + 1) // 2 * 2) % 4
            psi[0] = i + 2
            return psall[:, i * 512:(i + 2) * 512], ["ps%d" % i, "ps%d" % (i + 1)]

        def nextps_s():
            i = 3 + psi[1] % 3
            psi[1] += 1
            return PSB[i], "ps%d" % i

        def nextps_o():
            i = 6 + psi[2] % 2
            psi[2] += 1
            return PSB[i], "ps%d" % i

        def act(out, in_, func, r, w, **kw):
            P.op("act", lambda E: E.activation(out=out, in_=in_, func=func, **kw), r, w)

        def tt(eng, out, a, b, op, r, w):
            P.op(eng, lambda E: E.tensor_tensor(out=out, in0=a, in1=b, op=op), r, w)

        def ts(eng, out, a, s1, s2, op0, op1, r, w):
            if s2 is None:
                P.op(eng, lambda E: E.tensor_single_scalar(out=out, in_=a, scalar=s1, op=op0), r, w)
            else:
                P.op(eng, lambda E: E.tensor_scalar(out=out, in0=a, scalar1=s1, scalar2=s2, op0=op0, op1=op1), r, w)

        def scan(out, d0, d1, init, r, w):
            P.op("dve", lambda E: E.tensor_tensor_scan(out=out, data0=d0, data1=d1, initial=init, op0=ALU.mult, op1=ALU.add), r, w)

        def stt(eng, out, a, s, b, op0, op1, r, w):
            P.op(eng, lambda E: E.scalar_tensor_tensor(out=out, in0=a, scalar=s, in1=b, op0=op0, op1=op1), r, w)

        def mm(out, lhsT, rhs, start, stop, r, w, inc=True):
            P.op("pe", lambda E: E.matmul(out, lhsT, rhs, start=start, stop=stop), r, w, inc=inc)

        def recip(out, in_, r, w):
            P.op("dve", lambda E: E.reciprocal(out=out, in_=in_), r, w)

        def memset(eng, ap, val, w):
            P.op(eng, lambda E: E.memset(ap, val), (), w)

        def copy(eng, out, in_, r, w):
            P.op(eng, lambda E: E.tensor_copy(out=out, in_=in_), r, w)

        P.dma("sp", "ld_c", c32, c32_d[:, :], writes=["c32"])
        P.dma("pool", "ld_lw", c16, c16_d[:, 0:256], writes=["c16"])
        for a_ in range(2):
            P.dma("pool", "ld_lw", tabs[:, a_, :], c16_d[:, 256 + a_ * L_LAT:256 + (a_ + 1) * L_LAT], writes=["tabs"])
        P.dma("sp", "ld_c", vecs, vecs_d[:, :, :], writes=["vecs"])
        P.dma("sp", "ld_c", badaT, bada_d[:, :, :], writes=["badaT"])
        P.dma("sp", "ld_c", sil, cT_d[:, :, :], writes=["sil"])
        memset("dve", epsr[:, 0:1], RMS_EPS, ["epsr"])
        memset("dve", epsr[:, 1:2], LN_EPS / (ALPHA * ALPHA), ["epsr"])
        memset("dve", epsr[:, 2:3], 1.0, ["epsr"])
        memset("dve", epsr[:, 3:4], 0.0, ["epsr"])
        memset("dve", onesb, 1.0 / 1024.0, ["onesb"])
        memset("pool", VA[:, :, :, 64:128], 1.0, ["VA"])

        act(sil, sil, AF.Silu, ["sil"], ["sil"])
        for l in layers:
            act(cvec[:, l, :], vecs[:, l, V_LAM:V_LAM + 4], AF.Exp, ["vecs"], ["cvec"], scale=-1.0)
            act(cvec[:, l, :], cvec[:, l, :], AF.Ln, ["cvec", "epsr"], ["cvec"], bias=epsr[:, 2:3])
            ts("dve", cvec[:, l, :], cvec[:, l, :], -8.0, None, ALU.mult, ALU.bypass, ["cvec"], ["cvec"])
        C.off = PBASE
        stg = [C.alloc([8, 512], F32) for _ in range(2)]
        modrow = C.alloc([3072], F32)[:, :]
        bi = 0
        for l in layers:
            for nb in range(6):
                sb_ = stg[bi % 2]
                key = "stg%d" % (bi % 2)
                P.dma("sp", "ld_ada%d" % (bi % 2), sb_, wada_d[l, :, :, nb * 512:(nb + 1) * 512], writes=[key])
                ps, pk = nextps()
                for kc in range(8):
                    mm(ps[0:3, :], sil[:, kc, :], sb_[:, kc, :], kc == 0, kc == 7, [key, "sil"], [pk], inc=(kc == 7))
                copy("dve", modrow[0:3, nb * 512:(nb + 1) * 512], ps[0:3, :], [pk], ["modrow"])
                bi += 1
            ps, pk = nextps()
            for j in range(24):
                mm(ps[:, j * 3:(j + 1) * 3], modrow[0:3, j * 128:(j + 1) * 128], ident[0:3, 0:3], True, True,
                   ["modrow", "c32"], [pk], inc=(j == 23))
            tt("dve", mods[:, l, :, :], ps[:, 0:72].rearrange("p (a b) -> p a b", a=24),
               badaT[:, l, :].unsqueeze(2).to_broadcast([128, 24, 3]), ALU.add, [pk, "badaT"], ["mods"])
            ts("dve", sc1[:, l, :, :], mods[:, l, 8:16, :], 1.0, None, ALU.add, ALU.bypass, ["mods"], ["sc1"])
            ts("dve", gA[:, l, :, :], mods[:, l, 16:24, :], 1.0 / ALPHA, None, ALU.mult, ALU.bypass, ["mods"], ["gA"])
        P.fence()

        def rms_rope(zps, zk, n, gvec, dsts, dstk, t0, latent, S):
            act(S["sq"][:, 0:n], zps[:, 0:n], AF.Square, [zk], ["r_sq"])
            ps2, pk2 = nextps()
            mm(ps2[:, 0:n], Bones, S["sq"][:, 0:n], True, True, ["r_sq", "c16"], [pk2])
            act(S["sd"][:, 0:n], ps2[:, 0:n], AF.Ln, [pk2, "epsr"], ["r_sd"], bias=epsr[:, 0:1])
            act(S["sd"][:, 0:n], S["sd"][:, 0:n], AF.Exp, ["r_sd"], ["r_sd"], scale=-0.5)
            if not latent:
                for (lo, hi_, dst) in dsts:
                    stt("dve", dst, zps[lo:hi_, 0:n], gvec[lo:hi_, :], S["sd"][lo:hi_, 0:n], ALU.mult, ALU.mult,
                        [zk, "r_sd", "vecs"], [dstk])
                return
            stt("dve", S["qn"][:, 0:n], zps[:, 0:n], gvec, S["sd"][:, 0:n], ALU.mult, ALU.mult,
                [zk, "r_sd", "vecs"], ["r_qn"])
            ps3, pk3 = nextps()
            mm(ps3[:, 0:n], Rmat, S["qn"][:, 0:n], True, True, ["r_qn", "c16"], [pk3])
            tt("pool", S["t1"][:, 0:n], S["qn"][:, 0:n], cosT[:, t0:t0 + n], ALU.mult, ["r_qn", "tabs"], ["r_t1"])
            tt("dve", S["t2"][:, 0:n], ps3[:, 0:n], sinT[:, t0:t0 + n], ALU.mult, [pk3, "tabs"], ["r_t2"])
            for (lo, hi_, dst) in dsts:
                tt("pool", dst, S["t1"][lo:hi_, 0:n], S["t2"][lo:hi_, 0:n], ALU.add, ["r_t1", "r_t2"], [dstk])

        def xkeys(t0, n):
            return ["xT%d" % b for b in range(t0 // 256, (t0 + n + 255) // 256)]

        ALLX = ["xT%d" % b for b in range(9)]

        def modulate(dst, dk, l, r, t0, n, first=False):
            for c in range(8):
                eng = "dve" if (first or c % 2 == 1) else "pool"
                ts(eng, dst[:, c, 0:n], xT[:, c, t0:t0 + n], sc1[:, l, c, r:r + 1], mods[:, l, c, r:r + 1],
                   ALU.mult, ALU.add, xkeys(t0, n) + ["sc1", "mods"], [dk, "%s_c%d" % (dk, c)])

        def inproj_fm(ps, pk, W, wk, col0, xm, xk, n):
            for kc in range(8):
                mm(ps[:, 0:n], W[:, kc, col0:col0 + 128], xm[:, kc, 0:n], kc == 0, kc == 7, [wk, xk], [pk], inc=(kc == 7))

        def stage(n):
            if n > stop:
                raise _Stop()

        W1P = (("k", 0, 256), ("v", 256, 128), ("rx", 384, 256), ("rg", 640, 256))
        W2P = (("u", C_U, 256), ("va", C_VA, 256), ("ga", C_GA, 256), ("bg", C_BG, 512), ("q", C_Q, 512))
        for nm, _, _ in W1P:
            P.newsem("ld_w1" + nm)
        for nm, _, _ in W2P:
            P.newsem("ld_w2" + nm)

        for s in range(nseq):
            try:
                for c in range(8):
                    P.dma("sp", "ld_x", xT[:, c, :], xin[s, :, c, :], writes=ALLX)
                for l in layers:
                    last = (l == DEPTH - 1)
                    P.dma("sp", "ld_c", anorm, anorm_d[l, :, :, :], writes=["anorm"])
                    P.dma("sp", "ld_c", absb, absb_d[l, :, :, :], writes=["absb"])
                    P.dma("pool", "ld_lw", awsT, aws_d[l, :, :, :], writes=["awsT"])
                    P.dma("pool", "ld_lw", lruW, lruw_d[l, :, :, :], writes=["lruW"])
                    stage(1)
                    P.fence()
                    C.off = PBASE
                    win1 = C.alloc([8, NC1], BF16)
                    xm1 = C.alloc([8, 512], BF16)
                    S1 = {"sq": C.alloc([512], BF16)[:, :], "sd": C.alloc([512], F32)[:, :],
                          "qn": C.alloc([512], BF16)[:, :], "t1": C.alloc([512], F32)[:, :]}
                    assert C.off - PBASE == NC2 * 8 * 2, (C.off - PBASE)
                    S1["t2"] = C.alloc([512], F32)[:, :]
                    sg = C.alloc([2, TOK], BF16)
                    xrb = C.alloc([2, TOK], BF16)
                    GBASE = C.off
                    rx = C.alloc([2, 2312], F32)
                    acc = C.alloc([TOK], F32)[:, :]
                    for nm, c0, ncl in W1P:
                        P.dma("pool", "ld_w1" + nm, win1[:, :, c0:c0 + ncl], win_d[l, :, :, c0:c0 + ncl], writes=["win1" + nm])
                    memset("pool", rx[:, :, 0:2], 0.0, ["rx"])
                    memset("pool", rx[:, :, 2050:2052], 0.0, ["rx"])
                    memset("pool", rx[:, :, 2308:2312], 0.0, ["rx"])
                    for (t0, n) in ((0, 512), (512, 512), (1024, 512), (1536, 512), (2048, 256)):
                        latent = t0 < L_LAT
                        r = s if latent else 2
                        modulate(xm1, "xm1", l, r, t0, n, first=(t0 == 0))
                        bk = lambda i: (PSB[i], "ps%d" % i)
                        gk = vecs[:, l, V_KG:V_KG + 1]
                        off = 2 + t0 if latent else 2052 + (t0 - L_LAT)
                        nt = n // 128
                        zK = [bk(0), bk(1)]
                        for g in range(2):
                            inproj_fm(zK[g][0], zK[g][1], win1, "win1k", C_K0 + g * 128, xm1, "xm1", n)
                        psV, pkV = bk(2)
                        for tl in range(nt):
                            for kc in range(8):
                                mm(psV[:, tl * 128:(tl + 1) * 128], xm1[:, kc, tl * 128:(tl + 1) * 128], win1[:, kc, C_V:C_V + 128],
                                   kc == 0, kc == 7, ["win1v", "xm1"], [pkV], inc=(kc == 7))
                        zX = [bk(3), bk(4)]
                        for ct in range(2):
                            inproj_fm(zX[ct][0], zX[ct][1], win1, "win1rx", C_RX + ct * 128, xm1, "xm1", n)
                        sqK = [S1["sq"], S1["qn"]]
                        for g in range(2):
                            act(sqK[g][:, 0:n], zK[g][0][:, 0:n], AF.Square, [zK[g][1]], ["r_sq%d" % g])
                        for ct in range(2):
                            act(rx[:, ct, off:off + n], zX[ct][0][:, 0:n], AF.Copy, [zX[ct][1]], ["rx"])
                        copy("dve", VA[:, t0 // 128:t0 // 128 + nt, :, 0:64],
                             psV[:, 0:nt * 128].rearrange("p (t g d) -> p t g d", t=nt, g=2), [pkV], ["VA"])
                        mK = [bk(7), bk(3)]
                        mm(mK[0][0][:, 0:n], Bones, sqK[0][:, 0:n], True, True, ["r_sq0", "c16"], [mK[0][1]])
                        zG = [bk(5), bk(6)]
                        for ct in range(2):
                            inproj_fm(zG[ct][0], zG[ct][1], win1, "win1rg", C_RG + ct * 128, xm1, "xm1", n)
                        mm(mK[1][0][:, 0:n], Bones, sqK[1][:, 0:n], True, True, ["r_sq1", "c16"], [mK[1][1]])
                        rsK = [S1["sd"], S1["t1"]]
                        for g in range(2):
                            act(rsK[g][:, 0:n], mK[g][0][:, 0:n], AF.Ln, [mK[g][1], "epsr"], ["r_sd%d" % g], bias=epsr[:, 0:1])
                            act(rsK[g][:, 0:n], rsK[g][:, 0:n], AF.Exp, ["r_sd%d" % g], ["r_sd%d" % g], scale=-0.5)
                        for g in range(2):
                            dstK = KT[:, g, t0:t0 + n]
                            if latent:
                                stt("dve", sqK[g][:, 0:n], zK[g][0][:, 0:n], gk, rsK[g][:, 0:n], ALU.mult, ALU.mult,
                                    [zK[g][1], "r_sd%d" % g, "vecs"], ["r_sq%d" % g])
                            else:
                                stt("dve", dstK, zK[g][0][:, 0:n], gk, rsK[g][:, 0:n], ALU.mult, ALU.mult,
                                    [zK[g][1], "r_sd%d" % g, "vecs"], ["KT"])
                        for ct in range(2):
                            act(sg[:, ct, t0:t0 + n], zG[ct][0][:, 0:n], AF.Silu, [zG[ct][1]], ["sg"])
                        if latent:
                            rK = [bk(4), bk(7)]
                            for g in range(2):
                                mm(rK[g][0][:, 0:n], Rmat, sqK[g][:, 0:n], True, True, ["r_sq%d" % g, "c16"], [rK[g][1]])
                            for g in range(2):
                                tt("pool", rsK[g][:, 0:n], sqK[g][:, 0:n], cosT[:, t0:t0 + n], ALU.mult,
                                   ["r_sq%d" % g, "tabs"], ["r_sd%d" % g])
                                tt("dve", S1["t2"][:, 0:n], rK[g][0][:, 0:n], sinT[:, t0:t0 + n], ALU.mult, [rK[g][1], "tabs"], ["r_t2"])
                                tt("pool", KT[:, g, t0:t0 + n], rsK[g][:, 0:n], S1["t2"][:, 0:n], ALU.add,
                                   ["r_sd%d" % g, "r_t2"], ["KT"])
                    stage(2)
                    for ct in range(2):
                        for (d0, o0, n) in ((2, 0, L_LAT), (2052, L_LAT, L_CTX)):
                            cw = lambda j: vecs[:, l, V_CW + ct * 4 + j:V_CW + ct * 4 + j + 1]
                            eng = "dve"
                            ts(eng, acc[:, 0:n], rx[:, ct, d0 - 2:d0 - 2 + n], cw(0), vecs[:, l, V_CB + ct:V_CB + ct + 1],
                               ALU.mult, ALU.add, ["rx", "vecs"], ["acc"])
                            stt(eng, acc[:, 0:n], rx[:, ct, d0 - 1:d0 - 1 + n], cw(1), acc[:, 0:n], ALU.mult, ALU.add,
                                ["rx", "vecs", "acc"], ["acc"])
                            stt(eng, acc[:, 0:n], rx[:, ct, d0:d0 + n], cw(2), acc[:, 0:n], ALU.mult, ALU.add,
                                ["rx", "vecs", "acc"], ["acc"])
                            stt(eng, xrb[:, ct, o0:o0 + n], rx[:, ct, d0 + 1:d0 + 1 + n], cw(3), acc[:, 0:n], ALU.mult, ALU.add,
                                ["rx", "vecs", "acc"], ["xrb"])
                    stage(3)
                    P.fence()
                    C.off = PBASE
                    win2 = C.alloc([8, NC2], BF16)
                    for nm, c0, ncl in W2P:
                        P.dma("pool", "ld_w2" + nm, win2[:, :, c0:c0 + ncl], win_d[l, :, :, NC1 + c0:NC1 + c0 + ncl],
                              writes=["win2" + nm])
                    C.off = GBASE
                    Ab = C.alloc([TOK], F32)[:, :]
                    Tb = C.alloc([TOK], F32)[:, :]
                    Bb = C.alloc([TOK], F32)[:, :]
                    Hf = C.alloc([TOK], F32)[:, :]
                    for ct in range(2):
                        for dd in range(2):
                            vi = dd * 2 + ct
                            for (t0, n) in ((0, 512), (512, 512), (1024, 512), (1536, 512), (2048, 256)):
                                ps, pk = nextps()
                                mm(ps[:, 0:n], lruW[:, (dd * 2 + 0) * 2 + ct, :], xrb[:, ct, t0:t0 + n], True, True,
                                   ["lruW", "xrb"], [pk])
                                act(Ab[:, t0:t0 + n], ps[:, 0:n], AF.Sigmoid, [pk, "vecs"], ["Ab"],
                                    bias=vecs[:, l, V_BR + vi:V_BR + vi + 1])
                                ps, pk = nextps()
                                mm(ps[:, 0:n], lruW[:, (dd * 2 + 1) * 2 + ct, :], xrb[:, ct, t0:t0 + n], True, True,
                                   ["lruW", "xrb"], [pk])
                                act(Bb[:, t0:t0 + n], ps[:, 0:n], AF.Sigmoid, [pk, "vecs"], ["Bb"],
                                    bias=vecs[:, l, V_BI + vi:V_BI + vi + 1])
                            act(Ab, Ab, AF.Exp, ["Ab", "cvec"], ["Ab"], scale=cvec[:, l, vi:vi + 1])
                            act(Tb, Ab, AF.Square, ["Ab"], ["Tb"])
                            act(Tb, Tb, AF.Ln, ["Tb", "epsr"], ["Tb"], scale=-1.0, bias=epsr[:, 2:3])
                            act(Tb, Tb, AF.Exp, ["Tb"], ["Tb"], scale=0.5)
                            tt("dve", Bb, Bb, xrb[:, ct, :], ALU.mult, ["Bb", "xrb"], ["Bb"])
                            tt("dve", Bb, Bb, Tb, ALU.mult, ["Bb", "Tb"], ["Bb"])
                            if dd == 0:
                                scan(Hf[:, L_LAT:TOK], Ab[:, L_LAT:TOK], Bb[:, L_LAT:TOK], 0.0, ["Ab", "Bb"], ["Hf"])
                                scan(Hf[:, 0:L_LAT], Ab[:, 0:L_LAT], Bb[:, 0:L_LAT], Hf[:, TOK - 1:TOK], ["Ab", "Bb", "Hf"], ["Hf"])
                            else:
                                scan(rev_ap(Tb), rev_ap(Ab), rev_ap(Bb), 0.0, ["Ab", "Bb"], ["Tb"])
                        tt("pool", Hf, Hf, Tb, ALU.add, ["Hf", "Tb"], ["Hf"])
                        tt("pool", ylru[:, ct, :], Hf, sg[:, ct, :], ALU.mult, ["Hf", "sg"], ["ylru"])
                    stage(4)
                    P.fence()
                    C.off = PBASE
                    win2 = C.alloc([8, NC2], BF16)
                    wo = C.alloc([8, 1024], BF16)
                    for kc in range(8):
                        P.dma("pool", "ld_wo", wo[:, kc, :], wo_d[l, :, kc, :], writes=["wo"])
                    xm = C.alloc([8, 256], BF16)
                    yT = C.alloc([6, 256], BF16)
                    ug = C.alloc([2, 256], F32)
                    sga = C.alloc([2, 256], F32)
                    vg = C.alloc([2, 256], F32)
                    vln = C.alloc([2, 256], BF16)
                    tmpg = C.alloc([512], F32)[:, :]
                    st = C.alloc([2, 8], F32)
                    qbd = C.alloc([4, 512], BF16)
                    sbg = C.alloc([4, 256], BF16)
                    bufA = C.alloc([1024], BF16)[:, :]
                    bufBC = C.alloc([2048], F32)[:, :]
                    bufB = bufBC[:, 0:1024]
                    bufC = bufBC[:, 1024:2048]
                    pT = [C.alloc([512], BF16)[:, :] for _ in range(3)]
                    rden = [C.alloc([256], F32)[:, :] for _ in range(2)]
                    otmp = [C.alloc([256], F32)[:, :]] * 2
                    sqb = C.alloc([8, 256], BF16)
                    mean_sb = C.alloc([256], F32)[:, :]
                    m2 = C.alloc([256], F32)[:, :]
                    rsl = C.alloc([256], F32)[:, :]
                    memset("pool", qbd[64:128, :, 0:256], 0.0, ["qbd"])
                    memset("pool", qbd[0:64, :, 256:512], 0.0, ["qbd"])
                    nblk = 8 if last else 9
                    cn = [0, 0]
                    def mod(qb):
                        t0 = qb * 256
                        latent = qb < 8
                        r = s if latent else 2
                        xk = "xT%d" % qb
                        modulate(xm, "xm", l, r, t0, 256, first=(qb == 0))

                    def front_a(qb):
                        t0 = qb * 256
                        latent = qb < 8
                        r = s if latent else 2
                        xk = "xT%d" % qb
                        bank = lambda i: (psall[:, i * 512:(i + 1) * 512], "ps%d" % i)
                        bank2 = lambda i: (psall[:, i * 512:(i + 2) * 512], ["ps%d" % i, "ps%d" % (i + 1)])
                        gq = vecs[:, l, V_QG:V_QG + 1]
                        v4 = lambda ap_, lo, hi_: ap_[lo:hi_, :].rearrange("p (c t) -> p c t", c=4)
                        psQ, pkQ = bank2(0)
                        for ct in range(4):
                            inproj_fm(psQ[:, ct * 256:(ct + 1) * 256], pkQ[ct // 2], win2, "win2q", C_Q + ct * 128, xm, "xm", 256)
                        act(bufA, psQ, AF.Square, pkQ, ["bufA"])
                        psU, pkU = bank(2)
                        for ct in range(2):
                            inproj_fm(psU[:, ct * 256:(ct + 1) * 256], pkU, win2, "win2u", C_U + ct * 128, xm, "xm", 256)
                        psV, pkV = bank(3)
                        for tl in range(2):
                            for kc in range(8):
                                mm(psV[:, tl * 256:(tl + 1) * 256], xm[:, kc, tl * 128:(tl + 1) * 128], win2[:, kc, C_VA:C_VA + 256],
                                   kc == 0, kc == 7, ["win2va", "xm"], [pkV], inc=(kc == 7))
                        psM, pkM = bank2(4)
                        for hf in range(2):
                            mm(psM[:, hf * 512:(hf + 1) * 512], Bones, bufA[:, hf * 512:(hf + 1) * 512], True, True,
                               ["bufA", "c16"], [pkM[hf]])
                        act(bufB, psM, AF.Ln, pkM + ["epsr"], ["bufB"], bias=epsr[:, 0:1])
                        act(bufB, bufB, AF.Exp, ["bufB"], ["bufB"], scale=-0.5)
                        if latent:
                            stt("dve", bufA, psQ, gq, bufB, ALU.mult, ALU.mult, pkQ + ["bufB", "vecs"], ["bufA"])
                        else:
                            stt("dve", qbd[0:64, :, 0:256], v4(psQ, 0, 64), gq[0:64, :], v4(bufB, 0, 64), ALU.mult, ALU.mult,
                                pkQ + ["bufB", "vecs"], ["qbd"])
                            stt("dve", qbd[64:128, :, 256:512], v4(psQ, 64, 128), gq[64:128, :], v4(bufB, 64, 128), ALU.mult, ALU.mult,
                                pkQ + ["bufB", "vecs"], ["qbd"])
                        act(ug.rearrange("p c t -> p (c t)"), psU, AF.Gelu_apprx_tanh, [pkU], ["ug"])
                        for tl in range(2):
                            act(vg[:, tl, :], psV[:, tl * 256:(tl + 1) * 256], AF.Gelu_apprx_tanh, [pkV], ["vg", "st"],
                                accum_out=st[:, tl, 0:1])
                        ps2b, pk2b = bank2(6)
                        for ct in range(4):
                            inproj_fm(ps2b[:, ct * 256:(ct + 1) * 256], pk2b[ct // 2], win2, "win2bg", C_BG + ct * 128, xm, "xm", 256)
                        psG, pkG = bank(2)
                        for ct in range(2):
                            inproj_fm(psG[:, ct * 256:(ct + 1) * 256], pkG, win2, "win2ga", C_GA + ct * 128, xm, "xm", 256)
                        if latent:
                            psR, pkR = bank2(4)
                            for hf in range(2):
                                mm(psR[:, hf * 512:(hf + 1) * 512], Rmat, bufA[:, hf * 512:(hf + 1) * 512], True, True,
                                   ["bufA", "c16"], [pkR[hf]])
                        act(sga.rearrange("p c t -> p (c t)"), psG, AF.Silu, [pkG], ["sga"])
                        act(sbg.rearrange("p c t -> p (c t)"), ps2b, AF.Silu, pk2b, ["sbg"])
                        tt("pool", ug, ug, sga, ALU.mult, ["ug", "sga"], ["ug"])
                        if latent:
                            cosb = cosT[:, t0:t0 + 256].unsqueeze(1).to_broadcast([128, 4, 256])
                            sinb = sinT[:, t0:t0 + 256].unsqueeze(1).to_broadcast([128, 4, 256])
                            tt("pool", v4(bufB, 0, 128), v4(bufA, 0, 128), cosb, ALU.mult, ["bufA", "tabs"], ["bufB"])
                            tt("dve", v4(bufC, 0, 128), v4(psR, 0, 128), sinb, ALU.mult, pkR + ["tabs"], ["bufC"])
                            tt("pool", qbd[0:64, :, 0:256], v4(bufB, 0, 64), v4(bufC, 0, 64), ALU.add, ["bufB", "bufC"], ["qbd"])
                            tt("pool", qbd[64:128, :, 256:512], v4(bufB, 64, 128), v4(bufC, 64, 128), ALU.add,
                               ["bufB", "bufC"], ["qbd"])

                    def front_b(qb):
                        t0 = qb * 256
                        latent = qb < 8
                        r = s if latent else 2
                        xk = "xT%d" % qb
                        for tl in range(2):
                            act(tmpg[:, 0:256], vg[:, tl, :], AF.Square, ["vg"], ["tmpg", "st"], accum_out=st[:, tl, 1:2])
                            ts("dve", st[:, tl, 2:3], st[:, tl, 0:1], 1.0 / 256.0, None, ALU.mult, ALU.bypass, ["st"], ["st"])
                            tt("dve", st[:, tl, 3:4], st[:, tl, 2:3], st[:, tl, 2:3], ALU.mult, ["st"], ["st"])
                            stt("dve", st[:, tl, 4:5], st[:, tl, 1:2], 1.0 / 256.0, st[:, tl, 3:4], ALU.mult, ALU.subtract,
                                ["st"], ["st"])
                        for tl in range(2):
                            act(st[:, tl, 5:6], st[:, tl, 4:5], AF.Ln, ["st", "epsr"], ["st"], bias=epsr[:, 0:1], scale=1.0)
                            act(st[:, tl, 6:7], st[:, tl, 5:6], AF.Exp, ["st"], ["st"], scale=-0.5)
                        for tl in range(2):
                            ts("dve", vg[:, tl, :], vg[:, tl, :], st[:, tl, 2:3], st[:, tl, 6:7], ALU.subtract, ALU.mult,
                               ["vg", "st"], ["vg"])
                            tt("pool", vg[:, tl, :], vg[:, tl, :], anorm[:, 0, :], ALU.mult, ["vg", "anorm"], ["vg"])
                            tt("pool", vln[:, tl, :], vg[:, tl, :], anorm[:, 1, :], ALU.add, ["vg", "anorm"], ["vln"])

                    def front_c(qb):
                        t0 = qb * 256
                        latent = qb < 8
                        r = s if latent else 2
                        xk = "xT%d" % qb
                        ps, pk = PSB[3], "ps3"
                        for tl in range(2):
                            for ct in range(2):
                                for hh in range(2):
                                    g = 2 * ct + hh
                                    co = (tl * 2 + ct) * 128
                                    mm(ps[hh * 64:(hh + 1) * 64, co:co + 128], vln[:, tl, g * 64:(g + 1) * 64], awsT[:, g, :],
                                       True, True, ["vln", "awsT"], [pk], inc=(tl == 1 and ct == 1 and hh == 1))
                        tt("dve", tmpg.rearrange("p (t c q) -> p t c q", t=2, c=2), ps.rearrange("p (t c q) -> p t c q", t=2, c=2),
                           absb.unsqueeze(1).to_broadcast([128, 2, 2, 128]), ALU.add, [pk, "absb"], ["tmpg"])
                        tt("pool", yT[:, 0:2, :].rearrange("p c (t q) -> p t c q", t=2),
                           tmpg.rearrange("p (t c q) -> p t c q", t=2, c=2),
                           ug.rearrange("p c (t q) -> p t c q", t=2), ALU.mult, ["tmpg", "ug"], ["yT"])

                    def attn(qb):
                        t0 = qb * 256
                        latent = qb < 8
                        r = s if latent else 2
                        xk = "xT%d" % qb
                        ktiles = list(range(18)) if latent else [16, 17]
                        for ct in range(4):
                            g = ct // 2
                            pO, kO = nextps_o()

                            def qk(j):
                                ps_, pk_ = nextps_s()
                                mm(ps_[:, 0:512], KT[:, g, j * 128:(j + 1) * 128], qbd[:, ct, :], True, True,
                                   ["KT", "qbd"], [pk_])
                                return ps_, pk_

                            pend = [qk(j) for j in ktiles[0:2]]
                            for ji, j in enumerate(ktiles):
                                ps_, pk_ = pend.pop(0)
                                if ji + 2 < len(ktiles):
                                    pend.append(qk(ktiles[ji + 2]))
                                pb = pT[cn[0] % 3]
                                pbk = "pT%d" % (cn[0] % 3)
                                cn[0] += 1
                                act(pb, ps_, AF.Exp, [pk_], [pbk], scale=0.125)
                                mm(pO[:, 0:512], VA[:, j, g, :], pb, ji == 0, ji == len(ktiles) - 1, ["VA", pbk], [kO])
                            rd = rden[cn[1] % 2]
                            ot = otmp[cn[1] % 2]
                            rk, ok = "rden%d" % (cn[1] % 2), "otmp"
                            cn[1] += 1
                            recip(rd[0:64, :], pO[64:128, 0:256], [kO], [rk])
                            recip(rd[64:128, :], pO[64:128, 256:512], [kO], [rk])
                            tt("dve", ot[0:64, :], pO[0:64, 0:256], rd[0:64, :], ALU.mult, [kO, rk], [ok])
                            tt("dve", ot[64:128, :], pO[0:64, 256:512], rd[64:128, :], ALU.mult, [kO, rk], [ok])
                            tt("pool", yT[:, 2 + ct, :], ot, sbg[:, ct, :], ALU.mult, [ok, "sbg"], ["yT"])

                    def back(qb):
                        t0 = qb * 256
                        latent = qb < 8
                        r = s if latent else 2
                        xk = "xT%d" % qb
                        pm, km = nextps_o()
                        pq, kq = nextps_o()
                        xblk = xT[:, :, t0:t0 + 256]

                        def ln_stats(dt_):
                            mm(pm[:, 0:256], onesb, xm[:, dt_, :], dt_ == 0, dt_ == 7, ["xm_c%d" % dt_, "onesb"], [km], inc=True)
                            mm(pq[:, 0:256], onesb, sqb[:, dt_, :], dt_ == 0, dt_ == 7, ["sqb%d" % dt_, "onesb"], [kq], inc=True)

                        for dt_ in range(8):
                            ps, pk = nextps()
                            for kc in range(8):
                                rhs = yT[:, kc, :] if kc < 6 else ylru[:, kc - 6, t0:t0 + 256]
                                mm(ps[:, 0:256], wo[:, kc, dt_ * 128:(dt_ + 1) * 128], rhs, kc == 0, kc == 7,
                                   ["wo", "yT", "ylru"], [pk], inc=(kc == 7))
                            stt("dve", xT[:, dt_, t0:t0 + 256], ps[:, 0:256], gA[:, l, dt_, r:r + 1], xT[:, dt_, t0:t0 + 256],
                                ALU.mult, ALU.add, [pk, "gA", xk], [xk])
                            act(xm[:, dt_, :], xT[:, dt_, t0:t0 + 256], AF.Copy, [xk], ["xm", "xm_c%d" % dt_])
                            act(sqb[:, dt_, :], xT[:, dt_, t0:t0 + 256], AF.Square, [xk], ["sqb%d" % dt_])
                            if dt_ >= 2:
                                ln_stats(dt_ - 2)
                        ln_stats(6)
                        ln_stats(7)
                        copy("dve", mean_sb, pm[:, 0:256], [km], ["mean_sb"])
                        tt("pool", m2, mean_sb, mean_sb, ALU.mult, ["mean_sb"], ["m2"])
                        tt("dve", m2, pq[:, 0:256], m2, ALU.subtract, [kq, "m2"], ["m2"])

                    def back_b(qb):
                        t0 = qb * 256
                        xk = "xT%d" % qb
                        xblk = xT[:, :, t0:t0 + 256]
                        act(rsl, m2, AF.Ln, ["m2", "epsr"], ["rsl"], bias=epsr[:, 1:2], scale=1.0)
                        act(rsl, rsl, AF.Exp, ["rsl"], ["rsl"], scale=-0.5)
                        tln8 = bufBC.rearrange("p (c t) -> p c t", c=8)
                        tt("pool", tln8, xblk, mean_sb.unsqueeze(1).to_broadcast([128, 8, 256]), ALU.subtract,
                           [xk, "mean_sb"], ["bufB", "bufC"])
                        tt("dve", tln8, tln8, rsl.unsqueeze(1).to_broadcast([128, 8, 256]), ALU.mult,
                           ["bufB", "bufC", "rsl"], ["bufB", "bufC"])
                        for dt_ in range(8):
                            sc_, bi_ = vecs[:, l, V_LNG + dt_:V_LNG + dt_ + 1], vecs[:, l, V_LNB + dt_:V_LNB + dt_ + 1]
                            ts("dve" if dt_ % 2 == 0 else "pool", xT[:, dt_, t0:t0 + 256], tln8[:, dt_, :], sc_, bi_,
                               ALU.mult, ALU.add, ["bufB", "bufC", "vecs"], [xk])

                    mod(0)
                    front_a(0)
                    front_b(0)
                    for qb in range(nblk):
                        if qb + 1 < nblk:
                            mod(qb + 1)
                        if qb >= 1:
                            back_b(qb - 1)
                        attn(qb)
                        front_c(qb)
                        if qb + 1 < nblk:
                            front_a(qb + 1)
                        back(qb)
                        if qb + 1 < nblk:
                            front_b(qb + 1)
                    back_b(nblk - 1)
            except _Stop:
                pass
            for c in range(8):
                P.dma("sp", "st_o", xout[s, :, c, :], xT[:, c, :], reads=ALLX)
        P.wait_all("sp", ["st_o", "st_d"])
        block = es.enter_context(nc.Block())
        P.emit(block)
        build_program.last_stats = (P.ninst, dict(P.cnt), C.hi)
    return nc


def _rope_tables():
    half, nf = 32, 16
    inv = (10000.0 ** (-np.arange(nf, dtype=np.float32) / nf)).astype(np.float32)
    t = np.arange(L_LAT)
    rows = (t // 64).astype(np.float32)
    cols = (t % 64).astype(np.float32)
    cos = np.zeros((64, L_LAT), np.float32)
    sin = np.zeros((64, L_LAT), np.float32)
    for d in range(64):
        pos = rows if d < 32 else cols
        f = d % 16
        ang = (pos * inv[f]).astype(np.float32)
        cos[d] = np.cos(ang)
        sgn = -1.0 if (d % 32) < 16 else 1.0
        sin[d] = sgn * np.sin(ang)
    return np.tile(cos, (2, 1)), np.tile(sin, (2, 1))


def _consts():
    ident = np.eye(128, dtype=np.float32)
    onesD = np.full((128, 128), 1.0 / 1024.0, np.float32)
    c32 = np.concatenate([ident, onesD], 1)
    R = np.zeros((128, 128), np.float32)
    for m in range(128):
        d = m % 64
        partner = d + 16 if (d % 32) < 16 else d - 16
        R[(m // 64) * 64 + partner, m] = 1.0
    B = np.zeros((128, 128), np.float32)
    B[0:64, 0:64] = 1.0 / 64.0
    B[64:128, 64:128] = 1.0 / 64.0
    cos, sin = _rope_tables()
    c16 = np.concatenate([R, B, cos, sin], 1)
    return np.ascontiguousarray(c32), np.ascontiguousarray(c16)


def _kc_layout(w):
    Ln, K, N = w.shape
    return np.ascontiguousarray(w.reshape(Ln, 8, 128, N).transpose(0, 2, 1, 3))


def prep_shared(inp):
    f = lambda k: np.asarray(inp[k], dtype=np.float32)
    w_in = f("w_in")
    a_u, a_v, a_g = w_in[:, :, 0:256], w_in[:, :, 256:512], w_in[:, :, 512:768]
    q, k, v = w_in[:, :, 768:1280], w_in[:, :, 1280:1408], w_in[:, :, 1408:1536]
    b_g, r_x, r_g = w_in[:, :, 1536:2048], w_in[:, :, 2048:2304], w_in[:, :, 2304:2560]
    k0, k1 = k[:, :, 0:64], k[:, :, 64:128]
    wperm = np.concatenate([k0, k0, k1, k1, v, r_x, r_g, a_u, a_g, a_v, q, b_g], axis=2)
    assert wperm.shape[2] == NC1 + NC2
    sh = {}
    sh["w_in"] = _kc_layout(wperm)
    sh["w_o"] = _kc_layout(f("w_o"))
    sh["w_ada"] = _kc_layout(f("w_ada"))
    sh["b_adaT"] = np.ascontiguousarray(f("b_ada").reshape(DEPTH, 24, 128).transpose(2, 0, 1))
    fm8 = lambda a: a.reshape(DEPTH, -1, 128).transpose(2, 0, 1)
    vec = np.zeros((128, DEPTH, NV), np.float32)
    vec[:, :, V_LNG:V_LNG + 8] = fm8(f("ln_g"))
    vec[:, :, V_LNB:V_LNB + 8] = fm8(f("ln_b"))
    vec[:, :, V_QG] = np.tile(f("q_norm_g"), (1, 2)).T
    vec[:, :, V_KG] = np.tile(f("k_norm_g"), (1, 2)).T
    cw = f("conv_w")
    for ct in range(2):
        for j in range(4):
            vec[:, :, V_CW + ct * 4 + j] = cw[:, j, ct * 128:(ct + 1) * 128].T
    vec[:, :, V_CB:V_CB + 2] = fm8(f("conv_b"))
    for nm, base in (("lru_br", V_BR), ("lru_bi", V_BI), ("lru_lam", V_LAM)):
        a = f(nm)
        for dd in range(2):
            for ct in range(2):
                vec[:, :, base + dd * 2 + ct] = a[:, dd, ct * 128:(ct + 1) * 128].T
    sh["vecs"] = vec
    an = np.stack([f("a_norm_g"), f("a_norm_b")], 1)
    sh["anorm"] = np.ascontiguousarray(np.broadcast_to(an[:, None], (DEPTH, 128, 2, 256)))
    bs = f("a_bs")
    ab = np.zeros((DEPTH, 128, 2, 128), np.float32)
    for ct in range(2):
        ab[:, 0:64, ct, :] = bs[:, 2 * ct, None, :]
        ab[:, 64:128, ct, :] = bs[:, 2 * ct + 1, None, :]
    sh["absb"] = ab
    sh["awsT"] = np.ascontiguousarray(f("a_ws").transpose(0, 3, 1, 2))
    lw = np.zeros((DEPTH, 128, 8, 128), np.float32)
    for dd in range(2):
        for gi, nm in enumerate(("lru_wr", "lru_wi")):
            w = f(nm)
            for ct in range(2):
                idx = (dd * 2 + gi) * 2 + ct
                for hb in range(2):
                    lw[:, hb * 64:(hb + 1) * 64, idx, hb * 64:(hb + 1) * 64] = w[:, dd, 2 * ct + hb]
    sh["lruW"] = lw
    sh["cst32"], sh["cst16"] = _consts()
    return sh


def prep_core(inp, b0, nseq):
    x = np.asarray(inp["x"], np.float32)
    ctx = np.asarray(inp["ctx"], np.float32)
    c = np.asarray(inp["c"], np.float32)
    c_ctx = np.asarray(inp["c_ctx"], np.float32)
    xin = np.empty((nseq, 128, 8, TOK), np.float32)
    for s in range(nseq):
        full = np.concatenate([x[b0 + s], ctx[b0 + s]], 0)
        xin[s] = full.T.reshape(8, 128, TOK).transpose(1, 0, 2)
    rows = [c[b0 + s] if s < nseq else c_ctx for s in range(2)] + [c_ctx]
    crow = np.stack(rows, 0)
    cT = np.ascontiguousarray(crow.T.reshape(8, 128, 3).transpose(1, 0, 2))
    return {"xin": xin, "cT": cT}


_PROG_CACHE = {}


def kernel(**inputs):
    n = 8
    nseq = 2
    key = ("full", nseq)
    if key not in _PROG_CACHE:
        _PROG_CACHE[key] = build_program(list(range(DEPTH)), nseq)
    nc = _PROG_CACHE[key]
    sh = prep_shared(inputs)
    in_maps = []
    for i in range(n):
        m = dict(sh)
        m.update(prep_core(inputs, i * nseq, nseq))
        in_maps.append(m)
    res = run_bass_kernel_spmd(nc, in_maps, core_ids=list(range(n)))
    out = np.empty((16, L_LAT, D), np.float32)
    for i in range(n):
        xo = np.asarray(res.results[i]["xout"])
        for s in range(nseq):
            out[i * nseq + s] = xo[s][:, :, 0:L_LAT].transpose(2, 1, 0).reshape(L_LAT, D)
    return out
```

```python
import math
import numpy as np
from contextlib import ExitStack
import concourse.bass as bass
import concourse.mybir as mybir
from concourse.bass_utils import run_bass_kernel_spmd

F32 = mybir.dt.float32
BF16 = mybir.dt.bfloat16
AF = mybir.ActivationFunctionType
ALU = mybir.AluOpType

D = 1024
L_LAT = 2048
L_CTX = 256
TOK = L_LAT + L_CTX
DEPTH = 4
ALPHA = (2.0 * DEPTH) ** 0.25
LN_EPS = 1e-6
RMS_EPS = 1e-6
NC1 = 896
NC2 = 1792
C_K0, C_K1, C_V, C_RX, C_RG = 0, 128, 256, 384, 640
C_U, C_GA, C_VA, C_Q, C_BG = 0, 256, 512, 768, 1280
NV = 40
V_LNG, V_LNB, V_QG, V_KG, V_CW, V_CB, V_BR, V_BI, V_LAM = 0, 8, 16, 17, 18, 26, 28, 32, 36

ENGS = ("pe", "act", "dve", "pool", "sp")


class Prog:
    def __init__(self, nc, es):
        self.nc = nc
        self.es = es
        self.q = {e: [] for e in ENGS}
        self.cnt = {}
        self.sem = {}
        for e in ENGS:
            self.sem[e] = es.enter_context(nc.semaphore("c_" + e))
            self.cnt[e] = 0
        self.known = {e: {} for e in ENGS}
        self.lastw = {}
        self.readers = {}
        self.pending = {e: False for e in ENGS}
        self.ninst = 0

    def newsem(self, name):
        self.sem[name] = self.es.enter_context(self.nc.semaphore(name))
        self.cnt[name] = 0
        return name

    def _deps(self, eng, reads, writes):
        deps = {}

        def add(sk, v):
            if v > deps.get(sk, 0):
                deps[sk] = v

        for k in reads:
            w = self.lastw.get(k)
            if w is not None:
                add(*w)
        for k in writes:
            w = self.lastw.get(k)
            if w is not None:
                add(*w)
            for sk, v in self.readers.get(k, {}).items():
                add(sk, v)
        if eng == "pe":
            deps.pop("pe", None)
        return deps

    def _emit_waits(self, eng, deps):
        kn = self.known[eng]
        for sk, v in deps.items():
            if sk not in ENGS:
                v = max(v, self.cnt[sk])
            if sk == eng and v > self.cnt[eng]:
                raise RuntimeError("self-dependency on pending op: " + eng)
            if kn.get(sk, 0) >= v:
                continue
            kn[sk] = v
            sem = self.sem[sk]
            self.q[eng].append(lambda E, sem=sem, v=v: E.wait_ge(sem, v))

    def op(self, eng, fn, reads=(), writes=(), inc=True):
        deps = self._deps(eng, reads, writes)
        self._emit_waits(eng, deps)
        n = self.cnt[eng] + 1
        for k in reads:
            self.readers.setdefault(k, {})[eng] = n
        for k in writes:
            self.lastw[k] = (eng, n)
            self.readers[k] = {}
        self.ninst += 1
        if inc:
            self.cnt[eng] = n
            sem = self.sem[eng]
            self.q[eng].append(lambda E, sem=sem: fn(E).then_inc(sem, 1))
            self.pending[eng] = False
        else:
            self.q[eng].append(lambda E: fn(E))
            self.pending[eng] = True

    def dma(self, queue, semname, out, in_, reads=(), writes=()):
        deps = self._deps(queue, reads, writes)
        self._emit_waits(queue, deps)
        n = self.cnt[semname] + 16
        self.cnt[semname] = n
        for k in reads:
            self.readers.setdefault(k, {})[semname] = n
        for k in writes:
            self.lastw[k] = (semname, n)
            self.readers[k] = {}
        sem = self.sem[semname]
        self.q[queue].append(lambda E, sem=sem: E.dma_start(out=out, in_=in_).then_inc(sem, 16))
        self.ninst += 1

    def fence(self):
        for e in ENGS:
            kn = self.known[e]
            for s, v in self.cnt.items():
                if s == e or v == 0 or kn.get(s, 0) >= v:
                    continue
                kn[s] = v
                sem = self.sem[s]
                self.q[e].append(lambda E, sem=sem, v=v: E.wait_ge(sem, v))

    def wait_all(self, eng, semnames):
        for s in semnames:
            v = self.cnt[s]
            if v > 0 and self.known[eng].get(s, 0) < v:
                self.known[eng][s] = v
                sem = self.sem[s]
                self.q[eng].append(lambda E, sem=sem, v=v: E.wait_ge(sem, v))

    def emit(self, block):
        for e in ENGS:
            assert not self.pending[e], "engine %s ends with un-inc'd op" % e
        q = self.q

        @block.tensor
        def _(E):
            for f in q["pe"]:
                f(E)

        @block.scalar
        def _(E):
            for f in q["act"]:
                f(E)

        @block.vector
        def _(E):
            for f in q["dve"]:
                f(E)

        @block.gpsimd
        def _(E):
            for f in q["pool"]:
                f(E)

        @block.sync
        def _(E):
            for f in q["sp"]:
                f(E)


class Carver:
    def __init__(self, scr, nbytes):
        self.scr = scr
        self.cap = nbytes
        self.off = 0
        self.hi = 0

    def alloc(self, shape, dt):
        esz = 4 if dt == F32 else 2
        n = 1
        for s in shape:
            n *= s
        nb = (n * esz + 31) // 32 * 32
        assert self.off + nb <= self.cap, ("SBUF scratch overflow", self.off, nb, self.cap)
        w0 = self.off // 4
        ap = self.scr[:, w0:w0 + nb // 4]
        if dt != F32:
            ap = ap.bitcast(dt)
        ap = ap[:, 0:n]
        if len(shape) == 2:
            ap = ap.rearrange("p (a b) -> p a b", a=shape[0])
        elif len(shape) == 3:
            ap = ap.rearrange("p (a b c) -> p a b c", a=shape[0], b=shape[1])
        self.off += nb
        self.hi = max(self.hi, self.off)
        return ap


def rev_ap(ap_):
    n = ap_.shape[1]
    pp = ap_.ap[0]
    return bass.AP(ap_.tensor, ap_.offset + (n - 1), [[pp[0], pp[1]], [-1, n]])


class _Stop(Exception):
    pass


def build_program(layers, nseq, first_in_program=True, dbg=None, stop=99):
    nc = bass.Bass("TRN2", target_bir_lowering=False)
    NL = DEPTH
    dram = lambda name, shape, dt=F32, kind="ExternalInput": nc.dram_tensor(name, shape, dt, kind=kind).ap()
    xin = dram("xin", [nseq, 128, 8, TOK])
    cT_d = dram("cT", [128, 8, 3])
    wada_d = dram("w_ada", [NL, 128, 8, 3072])
    bada_d = dram("b_adaT", [128, NL, 24])
    win_d = dram("w_in", [NL, 128, 8, NC1 + NC2])
    wo_d = dram("w_o", [NL, 128, 8, 1024])
    vecs_d = dram("vecs", [128, NL, NV])
    anorm_d = dram("anorm", [NL, 128, 2, 256])
    absb_d = dram("absb", [NL, 128, 2, 128])
    aws_d = dram("awsT", [NL, 128, 4, 128])
    lruw_d = dram("lruW", [NL, 128, 8, 128])
    c32_d = dram("cst32", [128, 256])
    c16_d = dram("cst16", [128, 256 + 2 * L_LAT])
    xout = dram("xout", [nseq, 128, 8, TOK], kind="ExternalOutput")
    dbg_out = {}
    if dbg:
        for name, shape in dbg.items():
            dbg_out[name] = dram("dbg_" + name, shape, kind="ExternalOutput")

    with ExitStack() as es:
        CAP = 212800
        scr = es.enter_context(nc.sbuf_tensor("scr", [128, CAP // 4], F32))
        psall = es.enter_context(nc.psum_tensor("psall", [128, 4096], F32))[:, :]
        PSB = [psall[:, i * 512:(i + 1) * 512] for i in range(8)]
        P = Prog(nc, es)
        for s in ("ld_x", "ld_c", "ld_w1", "ld_w2", "ld_wo", "ld_ada0", "ld_ada1", "ld_lw", "st_o", "st_d"):
            P.newsem(s)
        C = Carver(scr, CAP)

        xT = C.alloc([8, TOK], F32)
        tabs = C.alloc([2, L_LAT], BF16)
        KT = C.alloc([2, TOK], BF16)
        VA = C.alloc([18, 2, 128], BF16)
        ylru = C.alloc([2, TOK], BF16)
        c32 = C.alloc([256], F32)[:, :]
        c16 = C.alloc([256], BF16)
        mods = C.alloc([NL, 24, 3], F32)
        sc1 = C.alloc([NL, 8, 3], F32)
        gA = C.alloc([NL, 8, 3], F32)
        vecs = C.alloc([NL, NV], F32)
        cvec = C.alloc([NL, 4], F32)
        badaT = C.alloc([NL, 24], F32)
        anorm = C.alloc([2, 256], F32)
        absb = C.alloc([2, 128], F32)
        awsT = C.alloc([4, 128], BF16)
        lruW = C.alloc([8, 128], BF16)
        sil = C.alloc([8, 3], F32)
        epsr = C.alloc([4], F32)
        onesb = C.alloc([128], BF16)
        PBASE = C.off
        ident = c32[:, 0:128]
        onesD = c32[:, 128:256]
        Rmat = c16[:, 0:128]
        Bones = c16[:, 128:256]
        cosT = tabs[:, 0, :]
        sinT = tabs[:, 1, :]

        psi = [0, 0, 0]

        def nextps():
            i = psi[0] % 4
            psi[0] += 1
            return PSB[i], "ps%d" % i

        def nextps2():
            i = ((psi[0] + 1) // 2 * 2) % 4
            psi[0] = i + 2
            return psall[:, i * 512:(i + 2) * 512], ["ps%d" % i, "ps%d" % (i + 1)]

        def nextps_s():
            i = 3 + psi[1] % 3
            psi[1] += 1
            return PSB[i], "ps%d" % i

        def nextps_o():
            i = 6 + psi[2] % 2
            psi[2] += 1
            return PSB[i], "ps%d" % i

        def act(out, in_, func, r, w, **kw):
            P.op("act", lambda E: E.activation(out=out, in_=in_, func=func, **kw), r, w)

        def tt(eng, out, a, b, op, r, w):
            P.op(eng, lambda E: E.tensor_tensor(out=out, in0=a, in1=b, op=op), r, w)

        def ts(eng, out, a, s1, s2, op0, op1, r, w):
            if s2 is None:
                P.op(eng, lambda E: E.tensor_single_scalar(out=out, in_=a, scalar=s1, op=op0), r, w)
            else:
                P.op(eng, lambda E: E.tensor_scalar(out=out, in0=a, scalar1=s1, scalar2=s2, op0=op0, op1=op1), r, w)

        def scan(out, d0, d1, init, r, w):
            P.op("dve", lambda E: E.tensor_tensor_scan(out=out, data0=d0, data1=d1, initial=init, op0=ALU.mult, op1=ALU.add), r, w)

        def stt(eng, out, a, s, b, op0, op1, r, w):
            P.op(eng, lambda E: E.scalar_tensor_tensor(out=out, in0=a, scalar=s, in1=b, op0=op0, op1=op1), r, w)

        def mm(out, lhsT, rhs, start, stop, r, w, inc=True):
            P.op("pe", lambda E: E.matmul(out, lhsT, rhs, start=start, stop=stop), r, w, inc=inc)

        def recip(out, in_, r, w):
            P.op("dve", lambda E: E.reciprocal(out=out, in_=in_), r, w)

        def memset(eng, ap, val, w):
            P.op(eng, lambda E: E.memset(ap, val), (), w)

        def copy(eng, out, in_, r, w):
            P.op(eng, lambda E: E.tensor_copy(out=out, in_=in_), r, w)

        P.dma("sp", "ld_c", c32, c32_d[:, :], writes=["c32"])
        P.dma("pool", "ld_lw", c16, c16_d[:, 0:256], writes=["c16"])
        for a_ in range(2):
            P.dma("pool", "ld_lw", tabs[:, a_, :], c16_d[:, 256 + a_ * L_LAT:256 + (a_ + 1) * L_LAT], writes=["tabs"])
        P.dma("sp", "ld_c", vecs, vecs_d[:, :, :], writes=["vecs"])
        P.dma("sp", "ld_c", badaT, bada_d[:, :, :], writes=["badaT"])
        P.dma("sp", "ld_c", sil, cT_d[:, :, :], writes=["sil"])
        memset("dve", epsr[:, 0:1], RMS_EPS, ["epsr"])
        memset("dve", epsr[:, 1:2], LN_EPS / (ALPHA * ALPHA), ["epsr"])
        memset("dve", epsr[:, 2:3], 1.0, ["epsr"])
        memset("dve", epsr[:, 3:4], 0.0, ["epsr"])
        memset("dve", onesb, 1.0 / 1024.0, ["onesb"])
        memset("pool", VA[:, :, :, 64:128], 1.0, ["VA"])

        act(sil, sil, AF.Silu, ["sil"], ["sil"])
        for l in layers:
            act(cvec[:, l, :], vecs[:, l, V_LAM:V_LAM + 4], AF.Exp, ["vecs"], ["cvec"], scale=-1.0)
            act(cvec[:, l, :], cvec[:, l, :], AF.Ln, ["cvec", "epsr"], ["cvec"], bias=epsr[:, 2:3])
            ts("dve", cvec[:, l, :], cvec[:, l, :], -8.0, None, ALU.mult, ALU.bypass, ["cvec"], ["cvec"])
        C.off = PBASE
        stg = [C.alloc([8, 512], F32) for _ in range(2)]
        modrow = C.alloc([3072], F32)[:, :]
        bi = 0
        for l in layers:
            for nb in range(6):
                sb_ = stg[bi % 2]
                key = "stg%d" % (bi % 2)
                P.dma("sp", "ld_ada%d" % (bi % 2), sb_, wada_d[l, :, :, nb * 512:(nb + 1) * 512], writes=[key])
                ps, pk = nextps()
                for kc in range(8):
                    mm(ps[0:3, :], sil[:, kc, :], sb_[:, kc, :], kc == 0, kc == 7, [key, "sil"], [pk], inc=(kc == 7))
                copy("dve", modrow[0:3, nb * 512:(nb + 1) * 512], ps[0:3, :], [pk], ["modrow"])
                bi += 1
            ps, pk = nextps()
            for j in range(24):
                mm(ps[:, j * 3:(j + 1) * 3], modrow[0:3, j * 128:(j + 1) * 128], ident[0:3, 0:3], True, True,
                   ["modrow", "c32"], [pk], inc=(j == 23))
            tt("dve", mods[:, l, :, :], ps[:, 0:72].rearrange("p (a b) -> p a b", a=24),
               badaT[:, l, :].unsqueeze(2).to_broadcast([128, 24, 3]), ALU.add, [pk, "badaT"], ["mods"])
            ts("dve", sc1[:, l, :, :], mods[:, l, 8:16, :], 1.0, None, ALU.add, ALU.bypass, ["mods"], ["sc1"])
            ts("dve", gA[:, l, :, :], mods[:, l, 16:24, :], 1.0 / ALPHA, None, ALU.mult, ALU.bypass, ["mods"], ["gA"])
        P.fence()

        def rms_rope(zps, zk, n, gvec, dsts, dstk, t0, latent, S):
            act(S["sq"][:, 0:n], zps[:, 0:n], AF.Square, [zk], ["r_sq"])
            ps2, pk2 = nextps()
            mm(ps2[:, 0:n], Bones, S["sq"][:, 0:n], True, True, ["r_sq", "c16"], [pk2])
            act(S["sd"][:, 0:n], ps2[:, 0:n], AF.Ln, [pk2, "epsr"], ["r_sd"], bias=epsr[:, 0:1])
            act(S["sd"][:, 0:n], S["sd"][:, 0:n], AF.Exp, ["r_sd"], ["r_sd"], scale=-0.5)
            if not latent:
                for (lo, hi_, dst) in dsts:
                    stt("dve", dst, zps[lo:hi_, 0:n], gvec[lo:hi_, :], S["sd"][lo:hi_, 0:n], ALU.mult, ALU.mult,
                        [zk, "r_sd", "vecs"], [dstk])
                return
            stt("dve", S["qn"][:, 0:n], zps[:, 0:n], gvec, S["sd"][:, 0:n], ALU.mult, ALU.mult,
                [zk, "r_sd", "vecs"], ["r_qn"])
            ps3, pk3 = nextps()
            mm(ps3[:, 0:n], Rmat, S["qn"][:, 0:n], True, True, ["r_qn", "c16"], [pk3])
            tt("pool", S["t1"][:, 0:n], S["qn"][:, 0:n], cosT[:, t0:t0 + n], ALU.mult, ["r_qn", "tabs"], ["r_t1"])
            tt("dve", S["t2"][:, 0:n], ps3[:, 0:n], sinT[:, t0:t0 + n], ALU.mult, [pk3, "tabs"], ["r_t2"])
            for (lo, hi_, dst) in dsts:
                tt("pool", dst, S["t1"][lo:hi_, 0:n], S["t2"][lo:hi_, 0:n], ALU.add, ["r_t1", "r_t2"], [dstk])

        def xkeys(t0, n):
            return ["xT%d" % b for b in range(t0 // 256, (t0 + n + 255) // 256)]

        ALLX = ["xT%d" % b for b in range(9)]

        def modulate(dst, dk, l, r, t0, n, first=False, use_act=False):
            for c in range(8):
                eng = "dve" if (first or c % 2 == 1) else "pool"
                if use_act and c % 2 == 0:
                    act(dst[:, c, 0:n], xT[:, c, t0:t0 + n], AF.Identity, xkeys(t0, n) + ["sc1", "mods"],
                        ["%s_c%d" % (dk, c)], scale=sc1[:, l, c, r:r + 1], bias=mods[:, l, c, r:r + 1])
                    continue
                ts(eng, dst[:, c, 0:n], xT[:, c, t0:t0 + n], sc1[:, l, c, r:r + 1], mods[:, l, c, r:r + 1],
                   ALU.mult, ALU.add, xkeys(t0, n) + ["sc1", "mods"], ["%s_c%d" % (dk, c)])

        def inproj_fm(ps, pk, W, wk, col0, xm, xk, n):
            for kc in range(8):
                mm(ps[:, 0:n], W[:, kc, col0:col0 + 128], xm[:, kc, 0:n], kc == 0, kc == 7, [wk, "%s_c%d" % (xk, kc)], [pk], inc=(kc == 7))

        def stage(n):
            if n > stop:
                raise _Stop()

        W1P = (("k", 0, 256), ("v", 256, 128), ("rx", 384, 256), ("rg", 640, 256))
        W2P = (("u", C_U, 256), ("va", C_VA, 256), ("ga", C_GA, 256), ("bg", C_BG, 512), ("q", C_Q, 512))
        for nm, _, _ in W1P:
            P.newsem("ld_w1" + nm)
        for nm, _, _ in W2P:
            P.newsem("ld_w2" + nm)

        for s in range(nseq):
            try:
                for c in range(8):
                    P.dma("sp", "ld_x", xT[:, c, :], xin[s, :, c, :], writes=ALLX)
                for l in layers:
                    last = (l == DEPTH - 1)
                    P.dma("sp", "ld_c", anorm, anorm_d[l, :, :, :], writes=["anorm"])
                    P.dma("sp", "ld_c", absb, absb_d[l, :, :, :], writes=["absb"])
                    P.dma("pool", "ld_lw", awsT, aws_d[l, :, :, :], writes=["awsT"])
                    P.dma("pool", "ld_lw", lruW, lruw_d[l, :, :, :], writes=["lruW"])
                    stage(1)
                    P.fence()
                    C.off = PBASE
                    win1 = C.alloc([8, NC1], BF16)
                    xm1 = C.alloc([8, 512], BF16)
                    S1 = {"sq": C.alloc([512], BF16)[:, :], "sd": C.alloc([512], F32)[:, :],
                          "qn": C.alloc([512], BF16)[:, :], "t1": C.alloc([512], F32)[:, :]}
                    assert C.off - PBASE == NC2 * 8 * 2, (C.off - PBASE)
                    S1["t2"] = C.alloc([512], F32)[:, :]
                    sg = C.alloc([2, TOK], BF16)
                    xrb = C.alloc([2, TOK], BF16)
                    GBASE = C.off
                    rx = C.alloc([2, 2312], F32)
                    acc = C.alloc([TOK], F32)[:, :]
                    xm1b = C.alloc([8, 512], BF16)
                    XM1 = [(xm1, "xm1"), (xm1b, "xm1b")]
                    P1B = ((0, 512), (512, 512), (1024, 512), (1536, 512), (2048, 256))
                    for nm, c0, ncl in W1P:
                        P.dma("pool", "ld_w1" + nm, win1[:, :, c0:c0 + ncl], win_d[l, :, :, c0:c0 + ncl], writes=["win1" + nm])
                    memset("pool", rx[:, :, 0:2], 0.0, ["rx"])
                    memset("pool", rx[:, :, 2050:2052], 0.0, ["rx"])
                    memset("pool", rx[:, :, 2308:2312], 0.0, ["rx"])
                    modulate(xm1, "xm1", l, s, 0, 512, first=True, use_act=True)
                    for bi1, (t0, n) in enumerate(P1B):
                        latent = t0 < L_LAT
                        r = s if latent else 2
                        xm1c, xk1 = XM1[bi1 % 2]
                        bk = lambda i: (PSB[i], "ps%d" % i)
                        gk = vecs[:, l, V_KG:V_KG + 1]
                        off = 2 + t0 if latent else 2052 + (t0 - L_LAT)
                        nt = n // 128
                        zK = [bk(0), bk(1)]
                        for g in range(2):
                            inproj_fm(zK[g][0], zK[g][1], win1, "win1k", C_K0 + g * 128, xm1c, xk1, n)
                        psV, pkV = bk(2)
                        for tl in range(nt):
                            for kc in range(8):
                                mm(psV[:, tl * 128:(tl + 1) * 128], xm1c[:, kc, tl * 128:(tl + 1) * 128], win1[:, kc, C_V:C_V + 128],
                                   kc == 0, kc == 7, ["win1v", "%s_c%d" % (xk1, kc)], [pkV], inc=(kc == 7))
                        zX = [bk(3), bk(4)]
                        for ct in range(2):
                            inproj_fm(zX[ct][0], zX[ct][1], win1, "win1rx", C_RX + ct * 128, xm1c, xk1, n)
                        sqK = [S1["sq"], S1["qn"]]
                        for g in range(2):
                            act(sqK[g][:, 0:n], zK[g][0][:, 0:n], AF.Square, [zK[g][1]], ["r_sq%d" % g])
                        if bi1 + 1 < len(P1B):
                            t0n, nn = P1B[bi1 + 1]
                            modulate(XM1[(bi1 + 1) % 2][0], XM1[(bi1 + 1) % 2][1], l, (s if t0n < L_LAT else 2), t0n, nn,
                                     first=True, use_act=True)
                        for ct in range(2):
                            act(rx[:, ct, off:off + n], zX[ct][0][:, 0:n], AF.Copy, [zX[ct][1]], ["rx"])
                        copy("dve", VA[:, t0 // 128:t0 // 128 + nt, :, 0:64],
                             psV[:, 0:nt * 128].rearrange("p (t g d) -> p t g d", t=nt, g=2), [pkV], ["VA"])
                        mK = [bk(7), bk(3)]
                        mm(mK[0][0][:, 0:n], Bones, sqK[0][:, 0:n], True, True, ["r_sq0", "c16"], [mK[0][1]])
                        zG = [bk(5), bk(6)]
                        for ct in range(2):
                            inproj_fm(zG[ct][0], zG[ct][1], win1, "win1rg", C_RG + ct * 128, xm1c, xk1, n)
                        mm(mK[1][0][:, 0:n], Bones, sqK[1][:, 0:n], True, True, ["r_sq1", "c16"], [mK[1][1]])
                        rsK = [S1["sd"], S1["t1"]]
                        for g in range(2):
                            act(rsK[g][:, 0:n], mK[g][0][:, 0:n], AF.Ln, [mK[g][1], "epsr"], ["r_sd%d" % g], bias=epsr[:, 0:1])
                            act(rsK[g][:, 0:n], rsK[g][:, 0:n], AF.Exp, ["r_sd%d" % g], ["r_sd%d" % g], scale=-0.5)
                        for g in range(2):
                            dstK = KT[:, g, t0:t0 + n]
                            if latent:
                                stt("dve", sqK[g][:, 0:n], zK[g][0][:, 0:n], gk, rsK[g][:, 0:n], ALU.mult, ALU.mult,
                                    [zK[g][1], "r_sd%d" % g, "vecs"], ["r_sq%d" % g])
                            else:
                                stt("dve", dstK, zK[g][0][:, 0:n], gk, rsK[g][:, 0:n], ALU.mult, ALU.mult,
                                    [zK[g][1], "r_sd%d" % g, "vecs"], ["KT"])
                        for ct in range(2):
                            act(sg[:, ct, t0:t0 + n], zG[ct][0][:, 0:n], AF.Silu, [zG[ct][1]], ["sg"])
                        if latent:
                            rK = [bk(4), bk(7)]
                            for g in range(2):
                                mm(rK[g][0][:, 0:n], Rmat, sqK[g][:, 0:n], True, True, ["r_sq%d" % g, "c16"], [rK[g][1]])
                            for g in range(2):
                                tt("pool", rsK[g][:, 0:n], sqK[g][:, 0:n], cosT[:, t0:t0 + n], ALU.mult,
                                   ["r_sq%d" % g, "tabs"], ["r_sd%d" % g])
                                tt("dve", S1["t2"][:, 0:n], rK[g][0][:, 0:n], sinT[:, t0:t0 + n], ALU.mult, [rK[g][1], "tabs"], ["r_t2"])
                                tt("pool", KT[:, g, t0:t0 + n], rsK[g][:, 0:n], S1["t2"][:, 0:n], ALU.add,
                                   ["r_sd%d" % g, "r_t2"], ["KT"])
                    stage(2)
                    for ct in range(2):
                        for (d0, o0, n) in ((2, 0, L_LAT), (2052, L_LAT, L_CTX)):
                            cw = lambda j: vecs[:, l, V_CW + ct * 4 + j:V_CW + ct * 4 + j + 1]
                            eng = "dve"
                            ts(eng, acc[:, 0:n], rx[:, ct, d0 - 2:d0 - 2 + n], cw(0), vecs[:, l, V_CB + ct:V_CB + ct + 1],
                               ALU.mult, ALU.add, ["rx", "vecs"], ["acc"])
                            stt(eng, acc[:, 0:n], rx[:, ct, d0 - 1:d0 - 1 + n], cw(1), acc[:, 0:n], ALU.mult, ALU.add,
                                ["rx", "vecs", "acc"], ["acc"])
                            stt(eng, acc[:, 0:n], rx[:, ct, d0:d0 + n], cw(2), acc[:, 0:n], ALU.mult, ALU.add,
                                ["rx", "vecs", "acc"], ["acc"])
                            stt(eng, xrb[:, ct, o0:o0 + n], rx[:, ct, d0 + 1:d0 + 1 + n], cw(3), acc[:, 0:n], ALU.mult, ALU.add,
                                ["rx", "vecs", "acc"], ["xrb"])
                    stage(3)
                    P.fence()
                    C.off = PBASE
                    win2 = C.alloc([8, NC2], BF16)
                    for nm, c0, ncl in W2P:
                        P.dma("pool", "ld_w2" + nm, win2[:, :, c0:c0 + ncl], win_d[l, :, :, NC1 + c0:NC1 + c0 + ncl],
                              writes=["win2" + nm])
                    C.off = GBASE
                    Ab = C.alloc([TOK], F32)[:, :]
                    Tb = C.alloc([TOK], F32)[:, :]
                    Bb = C.alloc([TOK], F32)[:, :]
                    Hf = C.alloc([TOK], F32)[:, :]
                    for ct in range(2):
                        for dd in range(2):
                            vi = dd * 2 + ct
                            for (t0, n) in ((0, 512), (512, 512), (1024, 512), (1536, 512), (2048, 256)):
                                ps, pk = nextps()
                                mm(ps[:, 0:n], lruW[:, (dd * 2 + 0) * 2 + ct, :], xrb[:, ct, t0:t0 + n], True, True,
                                   ["lruW", "xrb"], [pk])
                                act(Ab[:, t0:t0 + n], ps[:, 0:n], AF.Sigmoid, [pk, "vecs"], ["Ab"],
                                    bias=vecs[:, l, V_BR + vi:V_BR + vi + 1])
                                ps, pk = nextps()
                                mm(ps[:, 0:n], lruW[:, (dd * 2 + 1) * 2 + ct, :], xrb[:, ct, t0:t0 + n], True, True,
                                   ["lruW", "xrb"], [pk])
                                act(Bb[:, t0:t0 + n], ps[:, 0:n], AF.Sigmoid, [pk, "vecs"], ["Bb"],
                                    bias=vecs[:, l, V_BI + vi:V_BI + vi + 1])
                            act(Ab, Ab, AF.Exp, ["Ab", "cvec"], ["Ab"], scale=cvec[:, l, vi:vi + 1])
                            act(Tb, Ab, AF.Square, ["Ab"], ["Tb"])
                            act(Tb, Tb, AF.Ln, ["Tb", "epsr"], ["Tb"], scale=-1.0, bias=epsr[:, 2:3])
                            act(Tb, Tb, AF.Exp, ["Tb"], ["Tb"], scale=0.5)
                            tt("dve", Bb, Bb, xrb[:, ct, :], ALU.mult, ["Bb", "xrb"], ["Bb"])
                            tt("dve", Bb, Bb, Tb, ALU.mult, ["Bb", "Tb"], ["Bb"])
                            if dd == 0:
                                scan(Hf[:, L_LAT:TOK], Ab[:, L_LAT:TOK], Bb[:, L_LAT:TOK], 0.0, ["Ab", "Bb"], ["Hf"])
                                scan(Hf[:, 0:L_LAT], Ab[:, 0:L_LAT], Bb[:, 0:L_LAT], Hf[:, TOK - 1:TOK], ["Ab", "Bb", "Hf"], ["Hf"])
                            else:
                                scan(rev_ap(Tb), rev_ap(Ab), rev_ap(Bb), 0.0, ["Ab", "Bb"], ["Tb"])
                        tt("pool", Hf, Hf, Tb, ALU.add, ["Hf", "Tb"], ["Hf"])
                        tt("pool", ylru[:, ct, :], Hf, sg[:, ct, :], ALU.mult, ["Hf", "sg"], ["ylru"])
                    stage(4)
                    P.fence()
                    C.off = PBASE
                    win2 = C.alloc([8, NC2], BF16)
                    wo = C.alloc([8, 1024], BF16)
                    for kc in range(8):
                        P.dma("pool", "ld_wo", wo[:, kc, :], wo_d[l, :, kc, :], writes=["wo"])
                    xm = C.alloc([8, 256], BF16)
                    yT = C.alloc([6, 256], BF16)
                    ug = C.alloc([2, 256], F32)
                    sga = C.alloc([2, 256], F32)
                    vg = C.alloc([2, 256], F32)
                    vln = C.alloc([2, 256], BF16)
                    tmpg = C.alloc([512], F32)[:, :]
                    st = C.alloc([2, 8], F32)
                    qbd = C.alloc([4, 512], BF16)
                    sbg = C.alloc([4, 256], BF16)
                    bufA = C.alloc([1024], BF16)[:, :]
                    bufBC = C.alloc([2048], F32)[:, :]
                    bufB = bufBC[:, 0:1024]
                    bufC = bufBC[:, 1024:2048]
                    pT = [C.alloc([512], BF16)[:, :] for _ in range(3)]
                    rden = [C.alloc([256], F32)[:, :] for _ in range(2)]
                    otmp = [C.alloc([256], F32)[:, :]] * 2
                    sqb = C.alloc([8, 256], BF16)
                    mean_sb = C.alloc([256], F32)[:, :]
                    m2 = C.alloc([256], F32)[:, :]
                    rsl = C.alloc([256], F32)[:, :]
                    memset("pool", qbd[64:128, :, 0:256], 0.0, ["qbd"])
                    memset("pool", qbd[0:64, :, 256:512], 0.0, ["qbd"])
                    nblk = 8 if last else 9
                    cn = [0, 0]
                    def mod(qb):
                        t0 = qb * 256
                        latent = qb < 8
                        r = s if latent else 2
                        xk = "xT%d" % qb
                        modulate(xm, "xm", l, r, t0, 256, first=(qb == 0))

                    def front_a(qb):
                        t0 = qb * 256
                        latent = qb < 8
                        r = s if latent else 2
                        xk = "xT%d" % qb
                        bank = lambda i: (psall[:, i * 512:(i + 1) * 512], "ps%d" % i)
                        bank2 = lambda i: (psall[:, i * 512:(i + 2) * 512], ["ps%d" % i, "ps%d" % (i + 1)])
                        gq = vecs[:, l, V_QG:V_QG + 1]
                        v4 = lambda ap_, lo, hi_: ap_[lo:hi_, :].rearrange("p (c t) -> p c t", c=4)
                        psQ, pkQ = bank2(0)
                        for ct in range(4):
                            inproj_fm(psQ[:, ct * 256:(ct + 1) * 256], pkQ[ct // 2], win2, "win2q", C_Q + ct * 128, xm, "xm", 256)
                        act(bufA, psQ, AF.Square, pkQ, ["bufA"])
                        psU, pkU = bank(2)
                        for ct in range(2):
                            inproj_fm(psU[:, ct * 256:(ct + 1) * 256], pkU, win2, "win2u", C_U + ct * 128, xm, "xm", 256)
                        psV, pkV = bank(3)
                        for tl in range(2):
                            for kc in range(8):
                                mm(psV[:, tl * 256:(tl + 1) * 256], xm[:, kc, tl * 128:(tl + 1) * 128], win2[:, kc, C_VA:C_VA + 256],
                                   kc == 0, kc == 7, ["win2va", "xm_c%d" % kc], [pkV], inc=(kc == 7))
                        psM, pkM = bank2(4)
                        for hf in range(2):
                            mm(psM[:, hf * 512:(hf + 1) * 512], Bones, bufA[:, hf * 512:(hf + 1) * 512], True, True,
                               ["bufA", "c16"], [pkM[hf]])
                        act(bufB, psM, AF.Ln, pkM + ["epsr"], ["bufB"], bias=epsr[:, 0:1])
                        act(bufB, bufB, AF.Exp, ["bufB"], ["bufB"], scale=-0.5)
                        if latent:
                            stt("dve", bufA, psQ, gq, bufB, ALU.mult, ALU.mult, pkQ + ["bufB", "vecs"], ["bufA"])
                        else:
                            stt("dve", qbd[0:64, :, 0:256], v4(psQ, 0, 64), gq[0:64, :], v4(bufB, 0, 64), ALU.mult, ALU.mult,
                                pkQ + ["bufB", "vecs"], ["qbd"])
                            stt("dve", qbd[64:128, :, 256:512], v4(psQ, 64, 128), gq[64:128, :], v4(bufB, 64, 128), ALU.mult, ALU.mult,
                                pkQ + ["bufB", "vecs"], ["qbd"])
                        act(ug.rearrange("p c t -> p (c t)"), psU, AF.Gelu_apprx_tanh, [pkU], ["ug"])
                        for tl in range(2):
                            act(vg[:, tl, :], psV[:, tl * 256:(tl + 1) * 256], AF.Gelu_apprx_tanh, [pkV], ["vg", "st"],
                                accum_out=st[:, tl, 0:1])
                        ps2b, pk2b = bank2(6)
                        for ct in range(4):
                            inproj_fm(ps2b[:, ct * 256:(ct + 1) * 256], pk2b[ct // 2], win2, "win2bg", C_BG + ct * 128, xm, "xm", 256)
                        psG, pkG = bank(2)
                        for ct in range(2):
                            inproj_fm(psG[:, ct * 256:(ct + 1) * 256], pkG, win2, "win2ga", C_GA + ct * 128, xm, "xm", 256)
                        if latent:
                            psR, pkR = bank2(4)
                            for hf in range(2):
                                mm(psR[:, hf * 512:(hf + 1) * 512], Rmat, bufA[:, hf * 512:(hf + 1) * 512], True, True,
                                   ["bufA", "c16"], [pkR[hf]])
                        act(sga.rearrange("p c t -> p (c t)"), psG, AF.Silu, [pkG], ["sga"])
                        act(sbg.rearrange("p c t -> p (c t)"), ps2b, AF.Silu, pk2b, ["sbg"])
                        tt("pool", ug, ug, sga, ALU.mult, ["ug", "sga"], ["ug"])
                        if latent:
                            cosb = cosT[:, t0:t0 + 256].unsqueeze(1).to_broadcast([128, 4, 256])
                            sinb = sinT[:, t0:t0 + 256].unsqueeze(1).to_broadcast([128, 4, 256])
                            tt("pool", v4(bufB, 0, 128), v4(bufA, 0, 128), cosb, ALU.mult, ["bufA", "tabs"], ["bufB"])
                            tt("dve", v4(bufC, 0, 128), v4(psR, 0, 128), sinb, ALU.mult, pkR + ["tabs"], ["bufC"])
                            tt("pool", qbd[0:64, :, 0:256], v4(bufB, 0, 64), v4(bufC, 0, 64), ALU.add, ["bufB", "bufC"], ["qbd"])
                            tt("pool", qbd[64:128, :, 256:512], v4(bufB, 64, 128), v4(bufC, 64, 128), ALU.add,
                               ["bufB", "bufC"], ["qbd"])

                    def front_b(qb):
                        t0 = qb * 256
                        latent = qb < 8
                        r = s if latent else 2
                        xk = "xT%d" % qb
                        for tl in range(2):
                            act(tmpg[:, 0:256], vg[:, tl, :], AF.Square, ["vg"], ["tmpg", "st"], accum_out=st[:, tl, 1:2])
                            ts("dve", st[:, tl, 2:3], st[:, tl, 0:1], 1.0 / 256.0, None, ALU.mult, ALU.bypass, ["st"], ["st"])
                            tt("dve", st[:, tl, 3:4], st[:, tl, 2:3], st[:, tl, 2:3], ALU.mult, ["st"], ["st"])
                            stt("dve", st[:, tl, 4:5], st[:, tl, 1:2], 1.0 / 256.0, st[:, tl, 3:4], ALU.mult, ALU.subtract,
                                ["st"], ["st"])
                        for tl in range(2):
                            act(st[:, tl, 5:6], st[:, tl, 4:5], AF.Ln, ["st", "epsr"], ["st"], bias=epsr[:, 0:1], scale=1.0)
                            act(st[:, tl, 6:7], st[:, tl, 5:6], AF.Exp, ["st"], ["st"], scale=-0.5)
                        for tl in range(2):
                            ts("dve", vg[:, tl, :], vg[:, tl, :], st[:, tl, 2:3], st[:, tl, 6:7], ALU.subtract, ALU.mult,
                               ["vg", "st"], ["vg"])
                            tt("pool", vg[:, tl, :], vg[:, tl, :], anorm[:, 0, :], ALU.mult, ["vg", "anorm"], ["vg"])
                            tt("pool", vln[:, tl, :], vg[:, tl, :], anorm[:, 1, :], ALU.add, ["vg", "anorm"], ["vln"])

                    def front_c(qb):
                        t0 = qb * 256
                        latent = qb < 8
                        r = s if latent else 2
                        xk = "xT%d" % qb
                        ps, pk = PSB[3], "ps3"
                        for tl in range(2):
                            for ct in range(2):
                                for hh in range(2):
                                    g = 2 * ct + hh
                                    co = (tl * 2 + ct) * 128
                                    mm(ps[hh * 64:(hh + 1) * 64, co:co + 128], vln[:, tl, g * 64:(g + 1) * 64], awsT[:, g, :],
                                       True, True, ["vln", "awsT"], [pk], inc=(tl == 1 and ct == 1 and hh == 1))
                        tt("dve", tmpg.rearrange("p (t c q) -> p t c q", t=2, c=2), ps.rearrange("p (t c q) -> p t c q", t=2, c=2),
                           absb.unsqueeze(1).to_broadcast([128, 2, 2, 128]), ALU.add, [pk, "absb"], ["tmpg"])
                        tt("pool", yT[:, 0:2, :].rearrange("p c (t q) -> p t c q", t=2),
                           tmpg.rearrange("p (t c q) -> p t c q", t=2, c=2),
                           ug.rearrange("p c (t q) -> p t c q", t=2), ALU.mult, ["tmpg", "ug"], ["yT"])

                    def attn(qb):
                        t0 = qb * 256
                        latent = qb < 8
                        r = s if latent else 2
                        xk = "xT%d" % qb
                        ktiles = list(range(18)) if latent else [16, 17]
                        for ct in range(4):
                            g = ct // 2
                            pO, kO = nextps_o()

                            def qk(j):
                                ps_, pk_ = nextps_s()
                                mm(ps_[:, 0:512], KT[:, g, j * 128:(j + 1) * 128], qbd[:, ct, :], True, True,
                                   ["KT", "qbd"], [pk_])
                                return ps_, pk_

                            pend = [qk(j) for j in ktiles[0:2]]
                            for ji, j in enumerate(ktiles):
                                ps_, pk_ = pend.pop(0)
                                if ji + 2 < len(ktiles):
                                    pend.append(qk(ktiles[ji + 2]))
                                pb = pT[cn[0] % 3]
                                pbk = "pT%d" % (cn[0] % 3)
                                cn[0] += 1
                                act(pb, ps_, AF.Exp, [pk_], [pbk], scale=0.125)
                                mm(pO[:, 0:512], VA[:, j, g, :], pb, ji == 0, ji == len(ktiles) - 1, ["VA", pbk], [kO])
                            rd = rden[cn[1] % 2]
                            ot = otmp[cn[1] % 2]
                            rk, ok = "rden%d" % (cn[1] % 2), "otmp"
                            cn[1] += 1
                            recip(rd[0:64, :], pO[64:128, 0:256], [kO], [rk])
                            recip(rd[64:128, :], pO[64:128, 256:512], [kO], [rk])
                            tt("dve", ot[0:64, :], pO[0:64, 0:256], rd[0:64, :], ALU.mult, [kO, rk], [ok])
                            tt("dve", ot[64:128, :], pO[0:64, 256:512], rd[64:128, :], ALU.mult, [kO, rk], [ok])
                            tt("pool", yT[:, 2 + ct, :], ot, sbg[:, ct, :], ALU.mult, [ok, "sbg"], ["yT"])

                    def back(qb):
                        t0 = qb * 256
                        latent = qb < 8
                        r = s if latent else 2
                        xk = "xT%d" % qb
                        pm, km = nextps_o()
                        pq, kq = nextps_o()
                        xblk = xT[:, :, t0:t0 + 256]

                        def ln_stats(dt_):
                            mm(pm[:, 0:256], onesb, xm[:, dt_, :], dt_ == 0, dt_ == 7, ["xm_c%d" % dt_, "onesb"], [km], inc=True)
                            mm(pq[:, 0:256], onesb, sqb[:, dt_, :], dt_ == 0, dt_ == 7, ["sqb%d" % dt_, "onesb"], [kq], inc=True)

                        for dt_ in range(8):
                            ps, pk = nextps()
                            for kc in range(8):
                                rhs = yT[:, kc, :] if kc < 6 else ylru[:, kc - 6, t0:t0 + 256]
                                mm(ps[:, 0:256], wo[:, kc, dt_ * 128:(dt_ + 1) * 128], rhs, kc == 0, kc == 7,
                                   ["wo", "yT", "ylru"], [pk], inc=(kc == 7))
                            stt("dve", xT[:, dt_, t0:t0 + 256], ps[:, 0:256], gA[:, l, dt_, r:r + 1], xT[:, dt_, t0:t0 + 256],
                                ALU.mult, ALU.add, [pk, "gA", xk], [xk])
                            act(xm[:, dt_, :], xT[:, dt_, t0:t0 + 256], AF.Copy, [xk], ["xm_c%d" % dt_])
                            act(sqb[:, dt_, :], xT[:, dt_, t0:t0 + 256], AF.Square, [xk], ["sqb%d" % dt_])
                            if dt_ >= 2:
                                ln_stats(dt_ - 2)
                        ln_stats(6)
                        ln_stats(7)
                        copy("dve", mean_sb, pm[:, 0:256], [km], ["mean_sb"])
                        tt("pool", m2, mean_sb, mean_sb, ALU.mult, ["mean_sb"], ["m2"])
                        tt("dve", m2, pq[:, 0:256], m2, ALU.subtract, [kq, "m2"], ["m2"])

                    def back_b(qb):
                        t0 = qb * 256
                        xk = "xT%d" % qb
                        xblk = xT[:, :, t0:t0 + 256]
                        act(rsl, m2, AF.Ln, ["m2", "epsr"], ["rsl"], bias=epsr[:, 1:2], scale=1.0)
                        act(rsl, rsl, AF.Exp, ["rsl"], ["rsl"], scale=-0.5)
                        tln8 = bufBC.rearrange("p (c t) -> p c t", c=8)
                        tt("pool", tln8, xblk, mean_sb.unsqueeze(1).to_broadcast([128, 8, 256]), ALU.subtract,
                           [xk, "mean_sb"], ["bufB", "bufC"])
                        tt("dve", tln8, tln8, rsl.unsqueeze(1).to_broadcast([128, 8, 256]), ALU.mult,
                           ["bufB", "bufC", "rsl"], ["bufB", "bufC"])
                        for dt_ in range(8):
                            sc_, bi_ = vecs[:, l, V_LNG + dt_:V_LNG + dt_ + 1], vecs[:, l, V_LNB + dt_:V_LNB + dt_ + 1]
                            ts("dve" if dt_ % 2 == 0 else "pool", xT[:, dt_, t0:t0 + 256], tln8[:, dt_, :], sc_, bi_,
                               ALU.mult, ALU.add, ["bufB", "bufC", "vecs"], [xk])

                    mod(0)
                    front_a(0)
                    front_b(0)
                    for qb in range(nblk):
                        if qb + 1 < nblk:
                            mod(qb + 1)
                        if qb >= 1:
                            back_b(qb - 1)
                        attn(qb)
                        front_c(qb)
                        if qb + 1 < nblk:
                            front_a(qb + 1)
                        back(qb)
                        if qb + 1 < nblk:
                            front_b(qb + 1)
                    back_b(nblk - 1)
            except _Stop:
                pass
            for c in range(8):
                P.dma("sp", "st_o", xout[s, :, c, :], xT[:, c, :], reads=ALLX)
        P.wait_all("sp", ["st_o", "st_d"])
        block = es.enter_context(nc.Block())
        P.emit(block)
        build_program.last_stats = (P.ninst, dict(P.cnt), C.hi)
    return nc


def _rope_tables():
    half, nf = 32, 16
    inv = (10000.0 ** (-np.arange(nf, dtype=np.float32) / nf)).astype(np.float32)
    t = np.arange(L_LAT)
    rows = (t // 64).astype(np.float32)
    cols = (t % 64).astype(np.float32)
    cos = np.zeros((64, L_LAT), np.float32)
    sin = np.zeros((64, L_LAT), np.float32)
    for d in range(64):
        pos = rows if d < 32 else cols
        f = d % 16
        ang = (pos * inv[f]).astype(np.float32)
        cos[d] = np.cos(ang)
        sgn = -1.0 if (d % 32) < 16 else 1.0
        sin[d] = sgn * np.sin(ang)
    return np.tile(cos, (2, 1)), np.tile(sin, (2, 1))


def _consts():
    ident = np.eye(128, dtype=np.float32)
    onesD = np.full((128, 128), 1.0 / 1024.0, np.float32)
    c32 = np.concatenate([ident, onesD], 1)
    R = np.zeros((128, 128), np.float32)
    for m in range(128):
        d = m % 64
        partner = d + 16 if (d % 32) < 16 else d - 16
        R[(m // 64) * 64 + partner, m] = 1.0
    B = np.zeros((128, 128), np.float32)
    B[0:64, 0:64] = 1.0 / 64.0
    B[64:128, 64:128] = 1.0 / 64.0
    cos, sin = _rope_tables()
    c16 = np.concatenate([R, B, cos, sin], 1)
    return np.ascontiguousarray(c32), np.ascontiguousarray(c16)


def _kc_layout(w):
    Ln, K, N = w.shape
    return np.ascontiguousarray(w.reshape(Ln, 8, 128, N).transpose(0, 2, 1, 3))


def prep_shared(inp):
    f = lambda k: np.asarray(inp[k], dtype=np.float32)
    w_in = f("w_in")
    a_u, a_v, a_g = w_in[:, :, 0:256], w_in[:, :, 256:512], w_in[:, :, 512:768]
    q, k, v = w_in[:, :, 768:1280], w_in[:, :, 1280:1408], w_in[:, :, 1408:1536]
    b_g, r_x, r_g = w_in[:, :, 1536:2048], w_in[:, :, 2048:2304], w_in[:, :, 2304:2560]
    k0, k1 = k[:, :, 0:64], k[:, :, 64:128]
    wperm = np.concatenate([k0, k0, k1, k1, v, r_x, r_g, a_u, a_g, a_v, q, b_g], axis=2)
    assert wperm.shape[2] == NC1 + NC2
    sh = {}
    sh["w_in"] = _kc_layout(wperm)
    sh["w_o"] = _kc_layout(f("w_o"))
    sh["w_ada"] = _kc_layout(f("w_ada"))
    sh["b_adaT"] = np.ascontiguousarray(f("b_ada").reshape(DEPTH, 24, 128).transpose(2, 0, 1))
    fm8 = lambda a: a.reshape(DEPTH, -1, 128).transpose(2, 0, 1)
    vec = np.zeros((128, DEPTH, NV), np.float32)
    vec[:, :, V_LNG:V_LNG + 8] = fm8(f("ln_g"))
    vec[:, :, V_LNB:V_LNB + 8] = fm8(f("ln_b"))
    vec[:, :, V_QG] = np.tile(f("q_norm_g"), (1, 2)).T
    vec[:, :, V_KG] = np.tile(f("k_norm_g"), (1, 2)).T
    cw = f("conv_w")
    for ct in range(2):
        for j in range(4):
            vec[:, :, V_CW + ct * 4 + j] = cw[:, j, ct * 128:(ct + 1) * 128].T
    vec[:, :, V_CB:V_CB + 2] = fm8(f("conv_b"))
    for nm, base in (("lru_br", V_BR), ("lru_bi", V_BI), ("lru_lam", V_LAM)):
        a = f(nm)
        for dd in range(2):
            for ct in range(2):
                vec[:, :, base + dd * 2 + ct] = a[:, dd, ct * 128:(ct + 1) * 128].T
    sh["vecs"] = vec
    an = np.stack([f("a_norm_g"), f("a_norm_b")], 1)
    sh["anorm"] = np.ascontiguousarray(np.broadcast_to(an[:, None], (DEPTH, 128, 2, 256)))
    bs = f("a_bs")
    ab = np.zeros((DEPTH, 128, 2, 128), np.float32)
    for ct in range(2):
        ab[:, 0:64, ct, :] = bs[:, 2 * ct, None, :]
        ab[:, 64:128, ct, :] = bs[:, 2 * ct + 1, None, :]
    sh["absb"] = ab
    sh["awsT"] = np.ascontiguousarray(f("a_ws").transpose(0, 3, 1, 2))
    lw = np.zeros((DEPTH, 128, 8, 128), np.float32)
    for dd in range(2):
        for gi, nm in enumerate(("lru_wr", "lru_wi")):
            w = f(nm)
            for ct in range(2):
                idx = (dd * 2 + gi) * 2 + ct
                for hb in range(2):
                    lw[:, hb * 64:(hb + 1) * 64, idx, hb * 64:(hb + 1) * 64] = w[:, dd, 2 * ct + hb]
    sh["lruW"] = lw
    sh["cst32"], sh["cst16"] = _consts()
    return sh


def prep_core(inp, b0, nseq):
    x = np.asarray(inp["x"], np.float32)
    ctx = np.asarray(inp["ctx"], np.float32)
    c = np.asarray(inp["c"], np.float32)
    c_ctx = np.asarray(inp["c_ctx"], np.float32)
    xin = np.empty((nseq, 128, 8, TOK), np.float32)
    for s in range(nseq):
        full = np.concatenate([x[b0 + s], ctx[b0 + s]], 0)
        xin[s] = full.T.reshape(8, 128, TOK).transpose(1, 0, 2)
    rows = [c[b0 + s] if s < nseq else c_ctx for s in range(2)] + [c_ctx]
    crow = np.stack(rows, 0)
    cT = np.ascontiguousarray(crow.T.reshape(8, 128, 3).transpose(1, 0, 2))
    return {"xin": xin, "cT": cT}


_PROG_CACHE = {}


def kernel(**inputs):
    n = 8
    nseq = 2
    key = ("full", nseq)
    if key not in _PROG_CACHE:
        _PROG_CACHE[key] = build_program(list(range(DEPTH)), nseq)
    nc = _PROG_CACHE[key]
    sh = prep_shared(inputs)
    in_maps = []
    for i in range(n):
        m = dict(sh)
        m.update(prep_core(inputs, i * nseq, nseq))
        in_maps.append(m)
    res = run_bass_kernel_spmd(nc, in_maps, core_ids=list(range(n)))
    out = np.empty((16, L_LAT, D), np.float32)
    for i in range(n):
        xo = np.asarray(res.results[i]["xout"])
        for s in range(nseq):
            out[i * nseq + s] = xo[s][:, :, 0:L_LAT].transpose(2, 1, 0).reshape(L_LAT, D)
    return out
```

```python
import math
import numpy as np
from contextlib import ExitStack
import concourse.bass as bass
import concourse.mybir as mybir
from concourse.bass_utils import run_bass_kernel_spmd

F32 = mybir.dt.float32
BF16 = mybir.dt.bfloat16
AF = mybir.ActivationFunctionType
ALU = mybir.AluOpType

D = 1024
L_LAT = 2048
L_CTX = 256
TOK = L_LAT + L_CTX
DEPTH = 4
ALPHA = (2.0 * DEPTH) ** 0.25
LN_EPS = 1e-6
RMS_EPS = 1e-6
NC1 = 896
NC2 = 1792
C_K0, C_K1, C_V, C_RX, C_RG = 0, 128, 256, 384, 640
C_U, C_GA, C_VA, C_Q, C_BG = 0, 256, 512, 768, 1280
NV = 40
V_LNG, V_LNB, V_QG, V_KG, V_CW, V_CB, V_BR, V_BI, V_LAM = 0, 8, 16, 17, 18, 26, 28, 32, 36

ENGS = ("pe", "act", "dve", "pool", "sp")


class Prog:
    def __init__(self, nc, es):
        self.nc = nc
        self.es = es
        self.q = {e: [] for e in ENGS}
        self.cnt = {}
        self.sem = {}
        for e in ENGS:
            self.sem[e] = es.enter_context(nc.semaphore("c_" + e))
            self.cnt[e] = 0
        self.known = {e: {} for e in ENGS}
        self.lastw = {}
        self.readers = {}
        self.pending = {e: False for e in ENGS}
        self.ninst = 0

    def newsem(self, name):
        self.sem[name] = self.es.enter_context(self.nc.semaphore(name))
        self.cnt[name] = 0
        return name

    def _deps(self, eng, reads, writes):
        deps = {}

        def add(sk, v):
            if v > deps.get(sk, 0):
                deps[sk] = v

        for k in reads:
            w = self.lastw.get(k)
            if w is not None:
                add(*w)
        for k in writes:
            w = self.lastw.get(k)
            if w is not None:
                add(*w)
            for sk, v in self.readers.get(k, {}).items():
                add(sk, v)
        if eng == "pe":
            deps.pop("pe", None)
        return deps

    def _emit_waits(self, eng, deps):
        kn = self.known[eng]
        for sk, v in deps.items():
            if sk not in ENGS:
                v = max(v, self.cnt[sk])
            if sk == eng and v > self.cnt[eng]:
                raise RuntimeError("self-dependency on pending op: " + eng)
            if kn.get(sk, 0) >= v:
                continue
            kn[sk] = v
            sem = self.sem[sk]
            self.q[eng].append(lambda E, sem=sem, v=v: E.wait_ge(sem, v))

    def op(self, eng, fn, reads=(), writes=(), inc=True):
        deps = self._deps(eng, reads, writes)
        self._emit_waits(eng, deps)
        n = self.cnt[eng] + 1
        for k in reads:
            self.readers.setdefault(k, {})[eng] = n
        for k in writes:
            self.lastw[k] = (eng, n)
            self.readers[k] = {}
        self.ninst += 1
        if inc:
            self.cnt[eng] = n
            sem = self.sem[eng]
            self.q[eng].append(lambda E, sem=sem: fn(E).then_inc(sem, 1))
            self.pending[eng] = False
        else:
            self.q[eng].append(lambda E: fn(E))
            self.pending[eng] = True

    def dma(self, queue, semname, out, in_, reads=(), writes=()):
        deps = self._deps(queue, reads, writes)
        self._emit_waits(queue, deps)
        n = self.cnt[semname] + 16
        self.cnt[semname] = n
        for k in reads:
            self.readers.setdefault(k, {})[semname] = n
        for k in writes:
            self.lastw[k] = (semname, n)
            self.readers[k] = {}
        sem = self.sem[semname]
        self.q[queue].append(lambda E, sem=sem: E.dma_start(out=out, in_=in_).then_inc(sem, 16))
        self.ninst += 1

    def fence(self):
        for e in ENGS:
            kn = self.known[e]
            for s, v in self.cnt.items():
                if s == e or v == 0 or kn.get(s, 0) >= v:
                    continue
                kn[s] = v
                sem = self.sem[s]
                self.q[e].append(lambda E, sem=sem, v=v: E.wait_ge(sem, v))

    def wait_all(self, eng, semnames):
        for s in semnames:
            v = self.cnt[s]
            if v > 0 and self.known[eng].get(s, 0) < v:
                self.known[eng][s] = v
                sem = self.sem[s]
                self.q[eng].append(lambda E, sem=sem, v=v: E.wait_ge(sem, v))

    def emit(self, block):
        for e in ENGS:
            assert not self.pending[e], "engine %s ends with un-inc'd op" % e
        q = self.q

        @block.tensor
        def _(E):
            for f in q["pe"]:
                f(E)

        @block.scalar
        def _(E):
            for f in q["act"]:
                f(E)

        @block.vector
        def _(E):
            for f in q["dve"]:
                f(E)

        @block.gpsimd
        def _(E):
            for f in q["pool"]:
                f(E)

        @block.sync
        def _(E):
            for f in q["sp"]:
                f(E)


class Carver:
    def __init__(self, scr, nbytes):
        self.scr = scr
        self.cap = nbytes
        self.off = 0
        self.hi = 0

    def alloc(self, shape, dt):
        esz = 4 if dt == F32 else 2
        n = 1
        for s in shape:
            n *= s
        nb = (n * esz + 31) // 32 * 32
        assert self.off + nb <= self.cap, ("SBUF scratch overflow", self.off, nb, self.cap)
        w0 = self.off // 4
        ap = self.scr[:, w0:w0 + nb // 4]
        if dt != F32:
            ap = ap.bitcast(dt)
        ap = ap[:, 0:n]
        if len(shape) == 2:
            ap = ap.rearrange("p (a b) -> p a b", a=shape[0])
        elif len(shape) == 3:
            ap = ap.rearrange("p (a b c) -> p a b c", a=shape[0], b=shape[1])
        self.off += nb
        self.hi = max(self.hi, self.off)
        return ap


def rev_ap(ap_):
    n = ap_.shape[1]
    pp = ap_.ap[0]
    return bass.AP(ap_.tensor, ap_.offset + (n - 1), [[pp[0], pp[1]], [-1, n]])


class _Stop(Exception):
    pass


def build_program(layers, nseq, first_in_program=True, dbg=None, stop=99):
    nc = bass.Bass("TRN2", target_bir_lowering=False)
    NL = DEPTH
    dram = lambda name, shape, dt=F32, kind="ExternalInput": nc.dram_tensor(name, shape, dt, kind=kind).ap()
    xin = dram("xin", [nseq, 128, 8, TOK])
    cT_d = dram("cT", [128, 8, 3])
    wada_d = dram("w_ada", [NL, 128, 8, 3072])
    bada_d = dram("b_adaT", [128, NL, 24])
    win_d = dram("w_in", [NL, 128, 8, NC1 + NC2])
    wo_d = dram("w_o", [NL, 128, 8, 1024])
    vecs_d = dram("vecs", [128, NL, NV])
    anorm_d = dram("anorm", [NL, 128, 2, 256])
    absb_d = dram("absb", [NL, 128, 2, 128])
    aws_d = dram("awsT", [NL, 128, 4, 128])
    lruw_d = dram("lruW", [NL, 128, 8, 128])
    c32_d = dram("cst32", [128, 256])
    c16_d = dram("cst16", [128, 256 + 2 * L_LAT])
    xout = dram("xout", [nseq, 128, 8, TOK], kind="ExternalOutput")
    dbg_out = {}
    if dbg:
        for name, shape in dbg.items():
            dbg_out[name] = dram("dbg_" + name, shape, kind="ExternalOutput")

    with ExitStack() as es:
        CAP = 212800
        scr = es.enter_context(nc.sbuf_tensor("scr", [128, CAP // 4], F32))
        psall = es.enter_context(nc.psum_tensor("psall", [128, 4096], F32))[:, :]
        PSB = [psall[:, i * 512:(i + 1) * 512] for i in range(8)]
        P = Prog(nc, es)
        for s in ("ld_x", "ld_c", "ld_w1", "ld_w2", "ld_wo", "ld_ada0", "ld_ada1", "ld_lw", "st_o", "st_d"):
            P.newsem(s)
        C = Carver(scr, CAP)

        xT = C.alloc([8, TOK], F32)
        tabs = C.alloc([2, L_LAT], BF16)
        KT = C.alloc([2, TOK], BF16)
        VA = C.alloc([18, 2, 128], BF16)
        ylru = C.alloc([2, TOK], BF16)
        c32 = C.alloc([256], F32)[:, :]
        c16 = C.alloc([256], BF16)
        mods = C.alloc([NL, 24, 3], F32)
        sc1 = C.alloc([NL, 8, 3], F32)
        gA = C.alloc([NL, 8, 3], F32)
        vecs = C.alloc([NL, NV], F32)
        cvec = C.alloc([NL, 4], F32)
        badaT = C.alloc([NL, 24], F32)
        anorm = C.alloc([2, 256], F32)
        absb = C.alloc([2, 128], F32)
        awsT = C.alloc([4, 128], BF16)
        lruW = C.alloc([8, 128], BF16)
        sil = C.alloc([8, 3], F32)
        epsr = C.alloc([4], F32)
        onesb = C.alloc([128], BF16)
        PBASE = C.off
        ident = c32[:, 0:128]
        onesD = c32[:, 128:256]
        Rmat = c16[:, 0:128]
        Bones = c16[:, 128:256]
        cosT = tabs[:, 0, :]
        sinT = tabs[:, 1, :]

        psi = [0, 0, 0]

        def nextps():
            i = psi[0] % 4
            psi[0] += 1
            return PSB[i], "ps%d" % i

        def nextps2():
            i = ((psi[0] + 1) // 2 * 2) % 4
            psi[0] = i + 2
            return psall[:, i * 512:(i + 2) * 512], ["ps%d" % i, "ps%d" % (i + 1)]

        def nextps_s():
            i = 3 + psi[1] % 3
            psi[1] += 1
            return PSB[i], "ps%d" % i

        def nextps_o():
            i = 6 + psi[2] % 2
            psi[2] += 1
            return PSB[i], "ps%d" % i

        def act(out, in_, func, r, w, **kw):
            P.op("act", lambda E: E.activation(out=out, in_=in_, func=func, **kw), r, w)

        def tt(eng, out, a, b, op, r, w):
            P.op(eng, lambda E: E.tensor_tensor(out=out, in0=a, in1=b, op=op), r, w)

        def ts(eng, out, a, s1, s2, op0, op1, r, w):
            if s2 is None:
                P.op(eng, lambda E: E.tensor_single_scalar(out=out, in_=a, scalar=s1, op=op0), r, w)
            else:
                P.op(eng, lambda E: E.tensor_scalar(out=out, in0=a, scalar1=s1, scalar2=s2, op0=op0, op1=op1), r, w)

        def scan(out, d0, d1, init, r, w):
            P.op("dve", lambda E: E.tensor_tensor_scan(out=out, data0=d0, data1=d1, initial=init, op0=ALU.mult, op1=ALU.add), r, w)

        def stt(eng, out, a, s, b, op0, op1, r, w):
            P.op(eng, lambda E: E.scalar_tensor_tensor(out=out, in0=a, scalar=s, in1=b, op0=op0, op1=op1), r, w)

        def mm(out, lhsT, rhs, start, stop, r, w, inc=True):
            P.op("pe", lambda E: E.matmul(out, lhsT, rhs, start=start, stop=stop), r, w, inc=inc)

        def recip(out, in_, r, w):
            P.op("dve", lambda E: E.reciprocal(out=out, in_=in_), r, w)

        def memset(eng, ap, val, w):
            P.op(eng, lambda E: E.memset(ap, val), (), w)

        def copy(eng, out, in_, r, w):
            P.op(eng, lambda E: E.tensor_copy(out=out, in_=in_), r, w)

        P.dma("sp", "ld_c", c32, c32_d[:, :], writes=["c32"])
        P.dma("pool", "ld_lw", c16, c16_d[:, 0:256], writes=["c16"])
        for a_ in range(2):
            P.dma("pool", "ld_lw", tabs[:, a_, :], c16_d[:, 256 + a_ * L_LAT:256 + (a_ + 1) * L_LAT], writes=["tabs"])
        P.dma("sp", "ld_c", vecs, vecs_d[:, :, :], writes=["vecs"])
        P.dma("sp", "ld_c", badaT, bada_d[:, :, :], writes=["badaT"])
        P.dma("sp", "ld_c", sil, cT_d[:, :, :], writes=["sil"])
        memset("dve", epsr[:, 0:1], RMS_EPS, ["epsr"])
        memset("dve", epsr[:, 1:2], LN_EPS / (ALPHA * ALPHA), ["epsr"])
        memset("dve", epsr[:, 2:3], 1.0, ["epsr"])
        memset("dve", epsr[:, 3:4], 0.0, ["epsr"])
        memset("dve", onesb, 1.0 / 1024.0, ["onesb"])
        memset("pool", VA[:, :, :, 64:128], 1.0, ["VA"])

        act(sil, sil, AF.Silu, ["sil"], ["sil"])
        for l in layers:
            act(cvec[:, l, :], vecs[:, l, V_LAM:V_LAM + 4], AF.Exp, ["vecs"], ["cvec"], scale=-1.0)
            act(cvec[:, l, :], cvec[:, l, :], AF.Ln, ["cvec", "epsr"], ["cvec"], bias=epsr[:, 2:3])
            ts("dve", cvec[:, l, :], cvec[:, l, :], -8.0, None, ALU.mult, ALU.bypass, ["cvec"], ["cvec"])
        C.off = PBASE
        stg = [C.alloc([8, 512], F32) for _ in range(2)]
        modrow = C.alloc([3072], F32)[:, :]
        bi = 0
        for l in layers:
            for nb in range(6):
                sb_ = stg[bi % 2]
                key = "stg%d" % (bi % 2)
                P.dma("sp", "ld_ada%d" % (bi % 2), sb_, wada_d[l, :, :, nb * 512:(nb + 1) * 512], writes=[key])
                ps, pk = nextps()
                for kc in range(8):
                    mm(ps[0:3, :], sil[:, kc, :], sb_[:, kc, :], kc == 0, kc == 7, [key, "sil"], [pk], inc=(kc == 7))
                copy("dve", modrow[0:3, nb * 512:(nb + 1) * 512], ps[0:3, :], [pk], ["modrow"])
                bi += 1
            ps, pk = nextps()
            for j in range(24):
                mm(ps[:, j * 3:(j + 1) * 3], modrow[0:3, j * 128:(j + 1) * 128], ident[0:3, 0:3], True, True,
                   ["modrow", "c32"], [pk], inc=(j == 23))
            tt("dve", mods[:, l, :, :], ps[:, 0:72].rearrange("p (a b) -> p a b", a=24),
               badaT[:, l, :].unsqueeze(2).to_broadcast([128, 24, 3]), ALU.add, [pk, "badaT"], ["mods"])
            ts("dve", sc1[:, l, :, :], mods[:, l, 8:16, :], 1.0, None, ALU.add, ALU.bypass, ["mods"], ["sc1"])
            ts("dve", gA[:, l, :, :], mods[:, l, 16:24, :], 1.0 / ALPHA, None, ALU.mult, ALU.bypass, ["mods"], ["gA"])
        P.fence()

        def rms_rope(zps, zk, n, gvec, dsts, dstk, t0, latent, S):
            act(S["sq"][:, 0:n], zps[:, 0:n], AF.Square, [zk], ["r_sq"])
            ps2, pk2 = nextps()
            mm(ps2[:, 0:n], Bones, S["sq"][:, 0:n], True, True, ["r_sq", "c16"], [pk2])
            act(S["sd"][:, 0:n], ps2[:, 0:n], AF.Ln, [pk2, "epsr"], ["r_sd"], bias=epsr[:, 0:1])
            act(S["sd"][:, 0:n], S["sd"][:, 0:n], AF.Exp, ["r_sd"], ["r_sd"], scale=-0.5)
            if not latent:
                for (lo, hi_, dst) in dsts:
                    stt("dve", dst, zps[lo:hi_, 0:n], gvec[lo:hi_, :], S["sd"][lo:hi_, 0:n], ALU.mult, ALU.mult,
                        [zk, "r_sd", "vecs"], [dstk])
                return
            stt("dve", S["qn"][:, 0:n], zps[:, 0:n], gvec, S["sd"][:, 0:n], ALU.mult, ALU.mult,
                [zk, "r_sd", "vecs"], ["r_qn"])
            ps3, pk3 = nextps()
            mm(ps3[:, 0:n], Rmat, S["qn"][:, 0:n], True, True, ["r_qn", "c16"], [pk3])
            tt("pool", S["t1"][:, 0:n], S["qn"][:, 0:n], cosT[:, t0:t0 + n], ALU.mult, ["r_qn", "tabs"], ["r_t1"])
            tt("dve", S["t2"][:, 0:n], ps3[:, 0:n], sinT[:, t0:t0 + n], ALU.mult, [pk3, "tabs"], ["r_t2"])
            for (lo, hi_, dst) in dsts:
                tt("pool", dst, S["t1"][lo:hi_, 0:n], S["t2"][lo:hi_, 0:n], ALU.add, ["r_t1", "r_t2"], [dstk])

        def xkeys(t0, n):
            return ["xT%d" % b for b in range(t0 // 256, (t0 + n + 255) // 256)]

        ALLX = ["xT%d" % b for b in range(9)]

        def modulate(dst, dk, l, r, t0, n, first=False, use_act=False):
            for c in range(8):
                eng = "dve" if (first or c % 2 == 1) else "pool"
                if use_act and c % 2 == 0:
                    act(dst[:, c, 0:n], xT[:, c, t0:t0 + n], AF.Identity, xkeys(t0, n) + ["sc1", "mods"],
                        ["%s_c%d" % (dk, c)], scale=sc1[:, l, c, r:r + 1], bias=mods[:, l, c, r:r + 1])
                    continue
                ts(eng, dst[:, c, 0:n], xT[:, c, t0:t0 + n], sc1[:, l, c, r:r + 1], mods[:, l, c, r:r + 1],
                   ALU.mult, ALU.add, xkeys(t0, n) + ["sc1", "mods"], ["%s_c%d" % (dk, c)])

        def inproj_fm(ps, pk, W, wk, col0, xm, xk, n):
            for kc in range(8):
                mm(ps[:, 0:n], W[:, kc, col0:col0 + 128], xm[:, kc, 0:n], kc == 0, kc == 7, [wk, "%s_c%d" % (xk, kc)], [pk], inc=(kc == 7))

        def stage(n):
            if n > stop:
                raise _Stop()

        W1P = (("k", 0, 256), ("v", 256, 128), ("rx", 384, 256), ("rg", 640, 256))
        W2P = (("u", C_U, 256), ("va", C_VA, 256), ("ga", C_GA, 256), ("bg", C_BG, 512), ("q", C_Q, 512))
        for nm, _, _ in W1P:
            P.newsem("ld_w1" + nm)
        for nm, _, _ in W2P:
            P.newsem("ld_w2" + nm)

        for s in range(nseq):
            try:
                for c in range(8):
                    P.dma("sp", "ld_x", xT[:, c, :], xin[s, :, c, :], writes=ALLX)
                for l in layers:
                    last = (l == DEPTH - 1)
                    P.dma("sp", "ld_c", anorm, anorm_d[l, :, :, :], writes=["anorm"])
                    P.dma("sp", "ld_c", absb, absb_d[l, :, :, :], writes=["absb"])
                    P.dma("pool", "ld_lw", awsT, aws_d[l, :, :, :], writes=["awsT"])
                    P.dma("pool", "ld_lw", lruW, lruw_d[l, :, :, :], writes=["lruW"])
                    stage(1)
                    P.fence()
                    C.off = PBASE
                    win1 = C.alloc([8, NC1], BF16)
                    xm1 = C.alloc([8, 512], BF16)
                    S1 = {"sq": C.alloc([512], BF16)[:, :], "sd": C.alloc([512], F32)[:, :],
                          "qn": C.alloc([512], BF16)[:, :], "t1": C.alloc([512], F32)[:, :]}
                    assert C.off - PBASE == NC2 * 8 * 2, (C.off - PBASE)
                    S1["t2"] = C.alloc([512], F32)[:, :]
                    sg = C.alloc([2, TOK], BF16)
                    xrb = C.alloc([2, TOK], BF16)
                    GBASE = C.off
                    rx = C.alloc([2, 2312], F32)
                    acc = C.alloc([TOK], F32)[:, :]
                    xm1b = C.alloc([8, 512], BF16)
                    XM1 = [(xm1, "xm1"), (xm1b, "xm1b")]
                    P1B = ((0, 512), (512, 512), (1024, 512), (1536, 512), (2048, 256))
                    for nm, c0, ncl in W1P:
                        P.dma("pool", "ld_w1" + nm, win1[:, :, c0:c0 + ncl], win_d[l, :, :, c0:c0 + ncl], writes=["win1" + nm])
                    memset("pool", rx[:, :, 0:2], 0.0, ["rx"])
                    memset("pool", rx[:, :, 2050:2052], 0.0, ["rx"])
                    memset("pool", rx[:, :, 2308:2312], 0.0, ["rx"])
                    modulate(xm1, "xm1", l, s, 0, 512, first=True, use_act=True)
                    for bi1, (t0, n) in enumerate(P1B):
                        latent = t0 < L_LAT
                        r = s if latent else 2
                        xm1c, xk1 = XM1[bi1 % 2]
                        bk = lambda i: (PSB[i], "ps%d" % i)
                        gk = vecs[:, l, V_KG:V_KG + 1]
                        off = 2 + t0 if latent else 2052 + (t0 - L_LAT)
                        nt = n // 128
                        zK = [bk(0), bk(1)]
                        for g in range(2):
                            inproj_fm(zK[g][0], zK[g][1], win1, "win1k", C_K0 + g * 128, xm1c, xk1, n)
                        psV, pkV = bk(2)
                        for tl in range(nt):
                            for kc in range(8):
                                mm(psV[:, tl * 128:(tl + 1) * 128], xm1c[:, kc, tl * 128:(tl + 1) * 128], win1[:, kc, C_V:C_V + 128],
                                   kc == 0, kc == 7, ["win1v", "%s_c%d" % (xk1, kc)], [pkV], inc=(kc == 7))
                        zX = [bk(3), bk(4)]
                        for ct in range(2):
                            inproj_fm(zX[ct][0], zX[ct][1], win1, "win1rx", C_RX + ct * 128, xm1c, xk1, n)
                        sqK = [S1["sq"], S1["qn"]]
                        for g in range(2):
                            act(sqK[g][:, 0:n], zK[g][0][:, 0:n], AF.Square, [zK[g][1]], ["r_sq%d" % g])
                        if bi1 + 1 < len(P1B):
                            t0n, nn = P1B[bi1 + 1]
                            modulate(XM1[(bi1 + 1) % 2][0], XM1[(bi1 + 1) % 2][1], l, (s if t0n < L_LAT else 2), t0n, nn,
                                     first=True, use_act=True)
                        for ct in range(2):
                            act(rx[:, ct, off:off + n], zX[ct][0][:, 0:n], AF.Copy, [zX[ct][1]], ["rx"])
                        copy("dve", VA[:, t0 // 128:t0 // 128 + nt, :, 0:64],
                             psV[:, 0:nt * 128].rearrange("p (t g d) -> p t g d", t=nt, g=2), [pkV], ["VA"])
                        mK = [bk(7), bk(3)]
                        mm(mK[0][0][:, 0:n], Bones, sqK[0][:, 0:n], True, True, ["r_sq0", "c16"], [mK[0][1]])
                        zG = [bk(5), bk(6)]
                        for ct in range(2):
                            inproj_fm(zG[ct][0], zG[ct][1], win1, "win1rg", C_RG + ct * 128, xm1c, xk1, n)
                        mm(mK[1][0][:, 0:n], Bones, sqK[1][:, 0:n], True, True, ["r_sq1", "c16"], [mK[1][1]])
                        rsK = [S1["sd"], S1["t1"]]
                        for g in range(2):
                            act(rsK[g][:, 0:n], mK[g][0][:, 0:n], AF.Ln, [mK[g][1], "epsr"], ["r_sd%d" % g], bias=epsr[:, 0:1])
                            act(rsK[g][:, 0:n], rsK[g][:, 0:n], AF.Exp, ["r_sd%d" % g], ["r_sd%d" % g], scale=-0.5)
                        for g in range(2):
                            dstK = KT[:, g, t0:t0 + n]
                            if latent:
                                stt("dve", sqK[g][:, 0:n], zK[g][0][:, 0:n], gk, rsK[g][:, 0:n], ALU.mult, ALU.mult,
                                    [zK[g][1], "r_sd%d" % g, "vecs"], ["r_sq%d" % g])
                            else:
                                stt("dve", dstK, zK[g][0][:, 0:n], gk, rsK[g][:, 0:n], ALU.mult, ALU.mult,
                                    [zK[g][1], "r_sd%d" % g, "vecs"], ["KT"])
                        for ct in range(2):
                            act(sg[:, ct, t0:t0 + n], zG[ct][0][:, 0:n], AF.Silu, [zG[ct][1]], ["sg"])
                        if latent:
                            rK = [bk(4), bk(7)]
                            for g in range(2):
                                mm(rK[g][0][:, 0:n], Rmat, sqK[g][:, 0:n], True, True, ["r_sq%d" % g, "c16"], [rK[g][1]])
                            for g in range(2):
                                tt("pool", rsK[g][:, 0:n], sqK[g][:, 0:n], cosT[:, t0:t0 + n], ALU.mult,
                                   ["r_sq%d" % g, "tabs"], ["r_sd%d" % g])
                                tt("dve", S1["t2"][:, 0:n], rK[g][0][:, 0:n], sinT[:, t0:t0 + n], ALU.mult, [rK[g][1], "tabs"], ["r_t2"])
                                tt("pool", KT[:, g, t0:t0 + n], rsK[g][:, 0:n], S1["t2"][:, 0:n], ALU.add,
                                   ["r_sd%d" % g, "r_t2"], ["KT"])
                    stage(2)
                    for ct in range(2):
                        for (d0, o0, n) in ((2, 0, L_LAT), (2052, L_LAT, L_CTX)):
                            cw = lambda j: vecs[:, l, V_CW + ct * 4 + j:V_CW + ct * 4 + j + 1]
                            eng = "dve"
                            ts(eng, acc[:, 0:n], rx[:, ct, d0 - 2:d0 - 2 + n], cw(0), vecs[:, l, V_CB + ct:V_CB + ct + 1],
                               ALU.mult, ALU.add, ["rx", "vecs"], ["acc"])
                            stt(eng, acc[:, 0:n], rx[:, ct, d0 - 1:d0 - 1 + n], cw(1), acc[:, 0:n], ALU.mult, ALU.add,
                                ["rx", "vecs", "acc"], ["acc"])
                            stt(eng, acc[:, 0:n], rx[:, ct, d0:d0 + n], cw(2), acc[:, 0:n], ALU.mult, ALU.add,
                                ["rx", "vecs", "acc"], ["acc"])
                            stt(eng, xrb[:, ct, o0:o0 + n], rx[:, ct, d0 + 1:d0 + 1 + n], cw(3), acc[:, 0:n], ALU.mult, ALU.add,
                                ["rx", "vecs", "acc"], ["xrb"])
                    stage(3)
                    P.fence()
                    C.off = PBASE
                    win2 = C.alloc([8, NC2], BF16)
                    for nm, c0, ncl in W2P:
                        P.dma("pool", "ld_w2" + nm, win2[:, :, c0:c0 + ncl], win_d[l, :, :, NC1 + c0:NC1 + c0 + ncl],
                              writes=["win2" + nm])
                    C.off = GBASE
                    XB = [(C.alloc([TOK], F32)[:, :], "X0"), (C.alloc([TOK], F32)[:, :], "X1")]
                    BB = [(C.alloc([TOK], F32)[:, :], "Bf"), None]
                    Hf = C.alloc([TOK], F32)[:, :]
                    BB[1] = (C.alloc([TOK], BF16)[:, :], "Bh")
                    kit = 0
                    for ct in range(2):
                        for dd in range(2):
                            vi = dd * 2 + ct
                            Ab, Ak = XB[kit % 2]
                            Tb, Tk = XB[(kit + 1) % 2]
                            Bb, Bk = BB[kit % 2]
                            kit += 1
                            for (t0, n) in ((0, 512), (512, 512), (1024, 512), (1536, 512), (2048, 256)):
                                ps, pk = nextps()
                                mm(ps[:, 0:n], lruW[:, (dd * 2 + 0) * 2 + ct, :], xrb[:, ct, t0:t0 + n], True, True,
                                   ["lruW", "xrb"], [pk])
                                act(Ab[:, t0:t0 + n], ps[:, 0:n], AF.Sigmoid, [pk, "vecs"], [Ak],
                                    bias=vecs[:, l, V_BR + vi:V_BR + vi + 1])
                                ps, pk = nextps()
                                mm(ps[:, 0:n], lruW[:, (dd * 2 + 1) * 2 + ct, :], xrb[:, ct, t0:t0 + n], True, True,
                                   ["lruW", "xrb"], [pk])
                                act(Bb[:, t0:t0 + n], ps[:, 0:n], AF.Sigmoid, [pk, "vecs"], [Bk],
                                    bias=vecs[:, l, V_BI + vi:V_BI + vi + 1])
                            act(Ab, Ab, AF.Exp, [Ak, "cvec"], [Ak], scale=cvec[:, l, vi:vi + 1])
                            act(Tb, Ab, AF.Square, [Ak], [Tk])
                            act(Tb, Tb, AF.Ln, [Tk, "epsr"], [Tk], scale=-1.0, bias=epsr[:, 2:3])
                            act(Tb, Tb, AF.Exp, [Tk], [Tk], scale=0.5)
                            tt("dve", Bb, Bb, xrb[:, ct, :], ALU.mult, [Bk, "xrb"], [Bk])
                            tt("dve", Bb, Bb, Tb, ALU.mult, [Bk, Tk], [Bk])
                            if dd == 0:
                                scan(Hf[:, L_LAT:TOK], Ab[:, L_LAT:TOK], Bb[:, L_LAT:TOK], 0.0, [Ak, Bk], ["Hf"])
                                scan(Hf[:, 0:L_LAT], Ab[:, 0:L_LAT], Bb[:, 0:L_LAT], Hf[:, TOK - 1:TOK], [Ak, Bk, "Hf"], ["Hf"])
                            else:
                                scan(rev_ap(Tb), rev_ap(Ab), rev_ap(Bb), 0.0, [Ak, Bk], [Tk])
                        tt("pool", Hf, Hf, Tb, ALU.add, ["Hf", Tk], ["Hf"])
                        tt("pool", ylru[:, ct, :], Hf, sg[:, ct, :], ALU.mult, ["Hf", "sg"], ["ylru"])
                    stage(4)
                    P.fence()
                    C.off = PBASE
                    win2 = C.alloc([8, NC2], BF16)
                    wo = C.alloc([8, 1024], BF16)
                    for kc in range(8):
                        P.dma("pool", "ld_wo", wo[:, kc, :], wo_d[l, :, kc, :], writes=["wo"])
                    xm = C.alloc([8, 256], BF16)
                    yT = C.alloc([6, 256], BF16)
                    ug = C.alloc([2, 256], F32)
                    sga = C.alloc([2, 256], F32)
                    vg = C.alloc([2, 256], F32)
                    vln = C.alloc([2, 256], BF16)
                    tmpg = C.alloc([512], F32)[:, :]
                    st = C.alloc([2, 8], F32)
                    qbd = C.alloc([4, 512], BF16)
                    sbg = C.alloc([4, 256], BF16)
                    bufA = C.alloc([1024], BF16)[:, :]
                    bufBC = C.alloc([2048], F32)[:, :]
                    bufB = bufBC[:, 0:1024]
                    bufC = bufBC[:, 1024:2048]
                    pT = [C.alloc([512], BF16)[:, :] for _ in range(3)]
                    rden = [C.alloc([256], F32)[:, :] for _ in range(2)]
                    otmp = [C.alloc([256], F32)[:, :]] * 2
                    sqb = C.alloc([8, 256], BF16)
                    mean_sb = C.alloc([256], F32)[:, :]
                    m2 = C.alloc([256], F32)[:, :]
                    rsl = C.alloc([256], F32)[:, :]
                    memset("pool", qbd[64:128, :, 0:256], 0.0, ["qbd"])
                    memset("pool", qbd[0:64, :, 256:512], 0.0, ["qbd"])
                    nblk = 8 if last else 9
                    cn = [0, 0]
                    def mod(qb):
                        t0 = qb * 256
                        latent = qb < 8
                        r = s if latent else 2
                        xk = "xT%d" % qb
                        modulate(xm, "xm", l, r, t0, 256, first=(qb == 0))

                    def front_a(qb):
                        t0 = qb * 256
                        latent = qb < 8
                        r = s if latent else 2
                        xk = "xT%d" % qb
                        bank = lambda i: (psall[:, i * 512:(i + 1) * 512], "ps%d" % i)
                        bank2 = lambda i: (psall[:, i * 512:(i + 2) * 512], ["ps%d" % i, "ps%d" % (i + 1)])
                        gq = vecs[:, l, V_QG:V_QG + 1]
                        v4 = lambda ap_, lo, hi_: ap_[lo:hi_, :].rearrange("p (c t) -> p c t", c=4)
                        psQ, pkQ = bank2(0)
                        for ct in range(4):
                            inproj_fm(psQ[:, ct * 256:(ct + 1) * 256], pkQ[ct // 2], win2, "win2q", C_Q + ct * 128, xm, "xm", 256)
                        act(bufA, psQ, AF.Square, pkQ, ["bufA"])
                        psU, pkU = bank(2)
                        for ct in range(2):
                            inproj_fm(psU[:, ct * 256:(ct + 1) * 256], pkU, win2, "win2u", C_U + ct * 128, xm, "xm", 256)
                        psV, pkV = bank(3)
                        for tl in range(2):
                            for kc in range(8):
                                mm(psV[:, tl * 256:(tl + 1) * 256], xm[:, kc, tl * 128:(tl + 1) * 128], win2[:, kc, C_VA:C_VA + 256],
                                   kc == 0, kc == 7, ["win2va", "xm_c%d" % kc], [pkV], inc=(kc == 7))
                        psM, pkM = bank2(4)
                        for hf in range(2):
                            mm(psM[:, hf * 512:(hf + 1) * 512], Bones, bufA[:, hf * 512:(hf + 1) * 512], True, True,
                               ["bufA", "c16"], [pkM[hf]])
                        act(bufB, psM, AF.Ln, pkM + ["epsr"], ["bufB"], bias=epsr[:, 0:1])
                        act(bufB, bufB, AF.Exp, ["bufB"], ["bufB"], scale=-0.5)
                        if latent:
                            stt("dve", bufA, psQ, gq, bufB, ALU.mult, ALU.mult, pkQ + ["bufB", "vecs"], ["bufA"])
                        else:
                            stt("dve", qbd[0:64, :, 0:256], v4(psQ, 0, 64), gq[0:64, :], v4(bufB, 0, 64), ALU.mult, ALU.mult,
                                pkQ + ["bufB", "vecs"], ["qbd"])
                            stt("dve", qbd[64:128, :, 256:512], v4(psQ, 64, 128), gq[64:128, :], v4(bufB, 64, 128), ALU.mult, ALU.mult,
                                pkQ + ["bufB", "vecs"], ["qbd"])
                        act(ug.rearrange("p c t -> p (c t)"), psU, AF.Gelu_apprx_tanh, [pkU], ["ug"])
                        for tl in range(2):
                            act(vg[:, tl, :], psV[:, tl * 256:(tl + 1) * 256], AF.Gelu_apprx_tanh, [pkV], ["vg", "st"],
                                accum_out=st[:, tl, 0:1])
                        ps2b, pk2b = bank2(6)
                        for ct in range(4):
                            inproj_fm(ps2b[:, ct * 256:(ct + 1) * 256], pk2b[ct // 2], win2, "win2bg", C_BG + ct * 128, xm, "xm", 256)
                        psG, pkG = bank(2)
                        for ct in range(2):
                            inproj_fm(psG[:, ct * 256:(ct + 1) * 256], pkG, win2, "win2ga", C_GA + ct * 128, xm, "xm", 256)
                        if latent:
                            psR, pkR = bank2(4)
                            for hf in range(2):
                                mm(psR[:, hf * 512:(hf + 1) * 512], Rmat, bufA[:, hf * 512:(hf + 1) * 512], True, True,
                                   ["bufA", "c16"], [pkR[hf]])
                        act(sga.rearrange("p c t -> p (c t)"), psG, AF.Silu, [pkG], ["sga"])
                        act(sbg.rearrange("p c t -> p (c t)"), ps2b, AF.Silu, pk2b, ["sbg"])
                        tt("pool", ug, ug, sga, ALU.mult, ["ug", "sga"], ["ug"])
                        if latent:
                            cosb = cosT[:, t0:t0 + 256].unsqueeze(1).to_broadcast([128, 4, 256])
                            sinb = sinT[:, t0:t0 + 256].unsqueeze(1).to_broadcast([128, 4, 256])
                            tt("pool", v4(bufB, 0, 128), v4(bufA, 0, 128), cosb, ALU.mult, ["bufA", "tabs"], ["bufB"])
                            tt("dve", v4(bufC, 0, 128), v4(psR, 0, 128), sinb, ALU.mult, pkR + ["tabs"], ["bufC"])
                            tt("pool", qbd[0:64, :, 0:256], v4(bufB, 0, 64), v4(bufC, 0, 64), ALU.add, ["bufB", "bufC"], ["qbd"])
                            tt("pool", qbd[64:128, :, 256:512], v4(bufB, 64, 128), v4(bufC, 64, 128), ALU.add,
                               ["bufB", "bufC"], ["qbd"])

                    def front_b(qb):
                        t0 = qb * 256
                        latent = qb < 8
                        r = s if latent else 2
                        xk = "xT%d" % qb
                        for tl in range(2):
                            act(tmpg[:, 0:256], vg[:, tl, :], AF.Square, ["vg"], ["tmpg", "st"], accum_out=st[:, tl, 1:2])
                            ts("dve", st[:, tl, 2:3], st[:, tl, 0:1], 1.0 / 256.0, None, ALU.mult, ALU.bypass, ["st"], ["st"])
                            tt("dve", st[:, tl, 3:4], st[:, tl, 2:3], st[:, tl, 2:3], ALU.mult, ["st"], ["st"])
                            stt("dve", st[:, tl, 4:5], st[:, tl, 1:2], 1.0 / 256.0, st[:, tl, 3:4], ALU.mult, ALU.subtract,
                                ["st"], ["st"])
                        for tl in range(2):
                            act(st[:, tl, 5:6], st[:, tl, 4:5], AF.Ln, ["st", "epsr"], ["st"], bias=epsr[:, 0:1], scale=1.0)
                            act(st[:, tl, 6:7], st[:, tl, 5:6], AF.Exp, ["st"], ["st"], scale=-0.5)
                        for tl in range(2):
                            ts("dve", vg[:, tl, :], vg[:, tl, :], st[:, tl, 2:3], st[:, tl, 6:7], ALU.subtract, ALU.mult,
                               ["vg", "st"], ["vg"])
                            tt("pool", vg[:, tl, :], vg[:, tl, :], anorm[:, 0, :], ALU.mult, ["vg", "anorm"], ["vg"])
                            tt("pool", vln[:, tl, :], vg[:, tl, :], anorm[:, 1, :], ALU.add, ["vg", "anorm"], ["vln"])

                    def front_c(qb):
                        t0 = qb * 256
                        latent = qb < 8
                        r = s if latent else 2
                        xk = "xT%d" % qb
                        ps, pk = PSB[3], "ps3"
                        for tl in range(2):
                            for ct in range(2):
                                for hh in range(2):
                                    g = 2 * ct + hh
                                    co = (tl * 2 + ct) * 128
                                    mm(ps[hh * 64:(hh + 1) * 64, co:co + 128], vln[:, tl, g * 64:(g + 1) * 64], awsT[:, g, :],
                                       True, True, ["vln", "awsT"], [pk], inc=(tl == 1 and ct == 1 and hh == 1))
                        tt("dve", tmpg.rearrange("p (t c q) -> p t c q", t=2, c=2), ps.rearrange("p (t c q) -> p t c q", t=2, c=2),
                           absb.unsqueeze(1).to_broadcast([128, 2, 2, 128]), ALU.add, [pk, "absb"], ["tmpg"])
                        tt("pool", yT[:, 0:2, :].rearrange("p c (t q) -> p t c q", t=2),
                           tmpg.rearrange("p (t c q) -> p t c q", t=2, c=2),
                           ug.rearrange("p c (t q) -> p t c q", t=2), ALU.mult, ["tmpg", "ug"], ["yT"])

                    def attn(qb):
                        t0 = qb * 256
                        latent = qb < 8
                        r = s if latent else 2
                        xk = "xT%d" % qb
                        ktiles = list(range(18)) if latent else [16, 17]
                        for ct in range(4):
                            g = ct // 2
                            pO, kO = nextps_o()

                            def qk(j):
                                ps_, pk_ = nextps_s()
                                mm(ps_[:, 0:512], KT[:, g, j * 128:(j + 1) * 128], qbd[:, ct, :], True, True,
                                   ["KT", "qbd"], [pk_])
                                return ps_, pk_

                            pend = [qk(j) for j in ktiles[0:2]]
                            for ji, j in enumerate(ktiles):
                                ps_, pk_ = pend.pop(0)
                                if ji + 2 < len(ktiles):
                                    pend.append(qk(ktiles[ji + 2]))
                                pb = pT[cn[0] % 3]
                                pbk = "pT%d" % (cn[0] % 3)
                                cn[0] += 1
                                act(pb, ps_, AF.Exp, [pk_], [pbk], scale=0.125)
                                mm(pO[:, 0:512], VA[:, j, g, :], pb, ji == 0, ji == len(ktiles) - 1, ["VA", pbk], [kO])
                            rd = rden[cn[1] % 2]
                            ot = otmp[cn[1] % 2]
                            rk, ok = "rden%d" % (cn[1] % 2), "otmp"
                            cn[1] += 1
                            recip(rd[0:64, :], pO[64:128, 0:256], [kO], [rk])
                            recip(rd[64:128, :], pO[64:128, 256:512], [kO], [rk])
                            tt("dve", ot[0:64, :], pO[0:64, 0:256], rd[0:64, :], ALU.mult, [kO, rk], [ok])
                            tt("dve", ot[64:128, :], pO[0:64, 256:512], rd[64:128, :], ALU.mult, [kO, rk], [ok])
                            tt("pool", yT[:, 2 + ct, :], ot, sbg[:, ct, :], ALU.mult, [ok, "sbg"], ["yT"])

                    def back(qb):
                        t0 = qb * 256
                        latent = qb < 8
                        r = s if latent else 2
                        xk = "xT%d" % qb
                        pm, km = nextps_o()
                        pq, kq = nextps_o()
                        xblk = xT[:, :, t0:t0 + 256]

                        def ln_stats(dt_):
                            mm(pm[:, 0:256], onesb, xm[:, dt_, :], dt_ == 0, dt_ == 7, ["xm_c%d" % dt_, "onesb"], [km], inc=True)
                            mm(pq[:, 0:256], onesb, sqb[:, dt_, :], dt_ == 0, dt_ == 7, ["sqb%d" % dt_, "onesb"], [kq], inc=True)

                        for dt_ in range(8):
                            ps, pk = nextps()
                            for kc in range(8):
                                rhs = yT[:, kc, :] if kc < 6 else ylru[:, kc - 6, t0:t0 + 256]
                                mm(ps[:, 0:256], wo[:, kc, dt_ * 128:(dt_ + 1) * 128], rhs, kc == 0, kc == 7,
                                   ["wo", "yT", "ylru"], [pk], inc=(kc == 7))
                            stt("dve", xT[:, dt_, t0:t0 + 256], ps[:, 0:256], gA[:, l, dt_, r:r + 1], xT[:, dt_, t0:t0 + 256],
                                ALU.mult, ALU.add, [pk, "gA", xk], [xk])
                            act(xm[:, dt_, :], xT[:, dt_, t0:t0 + 256], AF.Copy, [xk], ["xm_c%d" % dt_])
                            act(sqb[:, dt_, :], xT[:, dt_, t0:t0 + 256], AF.Square, [xk], ["sqb%d" % dt_])
                            if dt_ >= 2:
                                ln_stats(dt_ - 2)
                        ln_stats(6)
                        ln_stats(7)
                        copy("dve", mean_sb, pm[:, 0:256], [km], ["mean_sb"])
                        tt("pool", m2, mean_sb, mean_sb, ALU.mult, ["mean_sb"], ["m2"])
                        tt("dve", m2, pq[:, 0:256], m2, ALU.subtract, [kq, "m2"], ["m2"])

                    def back_b(qb):
                        t0 = qb * 256
                        xk = "xT%d" % qb
                        xblk = xT[:, :, t0:t0 + 256]
                        act(rsl, m2, AF.Ln, ["m2", "epsr"], ["rsl"], bias=epsr[:, 1:2], scale=1.0)
                        act(rsl, rsl, AF.Exp, ["rsl"], ["rsl"], scale=-0.5)
                        tln8 = bufBC.rearrange("p (c t) -> p c t", c=8)
                        tt("pool", tln8, xblk, mean_sb.unsqueeze(1).to_broadcast([128, 8, 256]), ALU.subtract,
                           [xk, "mean_sb"], ["bufB", "bufC"])
                        tt("dve", tln8, tln8, rsl.unsqueeze(1).to_broadcast([128, 8, 256]), ALU.mult,
                           ["bufB", "bufC", "rsl"], ["bufB", "bufC"])
                        for dt_ in range(8):
                            sc_, bi_ = vecs[:, l, V_LNG + dt_:V_LNG + dt_ + 1], vecs[:, l, V_LNB + dt_:V_LNB + dt_ + 1]
                            ts("dve" if dt_ % 2 == 0 else "pool", xT[:, dt_, t0:t0 + 256], tln8[:, dt_, :], sc_, bi_,
                               ALU.mult, ALU.add, ["bufB", "bufC", "vecs"], [xk])

                    mod(0)
                    front_a(0)
                    front_b(0)
                    for qb in range(nblk):
                        if qb + 1 < nblk:
                            mod(qb + 1)
                        if qb >= 1:
                            back_b(qb - 1)
                        attn(qb)
                        front_c(qb)
                        if qb + 1 < nblk:
                            front_a(qb + 1)
                        back(qb)
                        if qb + 1 < nblk:
                            front_b(qb + 1)
                    back_b(nblk - 1)
            except _Stop:
                pass
            for c in range(8):
                P.dma("sp", "st_o", xout[s, :, c, :], xT[:, c, :], reads=ALLX)
        P.wait_all("sp", ["st_o", "st_d"])
        block = es.enter_context(nc.Block())
        P.emit(block)
        build_program.last_stats = (P.ninst, dict(P.cnt), C.hi)
    return nc


def _rope_tables():
    half, nf = 32, 16
    inv = (10000.0 ** (-np.arange(nf, dtype=np.float32) / nf)).astype(np.float32)
    t = np.arange(L_LAT)
    rows = (t // 64).astype(np.float32)
    cols = (t % 64).astype(np.float32)
    cos = np.zeros((64, L_LAT), np.float32)
    sin = np.zeros((64, L_LAT), np.float32)
    for d in range(64):
        pos = rows if d < 32 else cols
        f = d % 16
        ang = (pos * inv[f]).astype(np.float32)
        cos[d] = np.cos(ang)
        sgn = -1.0 if (d % 32) < 16 else 1.0
        sin[d] = sgn * np.sin(ang)
    return np.tile(cos, (2, 1)), np.tile(sin, (2, 1))


def _consts():
    ident = np.eye(128, dtype=np.float32)
    onesD = np.full((128, 128), 1.0 / 1024.0, np.float32)
    c32 = np.concatenate([ident, onesD], 1)
    R = np.zeros((128, 128), np.float32)
    for m in range(128):
        d = m % 64
        partner = d + 16 if (d % 32) < 16 else d - 16
        R[(m // 64) * 64 + partner, m] = 1.0
    B = np.zeros((128, 128), np.float32)
    B[0:64, 0:64] = 1.0 / 64.0
    B[64:128, 64:128] = 1.0 / 64.0
    cos, sin = _rope_tables()
    c16 = np.concatenate([R, B, cos, sin], 1)
    return np.ascontiguousarray(c32), np.ascontiguousarray(c16)


def _kc_layout(w):
    Ln, K, N = w.shape
    return np.ascontiguousarray(w.reshape(Ln, 8, 128, N).transpose(0, 2, 1, 3))


def prep_shared(inp):
    f = lambda k: np.asarray(inp[k], dtype=np.float32)
    w_in = f("w_in")
    a_u, a_v, a_g = w_in[:, :, 0:256], w_in[:, :, 256:512], w_in[:, :, 512:768]
    q, k, v = w_in[:, :, 768:1280], w_in[:, :, 1280:1408], w_in[:, :, 1408:1536]
    b_g, r_x, r_g = w_in[:, :, 1536:2048], w_in[:, :, 2048:2304], w_in[:, :, 2304:2560]
    k0, k1 = k[:, :, 0:64], k[:, :, 64:128]
    wperm = np.concatenate([k0, k0, k1, k1, v, r_x, r_g, a_u, a_g, a_v, q, b_g], axis=2)
    assert wperm.shape[2] == NC1 + NC2
    sh = {}
    sh["w_in"] = _kc_layout(wperm)
    sh["w_o"] = _kc_layout(f("w_o"))
    sh["w_ada"] = _kc_layout(f("w_ada"))
    sh["b_adaT"] = np.ascontiguousarray(f("b_ada").reshape(DEPTH, 24, 128).transpose(2, 0, 1))
    fm8 = lambda a: a.reshape(DEPTH, -1, 128).transpose(2, 0, 1)
    vec = np.zeros((128, DEPTH, NV), np.float32)
    vec[:, :, V_LNG:V_LNG + 8] = fm8(f("ln_g"))
    vec[:, :, V_LNB:V_LNB + 8] = fm8(f("ln_b"))
    vec[:, :, V_QG] = np.tile(f("q_norm_g"), (1, 2)).T
    vec[:, :, V_KG] = np.tile(f("k_norm_g"), (1, 2)).T
    cw = f("conv_w")
    for ct in range(2):
        for j in range(4):
            vec[:, :, V_CW + ct * 4 + j] = cw[:, j, ct * 128:(ct + 1) * 128].T
    vec[:, :, V_CB:V_CB + 2] = fm8(f("conv_b"))
    for nm, base in (("lru_br", V_BR), ("lru_bi", V_BI), ("lru_lam", V_LAM)):
        a = f(nm)
        for dd in range(2):
            for ct in range(2):
                vec[:, :, base + dd * 2 + ct] = a[:, dd, ct * 128:(ct + 1) * 128].T
    sh["vecs"] = vec
    an = np.stack([f("a_norm_g"), f("a_norm_b")], 1)
    sh["anorm"] = np.ascontiguousarray(np.broadcast_to(an[:, None], (DEPTH, 128, 2, 256)))
    bs = f("a_bs")
    ab = np.zeros((DEPTH, 128, 2, 128), np.float32)
    for ct in range(2):
        ab[:, 0:64, ct, :] = bs[:, 2 * ct, None, :]
        ab[:, 64:128, ct, :] = bs[:, 2 * ct + 1, None, :]
    sh["absb"] = ab
    sh["awsT"] = np.ascontiguousarray(f("a_ws").transpose(0, 3, 1, 2))
    lw = np.zeros((DEPTH, 128, 8, 128), np.float32)
    for dd in range(2):
        for gi, nm in enumerate(("lru_wr", "lru_wi")):
            w = f(nm)
            for ct in range(2):
                idx = (dd * 2 + gi) * 2 + ct
                for hb in range(2):
                    lw[:, hb * 64:(hb + 1) * 64, idx, hb * 64:(hb + 1) * 64] = w[:, dd, 2 * ct + hb]
    sh["lruW"] = lw
    sh["cst32"], sh["cst16"] = _consts()
    return sh


def prep_core(inp, b0, nseq):
    x = np.asarray(inp["x"], np.float32)
    ctx = np.asarray(inp["ctx"], np.float32)
    c = np.asarray(inp["c"], np.float32)
    c_ctx = np.asarray(inp["c_ctx"], np.float32)
    xin = np.empty((nseq, 128, 8, TOK), np.float32)
    for s in range(nseq):
        full = np.concatenate([x[b0 + s], ctx[b0 + s]], 0)
        xin[s] = full.T.reshape(8, 128, TOK).transpose(1, 0, 2)
    rows = [c[b0 + s] if s < nseq else c_ctx for s in range(2)] + [c_ctx]
    crow = np.stack(rows, 0)
    cT = np.ascontiguousarray(crow.T.reshape(8, 128, 3).transpose(1, 0, 2))
    return {"xin": xin, "cT": cT}


_PROG_CACHE = {}


def kernel(**inputs):
    n = 8
    nseq = 2
    key = ("full", nseq)
    if key not in _PROG_CACHE:
        _PROG_CACHE[key] = build_program(list(range(DEPTH)), nseq)
    nc = _PROG_CACHE[key]
    sh = prep_shared(inputs)
    in_maps = []
    for i in range(n):
        m = dict(sh)
        m.update(prep_core(inputs, i * nseq, nseq))
        in_maps.append(m)
    res = run_bass_kernel_spmd(nc, in_maps, core_ids=list(range(n)))
    out = np.empty((16, L_LAT, D), np.float32)
    for i in range(n):
        xo = np.asarray(res.results[i]["xout"])
        for s in range(nseq):
            out[i * nseq + s] = xo[s][:, :, 0:L_LAT].transpose(2, 1, 0).reshape(L_LAT, D)
    return out
```

```python
import math
import numpy as np
from contextlib import ExitStack
import concourse.bass as bass
import concourse.mybir as mybir
from concourse.bass_utils import run_bass_kernel_spmd

F32 = mybir.dt.float32
BF16 = mybir.dt.bfloat16
AF = mybir.ActivationFunctionType
ALU = mybir.AluOpType

D = 1024
L_LAT = 2048
L_CTX = 256
TOK = L_LAT + L_CTX
DEPTH = 4
ALPHA = (2.0 * DEPTH) ** 0.25
LN_EPS = 1e-6
RMS_EPS = 1e-6
NC1 = 896
NC2 = 1792
C_K0, C_K1, C_V, C_RX, C_RG = 0, 128, 256, 384, 640
C_U, C_GA, C_VA, C_Q, C_BG = 0, 256, 512, 768, 1280
NV = 40
V_LNG, V_LNB, V_QG, V_KG, V_CW, V_CB, V_BR, V_BI, V_LAM = 0, 8, 16, 17, 18, 26, 28, 32, 36

ENGS = ("pe", "act", "dve", "pool", "sp")


class Prog:
    def __init__(self, nc, es):
        self.nc = nc
        self.es = es
        self.q = {e: [] for e in ENGS}
        self.cnt = {}
        self.sem = {}
        for e in ENGS:
            self.sem[e] = es.enter_context(nc.semaphore("c_" + e))
            self.cnt[e] = 0
        self.known = {e: {} for e in ENGS}
        self.lastw = {}
        self.readers = {}
        self.pending = {e: False for e in ENGS}
        self.ninst = 0

    def newsem(self, name):
        self.sem[name] = self.es.enter_context(self.nc.semaphore(name))
        self.cnt[name] = 0
        return name

    def _deps(self, eng, reads, writes):
        deps = {}

        def add(sk, v):
            if v > deps.get(sk, 0):
                deps[sk] = v

        for k in reads:
            w = self.lastw.get(k)
            if w is not None:
                add(*w)
        for k in writes:
            w = self.lastw.get(k)
            if w is not None:
                add(*w)
            for sk, v in self.readers.get(k, {}).items():
                add(sk, v)
        if eng == "pe":
            deps.pop("pe", None)
        return deps

    def _emit_waits(self, eng, deps):
        kn = self.known[eng]
        for sk, v in deps.items():
            if sk not in ENGS:
                v = max(v, self.cnt[sk])
            if sk == eng and v > self.cnt[eng]:
                raise RuntimeError("self-dependency on pending op: " + eng)
            if kn.get(sk, 0) >= v:
                continue
            kn[sk] = v
            sem = self.sem[sk]
            self.q[eng].append(lambda E, sem=sem, v=v: E.wait_ge(sem, v))

    def op(self, eng, fn, reads=(), writes=(), inc=True):
        deps = self._deps(eng, reads, writes)
        self._emit_waits(eng, deps)
        n = self.cnt[eng] + 1
        for k in reads:
            self.readers.setdefault(k, {})[eng] = n
        for k in writes:
            self.lastw[k] = (eng, n)
            self.readers[k] = {}
        self.ninst += 1
        if inc:
            self.cnt[eng] = n
            sem = self.sem[eng]
            self.q[eng].append(lambda E, sem=sem: fn(E).then_inc(sem, 1))
            self.pending[eng] = False
        else:
            self.q[eng].append(lambda E: fn(E))
            self.pending[eng] = True

    def dma(self, queue, semname, out, in_, reads=(), writes=()):
        deps = self._deps(queue, reads, writes)
        self._emit_waits(queue, deps)
        n = self.cnt[semname] + 16
        self.cnt[semname] = n
        for k in reads:
            self.readers.setdefault(k, {})[semname] = n
        for k in writes:
            self.lastw[k] = (semname, n)
            self.readers[k] = {}
        sem = self.sem[semname]
        self.q[queue].append(lambda E, sem=sem: E.dma_start(out=out, in_=in_).then_inc(sem, 16))
        self.ninst += 1

    def fence(self):
        for e in ENGS:
            kn = self.known[e]
            for s, v in self.cnt.items():
                if s == e or v == 0 or kn.get(s, 0) >= v:
                    continue
                kn[s] = v
                sem = self.sem[s]
                self.q[e].append(lambda E, sem=sem, v=v: E.wait_ge(sem, v))

    def wait_all(self, eng, semnames):
        for s in semnames:
            v = self.cnt[s]
            if v > 0 and self.known[eng].get(s, 0) < v:
                self.known[eng][s] = v
                sem = self.sem[s]
                self.q[eng].append(lambda E, sem=sem, v=v: E.wait_ge(sem, v))

    def emit(self, block):
        for e in ENGS:
            assert not self.pending[e], "engine %s ends with un-inc'd op" % e
        q = self.q

        @block.tensor
        def _(E):
            for f in q["pe"]:
                f(E)

        @block.scalar
        def _(E):
            for f in q["act"]:
                f(E)

        @block.vector
        def _(E):
            for f in q["dve"]:
                f(E)

        @block.gpsimd
        def _(E):
            for f in q["pool"]:
                f(E)

        @block.sync
        def _(E):
            for f in q["sp"]:
                f(E)


class Carver:
    def __init__(self, scr, nbytes):
        self.scr = scr
        self.cap = nbytes
        self.off = 0
        self.hi = 0

    def alloc(self, shape, dt):
        esz = 4 if dt == F32 else 2
        n = 1
        for s in shape:
            n *= s
        nb = (n * esz + 31) // 32 * 32
        assert self.off + nb <= self.cap, ("SBUF scratch overflow", self.off, nb, self.cap)
        w0 = self.off // 4
        ap = self.scr[:, w0:w0 + nb // 4]
        if dt != F32:
            ap = ap.bitcast(dt)
        ap = ap[:, 0:n]
        if len(shape) == 2:
            ap = ap.rearrange("p (a b) -> p a b", a=shape[0])
        elif len(shape) == 3:
            ap = ap.rearrange("p (a b c) -> p a b c", a=shape[0], b=shape[1])
        self.off += nb
        self.hi = max(self.hi, self.off)
        return ap


def rev_ap(ap_):
    n = ap_.shape[1]
    pp = ap_.ap[0]
    return bass.AP(ap_.tensor, ap_.offset + (n - 1), [[pp[0], pp[1]], [-1, n]])


class _Stop(Exception):
    pass


def build_program(layers, nseq, first_in_program=True, dbg=None, stop=99):
    nc = bass.Bass("TRN2", target_bir_lowering=False)
    NL = DEPTH
    dram = lambda name, shape, dt=F32, kind="ExternalInput": nc.dram_tensor(name, shape, dt, kind=kind).ap()
    xin = dram("xin", [nseq, 128, 8, TOK])
    cT_d = dram("cT", [128, 8, 3])
    wada_d = dram("w_ada", [NL, 128, 8, 3072])
    bada_d = dram("b_adaT", [128, NL, 24])
    win_d = dram("w_in", [NL, 128, 8, NC1 + NC2])
    wo_d = dram("w_o", [NL, 128, 8, 1024])
    vecs_d = dram("vecs", [128, NL, NV])
    anorm_d = dram("anorm", [NL, 128, 2, 256])
    absb_d = dram("absb", [NL, 128, 2, 128])
    aws_d = dram("awsT", [NL, 128, 4, 128])
    lruw_d = dram("lruW", [NL, 128, 8, 128])
    c32_d = dram("cst32", [128, 256])
    c16_d = dram("cst16", [128, 256 + 2 * L_LAT])
    xout = dram("xout", [nseq, 128, 8, TOK], kind="ExternalOutput")
    dbg_out = {}
    if dbg:
        for name, shape in dbg.items():
            dbg_out[name] = dram("dbg_" + name, shape, kind="ExternalOutput")

    with ExitStack() as es:
        CAP = 212800
        scr = es.enter_context(nc.sbuf_tensor("scr", [128, CAP // 4], F32))
        psall = es.enter_context(nc.psum_tensor("psall", [128, 4096], F32))[:, :]
        PSB = [psall[:, i * 512:(i + 1) * 512] for i in range(8)]
        P = Prog(nc, es)
        for s in ("ld_x", "ld_c", "ld_w1", "ld_w2", "ld_wo", "ld_ada0", "ld_ada1", "ld_lw", "st_o", "st_d"):
            P.newsem(s)
        C = Carver(scr, CAP)

        xT = C.alloc([8, TOK], F32)
        tabs = C.alloc([2, L_LAT], BF16)
        KT = C.alloc([2, TOK], BF16)
        VA = C.alloc([18, 2, 128], BF16)
        ylru = C.alloc([2, TOK], BF16)
        c32 = C.alloc([256], F32)[:, :]
        c16 = C.alloc([256], BF16)
        mods = C.alloc([NL, 24, 3], F32)
        sc1 = C.alloc([NL, 8, 3], F32)
        gA = C.alloc([NL, 8, 3], F32)
        vecs = C.alloc([NL, NV], F32)
        cvec = C.alloc([NL, 4], F32)
        badaT = C.alloc([NL, 24], F32)
        anorm = C.alloc([2, 256], F32)
        absb = C.alloc([2, 128], F32)
        awsT = C.alloc([4, 128], BF16)
        lruW = C.alloc([8, 128], BF16)
        sil = C.alloc([8, 3], F32)
        epsr = C.alloc([4], F32)
        onesb = C.alloc([128], BF16)
        PBASE = C.off
        ident = c32[:, 0:128]
        onesD = c32[:, 128:256]
        Rmat = c16[:, 0:128]
        Bones = c16[:, 128:256]
        cosT = tabs[:, 0, :]
        sinT = tabs[:, 1, :]

        psi = [0, 0, 0]

        def nextps():
            i = psi[0] % 4
            psi[0] += 1
            return PSB[i], "ps%d" % i

        def nextps2():
            i = ((psi[0] + 1) // 2 * 2) % 4
            psi[0] = i + 2
            return psall[:, i * 512:(i + 2) * 512], ["ps%d" % i, "ps%d" % (i + 1)]

        def nextps_s():
            i = 3 + psi[1] % 3
            psi[1] += 1
            return PSB[i], "ps%d" % i

        def nextps_o():
            i = 6 + psi[2] % 2
            psi[2] += 1
            return PSB[i], "ps%d" % i

        def act(out, in_, func, r, w, **kw):
            P.op("act", lambda E: E.activation(out=out, in_=in_, func=func, **kw), r, w)

        def tt(eng, out, a, b, op, r, w):
            P.op(eng, lambda E: E.tensor_tensor(out=out, in0=a, in1=b, op=op), r, w)

        def ts(eng, out, a, s1, s2, op0, op1, r, w):
            if s2 is None:
                P.op(eng, lambda E: E.tensor_single_scalar(out=out, in_=a, scalar=s1, op=op0), r, w)
            else:
                P.op(eng, lambda E: E.tensor_scalar(out=out, in0=a, scalar1=s1, scalar2=s2, op0=op0, op1=op1), r, w)

        def scan(out, d0, d1, init, r, w):
            P.op("dve", lambda E: E.tensor_tensor_scan(out=out, data0=d0, data1=d1, initial=init, op0=ALU.mult, op1=ALU.add), r, w)

        def stt(eng, out, a, s, b, op0, op1, r, w):
            P.op(eng, lambda E: E.scalar_tensor_tensor(out=out, in0=a, scalar=s, in1=b, op0=op0, op1=op1), r, w)

        def mm(out, lhsT, rhs, start, stop, r, w, inc=True):
            P.op("pe", lambda E: E.matmul(out, lhsT, rhs, start=start, stop=stop), r, w, inc=inc)

        def recip(out, in_, r, w):
            P.op("dve", lambda E: E.reciprocal(out=out, in_=in_), r, w)

        def memset(eng, ap, val, w):
            P.op(eng, lambda E: E.memset(ap, val), (), w)

        def copy(eng, out, in_, r, w):
            P.op(eng, lambda E: E.tensor_copy(out=out, in_=in_), r, w)

        P.dma("sp", "ld_c", c32, c32_d[:, :], writes=["c32"])
        P.dma("pool", "ld_lw", c16, c16_d[:, 0:256], writes=["c16"])
        for a_ in range(2):
            P.dma("pool", "ld_lw", tabs[:, a_, :], c16_d[:, 256 + a_ * L_LAT:256 + (a_ + 1) * L_LAT], writes=["tabs"])
        P.dma("sp", "ld_c", vecs, vecs_d[:, :, :], writes=["vecs"])
        P.dma("sp", "ld_c", badaT, bada_d[:, :, :], writes=["badaT"])
        P.dma("sp", "ld_c", sil, cT_d[:, :, :], writes=["sil"])
        memset("dve", epsr[:, 0:1], RMS_EPS, ["epsr"])
        memset("dve", epsr[:, 1:2], LN_EPS / (ALPHA * ALPHA), ["epsr"])
        memset("dve", epsr[:, 2:3], 1.0, ["epsr"])
        memset("dve", epsr[:, 3:4], 0.0, ["epsr"])
        memset("dve", onesb, 1.0 / 1024.0, ["onesb"])
        memset("pool", VA[:, :, :, 64:128], 1.0, ["VA"])

        act(sil, sil, AF.Silu, ["sil"], ["sil"])
        for l in layers:
            act(cvec[:, l, :], vecs[:, l, V_LAM:V_LAM + 4], AF.Exp, ["vecs"], ["cvec"], scale=-1.0)
            act(cvec[:, l, :], cvec[:, l, :], AF.Ln, ["cvec", "epsr"], ["cvec"], bias=epsr[:, 2:3])
            ts("dve", cvec[:, l, :], cvec[:, l, :], -8.0, None, ALU.mult, ALU.bypass, ["cvec"], ["cvec"])
        C.off = PBASE
        stg = [C.alloc([8, 512], F32) for _ in range(2)]
        modrow = C.alloc([3072], F32)[:, :]
        bi = 0
        for l in layers:
            for nb in range(6):
                sb_ = stg[bi % 2]
                key = "stg%d" % (bi % 2)
                P.dma("sp", "ld_ada%d" % (bi % 2), sb_, wada_d[l, :, :, nb * 512:(nb + 1) * 512], writes=[key])
                ps, pk = nextps()
                for kc in range(8):
                    mm(ps[0:3, :], sil[:, kc, :], sb_[:, kc, :], kc == 0, kc == 7, [key, "sil"], [pk], inc=(kc == 7))
                copy("dve", modrow[0:3, nb * 512:(nb + 1) * 512], ps[0:3, :], [pk], ["modrow"])
                bi += 1
            ps, pk = nextps()
            for j in range(24):
                mm(ps[:, j * 3:(j + 1) * 3], modrow[0:3, j * 128:(j + 1) * 128], ident[0:3, 0:3], True, True,
                   ["modrow", "c32"], [pk], inc=(j == 23))
            tt("dve", mods[:, l, :, :], ps[:, 0:72].rearrange("p (a b) -> p a b", a=24),
               badaT[:, l, :].unsqueeze(2).to_broadcast([128, 24, 3]), ALU.add, [pk, "badaT"], ["mods"])
            ts("dve", sc1[:, l, :, :], mods[:, l, 8:16, :], 1.0, None, ALU.add, ALU.bypass, ["mods"], ["sc1"])
            ts("dve", gA[:, l, :, :], mods[:, l, 16:24, :], 1.0 / ALPHA, None, ALU.mult, ALU.bypass, ["mods"], ["gA"])
        P.fence()

        def rms_rope(zps, zk, n, gvec, dsts, dstk, t0, latent, S):
            act(S["sq"][:, 0:n], zps[:, 0:n], AF.Square, [zk], ["r_sq"])
            ps2, pk2 = nextps()
            mm(ps2[:, 0:n], Bones, S["sq"][:, 0:n], True, True, ["r_sq", "c16"], [pk2])
            act(S["sd"][:, 0:n], ps2[:, 0:n], AF.Ln, [pk2, "epsr"], ["r_sd"], bias=epsr[:, 0:1])
            act(S["sd"][:, 0:n], S["sd"][:, 0:n], AF.Exp, ["r_sd"], ["r_sd"], scale=-0.5)
            if not latent:
                for (lo, hi_, dst) in dsts:
                    stt("dve", dst, zps[lo:hi_, 0:n], gvec[lo:hi_, :], S["sd"][lo:hi_, 0:n], ALU.mult, ALU.mult,
                        [zk, "r_sd", "vecs"], [dstk])
                return
            stt("dve", S["qn"][:, 0:n], zps[:, 0:n], gvec, S["sd"][:, 0:n], ALU.mult, ALU.mult,
                [zk, "r_sd", "vecs"], ["r_qn"])
            ps3, pk3 = nextps()
            mm(ps3[:, 0:n], Rmat, S["qn"][:, 0:n], True, True, ["r_qn", "c16"], [pk3])
            tt("pool", S["t1"][:, 0:n], S["qn"][:, 0:n], cosT[:, t0:t0 + n], ALU.mult, ["r_qn", "tabs"], ["r_t1"])
            tt("dve", S["t2"][:, 0:n], ps3[:, 0:n], sinT[:, t0:t0 + n], ALU.mult, [pk3, "tabs"], ["r_t2"])
            for (lo, hi_, dst) in dsts:
                tt("pool", dst, S["t1"][lo:hi_, 0:n], S["t2"][lo:hi_, 0:n], ALU.add, ["r_t1", "r_t2"], [dstk])

        def xkeys(t0, n):
            return ["xT%d" % b for b in range(t0 // 256, (t0 + n + 255) // 256)]

        ALLX = ["xT%d" % b for b in range(9)]

        def modulate(dst, dk, l, r, t0, n, first=False, use_act=False):
            for c in range(8):
                eng = "dve" if (first or c % 2 == 1) else "pool"
                if use_act and c % 2 == 0:
                    act(dst[:, c, 0:n], xT[:, c, t0:t0 + n], AF.Identity, xkeys(t0, n) + ["sc1", "mods"],
                        ["%s_c%d" % (dk, c)], scale=sc1[:, l, c, r:r + 1], bias=mods[:, l, c, r:r + 1])
                    continue
                ts(eng, dst[:, c, 0:n], xT[:, c, t0:t0 + n], sc1[:, l, c, r:r + 1], mods[:, l, c, r:r + 1],
                   ALU.mult, ALU.add, xkeys(t0, n) + ["sc1", "mods"], ["%s_c%d" % (dk, c)])

        def inproj_fm(ps, pk, W, wk, col0, xm, xk, n):
            for kc in range(8):
                mm(ps[:, 0:n], W[:, kc, col0:col0 + 128], xm[:, kc, 0:n], kc == 0, kc == 7, [wk, "%s_c%d" % (xk, kc)], [pk], inc=(kc == 7))

        def stage(n):
            if n > stop:
                raise _Stop()

        W1P = (("k", 0, 256), ("v", 256, 128), ("rx", 384, 256), ("rg", 640, 256))
        W2P = (("u", C_U, 256), ("va", C_VA, 256), ("ga", C_GA, 256), ("bg", C_BG, 512), ("q", C_Q, 512))
        for nm, _, _ in W1P:
            P.newsem("ld_w1" + nm)
        for nm, _, _ in W2P:
            P.newsem("ld_w2" + nm)

        for s in range(nseq):
            try:
                for c in range(8):
                    P.dma("sp", "ld_x", xT[:, c, :], xin[s, :, c, :], writes=ALLX)
                for l in layers:
                    last = (l == DEPTH - 1)
                    P.dma("sp", "ld_c", anorm, anorm_d[l, :, :, :], writes=["anorm"])
                    P.dma("sp", "ld_c", absb, absb_d[l, :, :, :], writes=["absb"])
                    P.dma("pool", "ld_lw", awsT, aws_d[l, :, :, :], writes=["awsT"])
                    P.dma("pool", "ld_lw", lruW, lruw_d[l, :, :, :], writes=["lruW"])
                    stage(1)
                    P.fence()
                    C.off = PBASE
                    win1 = C.alloc([8, NC1], BF16)
                    xm1 = C.alloc([8, 512], BF16)
                    S1 = {"sq": C.alloc([512], BF16)[:, :], "sd": C.alloc([512], F32)[:, :],
                          "qn": C.alloc([512], BF16)[:, :], "t1": C.alloc([512], F32)[:, :]}
                    assert C.off - PBASE == NC2 * 8 * 2, (C.off - PBASE)
                    S1["t2"] = C.alloc([512], F32)[:, :]
                    sg = C.alloc([2, TOK], BF16)
                    xrb = C.alloc([2, TOK], BF16)
                    GBASE = C.off
                    rx = C.alloc([2, 2312], F32)
                    acc = C.alloc([TOK], F32)[:, :]
                    xm1b = C.alloc([8, 512], BF16)
                    XM1 = [(xm1, "xm1"), (xm1b, "xm1b")]
                    P1B = ((0, 512), (512, 512), (1024, 512), (1536, 512), (2048, 256))
                    for nm, c0, ncl in W1P:
                        P.dma("pool", "ld_w1" + nm, win1[:, :, c0:c0 + ncl], win_d[l, :, :, c0:c0 + ncl], writes=["win1" + nm])
                    memset("pool", rx[:, :, 0:2], 0.0, ["rx"])
                    memset("pool", rx[:, :, 2050:2052], 0.0, ["rx"])
                    memset("pool", rx[:, :, 2308:2312], 0.0, ["rx"])
                    modulate(xm1, "xm1", l, s, 0, 512, first=True, use_act=True)
                    for bi1, (t0, n) in enumerate(P1B):
                        latent = t0 < L_LAT
                        r = s if latent else 2
                        xm1c, xk1 = XM1[bi1 % 2]
                        bk = lambda i: (PSB[i], "ps%d" % i)
                        gk = vecs[:, l, V_KG:V_KG + 1]
                        off = 2 + t0 if latent else 2052 + (t0 - L_LAT)
                        nt = n // 128
                        zK = [bk(0), bk(1)]
                        for g in range(2):
                            inproj_fm(zK[g][0], zK[g][1], win1, "win1k", C_K0 + g * 128, xm1c, xk1, n)
                        psV, pkV = bk(2)
                        for tl in range(nt):
                            for kc in range(8):
                                mm(psV[:, tl * 128:(tl + 1) * 128], xm1c[:, kc, tl * 128:(tl + 1) * 128], win1[:, kc, C_V:C_V + 128],
                                   kc == 0, kc == 7, ["win1v", "%s_c%d" % (xk1, kc)], [pkV], inc=(kc == 7))
                        zX = [bk(3), bk(4)]
                        for ct in range(2):
                            inproj_fm(zX[ct][0], zX[ct][1], win1, "win1rx", C_RX + ct * 128, xm1c, xk1, n)
                        sqK = [S1["sq"], S1["qn"]]
                        for g in range(2):
                            act(sqK[g][:, 0:n], zK[g][0][:, 0:n], AF.Square, [zK[g][1]], ["r_sq%d" % g])
                        if bi1 + 1 < len(P1B):
                            t0n, nn = P1B[bi1 + 1]
                            modulate(XM1[(bi1 + 1) % 2][0], XM1[(bi1 + 1) % 2][1], l, (s if t0n < L_LAT else 2), t0n, nn,
                                     first=True, use_act=True)
                        for ct in range(2):
                            act(rx[:, ct, off:off + n], zX[ct][0][:, 0:n], AF.Copy, [zX[ct][1]], ["rx"])
                        copy("dve", VA[:, t0 // 128:t0 // 128 + nt, :, 0:64],
                             psV[:, 0:nt * 128].rearrange("p (t g d) -> p t g d", t=nt, g=2), [pkV], ["VA"])
                        mK = [bk(7), bk(3)]
                        mm(mK[0][0][:, 0:n], Bones, sqK[0][:, 0:n], True, True, ["r_sq0", "c16"], [mK[0][1]])
                        zG = [bk(5), bk(6)]
                        for ct in range(2):
                            inproj_fm(zG[ct][0], zG[ct][1], win1, "win1rg", C_RG + ct * 128, xm1c, xk1, n)
                        mm(mK[1][0][:, 0:n], Bones, sqK[1][:, 0:n], True, True, ["r_sq1", "c16"], [mK[1][1]])
                        rsK = [S1["sd"], S1["t1"]]
                        for g in range(2):
                            act(rsK[g][:, 0:n], mK[g][0][:, 0:n], AF.Ln, [mK[g][1], "epsr"], ["r_sd%d" % g], bias=epsr[:, 0:1])
                            act(rsK[g][:, 0:n], rsK[g][:, 0:n], AF.Exp, ["r_sd%d" % g], ["r_sd%d" % g], scale=-0.5)
                        for g in range(2):
                            dstK = KT[:, g, t0:t0 + n]
                            if latent:
                                stt("dve", sqK[g][:, 0:n], zK[g][0][:, 0:n], gk, rsK[g][:, 0:n], ALU.mult, ALU.mult,
                                    [zK[g][1], "r_sd%d" % g, "vecs"], ["r_sq%d" % g])
                            else:
                                stt("dve", dstK, zK[g][0][:, 0:n], gk, rsK[g][:, 0:n], ALU.mult, ALU.mult,
                                    [zK[g][1], "r_sd%d" % g, "vecs"], ["KT"])
                        for ct in range(2):
                            act(sg[:, ct, t0:t0 + n], zG[ct][0][:, 0:n], AF.Silu, [zG[ct][1]], ["sg"])
                        if latent:
                            rK = [bk(4), bk(7)]
                            for g in range(2):
                                mm(rK[g][0][:, 0:n], Rmat, sqK[g][:, 0:n], True, True, ["r_sq%d" % g, "c16"], [rK[g][1]])
                            for g in range(2):
                                tt("pool", rsK[g][:, 0:n], sqK[g][:, 0:n], cosT[:, t0:t0 + n], ALU.mult,
                                   ["r_sq%d" % g, "tabs"], ["r_sd%d" % g])
                                tt("dve", S1["t2"][:, 0:n], rK[g][0][:, 0:n], sinT[:, t0:t0 + n], ALU.mult, [rK[g][1], "tabs"], ["r_t2"])
                                tt("pool", KT[:, g, t0:t0 + n], rsK[g][:, 0:n], S1["t2"][:, 0:n], ALU.add,
                                   ["r_sd%d" % g, "r_t2"], ["KT"])
                    stage(2)
                    for ct in range(2):
                        for (d0, o0, n) in ((2, 0, L_LAT), (2052, L_LAT, L_CTX)):
                            cw = lambda j: vecs[:, l, V_CW + ct * 4 + j:V_CW + ct * 4 + j + 1]
                            eng = "dve"
                            ts(eng, acc[:, 0:n], rx[:, ct, d0 - 2:d0 - 2 + n], cw(0), vecs[:, l, V_CB + ct:V_CB + ct + 1],
                               ALU.mult, ALU.add, ["rx", "vecs"], ["acc"])
                            stt(eng, acc[:, 0:n], rx[:, ct, d0 - 1:d0 - 1 + n], cw(1), acc[:, 0:n], ALU.mult, ALU.add,
                                ["rx", "vecs", "acc"], ["acc"])
                            stt(eng, acc[:, 0:n], rx[:, ct, d0:d0 + n], cw(2), acc[:, 0:n], ALU.mult, ALU.add,
                                ["rx", "vecs", "acc"], ["acc"])
                            stt(eng, xrb[:, ct, o0:o0 + n], rx[:, ct, d0 + 1:d0 + 1 + n], cw(3), acc[:, 0:n], ALU.mult, ALU.add,
                                ["rx", "vecs", "acc"], ["xrb"])
                    stage(3)
                    P.fence()
                    C.off = PBASE
                    win2 = C.alloc([8, NC2], BF16)
                    for nm, c0, ncl in W2P:
                        P.dma("pool", "ld_w2" + nm, win2[:, :, c0:c0 + ncl], win_d[l, :, :, NC1 + c0:NC1 + c0 + ncl],
                              writes=["win2" + nm])
                    C.off = GBASE
                    Ab = C.alloc([TOK], F32)[:, :]
                    Tb = C.alloc([TOK], F32)[:, :]
                    Bb = C.alloc([TOK], F32)[:, :]
                    Hf = C.alloc([TOK], F32)[:, :]
                    for ct in range(2):
                        for dd in range(2):
                            vi = dd * 2 + ct
                            for (t0, n) in ((0, 512), (512, 512), (1024, 512), (1536, 512), (2048, 256)):
                                ps, pk = nextps()
                                mm(ps[:, 0:n], lruW[:, (dd * 2 + 0) * 2 + ct, :], xrb[:, ct, t0:t0 + n], True, True,
                                   ["lruW", "xrb"], [pk])
                                act(Ab[:, t0:t0 + n], ps[:, 0:n], AF.Sigmoid, [pk, "vecs"], ["Ab"],
                                    bias=vecs[:, l, V_BR + vi:V_BR + vi + 1])
                                ps, pk = nextps()
                                mm(ps[:, 0:n], lruW[:, (dd * 2 + 1) * 2 + ct, :], xrb[:, ct, t0:t0 + n], True, True,
                                   ["lruW", "xrb"], [pk])
                                act(Bb[:, t0:t0 + n], ps[:, 0:n], AF.Sigmoid, [pk, "vecs"], ["Bb"],
                                    bias=vecs[:, l, V_BI + vi:V_BI + vi + 1])
                            act(Ab, Ab, AF.Exp, ["Ab", "cvec"], ["Ab"], scale=cvec[:, l, vi:vi + 1])
                            act(Tb, Ab, AF.Square, ["Ab"], ["Tb"])
                            act(Tb, Tb, AF.Ln, ["Tb", "epsr"], ["Tb"], scale=-1.0, bias=epsr[:, 2:3])
                            act(Tb, Tb, AF.Exp, ["Tb"], ["Tb"], scale=0.5)
                            tt("dve", Bb, Bb, xrb[:, ct, :], ALU.mult, ["Bb", "xrb"], ["Bb"])
                            tt("dve", Bb, Bb, Tb, ALU.mult, ["Bb", "Tb"], ["Bb"])
                            if dd == 0:
                                scan(Hf[:, L_LAT:TOK], Ab[:, L_LAT:TOK], Bb[:, L_LAT:TOK], 0.0, ["Ab", "Bb"], ["Hf"])
                                scan(Hf[:, 0:L_LAT], Ab[:, 0:L_LAT], Bb[:, 0:L_LAT], Hf[:, TOK - 1:TOK], ["Ab", "Bb", "Hf"], ["Hf"])
                            else:
                                scan(rev_ap(Tb), rev_ap(Ab), rev_ap(Bb), 0.0, ["Ab", "Bb"], ["Tb"])
                        tt("pool", Hf, Hf, Tb, ALU.add, ["Hf", "Tb"], ["Hf"])
                        tt("pool", ylru[:, ct, :], Hf, sg[:, ct, :], ALU.mult, ["Hf", "sg"], ["ylru"])
                    stage(4)
                    P.fence()
                    C.off = PBASE
                    win2 = C.alloc([8, NC2], BF16)
                    wo = C.alloc([8, 1024], BF16)
                    for kc in range(8):
                        P.dma("pool", "ld_wo", wo[:, kc, :], wo_d[l, :, kc, :], writes=["wo"])
                    xm = C.alloc([8, 256], BF16)
                    yT = C.alloc([6, 256], BF16)
                    ug = C.alloc([2, 256], F32)
                    sga = C.alloc([2, 256], F32)
                    vg = C.alloc([2, 256], F32)
                    vln = C.alloc([2, 256], BF16)
                    tmpg = C.alloc([512], F32)[:, :]
                    st = C.alloc([2, 8], F32)
                    qbd = C.alloc([4, 512], BF16)
                    sbg = C.alloc([4, 256], BF16)
                    bufA = C.alloc([1024], BF16)[:, :]
                    bufBC = C.alloc([2048], F32)[:, :]
                    bufB = bufBC[:, 0:1024]
                    bufC = bufBC[:, 1024:2048]
                    pT = [C.alloc([512], BF16)[:, :] for _ in range(3)]
                    rden = [C.alloc([256], F32)[:, :] for _ in range(2)]
                    otmp = [C.alloc([256], F32)[:, :]] * 2
                    sqb = C.alloc([8, 256], BF16)
                    mean_sb = C.alloc([256], F32)[:, :]
                    m2 = C.alloc([256], F32)[:, :]
                    rsl = C.alloc([256], F32)[:, :]
                    junkv = S_JUNK = vln.rearrange("p c t -> p (c t)")
                    memset("pool", qbd[64:128, :, 0:256], 0.0, ["qbd"])
                    memset("pool", qbd[0:64, :, 256:512], 0.0, ["qbd"])
                    nblk = 8 if last else 9
                    cn = [0, 0]
                    def mod(qb):
                        t0 = qb * 256
                        latent = qb < 8
                        r = s if latent else 2
                        xk = "xT%d" % qb
                        modulate(xm, "xm", l, r, t0, 256, first=(qb == 0))

                    def front_a(qb):
                        t0 = qb * 256
                        latent = qb < 8
                        r = s if latent else 2
                        xk = "xT%d" % qb
                        bank = lambda i: (psall[:, i * 512:(i + 1) * 512], "ps%d" % i)
                        bank2 = lambda i: (psall[:, i * 512:(i + 2) * 512], ["ps%d" % i, "ps%d" % (i + 1)])
                        gq = vecs[:, l, V_QG:V_QG + 1]
                        v4 = lambda ap_, lo, hi_: ap_[lo:hi_, :].rearrange("p (c t) -> p c t", c=4)
                        psQ, pkQ = bank2(0)
                        for ct in range(4):
                            inproj_fm(psQ[:, ct * 256:(ct + 1) * 256], pkQ[ct // 2], win2, "win2q", C_Q + ct * 128, xm, "xm", 256)
                        act(bufA, psQ, AF.Square, pkQ, ["bufA"])
                        psU, pkU = bank(2)
                        for ct in range(2):
                            inproj_fm(psU[:, ct * 256:(ct + 1) * 256], pkU, win2, "win2u", C_U + ct * 128, xm, "xm", 256)
                        psV, pkV = bank(3)
                        for tl in range(2):
                            for kc in range(8):
                                mm(psV[:, tl * 256:(tl + 1) * 256], xm[:, kc, tl * 128:(tl + 1) * 128], win2[:, kc, C_VA:C_VA + 256],
                                   kc == 0, kc == 7, ["win2va", "xm_c%d" % kc], [pkV], inc=(kc == 7))
                        psM, pkM = bank2(4)
                        for hf in range(2):
                            mm(psM[:, hf * 512:(hf + 1) * 512], Bones, bufA[:, hf * 512:(hf + 1) * 512], True, True,
                               ["bufA", "c16"], [pkM[hf]])
                        act(bufB, psM, AF.Ln, pkM + ["epsr"], ["bufB"], bias=epsr[:, 0:1])
                        act(bufB, bufB, AF.Exp, ["bufB"], ["bufB"], scale=-0.5)
                        if latent:
                            stt("dve", bufA, psQ, gq, bufB, ALU.mult, ALU.mult, pkQ + ["bufB", "vecs"], ["bufA"])
                        else:
                            stt("dve", qbd[0:64, :, 0:256], v4(psQ, 0, 64), gq[0:64, :], v4(bufB, 0, 64), ALU.mult, ALU.mult,
                                pkQ + ["bufB", "vecs"], ["qbd"])
                            stt("dve", qbd[64:128, :, 256:512], v4(psQ, 64, 128), gq[64:128, :], v4(bufB, 64, 128), ALU.mult, ALU.mult,
                                pkQ + ["bufB", "vecs"], ["qbd"])
                        act(ug.rearrange("p c t -> p (c t)"), psU, AF.Gelu_apprx_tanh, [pkU], ["ug"])
                        for tl in range(2):
                            act(vg[:, tl, :], psV[:, tl * 256:(tl + 1) * 256], AF.Gelu_apprx_tanh, [pkV], ["vg", "st"],
                                accum_out=st[:, tl, 0:1])
                        for tl in range(2):
                            act(junkv[:, 0:256], vg[:, tl, :], AF.Square, ["vg"], ["vln", "st"], accum_out=st[:, tl, 1:2])
                            ts("dve", st[:, tl, 2:3], st[:, tl, 0:1], 1.0 / 256.0, None, ALU.mult, ALU.bypass, ["st"], ["st"])
                            tt("dve", st[:, tl, 3:4], st[:, tl, 2:3], st[:, tl, 2:3], ALU.mult, ["st"], ["st"])
                            stt("dve", st[:, tl, 4:5], st[:, tl, 1:2], 1.0 / 256.0, st[:, tl, 3:4], ALU.mult, ALU.subtract,
                                ["st"], ["st"])
                        ps2b, pk2b = bank2(6)
                        for ct in range(4):
                            inproj_fm(ps2b[:, ct * 256:(ct + 1) * 256], pk2b[ct // 2], win2, "win2bg", C_BG + ct * 128, xm, "xm", 256)
                        psG, pkG = bank(2)
                        for ct in range(2):
                            inproj_fm(psG[:, ct * 256:(ct + 1) * 256], pkG, win2, "win2ga", C_GA + ct * 128, xm, "xm", 256)
                        if latent:
                            psR, pkR = bank2(4)
                            for hf in range(2):
                                mm(psR[:, hf * 512:(hf + 1) * 512], Rmat, bufA[:, hf * 512:(hf + 1) * 512], True, True,
                                   ["bufA", "c16"], [pkR[hf]])
                        act(sga.rearrange("p c t -> p (c t)"), psG, AF.Silu, [pkG], ["sga"])
                        act(sbg.rearrange("p c t -> p (c t)"), ps2b, AF.Silu, pk2b, ["sbg"])
                        tt("pool", ug, ug, sga, ALU.mult, ["ug", "sga"], ["ug"])
                        if latent:
                            cosb = cosT[:, t0:t0 + 256].unsqueeze(1).to_broadcast([128, 4, 256])
                            sinb = sinT[:, t0:t0 + 256].unsqueeze(1).to_broadcast([128, 4, 256])
                            tt("pool", v4(bufB, 0, 128), v4(bufA, 0, 128), cosb, ALU.mult, ["bufA", "tabs"], ["bufB"])
                            tt("dve", v4(bufC, 0, 128), v4(psR, 0, 128), sinb, ALU.mult, pkR + ["tabs"], ["bufC"])
                            tt("pool", qbd[0:64, :, 0:256], v4(bufB, 0, 64), v4(bufC, 0, 64), ALU.add, ["bufB", "bufC"], ["qbd"])
                            tt("pool", qbd[64:128, :, 256:512], v4(bufB, 64, 128), v4(bufC, 64, 128), ALU.add,
                               ["bufB", "bufC"], ["qbd"])

                    def front_b(qb):
                        t0 = qb * 256
                        latent = qb < 8
                        r = s if latent else 2
                        xk = "xT%d" % qb
                        for tl in range(2):
                            act(st[:, tl, 5:6], st[:, tl, 4:5], AF.Ln, ["st", "epsr"], ["st"], bias=epsr[:, 0:1], scale=1.0)
                            act(st[:, tl, 6:7], st[:, tl, 5:6], AF.Exp, ["st"], ["st"], scale=-0.5)
                        for tl in range(2):
                            ts("dve", vg[:, tl, :], vg[:, tl, :], st[:, tl, 2:3], st[:, tl, 6:7], ALU.subtract, ALU.mult,
                               ["vg", "st"], ["vg"])
                            tt("pool", vg[:, tl, :], vg[:, tl, :], anorm[:, 0, :], ALU.mult, ["vg", "anorm"], ["vg"])
                            tt("pool", vln[:, tl, :], vg[:, tl, :], anorm[:, 1, :], ALU.add, ["vg", "anorm"], ["vln"])

                    def front_c(qb):
                        t0 = qb * 256
                        latent = qb < 8
                        r = s if latent else 2
                        xk = "xT%d" % qb
                        ps, pk = PSB[3], "ps3"
                        for tl in range(2):
                            for ct in range(2):
                                for hh in range(2):
                                    g = 2 * ct + hh
                                    co = (tl * 2 + ct) * 128
                                    mm(ps[hh * 64:(hh + 1) * 64, co:co + 128], vln[:, tl, g * 64:(g + 1) * 64], awsT[:, g, :],
                                       True, True, ["vln", "awsT"], [pk], inc=(tl == 1 and ct == 1 and hh == 1))
                        tt("dve", tmpg.rearrange("p (t c q) -> p t c q", t=2, c=2), ps.rearrange("p (t c q) -> p t c q", t=2, c=2),
                           absb.unsqueeze(1).to_broadcast([128, 2, 2, 128]), ALU.add, [pk, "absb"], ["tmpg"])
                        tt("pool", yT[:, 0:2, :].rearrange("p c (t q) -> p t c q", t=2),
                           tmpg.rearrange("p (t c q) -> p t c q", t=2, c=2),
                           ug.rearrange("p c (t q) -> p t c q", t=2), ALU.mult, ["tmpg", "ug"], ["yT"])

                    def attn(qb):
                        t0 = qb * 256
                        latent = qb < 8
                        r = s if latent else 2
                        xk = "xT%d" % qb
                        ktiles = list(range(18)) if latent else [16, 17]
                        for ct in range(4):
                            g = ct // 2
                            pO, kO = nextps_o()

                            def qk(j):
                                ps_, pk_ = nextps_s()
                                mm(ps_[:, 0:512], KT[:, g, j * 128:(j + 1) * 128], qbd[:, ct, :], True, True,
                                   ["KT", "qbd"], [pk_])
                                return ps_, pk_

                            pend = [qk(j) for j in ktiles[0:2]]
                            for ji, j in enumerate(ktiles):
                                ps_, pk_ = pend.pop(0)
                                if ji + 2 < len(ktiles):
                                    pend.append(qk(ktiles[ji + 2]))
                                pb = pT[cn[0] % 3]
                                pbk = "pT%d" % (cn[0] % 3)
                                cn[0] += 1
                                act(pb, ps_, AF.Exp, [pk_], [pbk], scale=0.125)
                                mm(pO[:, 0:512], VA[:, j, g, :], pb, ji == 0, ji == len(ktiles) - 1, ["VA", pbk], [kO])
                            rd = rden[cn[1] % 2]
                            ot = otmp[cn[1] % 2]
                            rk, ok = "rden%d" % (cn[1] % 2), "otmp"
                            cn[1] += 1
                            recip(rd[0:64, :], pO[64:128, 0:256], [kO], [rk])
                            recip(rd[64:128, :], pO[64:128, 256:512], [kO], [rk])
                            tt("dve", ot[0:64, :], pO[0:64, 0:256], rd[0:64, :], ALU.mult, [kO, rk], [ok])
                            tt("dve", ot[64:128, :], pO[0:64, 256:512], rd[64:128, :], ALU.mult, [kO, rk], [ok])
                            tt("pool", yT[:, 2 + ct, :], ot, sbg[:, ct, :], ALU.mult, [ok, "sbg"], ["yT"])

                    def back(qb):
                        t0 = qb * 256
                        latent = qb < 8
                        r = s if latent else 2
                        xk = "xT%d" % qb
                        pm, km = nextps_o()
                        pq, kq = nextps_o()
                        xblk = xT[:, :, t0:t0 + 256]

                        def ln_stats(dt_):
                            mm(pm[:, 0:256], onesb, xm[:, dt_, :], dt_ == 0, dt_ == 7, ["xm_c%d" % dt_, "onesb"], [km], inc=True)
                            mm(pq[:, 0:256], onesb, sqb[:, dt_, :], dt_ == 0, dt_ == 7, ["sqb%d" % dt_, "onesb"], [kq], inc=True)

                        for dt_ in range(8):
                            ps, pk = nextps()
                            for kc in range(8):
                                rhs = yT[:, kc, :] if kc < 6 else ylru[:, kc - 6, t0:t0 + 256]
                                mm(ps[:, 0:256], wo[:, kc, dt_ * 128:(dt_ + 1) * 128], rhs, kc == 0, kc == 7,
                                   ["wo", "yT", "ylru"], [pk], inc=(kc == 7))
                            stt("dve", xT[:, dt_, t0:t0 + 256], ps[:, 0:256], gA[:, l, dt_, r:r + 1], xT[:, dt_, t0:t0 + 256],
                                ALU.mult, ALU.add, [pk, "gA", xk], [xk])
                            act(xm[:, dt_, :], xT[:, dt_, t0:t0 + 256], AF.Copy, [xk], ["xm_c%d" % dt_])
                            act(sqb[:, dt_, :], xT[:, dt_, t0:t0 + 256], AF.Square, [xk], ["sqb%d" % dt_])
                            if dt_ >= 2:
                                ln_stats(dt_ - 2)
                        ln_stats(6)
                        ln_stats(7)
                        copy("dve", mean_sb, pm[:, 0:256], [km], ["mean_sb"])
                        tt("pool", m2, mean_sb, mean_sb, ALU.mult, ["mean_sb"], ["m2"])
                        tt("dve", m2, pq[:, 0:256], m2, ALU.subtract, [kq, "m2"], ["m2"])

                    def back_b(qb):
                        t0 = qb * 256
                        xk = "xT%d" % qb
                        xblk = xT[:, :, t0:t0 + 256]
                        act(rsl, m2, AF.Ln, ["m2", "epsr"], ["rsl"], bias=epsr[:, 1:2], scale=1.0)
                        act(rsl, rsl, AF.Exp, ["rsl"], ["rsl"], scale=-0.5)
                        tln8 = bufBC.rearrange("p (c t) -> p c t", c=8)
                        tt("pool", tln8, xblk, mean_sb.unsqueeze(1).to_broadcast([128, 8, 256]), ALU.subtract,
                           [xk, "mean_sb"], ["bufB", "bufC"])
                        tt("dve", tln8, tln8, rsl.unsqueeze(1).to_broadcast([128, 8, 256]), ALU.mult,
                           ["bufB", "bufC", "rsl"], ["bufB", "bufC"])
                        for dt_ in range(8):
                            sc_, bi_ = vecs[:, l, V_LNG + dt_:V_LNG + dt_ + 1], vecs[:, l, V_LNB + dt_:V_LNB + dt_ + 1]
                            ts("dve" if dt_ % 2 == 0 else "pool", xT[:, dt_, t0:t0 + 256], tln8[:, dt_, :], sc_, bi_,
                               ALU.mult, ALU.add, ["bufB", "bufC", "vecs"], [xk])

                    mod(0)
                    front_a(0)
                    front_b(0)
                    for qb in range(nblk):
                        if qb + 1 < nblk:
                            mod(qb + 1)
                        if qb >= 1:
                            back_b(qb - 1)
                        attn(qb)
                        front_c(qb)
                        if qb + 1 < nblk:
                            front_a(qb + 1)
                        back(qb)
                        if qb + 1 < nblk:
                            front_b(qb + 1)
                    back_b(nblk - 1)
            except _Stop:
                pass
            for c in range(8):
                P.dma("sp", "st_o", xout[s, :, c, :], xT[:, c, :], reads=ALLX)
        P.wait_all("sp", ["st_o", "st_d"])
        block = es.enter_context(nc.Block())
        P.emit(block)
        build_program.last_stats = (P.ninst, dict(P.cnt), C.hi)
    return nc


def _rope_tables():
    half, nf = 32, 16
    inv = (10000.0 ** (-np.arange(nf, dtype=np.float32) / nf)).astype(np.float32)
    t = np.arange(L_LAT)
    rows = (t // 64).astype(np.float32)
    cols = (t % 64).astype(np.float32)
    cos = np.zeros((64, L_LAT), np.float32)
    sin = np.zeros((64, L_LAT), np.float32)
    for d in range(64):
        pos = rows if d < 32 else cols
        f = d % 16
        ang = (pos * inv[f]).astype(np.float32)
        cos[d] = np.cos(ang)
        sgn = -1.0 if (d % 32) < 16 else 1.0
        sin[d] = sgn * np.sin(ang)
    return np.tile(cos, (2, 1)), np.tile(sin, (2, 1))


def _consts():
    ident = np.eye(128, dtype=np.float32)
    onesD = np.full((128, 128), 1.0 / 1024.0, np.float32)
    c32 = np.concatenate([ident, onesD], 1)
    R = np.zeros((128, 128), np.float32)
    for m in range(128):
        d = m % 64
        partner = d + 16 if (d % 32) < 16 else d - 16
        R[(m // 64) * 64 + partner, m] = 1.0
    B = np.zeros((128, 128), np.float32)
    B[0:64, 0:64] = 1.0 / 64.0
    B[64:128, 64:128] = 1.0 / 64.0
    cos, sin = _rope_tables()
    c16 = np.concatenate([R, B, cos, sin], 1)
    return np.ascontiguousarray(c32), np.ascontiguousarray(c16)


def _kc_layout(w):
    Ln, K, N = w.shape
    return np.ascontiguousarray(w.reshape(Ln, 8, 128, N).transpose(0, 2, 1, 3))


def prep_shared(inp):
    f = lambda k: np.asarray(inp[k], dtype=np.float32)
    w_in = f("w_in")
    a_u, a_v, a_g = w_in[:, :, 0:256], w_in[:, :, 256:512], w_in[:, :, 512:768]
    q, k, v = w_in[:, :, 768:1280], w_in[:, :, 1280:1408], w_in[:, :, 1408:1536]
    b_g, r_x, r_g = w_in[:, :, 1536:2048], w_in[:, :, 2048:2304], w_in[:, :, 2304:2560]
    k0, k1 = k[:, :, 0:64], k[:, :, 64:128]
    wperm = np.concatenate([k0, k0, k1, k1, v, r_x, r_g, a_u, a_g, a_v, q, b_g], axis=2)
    assert wperm.shape[2] == NC1 + NC2
    sh = {}
    sh["w_in"] = _kc_layout(wperm)
    sh["w_o"] = _kc_layout(f("w_o"))
    sh["w_ada"] = _kc_layout(f("w_ada"))
    sh["b_adaT"] = np.ascontiguousarray(f("b_ada").reshape(DEPTH, 24, 128).transpose(2, 0, 1))
    fm8 = lambda a: a.reshape(DEPTH, -1, 128).transpose(2, 0, 1)
    vec = np.zeros((128, DEPTH, NV), np.float32)
    vec[:, :, V_LNG:V_LNG + 8] = fm8(f("ln_g"))
    vec[:, :, V_LNB:V_LNB + 8] = fm8(f("ln_b"))
    vec[:, :, V_QG] = np.tile(f("q_norm_g"), (1, 2)).T
    vec[:, :, V_KG] = np.tile(f("k_norm_g"), (1, 2)).T
    cw = f("conv_w")
    for ct in range(2):
        for j in range(4):
            vec[:, :, V_CW + ct * 4 + j] = cw[:, j, ct * 128:(ct + 1) * 128].T
    vec[:, :, V_CB:V_CB + 2] = fm8(f("conv_b"))
    for nm, base in (("lru_br", V_BR), ("lru_bi", V_BI), ("lru_lam", V_LAM)):
        a = f(nm)
        for dd in range(2):
            for ct in range(2):
                vec[:, :, base + dd * 2 + ct] = a[:, dd, ct * 128:(ct + 1) * 128].T
    sh["vecs"] = vec
    an = np.stack([f("a_norm_g"), f("a_norm_b")], 1)
    sh["anorm"] = np.ascontiguousarray(np.broadcast_to(an[:, None], (DEPTH, 128, 2, 256)))
    bs = f("a_bs")
    ab = np.zeros((DEPTH, 128, 2, 128), np.float32)
    for ct in range(2):
        ab[:, 0:64, ct, :] = bs[:, 2 * ct, None, :]
        ab[:, 64:128, ct, :] = bs[:, 2 * ct + 1, None, :]
    sh["absb"] = ab
    sh["awsT"] = np.ascontiguousarray(f("a_ws").transpose(0, 3, 1, 2))
    lw = np.zeros((DEPTH, 128, 8, 128), np.float32)
    for dd in range(2):
        for gi, nm in enumerate(("lru_wr", "lru_wi")):
            w = f(nm)
            for ct in range(2):
                idx = (dd * 2 + gi) * 2 + ct
                for hb in range(2):
                    lw[:, hb * 64:(hb + 1) * 64, idx, hb * 64:(hb + 1) * 64] = w[:, dd, 2 * ct + hb]
    sh["lruW"] = lw
    sh["cst32"], sh["cst16"] = _consts()
    return sh


def prep_core(inp, b0, nseq):
    x = np.asarray(inp["x"], np.float32)
    ctx = np.asarray(inp["ctx"], np.float32)
    c = np.asarray(inp["c"], np.float32)
    c_ctx = np.asarray(inp["c_ctx"], np.float32)
    xin = np.empty((nseq, 128, 8, TOK), np.float32)
    for s in range(nseq):
        full = np.concatenate([x[b0 + s], ctx[b0 + s]], 0)
        xin[s] = full.T.reshape(8, 128, TOK).transpose(1, 0, 2)
    rows = [c[b0 + s] if s < nseq else c_ctx for s in range(2)] + [c_ctx]
    crow = np.stack(rows, 0)
    cT = np.ascontiguousarray(crow.T.reshape(8, 128, 3).transpose(1, 0, 2))
    return {"xin": xin, "cT": cT}


_PROG_CACHE = {}


def kernel(**inputs):
    n = 8
    nseq = 2
    key = ("full", nseq)
    if key not in _PROG_CACHE:
        _PROG_CACHE[key] = build_program(list(range(DEPTH)), nseq)
    nc = _PROG_CACHE[key]
    sh = prep_shared(inputs)
    in_maps = []
    for i in range(n):
        m = dict(sh)
        m.update(prep_core(inputs, i * nseq, nseq))
        in_maps.append(m)
    res = run_bass_kernel_spmd(nc, in_maps, core_ids=list(range(n)))
    out = np.empty((16, L_LAT, D), np.float32)
    for i in range(n):
        xo = np.asarray(res.results[i]["xout"])
        for s in range(nseq):
            out[i * nseq + s] = xo[s][:, :, 0:L_LAT].transpose(2, 1, 0).reshape(L_LAT, D)
    return out
```
